# Optimizing a Trainium2 kernel written in Bass

```python
import math
import jax, jax.numpy as jnp
from jax import lax
import numpy as np

D_MODEL = 1024
BATCH = 8
SEQ = 4096
DEPTH = 2

HEAD_DIM = 64
HY_WIDTH = D_MODEL // 2
HY_ORDER = 2
HY_FILT_HIDDEN = 64
HY_POS_BANDS = 16
HY_EMB = 1 + 2 * HY_POS_BANDS
HY_FAST_DECAY_PCT = 0.3
HY_SLOW_DECAY_PCT = 1.5
HY_DECAY_TARGET = 1e-2
WINDOWS = (128, 512, 2048)
DILATIONS = (1, 4, 16)
N_GROUPS = 3
HEADS_PER_GROUP = 4
N_ATT_HEADS = N_GROUPS * HEADS_PER_GROUP
ATT_WIDTH = N_ATT_HEADS * HEAD_DIM
ATT_OUT = HEADS_PER_GROUP * HEAD_DIM
REL_BUCKETS = 32
REL_MAX_DISTANCE = 1024
D_FF = (8 * D_MODEL // 3) // 128 * 128
N_BRANCH = 2
IN_WIDTH = 3 * HY_WIDTH + 3 * ATT_WIDTH
EPS = 1e-6
NEG = -1e30

kernel_name = "hybrid_hyena_dilated_attn_macaron"


def rms_norm(x, g):
    xf = x.astype(jnp.float32)
    y = xf * lax.rsqrt(jnp.mean(xf * xf, axis=-1, keepdims=True) + EPS)
    return (y * g.astype(jnp.float32)).astype(x.dtype)


def swiglu(h, w_gate, w_up, w_down):
    return (jax.nn.silu(h @ w_gate) * (h @ w_up)) @ w_down


def short_conv3(u, w, b):
    up = jnp.pad(u, ((0, 0), (1, 1), (0, 0)))
    return up[:, :-2] * w[0] + up[:, 1:-1] * w[1] + up[:, 2:] * w[2] + b


def hyena_filters(L, w1, b1, w2, b2, w3):
    f32 = jnp.float32
    t = jnp.linspace(0.0, 1.0, L, dtype=f32)[:, None]
    w = (2.0 * math.pi / L) * jnp.arange(L, dtype=f32)[:, None]
    f = jnp.linspace(1e-4, HY_POS_BANDS - 1, HY_POS_BANDS, dtype=f32)[None]
    z = jnp.concatenate([t, jnp.cos(f * w), -jnp.sin(f * w)], axis=-1)
    hid = jnp.sin(z @ w1.astype(f32) + b1.astype(f32))
    hid = jnp.sin(hid @ w2.astype(f32) + b2.astype(f32))
    k = (hid @ w3.astype(f32)).reshape(L, HY_ORDER, 2, HY_WIDTH)
    max_decay = math.log(HY_DECAY_TARGET) / HY_FAST_DECAY_PCT
    min_decay = math.log(HY_DECAY_TARGET) / HY_SLOW_DECAY_PCT
    deltas = jnp.abs(jnp.linspace(min_decay, max_decay, HY_WIDTH, dtype=f32))
    k = k * jnp.exp(-t * deltas[None])[:, None, None, :]
    full = jnp.concatenate([k[:, :, 0], jnp.zeros((1, HY_ORDER, HY_WIDTH), f32), k[:0:-1, :, 1]], axis=0)
    full = full * lax.rsqrt(jnp.sum(full * full, axis=0, keepdims=True) + EPS)
    return jnp.fft.rfft(full, axis=0)


def long_conv(u, kf):
    L = u.shape[1]
    U = jnp.fft.rfft(u.astype(jnp.float32), n=2 * L, axis=1)
    return jnp.fft.irfft(U * kf[None], n=2 * L, axis=1)[:, :L]


def t5_bucket(rel):
    half = REL_BUCKETS // 2
    exact = half // 2
    ret = jnp.where(rel > 0, half, 0)
    n = jnp.abs(rel)
    nf = jnp.maximum(n, 1).astype(jnp.float32)
    large = exact + (jnp.log(nf / exact) / math.log(REL_MAX_DISTANCE / exact) * (half - exact)).astype(jnp.int32)
    large = jnp.minimum(large, half - 1)
    return ret + jnp.where(n < exact, n, large)


def dilated_group_attn(q, k, v, bias_table, dilation, n_side):
    b, s, h, dh = q.shape
    m = s // dilation
    nb = -(-m // n_side)
    mp = nb * n_side
    bd = b * dilation

    def by_residue(t):
        return t.reshape(b, m, dilation, h, dh).transpose(0, 2, 1, 3, 4).reshape(bd, m, h, dh)

    def key_windows(t):
        tp = jnp.pad(t, ((0, 0), (n_side, n_side + mp - m), (0, 0), (0, 0))).reshape(bd, nb + 2, n_side, h, dh)
        return jnp.concatenate([tp[:, :-2], tp[:, 1:-1], tp[:, 2:]], axis=2)

    qb = jnp.pad(by_residue(q), ((0, 0), (0, mp - m), (0, 0), (0, 0))).reshape(bd, nb, n_side, h, dh)
    kw = key_windows(by_residue(k))
    vw = key_windows(by_residue(v))
    scores = jnp.einsum('bnqhd,bnkhd->bhnqk', qb.astype(jnp.float32), kw.astype(jnp.float32)) * (HEAD_DIM ** -0.5)
    qi = jnp.arange(n_side)
    ki = jnp.arange(3 * n_side)
    rel = ki[None, :] - n_side - qi[:, None]
    key_pos = jnp.arange(nb)[:, None] * n_side + ki[None, :] - n_side
    valid = (jnp.abs(rel) <= n_side)[None] & ((key_pos >= 0) & (key_pos < m))[:, None, :]
    bias = bias_table[t5_bucket(rel * dilation)].astype(jnp.float32).transpose(2, 0, 1)
    scores = jnp.where(valid, scores + bias[:, None], NEG)
    mx = jnp.max(scores, axis=-1, keepdims=True)
    p = jnp.exp(scores - mx)
    denom = jnp.sum(p, axis=-1, keepdims=True)
    out = jnp.einsum('bhnqk,bnkhd->bnqhd', p, vw.astype(jnp.float32)) / denom.transpose(0, 2, 3, 1, 4)
    lse = (mx + jnp.log(denom))[..., 0].transpose(0, 2, 3, 1)
    out = out.reshape(bd, mp, h, dh)[:, :m].reshape(b, dilation, m, h, dh).transpose(0, 2, 1, 3, 4).reshape(b, s, h, dh)
    lse = lse.reshape(bd, mp, h)[:, :m].reshape(b, dilation, m, h).transpose(0, 2, 1, 3).reshape(b, s, h)
    return out, lse


def hybrid_mixer(h, rel_bias, w_in, w_gate, b_gate, hy_conv_w, hy_conv_b, hy_filt_w1, hy_filt_b1,
                 hy_filt_w2, hy_filt_b2, hy_filt_w3, hy_skip, q_norm, k_norm, w_hy_proj, w_at_proj, w_out):
    b, s, _ = h.shape
    proj = h @ w_in
    hy = short_conv3(proj[..., :3 * HY_WIDTH], hy_conv_w, hy_conv_b)
    z, x1, x2 = jnp.split(hy, 3, axis=-1)
    kf = hyena_filters(s, hy_filt_w1, hy_filt_b1, hy_filt_w2, hy_filt_b2, hy_filt_w3)
    for o, gate in enumerate((x1, x2)):
        z = gate * (long_conv(z, kf[:, o]).astype(z.dtype) + hy_skip[o] * z)
    y_hy = z
    qkv = proj[..., 3 * HY_WIDTH:].reshape(b, s, 3, N_GROUPS, HEADS_PER_GROUP, HEAD_DIM)
    q = rms_norm(qkv[:, :, 0], q_norm)
    k = rms_norm(qkv[:, :, 1], k_norm)
    v = qkv[:, :, 2]
    outs, lses = [], []
    for g in range(N_GROUPS):
        n_side = WINDOWS[g] // (2 * DILATIONS[g])
        o_g, l_g = dilated_group_attn(q[:, :, g], k[:, :, g], v[:, :, g],
                                      rel_bias[:, g * HEADS_PER_GROUP:(g + 1) * HEADS_PER_GROUP],
                                      DILATIONS[g], n_side)
        outs.append(o_g)
        lses.append(l_g)
    alpha = jax.nn.softmax(jnp.stack(lses), axis=0)
    y_at = jnp.einsum('gbsh,gbshd->bshd', alpha, jnp.stack(outs)).reshape(b, s, ATT_OUT).astype(h.dtype)
    gates = jax.nn.sigmoid(h @ w_gate + b_gate).reshape(b, s, N_BRANCH, D_MODEL)
    y = gates[:, :, 0] * (y_hy @ w_hy_proj) + gates[:, :, 1] * (y_at @ w_at_proj)
    return y @ w_out


def setup_inputs(seed: int = 0) -> dict:
    key = jax.random.key(seed)
    ks = jax.random.split(key, 32)
    f32 = jnp.float32

    def nrm(k, shape, scale):
        return jax.random.normal(k, shape, f32) * scale

    def gain(k, shape):
        return 1.0 + 0.01 * jax.random.normal(k, shape, f32)

    L = DEPTH
    return {
        'x': nrm(ks[0], (BATCH, SEQ, D_MODEL), 1.0),
        'rel_bias': nrm(ks[1], (REL_BUCKETS, N_ATT_HEADS), 0.5),
        'ffn1_norm': gain(ks[2], (L, D_MODEL)),
        'ffn1_w_gate': nrm(ks[3], (L, D_MODEL, D_FF), D_MODEL ** -0.5),
        'ffn1_w_up': nrm(ks[4], (L, D_MODEL, D_FF), D_MODEL ** -0.5),
        'ffn1_w_down': nrm(ks[5], (L, D_FF, D_MODEL), D_FF ** -0.5),
        'mix_norm': gain(ks[6], (L, D_MODEL)),
        'w_in': nrm(ks[7], (L, D_MODEL, IN_WIDTH), D_MODEL ** -0.5),
        'w_gate': nrm(ks[8], (L, D_MODEL, N_BRANCH * D_MODEL), D_MODEL ** -0.5),
        'b_gate': nrm(ks[9], (L, N_BRANCH * D_MODEL), 0.01),
        'hy_conv_w': nrm(ks[10], (L, 3, 3 * HY_WIDTH), 3 ** -0.5),
        'hy_conv_b': nrm(ks[11], (L, 3 * HY_WIDTH), 0.01),
        'hy_filt_w1': nrm(ks[12], (L, HY_EMB, HY_FILT_HIDDEN), HY_EMB ** -0.5),
        'hy_filt_b1': nrm(ks[13], (L, HY_FILT_HIDDEN), 0.01),
        'hy_filt_w2': nrm(ks[14], (L, HY_FILT_HIDDEN, HY_FILT_HIDDEN), HY_FILT_HIDDEN ** -0.5),
        'hy_filt_b2': nrm(ks[15], (L, HY_FILT_HIDDEN), 0.01),
        'hy_filt_w3': nrm(ks[16], (L, HY_FILT_HIDDEN, HY_ORDER * 2 * HY_WIDTH), HY_FILT_HIDDEN ** -0.5),
        'hy_skip': nrm(ks[17], (L, HY_ORDER, HY_WIDTH), 1.0),
        'q_norm': gain(ks[18], (L, HEAD_DIM)),
        'k_norm': gain(ks[19], (L, HEAD_DIM)),
        'w_hy_proj': nrm(ks[20], (L, HY_WIDTH, D_MODEL), HY_WIDTH ** -0.5),
        'w_at_proj': nrm(ks[21], (L, ATT_OUT, D_MODEL), ATT_OUT ** -0.5),
        'w_out': nrm(ks[22], (L, D_MODEL, D_MODEL), D_MODEL ** -0.5),
        'ffn2_norm': gain(ks[23], (L, D_MODEL)),
        'ffn2_w_gate': nrm(ks[24], (L, D_MODEL, D_FF), D_MODEL ** -0.5),
        'ffn2_w_up': nrm(ks[25], (L, D_MODEL, D_FF), D_MODEL ** -0.5),
        'ffn2_w_down': nrm(ks[26], (L, D_FF, D_MODEL), D_FF ** -0.5),
    }


def reference(x, rel_bias, ffn1_norm, ffn1_w_gate, ffn1_w_up, ffn1_w_down, mix_norm, w_in, w_gate, b_gate,
              hy_conv_w, hy_conv_b, hy_filt_w1, hy_filt_b1, hy_filt_w2, hy_filt_b2, hy_filt_w3, hy_skip,
              q_norm, k_norm, w_hy_proj, w_at_proj, w_out, ffn2_norm, ffn2_w_gate, ffn2_w_up, ffn2_w_down):
    for l in range(DEPTH):
        x = x + 0.5 * swiglu(rms_norm(x, ffn1_norm[l]), ffn1_w_gate[l], ffn1_w_up[l], ffn1_w_down[l])
        h = rms_norm(x, mix_norm[l])
        x = x + hybrid_mixer(h, rel_bias, w_in[l], w_gate[l], b_gate[l], hy_conv_w[l], hy_conv_b[l],
                             hy_filt_w1[l], hy_filt_b1[l], hy_filt_w2[l], hy_filt_b2[l], hy_filt_w3[l],
                             hy_skip[l], q_norm[l], k_norm[l], w_hy_proj[l], w_at_proj[l], w_out[l])
        x = x + 0.5 * swiglu(rms_norm(x, ffn2_norm[l]), ffn2_w_gate[l], ffn2_w_up[l], ffn2_w_down[l])
    return x
```

```python
import math
from contextlib import ExitStack

import numpy as np
import ml_dtypes

import concourse.bass as bass
import concourse.mybir as mybir
from concourse.alu_op_type import AluOpType as ALU
from concourse.bass_utils import run_bass_kernel_spmd

F32 = mybir.dt.float32
BF16 = mybir.dt.bfloat16
AF = mybir.ActivationFunctionType

D_MODEL = 1024
BATCH = 8
SEQ = 4096
DEPTH = 2
HEAD_DIM = 64
HYW = 512
HY_EMB = 33
HID = 64
WINDOWS = (128, 512, 2048)
DILS = (1, 4, 16)
D_FF = 2688
NFC = D_FF // 128
IN_WIDTH = 3840
EPS = 1e-6
NFFT = 8192
NTB = SEQ // 128

WEIGHT_NAMES = ['rel_bias', 'ffn1_norm', 'ffn1_w_gate', 'ffn1_w_up', 'ffn1_w_down', 'mix_norm', 'w_in',
                'w_gate', 'b_gate', 'hy_conv_w', 'hy_conv_b', 'hy_filt_w1', 'hy_filt_b1', 'hy_filt_w2',
                'hy_filt_b2', 'hy_filt_w3', 'hy_skip', 'q_norm', 'k_norm', 'w_hy_proj', 'w_at_proj',
                'w_out', 'ffn2_norm', 'ffn2_w_gate', 'ffn2_w_up', 'ffn2_w_down']


class Reg:
    __slots__ = ('name', 'prev', 'writers', 'readers', 'slot', 'dcnt', 'lastdma')

    def __init__(self, name):
        self.name = name
        self.prev = []
        self.writers = []
        self.readers = []
        self.slot = None
        self.dcnt = 0
        self.lastdma = None


class Op:
    __slots__ = ('eng', 'fn', 'deps', 'is_dma', 'reg', 'dval', 'sig', 'sigval', 'slot')


class Prog:
    ENG = {'pe': 'tensor', 'act': 'scalar', 'dve': 'vector', 'pool': 'gpsimd', 'sp': 'sync'}

    def __init__(self, nc):
        self.nc = nc
        self.ops = []
        self.regs = []
        self.last = {}
        self.dma_regs = set()
        self.slot_cnt = []
        self.free_slots = []

    def reg(self, name):
        r = Reg(name)
        self.regs.append(r)
        return r

    def _add(self, eng, fn, reads, writes, conc, is_dma):
        i = len(self.ops)
        deps = set()
        for r in reads:
            deps.update(r.writers)
            if not r.writers:
                deps.update(r.prev)
        for w in writes:
            if conc:
                if w.readers:
                    w.prev = w.readers + w.writers
                    w.writers = []
                    w.readers = []
                deps.update(w.prev)
            else:
                deps.update(w.prev)
                deps.update(w.writers)
                deps.update(w.readers)
        for w in writes:
            if conc:
                w.writers.append(i)
            else:
                w.prev = []
                w.writers = [i]
                w.readers = []
        for r in reads:
            r.readers.append(i)
        deps.discard(i)
        o = Op()
        o.eng = eng
        o.fn = fn
        o.deps = deps
        o.is_dma = is_dma
        o.reg = None
        o.dval = 0
        o.sig = False
        o.sigval = 0
        o.slot = None
        if is_dma:
            w = writes[0]
            if w.slot is None:
                if self.free_slots:
                    w.slot = self.free_slots.pop()
                else:
                    w.slot = len(self.slot_cnt)
                    self.slot_cnt.append(0)
            self.slot_cnt[w.slot] += 16
            w.dcnt = self.slot_cnt[w.slot]
            o.reg = w
            o.slot = w.slot
            o.dval = w.dcnt
            w.lastdma = i
            self.dma_regs.add(w)
        else:
            self.last[eng] = i
        self.ops.append(o)
        return i

    def op(self, eng, fn, reads=(), writes=(), conc=False):
        return self._add(eng, fn, list(reads), list(writes), conc, False)

    def dma(self, q, out, in_, reads=(), writes=(), conc=False, **kw):
        nc = self.nc
        e = getattr(nc, self.ENG[q])

        def fn():
            return e.dma_start(out=out, in_=in_, **kw)
        assert len(writes) == 1
        return self._add(q, fn, list(reads), list(writes), conc, True)

    def barrier(self):
        deps = set(self.last.values())
        for r in self.dma_regs:
            if r.lastdma is not None:
                deps.add(r.lastdma)
        for e in self.ENG:
            o = Op()
            o.eng = e
            o.fn = None
            o.deps = set(deps)
            o.is_dma = False
            o.reg = None
            o.dval = 0
            o.sig = False
            o.sigval = 0
            o.slot = None
            self.ops.append(o)
        for r in self.regs:
            r.prev = []
            r.writers = []
            r.readers = []
        for r in self.dma_regs:
            self.free_slots.append(r.slot)
            r.slot = None
            r.lastdma = None
        self.dma_regs = set()

    def emit(self, es):
        nc = self.nc
        ops = self.ops
        for o in ops:
            if o.eng == 'pe' and not o.is_dma:
                o.deps = {d for d in o.deps if ops[d].is_dma or ops[d].eng != 'pe'}
            for d in o.deps:
                ops[d].sig = True
        esem = {e: es.enter_context(nc.semaphore("sem_" + e)) for e in self.ENG}
        cnt = {e: 0 for e in self.ENG}
        ssem = [es.enter_context(nc.semaphore("semslot%d" % i)) for i in range(len(self.slot_cnt))]
        for o in ops:
            if o.is_dma:
                pass
            elif o.sig and o.fn is not None:
                cnt[o.eng] += 1
                o.sigval = cnt[o.eng]
        known = {e: {} for e in self.ENG}
        nwait = 0
        for o in ops:
            e = o.eng
            eng = getattr(nc, self.ENG[e])
            need = {}
            for d in o.deps:
                od = ops[d]
                if od.is_dma:
                    key = ('r', od.slot)
                    sem = ssem[od.slot]
                    val = od.dval
                else:
                    if od.fn is None:
                        continue
                    key = ('e', od.eng)
                    sem = esem[od.eng]
                    val = od.sigval
                if need.get(key, (None, 0))[1] < val:
                    need[key] = (sem, val)
            kn = known[e]
            for key, (sem, val) in need.items():
                if kn.get(key, 0) < val:
                    eng.wait_ge(sem, val)
                    kn[key] = val
                    nwait += 1
            if o.fn is None:
                continue
            ins = o.fn()
            if o.is_dma:
                ins.then_inc(ssem[o.slot], 16)
            elif o.sig:
                ins.then_inc(esem[e], 1)
        return nwait


_CONST_CACHE = {}


def t5_bucket_np(rel):
    half = 16
    exact = 8
    ret = np.where(rel > 0, half, 0)
    n = np.abs(rel)
    nf = np.maximum(n, 1).astype(np.float32)
    large = exact + (np.log(nf / exact) / np.float32(math.log(1024 / exact)) * (half - exact)).astype(np.int32)
    large = np.minimum(large, half - 1)
    return ret + np.where(n < exact, n, large)


def host_constants():
    if _CONST_CACHE:
        return _CONST_CACHE
    c = {}
    L = SEQ
    n = np.arange(L, dtype=np.int64)
    kk = np.arange(NFFT // 2, dtype=np.int64)
    ph = (n[:, None] * (2 * kk[None, :] + 1)) % (2 * NFFT)
    ang = ph.astype(np.float64) * (2.0 * math.pi / (2 * NFFT))
    Fc = np.cos(ang)
    Fs = -np.sin(ang)
    Fm = np.concatenate([Fc, Fs], axis=1).astype(np.float32)
    del Fc, Fs, ang, ph
    Fh = Fm.reshape(32, 128, 64, 128).transpose(2, 1, 0, 3)
    c['Fh'] = np.ascontiguousarray(Fh).astype(ml_dtypes.bfloat16)
    Gh = (Fm * np.float32(2.0 / NFFT)).reshape(32, 128, 64, 128).transpose(0, 3, 2, 1)
    c['Gh'] = np.ascontiguousarray(Gh).astype(ml_dtypes.bfloat16)
    del Fm
    t = np.linspace(0.0, 1.0, L, dtype=np.float32)[:, None]
    w = (np.float32(2.0 * math.pi / L)) * np.arange(L, dtype=np.float32)[:, None]
    f = np.linspace(1e-4, 15, 16, dtype=np.float32)[None]
    z = np.concatenate([t, np.cos(f * w), -np.sin(f * w)], axis=-1).astype(np.float32)
    c['zfeat'] = np.ascontiguousarray(z.T)
    max_decay = math.log(1e-2) / 0.3
    min_decay = math.log(1e-2) / 1.5
    deltas = np.abs(np.linspace(min_decay, max_decay, HYW, dtype=np.float32))
    dec = np.exp(-t * deltas[None]).astype(np.float32)
    c['dec_f'] = dec
    decb = dec.copy()
    decb[0] = 0.0
    c['dec_b'] = decb
    c['ident'] = np.eye(128, dtype=np.float32).astype(ml_dtypes.bfloat16)
    ob = np.zeros((128, 128), np.float32)
    ob[:64, :64] = 1.0
    ob[64:, 64:] = 1.0
    c['ones_blk'] = ob
    c['ones_all'] = np.ones((128, 128), np.float32)
    c['jrev'] = np.ascontiguousarray(np.eye(128, dtype=np.float32)[::-1])
    oh = np.zeros((3, 32, 512), np.float32)
    bm = np.zeros((4, 512), np.float32)
    for g in range(3):
        for i in range(191, 320):
            rel = 255 - i
            b = int(t5_bucket_np(np.array([rel * DILS[g]]))[0])
            oh[g, b, i] = 1.0
    bm[:, 191:320] = 1.0
    c['oh'] = oh
    c['bm'] = bm
    _CONST_CACHE.update(c)
    return c


CONST_SPECS = [('Fh', [64, 128, 32, 128], BF16), ('Gh', [32, 128, 64, 128], BF16), ('zfeat', [33, SEQ], F32),
               ('dec_f', [SEQ, HYW], F32), ('dec_b', [SEQ, HYW], F32), ('ident', [128, 128], BF16),
               ('ones_blk', [128, 128], F32), ('ones_all', [128, 128], F32), ('jrev', [128, 128], F32), ('oh', [3, 32, 512], F32),
               ('bm', [4, 512], F32)]

WEIGHT_SHAPES = {
    'rel_bias': [32, 12], 'ffn1_norm': [2, 1024], 'ffn1_w_gate': [2, 1024, 2688], 'ffn1_w_up': [2, 1024, 2688],
    'ffn1_w_down': [2, 2688, 1024], 'mix_norm': [2, 1024], 'w_in': [2, 1024, 3840], 'w_gate': [2, 1024, 2048],
    'b_gate': [2, 2048], 'hy_conv_w': [2, 3, 1536], 'hy_conv_b': [2, 1536], 'hy_filt_w1': [2, 33, 64],
    'hy_filt_b1': [2, 64], 'hy_filt_w2': [2, 64, 64], 'hy_filt_b2': [2, 64], 'hy_filt_w3': [2, 64, 2048],
    'hy_skip': [2, 2, 512], 'q_norm': [2, 64], 'k_norm': [2, 64], 'w_hy_proj': [2, 512, 1024],
    'w_at_proj': [2, 256, 1024], 'w_out': [2, 1024, 1024], 'ffn2_norm': [2, 1024], 'ffn2_w_gate': [2, 1024, 2688],
    'ffn2_w_up': [2, 1024, 2688], 'ffn2_w_down': [2, 2688, 1024]}


def sl(start, n, step):
    return slice(start, start + (n - 1) * step + 1, step)


def bcast_rows(ap1d_tensor, offset, n, parts=128):
    return bass.AP(ap1d_tensor, offset, [[0, parts], [1, n]])


class Builder:
    def __init__(self, stages=None, dbg=False, mix_upto=None):
        self.mix_upto = mix_upto
        self.nc = bass.Bass("TRN2", target_bir_lowering=False)
        self.P = Prog(self.nc)
        self.stages = stages
        self.dbg = dbg
        self.uid = 0

    def sb(self, es, name, shape, dt):
        self.uid += 1
        t = es.enter_context(self.nc.sbuf_tensor("%s_%d" % (name, self.uid), shape, dt))
        return t, self.P.reg(name)

    def ps(self, es, name, shape, dt):
        self.uid += 1
        t = es.enter_context(self.nc.psum_tensor("%s_%d" % (name, self.uid), shape, dt))
        return t, self.P.reg(name)

    def dram(self, name, shape, dt, kind="Internal"):
        t = self.nc.dram_tensor(name, shape, dt, kind=kind)
        return t, self.P.reg(name)

    def declare(self):
        nc = self.nc
        self.x_t, self.x_r = self.dram("x", [SEQ, D_MODEL], F32, "ExternalInput")
        SK = "ExternalOutput" if self.dbg else "Internal"
        self.y_t, self.y_r = self.dram("y", [SEQ, D_MODEL], F32, "ExternalOutput")
        self.w = {}
        self.wr = self.P.reg("weights")
        for k in WEIGHT_NAMES:
            self.w[k] = nc.dram_tensor(k, WEIGHT_SHAPES[k], F32, kind="ExternalInput")
        self.c = {}
        for k, shp, dt in CONST_SPECS:
            self.c[k] = nc.dram_tensor(k, shp, dt, kind="ExternalInput")
        self.xs_t = []
        for i in range(6):
            self.xs_t.append(self.dram("xres%d" % i, [SEQ, D_MODEL], F32))
        self.hy_t, self.hy_r = self.dram("hy_s", [SEQ, 1536], F32, SK)
        self.z1_t, self.z1_r = self.dram("z1_s", [SEQ, HYW], F32, SK)
        self.qk_t, self.qk_r = self.dram("qk_s", [12, 128, SEQ], BF16, SK)
        self.gt_t, self.gt_r = self.dram("gt_s", [16, 128, SEQ], F32, SK)
        self.vd_t = []
        for g in range(3):
            D = DILS[g]
            m = SEQ // D
            self.vd_t.append(self.dram("vd_s%d" % g, [D, m + 128, 4, 128], BF16, SK))
        self.kf_t, self.kf_r = self.dram("kf_s", [2, 64, 128, HYW], F32, SK)
        self.att_t, self.att_r = self.dram("att_s", [3, SEQ, 4, 128], F32, SK)
        self.rv_t, self.rv_r = self.dram("rv_s", [12, 512], F32, SK)
        self.z2_t, self.z2_r = self.dram("z2_s", [SEQ, HYW], F32, SK)

    def norm_to_hT(self, es, src_t, src_r, tok0, ntb, gbc, hT, hT_r, hoff, tiles):
        nc, P = self.nc, self.P
        (xs, xs_rs, junk, junk_r, ss, ss_r, rs, rs_r, hb, hb_r, pst, pst_r, ident, ident_r) = tiles
        src = src_t.ap()
        for tb in range(ntb):
            xb = xs[:, tb % len(xs_rs), :]
            xr = xs_rs[tb % len(xs_rs)]
            r0 = tok0 + tb * 128
            P.dma('sp', xb, src[r0:r0 + 128, :], reads=[src_r], writes=[xr])
            b2 = tb % 2
            P.op('act', lambda xb=xb, tb=tb: nc.scalar.activation(out=junk[:], in_=xb, func=AF.Square,
                                                                  accum_out=ss[:, tb:tb + 1]),
                 reads=[xr], writes=[junk_r, ss_r[tb]])
            P.op('dve', lambda tb=tb: nc.vector.tensor_scalar(rs[:, tb:tb + 1], ss[:, tb:tb + 1], 1.0 / D_MODEL, EPS,
                                                              ALU.mult, ALU.add),
                 reads=[ss_r[tb]], writes=[rs_r[tb]])
            P.op('act', lambda tb=tb: nc.scalar.activation(out=rs[:, tb:tb + 1], in_=rs[:, tb:tb + 1], func=AF.Sqrt),
                 reads=[rs_r[tb]], writes=[rs_r[tb]])
            P.op('dve', lambda tb=tb: nc.vector.reciprocal(rs[:, tb:tb + 1], rs[:, tb:tb + 1]),
                 reads=[rs_r[tb]], writes=[rs_r[tb]])
            P.op('dve', lambda xb=xb, tb=tb, b2=b2: nc.vector.scalar_tensor_tensor(
                hb[b2][:], xb, rs[:, tb:tb + 1], gbc[0][:], ALU.mult, ALU.mult),
                reads=[xr, rs_r[tb], gbc[1]], writes=[hb_r[b2]])
            for kc in range(8):
                P.op('pe', lambda kc=kc, b2=b2: nc.tensor.transpose(pst[b2][:, kc, :], hb[b2][:, kc * 128:(kc + 1) * 128],
                                                                   ident[:]),
                     reads=[hb_r[b2], ident_r], writes=[pst_r[b2]])
            c0 = hoff + tb * 128
            P.op('act', lambda b2=b2, c0=c0: nc.scalar.copy(out=hT[:, :, c0:c0 + 128], in_=pst[b2][:]),
                 reads=[pst_r[b2]], writes=[hT_r], conc=True)

    def load_const_tiles(self, es):
        nc, P = self.nc, self.P
        ident, ident_r = self.sb(es, "ident", [128, 128], BF16)
        P.dma('sp', ident[:], self.c['ident'].ap(), reads=[self.wr], writes=[ident_r])
        return ident, ident_r

    def ffn_phase(self, l, which, src, dst):
        nc, P = self.nc, self.P
        src_t, src_r = src
        dst_t, dst_r = dst
        wg_d = self.w['ffn%d_w_gate' % which].ap()
        wu_d = self.w['ffn%d_w_up' % which].ap()
        wd_d = self.w['ffn%d_w_down' % which].ap()
        gn_t = self.w['ffn%d_norm' % which]
        T = 1024
        with ExitStack() as es:
            ident, ident_r = self.load_const_tiles(es)
            gbc = self.sb(es, "gbc", [128, D_MODEL], F32)
            P.dma('sp', gbc[0][:], bcast_rows(gn_t, l * D_MODEL, D_MODEL), reads=[self.wr], writes=[gbc[1]])
            xs, _ = self.sb(es, "xs", [128, 8, D_MODEL], F32)
            xs_rs = [P.reg("xs%d" % i) for i in range(8)]
            junk, junk_r = self.sb(es, "junk", [128, D_MODEL], F32)
            ss, _ = self.sb(es, "ss", [128, 8], F32)
            ss_r = [P.reg("ss%d" % i) for i in range(8)]
            rs, _ = self.sb(es, "rs", [128, 8], F32)
            rs_r = [P.reg("rs%d" % i) for i in range(8)]
            hb, hb_r = [], []
            pst, pst_r = [], []
            for i in range(2):
                a, b = self.sb(es, "hb", [128, D_MODEL], BF16)
                hb.append(a)
                hb_r.append(b)
                a, b = self.ps(es, "pst", [128, 8, 128], BF16)
                pst.append(a)
                pst_r.append(b)
            hT, hT_r = self.sb(es, "hT", [128, 8, T], BF16)
            act, act_r = self.sb(es, "act", [128, NFC, T], BF16)
            wd, wd_r = self.sb(es, "wd", [128, NFC, D_MODEL], BF16)
            wst, wst_r, wbf, wbf_r = [], [], [], []
            for i in range(4):
                a, b = self.sb(es, "wst", [128, 8, 128], F32)
                wst.append(a)
                wst_r.append(b)
                a, b = self.sb(es, "wbf", [128, 8, 128], BF16)
                wbf.append(a)
                wbf_r.append(b)
            wds, wds_r = [], []
            for i in range(2):
                a, b = self.sb(es, "wds", [128, D_MODEL], F32)
                wds.append(a)
                wds_r.append(b)
            sg, sg_r = [], []
            pg, pg_r, pu, pu_r, po, po_r = [], [], [], [], [], []
            for i in range(2):
                a, b = self.sb(es, "sg", [128, 512], F32)
                sg.append(a)
                sg_r.append(b)
                a, b = self.ps(es, "pg", [128, 512], F32)
                pg.append(a)
                pg_r.append(b)
                a, b = self.ps(es, "pu", [128, 512], F32)
                pu.append(a)
                pu_r.append(b)
                a, b = self.ps(es, "po", [128, 512], F32)
                po.append(a)
                po_r.append(b)
            for j in range(NFC):
                b2 = j % 2
                P.dma('sp', wds[b2][:], wd_d[l, j * 128:(j + 1) * 128, :], reads=[self.wr], writes=[wds_r[b2]])
                P.op('pool', lambda j=j, b2=b2: nc.gpsimd.tensor_copy(out=wd[:, j, :], in_=wds[b2][:]),
                     reads=[wds_r[b2]], writes=[wd_r], conc=True)
            tiles = (xs, xs_rs, junk, junk_r, ss, ss_r, rs, rs_r, hb, hb_r, pst, pst_r, ident, ident_r)
            src_ap = src_t.ap()
            dst_ap = dst_t.ap()
            for st in range(SEQ // T):
                self.norm_to_hT(es, src_t, src_r, st * T, 8, gbc, hT, hT_r, 0, tiles)
                for j in range(NFC):
                    bg = (2 * j) % 4
                    bu = (2 * j + 1) % 4
                    P.dma('sp', wst[bg][:], wg_d[l, :, j * 128:(j + 1) * 128].rearrange("(kc p) m -> p kc m", p=128),
                          reads=[self.wr], writes=[wst_r[bg]])
                    P.dma('sp', wst[bu][:], wu_d[l, :, j * 128:(j + 1) * 128].rearrange("(kc p) m -> p kc m", p=128),
                          reads=[self.wr], writes=[wst_r[bu]])
                    P.op('pool', lambda bg=bg: nc.gpsimd.tensor_copy(out=wbf[bg][:], in_=wst[bg][:]),
                         reads=[wst_r[bg]], writes=[wbf_r[bg]])
                    P.op('pool', lambda bu=bu: nc.gpsimd.tensor_copy(out=wbf[bu][:], in_=wst[bu][:]),
                         reads=[wst_r[bu]], writes=[wbf_r[bu]])
                    for th in range(2):
                        pb = (2 * j + th) % 2
                        for kc in range(8):
                            P.op('pe', lambda kc=kc, th=th, pb=pb, bg=bg: nc.tensor.matmul(
                                pg[pb][:], wbf[bg][:, kc, :], hT[:, kc, th * 512:(th + 1) * 512],
                                start=(kc == 0), stop=(kc == 7)),
                                reads=[wbf_r[bg], hT_r], writes=[pg_r[pb]])
                        for kc in range(8):
                            P.op('pe', lambda kc=kc, th=th, pb=pb, bu=bu: nc.tensor.matmul(
                                pu[pb][:], wbf[bu][:, kc, :], hT[:, kc, th * 512:(th + 1) * 512],
                                start=(kc == 0), stop=(kc == 7)),
                                reads=[wbf_r[bu], hT_r], writes=[pu_r[pb]])
                        P.op('act', lambda pb=pb: nc.scalar.activation(out=sg[pb][:], in_=pg[pb][:], func=AF.Silu),
                             reads=[pg_r[pb]], writes=[sg_r[pb]])
                        P.op('dve', lambda pb=pb, j=j, th=th: nc.vector.tensor_tensor(
                            act[:, j, th * 512:(th + 1) * 512], sg[pb][:], pu[pb][:], ALU.mult),
                            reads=[sg_r[pb], pu_r[pb]], writes=[act_r], conc=True)
                for tb in range(8):
                    for dh in range(2):
                        pb = (2 * tb + dh) % 2
                        for j in range(NFC):
                            P.op('pe', lambda j=j, tb=tb, dh=dh, pb=pb: nc.tensor.matmul(
                                po[pb][:], act[:, j, tb * 128:(tb + 1) * 128], wd[:, j, dh * 512:(dh + 1) * 512],
                                start=(j == 0), stop=(j == NFC - 1)),
                                reads=[act_r, wd_r], writes=[po_r[pb]])
                        P.op('dve', lambda tb=tb, dh=dh, pb=pb: nc.vector.scalar_tensor_tensor(
                            xs[:, tb, dh * 512:(dh + 1) * 512], po[pb][:], 0.5, xs[:, tb, dh * 512:(dh + 1) * 512],
                            ALU.mult, ALU.add),
                            reads=[po_r[pb], xs_rs[tb]], writes=[xs_rs[tb]])
                    r0 = st * T + tb * 128
                    P.dma('sp', dst_ap[r0:r0 + 128, :], xs[:, tb, :], reads=[xs_rs[tb]], writes=[dst_r], conc=True)
            P.barrier()

    def attn_bias_setup(self, es_glob):
        nc, P = self.nc, self.P
        self.EB, self.EB_r = self.sb(es_glob, "EB", [128, 12, 256], F32)
        with ExitStack() as es:
            tb_, tb_r = self.sb(es, "relb", [32, 12], F32)
            oh, oh_r = self.sb(es, "oh", [32, 3, 512], F32)
            bm, bm_r = self.sb(es, "bm", [4, 512], F32)
            ee, ee_r = self.sb(es, "ee", [4, 512], F32)
            pp, pp_r = self.ps(es, "pbias", [4, 512], F32)
            P.dma('sp', tb_[:], self.w['rel_bias'].ap(), reads=[self.wr], writes=[tb_r])
            P.dma('sp', oh[:], self.c['oh'].ap().rearrange("g b i -> b g i"), reads=[self.wr], writes=[oh_r])
            P.dma('sp', bm[:], self.c['bm'].ap(), reads=[self.wr], writes=[bm_r])
            for g in range(3):
                P.op('pe', lambda g=g: nc.tensor.matmul(pp[:], tb_[:, g * 4:(g + 1) * 4], oh[:, g, :], start=True, stop=True),
                     reads=[tb_r, oh_r], writes=[pp_r])
                P.op('act', lambda: nc.scalar.activation(out=ee[:], in_=pp[:], func=AF.Exp), reads=[pp_r], writes=[ee_r])
                P.op('dve', lambda: nc.vector.tensor_tensor(ee[:], ee[:], bm[:], ALU.mult), reads=[ee_r, bm_r], writes=[ee_r])
                P.dma('sp', self.rv_t.ap()[g * 4:(g + 1) * 4, :], ee[:], reads=[ee_r], writes=[self.rv_r], conc=True)
            ebr, ebr_r = self.sb(es, "ebr", [128, 12, 256], F32)
            jrev, jrev_r = self.sb(es, "jrev", [128, 128], F32)
            pj, pj_r = self.ps(es, "pj", [128, 512], F32)
            P.dma('sp', jrev[:], self.c['jrev'].ap(), reads=[self.wr], writes=[jrev_r])
            for h in range(12):
                srcA = bass.AP(self.rv_t, h * 512 + 192, [[1, 128], [1, 128]])
                srcB = bass.AP(self.rv_t, h * 512 + 64, [[1, 128], [1, 128]])
                P.dma('sp', ebr[:, h, 0:128], srcA, reads=[self.rv_r], writes=[ebr_r], conc=True)
                P.dma('sp', ebr[:, h, 128:256], srcB, reads=[self.rv_r], writes=[ebr_r], conc=True)
            for hp in range(6):
                P.op('pe', lambda hp=hp: nc.tensor.matmul(pj[:], jrev[:], ebr[:, 2 * hp:2 * hp + 2, :].rearrange("p h e -> p (h e)"),
                                                          start=True, stop=True),
                     reads=[jrev_r, ebr_r], writes=[pj_r])
                P.op('act', lambda hp=hp: nc.scalar.copy(out=self.EB[:, 2 * hp:2 * hp + 2, :].rearrange("p h e -> p (h e)"),
                                                         in_=pj[:]),
                     reads=[pj_r], writes=[self.EB_r], conc=True)
            P.barrier()

    def mix_proj(self, l, src):
        nc, P = self.nc, self.P
        src_t, src_r = src
        u, u_r = self.u, self.u_r
        w_in = self.w['w_in'].ap()
        HP = SEQ + 2
        with ExitStack() as es:
            ident, ident_r = self.load_const_tiles(es)
            gbc = self.sb(es, "gbc", [128, D_MODEL], F32)
            P.dma('sp', gbc[0][:], bcast_rows(self.w['mix_norm'], l * D_MODEL, D_MODEL), reads=[self.wr], writes=[gbc[1]])
            hT, hT_r = self.sb(es, "hTm", [128, 8, HP], BF16)
            P.op('dve', lambda: nc.vector.memset(hT[:, :, 0:1], 0.0), writes=[hT_r], conc=True)
            P.op('dve', lambda: nc.vector.memset(hT[:, :, HP - 1:HP], 0.0), writes=[hT_r], conc=True)
            with ExitStack() as es2:
                xs, _ = self.sb(es2, "xs", [128, 2, D_MODEL], F32)
                xs_rs = [P.reg("xsm%d" % i) for i in range(2)]
                junk, junk_r = self.sb(es2, "junk", [128, D_MODEL], F32)
                ss, _ = self.sb(es2, "ss", [128, NTB], F32)
                ss_r = [P.reg("ssm%d" % i) for i in range(NTB)]
                rs, _ = self.sb(es2, "rs", [128, NTB], F32)
                rs_r = [P.reg("rsm%d" % i) for i in range(NTB)]
                hb, hb_r, pst, pst_r = [], [], [], []
                for i in range(2):
                    a, b = self.sb(es2, "hb", [128, D_MODEL], BF16)
                    hb.append(a)
                    hb_r.append(b)
                    a, b = self.ps(es2, "pst", [128, 8, 128], BF16)
                    pst.append(a)
                    pst_r.append(b)
                tiles = (xs, xs_rs, junk, junk_r, ss, ss_r, rs, rs_r, hb, hb_r, pst, pst_r, ident, ident_r)
                self.norm_to_hT(es2, src_t, src_r, 0, NTB, gbc, hT, hT_r, 1, tiles)
                P.barrier()
            pA, pA_r, pS, pS_r = [], [], [], []
            for i in range(2):
                a, b = self.ps(es, "pA", [128, 512], F32)
                pA.append(a)
                pA_r.append(b)
                a, b = self.ps(es, "pS", [128, 512], F32)
                pS.append(a)
                pS_r.append(b)
            with ExitStack() as es2:
                wsth, wsth_r = self.sb(es2, "wsth", [128, 8, 512], F32)
                cwb, cwb_r = self.sb(es2, "cwb", [128, 3, 512], F32)
                bbc, bbc_r = self.sb(es2, "bbc", [128, 512], F32)
                wj, wj_r = [], []
                for j in range(3):
                    a, b = self.sb(es2, "wj", [128, 8, 512], BF16)
                    wj.append(a)
                    wj_r.append(b)
                ho, ho_r = [], []
                for i in range(2):
                    a, b = self.sb(es2, "ho", [128, 512], F32)
                    ho.append(a)
                    ho_r.append(b)
                for cg in range(3):
                    P.dma('sp', wsth[:], w_in[l, :, cg * 512:(cg + 1) * 512].rearrange("(kc p) m -> p kc m", p=128),
                          reads=[self.wr], writes=[wsth_r])
                    for j in range(3):
                        P.dma('sp', cwb[:, j, :], bcast_rows(self.w['hy_conv_w'], (l * 3 + j) * 1536 + cg * 512, 512),
                              reads=[self.wr], writes=[cwb_r], conc=True)
                    P.dma('sp', bbc[:], bcast_rows(self.w['hy_conv_b'], l * 1536 + cg * 512, 512), reads=[self.wr],
                          writes=[bbc_r])
                    for j in range(3):
                        for kc in range(8):
                            P.op('pool', lambda j=j, kc=kc: nc.gpsimd.tensor_tensor(wj[j][:, kc, :], wsth[:, kc, :], cwb[:, j, :],
                                                                                   ALU.mult),
                                 reads=[wsth_r, cwb_r], writes=[wj_r[j]], conc=True)
                    for tb in range(NTB):
                        pb = tb % 2
                        n = 0
                        for j in range(3):
                            for kc in range(8):
                                c0 = 1 + tb * 128 + (j - 1)
                                P.op('pe', lambda j=j, kc=kc, c0=c0, pb=pb, n=n: nc.tensor.matmul(
                                    pA[pb][:], hT[:, kc, c0:c0 + 128], wj[j][:, kc, :], start=(n == 0), stop=(n == 23)),
                                    reads=[hT_r, wj_r[j]], writes=[pA_r[pb]])
                                n += 1
                        P.op('dve', lambda pb=pb: nc.vector.tensor_tensor(ho[pb][:], pA[pb][:], bbc[:], ALU.add),
                             reads=[pA_r[pb], bbc_r], writes=[ho_r[pb]])
                        P.dma('sp', self.hy_t.ap()[tb * 128:(tb + 1) * 128, cg * 512:(cg + 1) * 512], ho[pb][:],
                              reads=[ho_r[pb]], writes=[self.hy_r], conc=True)
                        if cg == 0:
                            P.op('act', lambda pb=pb, tb=tb: nc.scalar.copy(out=u[:, tb, :], in_=ho[pb][:]),
                                 reads=[ho_r[pb]], writes=[u_r], conc=True)
                P.barrier()
            with ExitStack() as es2:
                wst, wst_r, wbf, wbf_r = [], [], [], []
                for i in range(2):
                    a, b = self.sb(es2, "wstq", [128, 8, 128], F32)
                    wst.append(a)
                    wst_r.append(b)
                    a, b = self.sb(es2, "wbfq", [128, 8, 128], BF16)
                    wbf.append(a)
                    wbf_r.append(b)
                onesb, onesb_r = self.sb(es2, "onesb", [128, 128], F32)
                P.dma('sp', onesb[:], self.c['ones_blk'].ap(), reads=[self.wr], writes=[onesb_r])
                gq, gq_r = self.sb(es2, "gq", [128, 1], F32)
                gk, gk_r = self.sb(es2, "gk", [128, 1], F32)
                for hlf in range(2):
                    P.dma('sp', gq[hlf * 64:(hlf + 1) * 64, :], bass.AP(self.w['q_norm'], l * 64, [[1, 64], [1, 1]]),
                          reads=[self.wr], writes=[gq_r], conc=True)
                    P.dma('sp', gk[hlf * 64:(hlf + 1) * 64, :], bass.AP(self.w['k_norm'], l * 64, [[1, 64], [1, 1]]),
                          reads=[self.wr], writes=[gk_r], conc=True)
                P.op('dve', lambda: nc.vector.tensor_scalar(gq[:], gq[:], 0.125, None, ALU.mult), reads=[gq_r], writes=[gq_r])
                bgt, bgt_r = self.sb(es2, "bgt", [128, 16], F32)
                P.dma('sp', bgt[:], bass.AP(self.w['b_gate'], l * 2048, [[1, 128], [128, 16]]), reads=[self.wr], writes=[bgt_r],
                      allow_slow_non_contiguous=True)
                qf, qf_r, sq, sq_r, rr, rr_r, qn, qn_r, gs, gs_r = [], [], [], [], [], [], [], [], [], []
                for i in range(2):
                    for lst, lstr, nm, dt in ((qf, qf_r, "qf", F32), (sq, sq_r, "sq", F32), (rr, rr_r, "rr", F32),
                                              (qn, qn_r, "qn", BF16), (gs, gs_r, "gs", F32)):
                        a, b = self.sb(es2, nm, [128, 512], dt)
                        lst.append(a)
                        lstr.append(b)
                it = 0
                for c in range(12):
                    wb = c % 2
                    col0 = 1536 + c * 128
                    P.dma('sp', wst[wb][:], w_in[l, :, col0:col0 + 128].rearrange("(kc p) m -> p kc m", p=128),
                          reads=[self.wr], writes=[wst_r[wb]])
                    P.op('pool', lambda wb=wb: nc.gpsimd.tensor_copy(out=wbf[wb][:], in_=wst[wb][:]),
                         reads=[wst_r[wb]], writes=[wbf_r[wb]])
                    gain, gain_r = (gq, gq_r) if c < 6 else (gk, gk_r)
                    for tq in range(8):
                        pb = it % 2
                        it += 1
                        for kc in range(8):
                            P.op('pe', lambda kc=kc, tq=tq, pb=pb, wb=wb: nc.tensor.matmul(
                                pA[pb][:], wbf[wb][:, kc, :], hT[:, kc, 1 + tq * 512:1 + (tq + 1) * 512],
                                start=(kc == 0), stop=(kc == 7)),
                                reads=[wbf_r[wb], hT_r], writes=[pA_r[pb]])
                        P.op('act', lambda pb=pb: nc.scalar.copy(out=qf[pb][:], in_=pA[pb][:]), reads=[pA_r[pb]], writes=[qf_r[pb]])
                        P.op('act', lambda pb=pb: nc.scalar.activation(out=sq[pb][:], in_=pA[pb][:], func=AF.Square),
                             reads=[pA_r[pb]], writes=[sq_r[pb]])
                        P.op('pe', lambda pb=pb: nc.tensor.matmul(pS[pb][:], onesb[:], sq[pb][:], start=True, stop=True),
                             reads=[onesb_r, sq_r[pb]], writes=[pS_r[pb]])
                        P.op('act', lambda pb=pb: nc.scalar.activation(out=rr[pb][:], in_=pS[pb][:], func=AF.Ln,
                                                                       scale=1.0 / 64, bias=self.eps_t[:]),
                             reads=[pS_r[pb], self.eps_r], writes=[rr_r[pb]])
                        P.op('act', lambda pb=pb: nc.scalar.activation(out=rr[pb][:], in_=rr[pb][:], func=AF.Exp, scale=-0.5),
                             reads=[rr_r[pb]], writes=[rr_r[pb]])
                        P.op('dve', lambda pb=pb, gain=gain: nc.vector.scalar_tensor_tensor(
                            qn[pb][:], qf[pb][:], gain[:], rr[pb][:], ALU.mult, ALU.mult),
                            reads=[qf_r[pb], gain_r, rr_r[pb]], writes=[qn_r[pb]])
                        P.dma('sp', self.qk_t.ap()[c, :, tq * 512:(tq + 1) * 512], qn[pb][:], reads=[qn_r[pb]],
                              writes=[self.qk_r], conc=True)
                w_gate = self.w['w_gate'].ap()
                for c in range(16):
                    wb = c % 2
                    P.dma('sp', wst[wb][:], w_gate[l, :, c * 128:(c + 1) * 128].rearrange("(kc p) m -> p kc m", p=128),
                          reads=[self.wr], writes=[wst_r[wb]])
                    P.op('pool', lambda wb=wb: nc.gpsimd.tensor_copy(out=wbf[wb][:], in_=wst[wb][:]),
                         reads=[wst_r[wb]], writes=[wbf_r[wb]])
                    for tq in range(8):
                        pb = it % 2
                        it += 1
                        for kc in range(8):
                            P.op('pe', lambda kc=kc, tq=tq, pb=pb, wb=wb: nc.tensor.matmul(
                                pA[pb][:], wbf[wb][:, kc, :], hT[:, kc, 1 + tq * 512:1 + (tq + 1) * 512],
                                start=(kc == 0), stop=(kc == 7)),
                                reads=[wbf_r[wb], hT_r], writes=[pA_r[pb]])
                        P.op('act', lambda pb=pb, c=c: nc.scalar.activation(out=gs[pb][:], in_=pA[pb][:], func=AF.Sigmoid,
                                                                            bias=bgt[:, c:c + 1]),
                             reads=[pA_r[pb], bgt_r], writes=[gs_r[pb]])
                        P.dma('sp', self.gt_t.ap()[c, :, tq * 512:(tq + 1) * 512], gs[pb][:], reads=[gs_r[pb]],
                              writes=[self.gt_r], conc=True)
                P.barrier()
            with ExitStack() as es2:
                wsv, wsv_r = self.sb(es2, "wsv", [128, 8, 256], F32)
                wbv, wbv_r = self.sb(es2, "wbv", [128, 8, 256], BF16)
                zt, zt_r = self.sb(es2, "zt", [64, 512], BF16)
                P.op('dve', lambda: nc.vector.memset(zt[:], 0.0), writes=[zt_r])
                vst, vst_r = [], []
                for i in range(2):
                    a, b = self.sb(es2, "vst", [128, 4, 128], BF16)
                    vst.append(a)
                    vst_r.append(b)
                    P.op('dve', lambda a=a: nc.vector.memset(a[:, :, 64:128], 1.0), writes=[b])
                it = 0
                for g in range(3):
                    D = DILS[g]
                    m = SEQ // D
                    vd_t, vd_r = self.vd_t[g]
                    col0 = 3072 + g * 256
                    P.dma('sp', wsv[:], w_in[l, :, col0:col0 + 256].rearrange("(kc p) m -> p kc m", p=128),
                          reads=[self.wr], writes=[wsv_r])
                    P.op('pool', lambda: nc.gpsimd.tensor_copy(out=wbv[:], in_=wsv[:]), reads=[wsv_r], writes=[wbv_r])
                    for r in range(D):
                        P.dma('sp', vd_t.ap()[r, 0:64, :, :].rearrange("t h e -> t (h e)"), zt[:], reads=[zt_r], writes=[vd_r],
                              conc=True)
                        P.dma('sp', vd_t.ap()[r, m + 64:m + 128, :, :].rearrange("t h e -> t (h e)"), zt[:], reads=[zt_r],
                              writes=[vd_r], conc=True)
                        for b in range(m // 128):
                            pb = it % 2
                            it += 1
                            t0 = 1 + r + D * 128 * b
                            for kc in range(8):
                                P.op('pe', lambda kc=kc, t0=t0, D=D, pb=pb: nc.tensor.matmul(
                                    pA[pb][:, 0:256], hT[:, kc, sl(t0, 128, D)], wbv[:, kc, :], start=(kc == 0), stop=(kc == 7)),
                                    reads=[hT_r, wbv_r], writes=[pA_r[pb]])
                            P.op('act', lambda pb=pb: nc.scalar.copy(
                                out=vst[pb][:, :, 0:64], in_=pA[pb][:, 0:256].rearrange("p (h e) -> p h e", h=4)),
                                reads=[pA_r[pb]], writes=[vst_r[pb]])
                            P.dma('sp', vd_t.ap()[r, 64 + 128 * b:64 + 128 * (b + 1), :, :], vst[pb][:], reads=[vst_r[pb]],
                                  writes=[vd_r], conc=True)
                P.barrier()

    def mix_filters(self, l):
        nc, P = self.nc, self.P
        with ExitStack() as es:
            es_h = ExitStack()
            h2, h2_r = self.sb(es, "h2", [HID, SEQ], F32)
            w3, w3_r = self.sb(es, "w3", [HID, 2048], F32)
            ph, ph_r = [], []
            for i in range(2):
                a, b = self.ps(es, "ph", [128, 512], F32)
                ph.append(a)
                ph_r.append(b)
            zT, zT_r = self.sb(es_h, "zT", [HY_EMB, SEQ], F32)
            P.dma('sp', zT[:], self.c['zfeat'].ap(), reads=[self.wr], writes=[zT_r])
            w1, w1_r = self.sb(es_h, "w1", [HY_EMB, HID], F32)
            w2, w2_r = self.sb(es_h, "w2", [HID, HID], F32)
            P.dma('sp', w1[:], self.w['hy_filt_w1'].ap()[l], reads=[self.wr], writes=[w1_r])
            P.dma('sp', w2[:], self.w['hy_filt_w2'].ap()[l], reads=[self.wr], writes=[w2_r])
            P.dma('sp', w3[:], self.w['hy_filt_w3'].ap()[l], reads=[self.wr], writes=[w3_r])
            bq, bq_r = self.sb(es_h, "bq", [HID, 4], F32)
            P.dma('sp', bq[:, 0:1], bass.AP(self.w['hy_filt_b1'], l * 64, [[1, 64], [1, 1]]), reads=[self.wr], writes=[bq_r],
                  conc=True)
            P.dma('sp', bq[:, 2:3], bass.AP(self.w['hy_filt_b2'], l * 64, [[1, 64], [1, 1]]), reads=[self.wr], writes=[bq_r],
                  conc=True)
            for c in (0, 2):
                P.op('dve', lambda c=c: nc.vector.tensor_scalar(bq[:, c:c + 1], bq[:, c:c + 1], 0.25, None, ALU.mult),
                     reads=[bq_r], writes=[bq_r])
                P.op('dve', lambda c=c: nc.vector.tensor_scalar(bq[:, c + 1:c + 2], bq[:, c:c + 1], math.pi / 2, None, ALU.add),
                     reads=[bq_r], writes=[bq_r])
            h1, h1_r = self.sb(es_h, "h1", [HID, SEQ], F32)
            s4, s4_r = self.sb(es_h, "s4", [HID, 512], F32)
            c4, c4_r = self.sb(es_h, "c4", [HID, 512], F32)
            tt, tt_r = self.sb(es_h, "tt", [HID, 512], F32)
            for layer in range(2):
                wt, wt_r, K_, src, src_r, dstt, dst_r = ((w1, w1_r, HY_EMB, zT, zT_r, h1, h1_r) if layer == 0 else
                                                        (w2, w2_r, HID, h1, h1_r, h2, h2_r))
                bc = 2 * layer
                for tq in range(8):
                    pb = tq % 2
                    P.op('pe', lambda tq=tq, pb=pb, wt=wt, src=src, K_=K_: nc.tensor.matmul(
                        ph[pb][0:HID, :], wt[0:K_, :], src[0:K_, tq * 512:(tq + 1) * 512], start=True, stop=True),
                        reads=[wt_r, src_r], writes=[ph_r[pb]])
                    P.op('act', lambda pb=pb, bc=bc: nc.scalar.activation(out=s4[:], in_=ph[pb][0:HID, :], func=AF.Sin, scale=0.25,
                                                                          bias=bq[:, bc:bc + 1]),
                         reads=[ph_r[pb], bq_r], writes=[s4_r])
                    P.op('act', lambda pb=pb, bc=bc: nc.scalar.activation(out=c4[:], in_=ph[pb][0:HID, :], func=AF.Sin, scale=0.25,
                                                                          bias=bq[:, bc + 1:bc + 2]),
                         reads=[ph_r[pb], bq_r], writes=[c4_r])
                    P.op('dve', lambda: nc.vector.tensor_tensor(tt[:], s4[:], c4[:], ALU.mult), reads=[s4_r, c4_r], writes=[tt_r])
                    P.op('dve', lambda: nc.vector.tensor_tensor(c4[:], s4[:], s4[:], ALU.mult), reads=[s4_r], writes=[c4_r])
                    P.op('dve', lambda: nc.vector.tensor_scalar(c4[:], c4[:], -8.0, 4.0, ALU.mult, ALU.add), reads=[c4_r],
                         writes=[c4_r])
                    P.op('dve', lambda tq=tq, dstt=dstt: nc.vector.tensor_tensor(dstt[:, tq * 512:(tq + 1) * 512], tt[:], c4[:],
                                                                                ALU.mult),
                         reads=[tt_r, c4_r], writes=[dst_r], conc=True)
            P.barrier()
            es_h.close()
            onesa, onesa_r = self.sb(es, "onesa", [128, 128], F32)
            P.dma('sp', onesa[:], self.c['ones_all'].ap(), reads=[self.wr], writes=[onesa_r])
            a_t, a_r = self.sb(es, "a_t", [128, NTB, 512], BF16)
            b_t, b_r = self.sb(es, "b_t", [128, NTB, 512], BF16)
            pq, pq_r = self.ps(es, "pq", [128, 512], F32)
            pk, pk_r = [], []
            for i in range(2):
                a, b = self.ps(es, "pk", [128, 512], F32)
                pk.append(a)
                pk_r.append(b)
            dfb, dfb_r, dbb, dbb_r, kfb, kfb_r, kbb, kbb_r, sqf, sqf_r, sqb, sqb_r = ([] for _ in range(12))
            for i in range(2):
                for lst, lstr, nm in ((dfb, dfb_r, "dfb"), (dbb, dbb_r, "dbb"), (kfb, kfb_r, "kfb"), (kbb, kbb_r, "kbb"),
                                      (sqf, sqf_r, "sqf"), (sqb, sqb_r, "sqb")):
                    a, b = self.sb(es, nm, [128, 512], F32)
                    lst.append(a)
                    lstr.append(b)
            nrm, nrm_r = self.sb(es, "nrm", [128, 512], F32)
            ft, ft_r = [], []
            for i in range(2):
                a, b = self.sb(es, "ft", [128, NTB, 128], BF16)
                ft.append(a)
                ft_r.append(b)
            ko, ko_r = [], []
            for i in range(2):
                a, b = self.sb(es, "ko", [128, 512], F32)
                ko.append(a)
                ko_r.append(b)
            for o in range(2):
                cf = o * 1024
                cb = o * 1024 + 512
                for tb in range(NTB):
                    pb = tb % 2
                    P.dma('sp', dfb[pb][:], self.c['dec_f'].ap()[tb * 128:(tb + 1) * 128, :], reads=[self.wr], writes=[dfb_r[pb]])
                    P.dma('sp', dbb[pb][:], self.c['dec_b'].ap()[tb * 128:(tb + 1) * 128, :], reads=[self.wr], writes=[dbb_r[pb]])
                    P.op('pe', lambda tb=tb, pb=pb, cf=cf: nc.tensor.matmul(ph[pb][:], h2[:, tb * 128:(tb + 1) * 128],
                                                                           w3[:, cf:cf + 512], start=True, stop=True),
                         reads=[h2_r, w3_r], writes=[ph_r[pb]])
                    P.op('pe', lambda tb=tb, pb=pb, cb=cb: nc.tensor.matmul(pk[pb][:], h2[:, tb * 128:(tb + 1) * 128],
                                                                           w3[:, cb:cb + 512], start=True, stop=True),
                         reads=[h2_r, w3_r], writes=[pk_r[pb]])
                    P.op('dve', lambda pb=pb: nc.vector.tensor_tensor(kfb[pb][:], ph[pb][:], dfb[pb][:], ALU.mult),
                         reads=[ph_r[pb], dfb_r[pb]], writes=[kfb_r[pb]])
                    P.op('dve', lambda pb=pb: nc.vector.tensor_tensor(kbb[pb][:], pk[pb][:], dbb[pb][:], ALU.mult),
                         reads=[pk_r[pb], dbb_r[pb]], writes=[kbb_r[pb]])
                    P.op('act', lambda pb=pb: nc.scalar.activation(out=sqf[pb][:], in_=kfb[pb][:], func=AF.Square),
                         reads=[kfb_r[pb]], writes=[sqf_r[pb]])
                    P.op('act', lambda pb=pb: nc.scalar.activation(out=sqb[pb][:], in_=kbb[pb][:], func=AF.Square),
                         reads=[kbb_r[pb]], writes=[sqb_r[pb]])
                    P.op('pe', lambda pb=pb, tb=tb: nc.tensor.matmul(pq[:], onesa[:], sqf[pb][:], start=(tb == 0), stop=False),
                         reads=[onesa_r, sqf_r[pb]], writes=[pq_r])
                    P.op('pe', lambda pb=pb, tb=tb: nc.tensor.matmul(pq[:], onesa[:], sqb[pb][:], start=False,
                                                                     stop=(tb == NTB - 1)),
                         reads=[onesa_r, sqb_r[pb]], writes=[pq_r])
                    P.op('pool', lambda pb=pb, tb=tb: nc.gpsimd.tensor_tensor(a_t[:, tb, :], kfb[pb][:], kbb[pb][:], ALU.add),
                         reads=[kfb_r[pb], kbb_r[pb]], writes=[a_r], conc=True)
                    P.op('pool', lambda pb=pb, tb=tb: nc.gpsimd.tensor_tensor(b_t[:, tb, :], kfb[pb][:], kbb[pb][:], ALU.subtract),
                         reads=[kfb_r[pb], kbb_r[pb]], writes=[b_r], conc=True)
                P.op('act', lambda: nc.scalar.activation(out=nrm[:], in_=pq[:], func=AF.Ln, bias=self.eps_t[:]),
                     reads=[pq_r, self.eps_r], writes=[nrm_r])
                P.op('act', lambda: nc.scalar.activation(out=nrm[:], in_=nrm[:], func=AF.Exp, scale=-0.5), reads=[nrm_r],
                     writes=[nrm_r])
                for s in range(64):
                    pb = s % 2
                    P.dma('sp', ft[pb][:], self.c['Fh'].ap()[s], reads=[self.wr], writes=[ft_r[pb]])
                    srct, srcr = (a_t, a_r) if s < 32 else (b_t, b_r)
                    for nb in range(NTB):
                        P.op('pe', lambda nb=nb, pb=pb, srct=srct: nc.tensor.matmul(pk[pb][:], ft[pb][:, nb, :], srct[:, nb, :],
                                                                                  start=(nb == 0), stop=(nb == NTB - 1)),
                             reads=[ft_r[pb], srcr], writes=[pk_r[pb]])
                    P.op('dve', lambda pb=pb: nc.vector.tensor_tensor(ko[pb][:], pk[pb][:], nrm[:], ALU.mult),
                         reads=[pk_r[pb], nrm_r], writes=[ko_r[pb]])
                    P.dma('sp', self.kf_t.ap()[o, s], ko[pb][:], reads=[ko_r[pb]], writes=[self.kf_r], conc=True)
            P.barrier()

    def mix_conv(self, l):
        nc, P = self.nc, self.P
        u, u_r = self.u, self.u_r
        with ExitStack() as es:
            Y, Y_r = self.sb(es, "Y", [128, 64, 512], BF16)
            ftr, ftr_r, fti, fti_r, kre, kre_r, kim, kim_r = ([] for _ in range(8))
            pre, pre_r, pim, pim_r = [], [], [], []
            for i in range(2):
                for lst, lstr, nm in ((ftr, ftr_r, "ftr"), (fti, fti_r, "fti")):
                    a, b = self.sb(es, nm, [128, NTB, 128], BF16)
                    lst.append(a)
                    lstr.append(b)
                for lst, lstr, nm in ((kre, kre_r, "kre"), (kim, kim_r, "kim")):
                    a, b = self.sb(es, nm, [128, 512], F32)
                    lst.append(a)
                    lstr.append(b)
                a, b = self.ps(es, "pre", [128, 512], F32)
                pre.append(a)
                pre_r.append(b)
                a, b = self.ps(es, "pim", [128, 512], F32)
                pim.append(a)
                pim_r.append(b)
            t1, t1_r = self.sb(es, "t1", [128, 512], F32)
            t2, t2_r = self.sb(es, "t2", [128, 512], F32)
            gt, gt_r = [], []
            for i in range(2):
                a, b = self.sb(es, "gtile", [128, 64, 128], BF16)
                gt.append(a)
                gt_r.append(b)
            pc, pc_r, gate, gate_r, zo, zo_r, zn, zn_r = ([] for _ in range(8))
            for i in range(2):
                a, b = self.ps(es, "pc", [128, 512], F32)
                pc.append(a)
                pc_r.append(b)
                for lst, lstr, nm in ((gate, gate_r, "gate"), (zo, zo_r, "zo"), (zn, zn_r, "zn")):
                    a, b = self.sb(es, nm, [128, 512], F32)
                    lst.append(a)
                    lstr.append(b)
            dbc, dbc_r = self.sb(es, "dbc", [128, 512], F32)
            for o in range(2):
                P.dma('sp', dbc[:], bcast_rows(self.w['hy_skip'], (l * 2 + o) * 512, 512), reads=[self.wr], writes=[dbc_r])
                for j in range(32):
                    pb = j % 2
                    P.dma('sp', ftr[pb][:], self.c['Fh'].ap()[j], reads=[self.wr], writes=[ftr_r[pb]])
                    P.dma('sp', fti[pb][:], self.c['Fh'].ap()[32 + j], reads=[self.wr], writes=[fti_r[pb]])
                    P.dma('sp', kre[pb][:], self.kf_t.ap()[o, j], reads=[self.kf_r], writes=[kre_r[pb]])
                    P.dma('sp', kim[pb][:], self.kf_t.ap()[o, 32 + j], reads=[self.kf_r], writes=[kim_r[pb]])
                    for nb in range(NTB):
                        P.op('pe', lambda nb=nb, pb=pb: nc.tensor.matmul(pre[pb][:], ftr[pb][:, nb, :], u[:, nb, :],
                                                                        start=(nb == 0), stop=(nb == NTB - 1)),
                             reads=[ftr_r[pb], u_r], writes=[pre_r[pb]])
                    for nb in range(NTB):
                        P.op('pe', lambda nb=nb, pb=pb: nc.tensor.matmul(pim[pb][:], fti[pb][:, nb, :], u[:, nb, :],
                                                                        start=(nb == 0), stop=(nb == NTB - 1)),
                             reads=[fti_r[pb], u_r], writes=[pim_r[pb]])
                    P.op('dve', lambda pb=pb: nc.vector.tensor_tensor(t1[:], pre[pb][:], kre[pb][:], ALU.mult),
                         reads=[pre_r[pb], kre_r[pb]], writes=[t1_r])
                    P.op('dve', lambda pb=pb: nc.vector.tensor_tensor(t2[:], pim[pb][:], kim[pb][:], ALU.mult),
                         reads=[pim_r[pb], kim_r[pb]], writes=[t2_r])
                    P.op('dve', lambda j=j: nc.vector.tensor_tensor(Y[:, j, :], t1[:], t2[:], ALU.subtract),
                         reads=[t1_r, t2_r], writes=[Y_r], conc=True)
                    P.op('dve', lambda pb=pb: nc.vector.tensor_tensor(t1[:], pre[pb][:], kim[pb][:], ALU.mult),
                         reads=[pre_r[pb], kim_r[pb]], writes=[t1_r])
                    P.op('dve', lambda pb=pb: nc.vector.tensor_tensor(t2[:], pim[pb][:], kre[pb][:], ALU.mult),
                         reads=[pim_r[pb], kre_r[pb]], writes=[t2_r])
                    P.op('dve', lambda j=j: nc.vector.tensor_tensor(Y[:, 32 + j, :], t1[:], t2[:], ALU.add),
                         reads=[t1_r, t2_r], writes=[Y_r], conc=True)
                for nb in range(NTB):
                    pb = nb % 2
                    P.dma('sp', gt[pb][:], self.c['Gh'].ap()[nb], reads=[self.wr], writes=[gt_r[pb]])
                    rows = slice(nb * 128, (nb + 1) * 128)
                    P.dma('sp', gate[pb][:], self.hy_t.ap()[rows, 512 * (1 + o):512 * (2 + o)], reads=[self.hy_r],
                          writes=[gate_r[pb]])
                    if o == 0:
                        P.dma('sp', zo[pb][:], self.hy_t.ap()[rows, 0:512], reads=[self.hy_r], writes=[zo_r[pb]])
                    else:
                        P.dma('sp', zo[pb][:], self.z1_t.ap()[rows, :], reads=[self.z1_r], writes=[zo_r[pb]])
                    for s in range(64):
                        P.op('pe', lambda s=s, pb=pb: nc.tensor.matmul(pc[pb][:], gt[pb][:, s, :], Y[:, s, :], start=(s == 0),
                                                                      stop=(s == 63)),
                             reads=[gt_r[pb], Y_r], writes=[pc_r[pb]])
                    P.op('dve', lambda pb=pb: nc.vector.tensor_tensor(zo[pb][:], zo[pb][:], dbc[:], ALU.mult),
                         reads=[zo_r[pb], dbc_r], writes=[zo_r[pb]])
                    P.op('dve', lambda pb=pb: nc.vector.tensor_tensor(zo[pb][:], pc[pb][:], zo[pb][:], ALU.add),
                         reads=[pc_r[pb], zo_r[pb]], writes=[zo_r[pb]])
                    P.op('dve', lambda pb=pb: nc.vector.tensor_tensor(zn[pb][:], zo[pb][:], gate[pb][:], ALU.mult),
                         reads=[zo_r[pb], gate_r[pb]], writes=[zn_r[pb]])
                    if o == 0 or self.dbg:
                        dstt, dstr = (self.z1_t, self.z1_r) if o == 0 else (self.z2_t, self.z2_r)
                        P.dma('sp', dstt.ap()[rows, :], zn[pb][:], reads=[zn_r[pb]], writes=[dstr], conc=True)
                    P.op('act', lambda pb=pb, nb=nb: nc.scalar.copy(out=u[:, nb, :], in_=zn[pb][:]), reads=[zn_r[pb]],
                         writes=[u_r], conc=True)
            P.barrier()

    def mix_attn(self, l):
        nc, P = self.nc, self.P
        EB, EB_r = self.EB, self.EB_r
        with ExitStack() as es:
            qh, qh_r = self.sb(es, "qh", [128, SEQ], BF16)
            kh, kh_r = self.sb(es, "kh", [128, SEQ + 2048], BF16)
            vt, vt_r = [], []
            for i in range(2):
                a, b = self.sb(es, "vt", [128, 33, 128], BF16)
                vt.append(a)
                vt_r.append(b)
            psc, psc_r, pov, pov_r, pe_, pe_r, pm, pm_r, ot, ot_r = ([] for _ in range(10))
            for i in range(2):
                a, b = self.ps(es, "psc", [128, 256], F32)
                psc.append(a)
                psc_r.append(b)
                a, b = self.ps(es, "pov", [128, 128], F32)
                pov.append(a)
                pov_r.append(b)
                a, b = self.sb(es, "pexp", [128, 256], F32)
                pe_.append(a)
                pe_r.append(b)
                a, b = self.sb(es, "pm", [128, 256], BF16)
                pm.append(a)
                pm_r.append(b)
                a, b = self.sb(es, "ot", [128, 128], F32)
                ot.append(a)
                ot_r.append(b)
            it = 0
            iv = 0
            for g in range(3):
                D = DILS[g]
                m = SEQ // D
                nblk = m // 128
                nch = nblk + 1
                vd_t, vd_r = self.vd_t[g]
                for hp in range(2):
                    cq = 2 * g + hp
                    ck = 6 + 2 * g + hp
                    P.dma('sp', qh[:], self.qk_t.ap()[cq], reads=[self.qk_r], writes=[qh_r])
                    P.op('pool', lambda: nc.gpsimd.memset(kh[:], 0.0), writes=[kh_r])
                    P.dma('sp', kh[:, 64 * D:64 * D + SEQ], self.qk_t.ap()[ck], reads=[self.qk_r], writes=[kh_r])
                    for hi in range(2):
                        hh = 2 * hp + hi
                        p0 = 64 * hi
                        for r in range(D):
                            vb = iv % 2
                            iv += 1
                            P.dma('sp', vt[vb][:, 0:nch, :], vd_t.ap()[r, :, hh, :].rearrange("(c p) e -> p c e", p=128),
                                  reads=[vd_r], writes=[vt_r[vb]])
                            for b in range(nblk):
                                pb = it % 2
                                it += 1
                                q0 = r + D * 128 * b
                                kA = r + D * 128 * b
                                kB = r + D * 128 * (b + 1)
                                P.op('pe', lambda pb=pb, p0=p0, q0=q0, kA=kA, D=D: nc.tensor.matmul(
                                    psc[pb][:, 0:128], kh[p0:p0 + 64, sl(kA, 128, D)], qh[p0:p0 + 64, sl(q0, 128, D)],
                                    start=True, stop=True), reads=[kh_r, qh_r], writes=[psc_r[pb]])
                                P.op('pe', lambda pb=pb, p0=p0, q0=q0, kB=kB, D=D: nc.tensor.matmul(
                                    psc[pb][:, 128:256], kh[p0:p0 + 64, sl(kB, 128, D)], qh[p0:p0 + 64, sl(q0, 128, D)],
                                    start=True, stop=True), reads=[kh_r, qh_r], writes=[psc_r[pb]])
                                P.op('act', lambda pb=pb: nc.scalar.activation(out=pe_[pb][:], in_=psc[pb][:], func=AF.Exp),
                                     reads=[psc_r[pb]], writes=[pe_r[pb]])
                                P.op('dve', lambda pb=pb, g=g, hh=hh: nc.vector.tensor_tensor(pm[pb][:], pe_[pb][:],
                                                                                             EB[:, g * 4 + hh, :], ALU.mult),
                                     reads=[pe_r[pb], EB_r], writes=[pm_r[pb]])
                                P.op('pe', lambda pb=pb, vb=vb, b=b: nc.tensor.matmul(pov[pb][:], pm[pb][:, 0:128], vt[vb][:, b, :],
                                                                                     start=True, stop=False),
                                     reads=[pm_r[pb], vt_r[vb]], writes=[pov_r[pb]])
                                P.op('pe', lambda pb=pb, vb=vb, b=b: nc.tensor.matmul(pov[pb][:], pm[pb][:, 128:256],
                                                                                     vt[vb][:, b + 1, :], start=False, stop=True),
                                     reads=[pm_r[pb], vt_r[vb]], writes=[pov_r[pb]])
                                P.op('act', lambda pb=pb: nc.scalar.copy(out=ot[pb][:], in_=pov[pb][:]), reads=[pov_r[pb]],
                                     writes=[ot_r[pb]])
                                P.dma('sp', self.att_t.ap()[g, sl(q0, 128, D), hh, :], ot[pb][:], reads=[ot_r[pb]],
                                      writes=[self.att_r], conc=True)
            P.barrier()

    def mix_out(self, l, src, dst):
        nc, P = self.nc, self.P
        src_t, src_r = src
        dst_t, dst_r = dst
        u, u_r = self.u, self.u_r
        T = 1024
        with ExitStack() as es:
            ident, ident_r = self.load_const_tiles(es)
            whp, whp_r = self.sb(es, "whp", [128, 4, D_MODEL], BF16)
            wap, wap_r = self.sb(es, "wap", [128, 2, D_MODEL], BF16)
            wout, wout_r = self.sb(es, "wout", [128, 8, D_MODEL], BF16)
            wds, wds_r = [], []
            for i in range(2):
                a, b = self.sb(es, "wdso", [128, D_MODEL], F32)
                wds.append(a)
                wds_r.append(b)
            iw = 0
            for (wt, wt_r, nkc, name) in ((whp, whp_r, 4, 'w_hy_proj'), (wap, wap_r, 2, 'w_at_proj'), (wout, wout_r, 8, 'w_out')):
                for kc in range(nkc):
                    b2 = iw % 2
                    iw += 1
                    P.dma('sp', wds[b2][:], self.w[name].ap()[l, kc * 128:(kc + 1) * 128, :], reads=[self.wr], writes=[wds_r[b2]])
                    P.op('pool', lambda wt=wt, kc=kc, b2=b2: nc.gpsimd.tensor_copy(out=wt[:, kc, :], in_=wds[b2][:]),
                         reads=[wds_r[b2]], writes=[wt_r], conc=True)
            yhyT, yhyT_r = self.sb(es, "yhyT", [128, 4, T], BF16)
            yatT, yatT_r = self.sb(es, "yatT", [128, 2, T], BF16)
            yT, yT_r = self.sb(es, "yT", [128, 8, T], BF16)
            att, att_r, s2, s2_r, yab, yab_r, rden, rden_r = ([] for _ in range(8))
            pst, pst_r, pst2, pst2_r = [], [], [], []
            for i in range(2):
                a, b = self.sb(es, "attl", [128, 3, 4, 128], F32)
                att.append(a)
                att_r.append(b)
                a, b = self.sb(es, "s2", [128, 4, 128], F32)
                s2.append(a)
                s2_r.append(b)
                a, b = self.sb(es, "yab", [128, 4, 64], BF16)
                yab.append(a)
                yab_r.append(b)
                a, b = self.sb(es, "rden", [128, 4], F32)
                rden.append(a)
                rden_r.append(b)
            a, b = self.ps(es, "psty", [128, 4, 128], BF16)
            pst.append(a)
            pst_r.append(b)
            a, b = self.ps(es, "psta", [128, 2, 128], BF16)
            pst2.append(a)
            pst2_r.append(b)
            pa, pa_r, pbb, pbb_r, po, po_r = [], [], [], [], [], []
            gA, gA_r, gB, gB_r, ta, ta_r, tb_, tb_r, xin, xin_r, xo, xo_r = ([] for _ in range(12))
            for i in range(2):
                for lst, lstr, nm in ((pa, pa_r, "pa"), (pbb, pbb_r, "pbb"), (po, po_r, "poo")):
                    a, b = self.ps(es, nm, [128, 512], F32)
                    lst.append(a)
                    lstr.append(b)
                for lst, lstr, nm in ((gA, gA_r, "gA"), (gB, gB_r, "gB"), (ta, ta_r, "ta"), (tb_, tb_r, "tbt")):
                    a, b = self.sb(es, nm, [128, 512], F32)
                    lst.append(a)
                    lstr.append(b)
                a, b = self.sb(es, "xin", [128, D_MODEL], F32)
                xin.append(a)
                xin_r.append(b)
                a, b = self.sb(es, "xo", [128, D_MODEL], F32)
                xo.append(a)
                xo_r.append(b)
            it = 0
            for st in range(SEQ // T):
                for tb in range(8):
                    gb = st * 8 + tb
                    b2 = gb % 2
                    for c in range(4):
                        P.op('pe', lambda c=c, gb=gb: nc.tensor.transpose(pst[0][:, c, :], u[:, gb, c * 128:(c + 1) * 128], ident[:]),
                             reads=[u_r, ident_r], writes=[pst_r[0]])
                    P.op('act', lambda tb=tb: nc.scalar.copy(out=yhyT[:, :, tb * 128:(tb + 1) * 128], in_=pst[0][:]),
                         reads=[pst_r[0]], writes=[yhyT_r], conc=True)
                    P.dma('sp', att[b2][:], self.att_t.ap()[:, gb * 128:(gb + 1) * 128, :, :].rearrange("g t h e -> t g h e"),
                          reads=[self.att_r], writes=[att_r[b2]])
                    P.op('dve', lambda b2=b2: nc.vector.tensor_tensor(s2[b2][:], att[b2][:, 0, :, :], att[b2][:, 1, :, :], ALU.add),
                         reads=[att_r[b2]], writes=[s2_r[b2]])
                    P.op('dve', lambda b2=b2: nc.vector.tensor_tensor(s2[b2][:], s2[b2][:], att[b2][:, 2, :, :], ALU.add),
                         reads=[att_r[b2], s2_r[b2]], writes=[s2_r[b2]])
                    P.op('dve', lambda b2=b2: nc.vector.reciprocal(rden[b2][:], s2[b2][:, :, 64]), reads=[s2_r[b2]],
                         writes=[rden_r[b2]])
                    for hh in range(4):
                        P.op('dve', lambda b2=b2, hh=hh: nc.vector.tensor_scalar(yab[b2][:, hh, :], s2[b2][:, hh, 0:64],
                                                                                rden[b2][:, hh:hh + 1], None, ALU.mult),
                             reads=[s2_r[b2], rden_r[b2]], writes=[yab_r[b2]], conc=True)
                    for c in range(2):
                        P.op('pe', lambda c=c, b2=b2: nc.tensor.transpose(
                            pst2[0][:, c, :], yab[b2][:, 2 * c:2 * c + 2, :].rearrange("p h e -> p (h e)"), ident[:]),
                            reads=[yab_r[b2], ident_r], writes=[pst2_r[0]])
                    P.op('act', lambda tb=tb: nc.scalar.copy(out=yatT[:, :, tb * 128:(tb + 1) * 128], in_=pst2[0][:]),
                         reads=[pst2_r[0]], writes=[yatT_r], conc=True)
                for dc in range(8):
                    for th in range(2):
                        pb = it % 2
                        it += 1
                        tsl = slice(st * T + th * 512, st * T + (th + 1) * 512)
                        P.dma('sp', gA[pb][:], self.gt_t.ap()[dc, :, tsl], reads=[self.gt_r], writes=[gA_r[pb]])
                        P.dma('sp', gB[pb][:], self.gt_t.ap()[8 + dc, :, tsl], reads=[self.gt_r], writes=[gB_r[pb]])
                        for kc in range(4):
                            P.op('pe', lambda kc=kc, dc=dc, th=th, pb=pb: nc.tensor.matmul(
                                pa[pb][:], whp[:, kc, dc * 128:(dc + 1) * 128], yhyT[:, kc, th * 512:(th + 1) * 512],
                                start=(kc == 0), stop=(kc == 3)), reads=[whp_r, yhyT_r], writes=[pa_r[pb]])
                        for kc in range(2):
                            P.op('pe', lambda kc=kc, dc=dc, th=th, pb=pb: nc.tensor.matmul(
                                pbb[pb][:], wap[:, kc, dc * 128:(dc + 1) * 128], yatT[:, kc, th * 512:(th + 1) * 512],
                                start=(kc == 0), stop=(kc == 1)), reads=[wap_r, yatT_r], writes=[pbb_r[pb]])
                        P.op('dve', lambda pb=pb: nc.vector.tensor_tensor(ta[pb][:], pa[pb][:], gA[pb][:], ALU.mult),
                             reads=[pa_r[pb], gA_r[pb]], writes=[ta_r[pb]])
                        P.op('dve', lambda pb=pb: nc.vector.tensor_tensor(tb_[pb][:], pbb[pb][:], gB[pb][:], ALU.mult),
                             reads=[pbb_r[pb], gB_r[pb]], writes=[tb_r[pb]])
                        P.op('pool', lambda pb=pb, dc=dc, th=th: nc.gpsimd.tensor_tensor(
                            yT[:, dc, th * 512:(th + 1) * 512], ta[pb][:], tb_[pb][:], ALU.add),
                            reads=[ta_r[pb], tb_r[pb]], writes=[yT_r], conc=True)
                for tb in range(8):
                    b2 = tb % 2
                    r0 = st * T + tb * 128
                    P.dma('sp', xin[b2][:], src_t.ap()[r0:r0 + 128, :], reads=[src_r], writes=[xin_r[b2]])
                    for dh in range(2):
                        pb = it % 2
                        it += 1
                        for kc in range(8):
                            P.op('pe', lambda kc=kc, tb=tb, dh=dh, pb=pb: nc.tensor.matmul(
                                po[pb][:], yT[:, kc, tb * 128:(tb + 1) * 128], wout[:, kc, dh * 512:(dh + 1) * 512],
                                start=(kc == 0), stop=(kc == 7)), reads=[yT_r, wout_r], writes=[po_r[pb]])
                        P.op('dve', lambda pb=pb, b2=b2, dh=dh: nc.vector.tensor_tensor(
                            xo[b2][:, dh * 512:(dh + 1) * 512], po[pb][:], xin[b2][:, dh * 512:(dh + 1) * 512], ALU.add),
                            reads=[po_r[pb], xin_r[b2]], writes=[xo_r[b2]], conc=True)
                    P.dma('sp', dst_t.ap()[r0:r0 + 128, :], xo[b2][:], reads=[xo_r[b2]], writes=[dst_r], conc=True)
            P.barrier()

    def mixer_phase(self, l, src, dst):
        upto = self.mix_upto
        order = ['proj', 'filt', 'conv', 'attn', 'out']
        n = len(order) if upto is None else order.index(upto) + 1
        with ExitStack() as es:
            self.u, self.u_r = self.sb(es, "u", [128, NTB, 512], BF16)
            for name in order[:n]:
                if name == 'proj':
                    self.mix_proj(l, src)
                elif name == 'filt':
                    self.mix_filters(l)
                elif name == 'conv':
                    self.mix_conv(l)
                elif name == 'attn':
                    self.mix_attn(l)
                else:
                    self.mix_out(l, src, dst)
            self.P.barrier()

    def build(self):
        self.declare()
        P = self.P
        with ExitStack() as es_sem:
            self.eps_t, self.eps_r = self.sb(es_sem, "eps", [128, 1], F32)
            P.op('dve', lambda: self.nc.vector.memset(self.eps_t[:], EPS), writes=[self.eps_r])
            self.attn_bias_setup(es_sem)
            cur = (self.x_t, self.x_r)
            si = 0
            stages = self.stages
            for l in range(DEPTH):
                for ph in ('ffn1', 'mix', 'ffn2'):
                    name = "%s_%d" % (ph, l)
                    if stages is not None and name not in stages:
                        continue
                    last = (stages is not None and name == stages[-1]) or (stages is None and l == DEPTH - 1 and ph == 'ffn2')
                    dst = (self.y_t, self.y_r) if last else self.xs_t[si]
                    si += 1
                    if ph == 'ffn1':
                        self.ffn_phase(l, 1, cur, dst)
                    elif ph == 'ffn2':
                        self.ffn_phase(l, 2, cur, dst)
                    else:
                        self.mixer_phase(l, cur, dst)
                    cur = dst
            P.barrier()
            nw = P.emit(es_sem)
            print("ops", len(P.ops), "waits", nw, "slots", len(P.slot_cnt))
        return self.nc


_NC_CACHE = {}


def kernel(**inputs):
    x = np.ascontiguousarray(np.asarray(inputs['x'], dtype=np.float32))
    consts = host_constants()
    if 'nc' not in _NC_CACHE:
        _NC_CACHE['nc'] = Builder().build()
    nc = _NC_CACHE['nc']
    shared = {k: np.ascontiguousarray(np.asarray(inputs[k], dtype=np.float32)) for k in WEIGHT_NAMES}
    shared.update(consts)
    in_maps = []
    for b in range(BATCH):
        m = dict(shared)
        m['x'] = x[b]
        in_maps.append(m)
    res = run_bass_kernel_spmd(nc, in_maps, core_ids=list(range(BATCH)))
    return np.stack([np.asarray(r['y'], dtype=np.float32) for r in res.results], axis=0)
```

```python
import math
from contextlib import ExitStack

import numpy as np
import ml_dtypes

import concourse.bass as bass
import concourse.mybir as mybir
from concourse.alu_op_type import AluOpType as ALU
from concourse.bass_utils import run_bass_kernel_spmd

F32 = mybir.dt.float32
BF16 = mybir.dt.bfloat16
AF = mybir.ActivationFunctionType

D_MODEL = 1024
BATCH = 8
SEQ = 4096
DEPTH = 2
HEAD_DIM = 64
HYW = 512
HY_EMB = 33
HID = 64
WINDOWS = (128, 512, 2048)
DILS = (1, 4, 16)
D_FF = 2688
NFC = D_FF // 128
IN_WIDTH = 3840
EPS = 1e-6
NFFT = 8192
NTB = SEQ // 128

WEIGHT_NAMES = ['rel_bias', 'ffn1_norm', 'ffn1_w_gate', 'ffn1_w_up', 'ffn1_w_down', 'mix_norm', 'w_in',
                'w_gate', 'b_gate', 'hy_conv_w', 'hy_conv_b', 'hy_filt_w1', 'hy_filt_b1', 'hy_filt_w2',
                'hy_filt_b2', 'hy_filt_w3', 'hy_skip', 'q_norm', 'k_norm', 'w_hy_proj', 'w_at_proj',
                'w_out', 'ffn2_norm', 'ffn2_w_gate', 'ffn2_w_up', 'ffn2_w_down']


class Reg:
    __slots__ = ('name', 'prev', 'writers', 'readers', 'slot', 'dcnt', 'lastdma')

    def __init__(self, name):
        self.name = name
        self.prev = []
        self.writers = []
        self.readers = []
        self.slot = None
        self.dcnt = 0
        self.lastdma = None


class Op:
    __slots__ = ('eng', 'fn', 'deps', 'is_dma', 'reg', 'dval', 'sig', 'sigval', 'slot')


class Prog:
    ENG = {'pe': 'tensor', 'act': 'scalar', 'dve': 'vector', 'pool': 'gpsimd', 'sp': 'sync'}

    def __init__(self, nc):
        self.nc = nc
        self.ops = []
        self.regs = []
        self.last = {}
        self.dma_regs = set()
        self.slot_cnt = []
        self.free_slots = []

    def reg(self, name):
        r = Reg(name)
        self.regs.append(r)
        return r

    def _add(self, eng, fn, reads, writes, conc, is_dma):
        i = len(self.ops)
        deps = set()
        for r in reads:
            deps.update(r.writers)
            if not r.writers:
                deps.update(r.prev)
        for w in writes:
            if conc:
                if w.readers:
                    w.prev = w.readers + w.writers
                    w.writers = []
                    w.readers = []
                deps.update(w.prev)
            else:
                deps.update(w.prev)
                deps.update(w.writers)
                deps.update(w.readers)
        for w in writes:
            if conc:
                w.writers.append(i)
            else:
                w.prev = []
                w.writers = [i]
                w.readers = []
        for r in reads:
            r.readers.append(i)
        deps.discard(i)
        o = Op()
        o.eng = eng
        o.fn = fn
        o.deps = deps
        o.is_dma = is_dma
        o.reg = None
        o.dval = 0
        o.sig = False
        o.sigval = 0
        o.slot = None
        if is_dma:
            w = writes[0]
            if w.slot is None:
                if self.free_slots:
                    w.slot = self.free_slots.pop()
                else:
                    w.slot = len(self.slot_cnt)
                    self.slot_cnt.append(0)
            self.slot_cnt[w.slot] += 16
            w.dcnt = self.slot_cnt[w.slot]
            o.reg = w
            o.slot = w.slot
            o.dval = w.dcnt
            w.lastdma = i
            self.dma_regs.add(w)
        else:
            self.last[eng] = i
        self.ops.append(o)
        return i

    def op(self, eng, fn, reads=(), writes=(), conc=False):
        return self._add(eng, fn, list(reads), list(writes), conc, False)

    def dma(self, q, out, in_, reads=(), writes=(), conc=False, **kw):
        nc = self.nc
        e = getattr(nc, self.ENG[q])

        def fn():
            return e.dma_start(out=out, in_=in_, **kw)
        assert len(writes) == 1
        return self._add(q, fn, list(reads), list(writes), conc, True)

    def barrier(self):
        deps = set(self.last.values())
        for r in self.dma_regs:
            if r.lastdma is not None:
                deps.add(r.lastdma)
        for e in self.ENG:
            o = Op()
            o.eng = e
            o.fn = None
            o.deps = set(deps)
            o.is_dma = False
            o.reg = None
            o.dval = 0
            o.sig = False
            o.sigval = 0
            o.slot = None
            self.ops.append(o)
        for r in self.regs:
            r.prev = []
            r.writers = []
            r.readers = []
        for r in self.dma_regs:
            self.free_slots.append(r.slot)
            r.slot = None
            r.lastdma = None
        self.dma_regs = set()

    def emit(self, es):
        nc = self.nc
        ops = self.ops
        for o in ops:
            if o.eng == 'pe' and not o.is_dma:
                o.deps = {d for d in o.deps if ops[d].is_dma or ops[d].eng != 'pe'}
            for d in o.deps:
                ops[d].sig = True
        esem = {e: es.enter_context(nc.semaphore("sem_" + e)) for e in self.ENG}
        cnt = {e: 0 for e in self.ENG}
        ssem = [es.enter_context(nc.semaphore("semslot%d" % i)) for i in range(len(self.slot_cnt))]
        for o in ops:
            if o.is_dma:
                pass
            elif o.sig and o.fn is not None:
                cnt[o.eng] += 1
                o.sigval = cnt[o.eng]
        known = {e: {} for e in self.ENG}
        nwait = 0
        for o in ops:
            e = o.eng
            eng = getattr(nc, self.ENG[e])
            need = {}
            for d in o.deps:
                od = ops[d]
                if od.is_dma:
                    key = ('r', od.slot)
                    sem = ssem[od.slot]
                    val = od.dval
                else:
                    if od.fn is None:
                        continue
                    key = ('e', od.eng)
                    sem = esem[od.eng]
                    val = od.sigval
                if need.get(key, (None, 0))[1] < val:
                    need[key] = (sem, val)
            kn = known[e]
            for key, (sem, val) in need.items():
                if kn.get(key, 0) < val:
                    eng.wait_ge(sem, val)
                    kn[key] = val
                    nwait += 1
            if o.fn is None:
                continue
            ins = o.fn()
            if o.is_dma:
                ins.then_inc(ssem[o.slot], 16)
            elif o.sig:
                ins.then_inc(esem[e], 1)
        return nwait


_CONST_CACHE = {}


def t5_bucket_np(rel):
    half = 16
    exact = 8
    ret = np.where(rel > 0, half, 0)
    n = np.abs(rel)
    nf = np.maximum(n, 1).astype(np.float32)
    large = exact + (np.log(nf / exact) / np.float32(math.log(1024 / exact)) * (half - exact)).astype(np.int32)
    large = np.minimum(large, half - 1)
    return ret + np.where(n < exact, n, large)


def host_constants():
    if _CONST_CACHE:
        return _CONST_CACHE
    c = {}
    L = SEQ
    t = np.linspace(0.0, 1.0, L, dtype=np.float32)[:, None]
    w = (np.float32(2.0 * math.pi / L)) * np.arange(L, dtype=np.float32)[:, None]
    f = np.linspace(1e-4, 15, 16, dtype=np.float32)[None]
    z = np.concatenate([t, np.cos(f * w), -np.sin(f * w)], axis=-1).astype(np.float32)
    c['zfeat'] = np.ascontiguousarray(z.T)
    max_decay = math.log(1e-2) / 0.3
    min_decay = math.log(1e-2) / 1.5
    deltas = np.abs(np.linspace(min_decay, max_decay, HYW, dtype=np.float32))
    dec = np.exp(-t * deltas[None]).astype(np.float32)
    c['dec_f'] = dec
    decb = dec.copy()
    decb[0] = 0.0
    c['dec_b'] = decb
    c['ident'] = np.eye(128, dtype=np.float32).astype(ml_dtypes.bfloat16)
    ob = np.zeros((128, 128), np.float32)
    ob[:64, :64] = 1.0
    ob[64:, 64:] = 1.0
    c['ones_blk'] = ob
    c['ones_all'] = np.ones((128, 128), np.float32)
    c['jrev'] = np.ascontiguousarray(np.eye(128, dtype=np.float32)[::-1])
    oh = np.zeros((3, 32, 512), np.float32)
    bm = np.zeros((4, 512), np.float32)
    for g in range(3):
        for i in range(191, 320):
            rel = 255 - i
            b = int(t5_bucket_np(np.array([rel * DILS[g]]))[0])
            oh[g, b, i] = 1.0
    bm[:, 191:320] = 1.0
    c['oh'] = oh
    c['bm'] = bm
    th = 2.0 * math.pi / NFFT
    p_ = np.arange(128, dtype=np.float64)
    r_ = np.arange(32, dtype=np.float64)
    k1_ = np.arange(128, dtype=np.float64) + 0.5
    k2_ = np.arange(32, dtype=np.float64)
    a1 = 2.0 * math.pi * np.outer(p_, k1_) / 256.0
    c['f1cat'] = np.concatenate([np.cos(a1), -np.sin(a1)], axis=1).astype(np.float32).astype(ml_dtypes.bfloat16)
    c['g1re'] = (np.cos(a1).T * (2.0 / NFFT)).astype(np.float32).astype(ml_dtypes.bfloat16)
    c['g1imn'] = (-np.sin(a1).T * (2.0 / NFFT)).astype(np.float32).astype(ml_dtypes.bfloat16)
    at = th * np.outer(r_, k1_)
    tre = np.tile(np.cos(at), (4, 1))
    tim = np.tile(-np.sin(at), (4, 1))
    c['tf_re'] = np.ascontiguousarray(np.concatenate([tre, tre], axis=1).astype(np.float32))
    c['tf_im'] = np.ascontiguousarray(np.concatenate([tim, tim], axis=1).astype(np.float32))
    tcre = np.cos(at).T
    tcim = np.sin(at).T
    c['tc_re'] = np.ascontiguousarray(np.tile(tcre, (1, 16)).astype(np.float32))
    c['tc_im'] = np.ascontiguousarray(np.tile(tcim, (1, 16)).astype(np.float32))
    ae = 2.0 * math.pi * np.outer(r_, k2_) / 32.0
    ere = np.cos(ae)
    eim = -np.sin(ae)

    def bd(m):
        o_ = np.zeros((128, 128), np.float64)
        for i in range(4):
            o_[32 * i:32 * i + 32, 32 * i:32 * i + 32] = m
        return o_.astype(np.float32).astype(ml_dtypes.bfloat16)
    c['bd_ere'] = bd(ere)
    c['bd_eren'] = bd(-ere)
    c['bd_eim'] = bd(eim)
    c['bd_eimn'] = bd(-eim)
    d2 = np.stack([c['dec_f'], c['dec_b']], axis=0)
    d2 = d2.reshape(2, 128, 32, 8, 64).transpose(3, 1, 2, 0, 4)
    c['dec2'] = np.ascontiguousarray(d2.astype(np.float32))
    dr = np.stack([c['dec_f'], c['dec_b']], axis=0).reshape(2, 128, 32, 512).transpose(2, 1, 0, 3)
    c['decr'] = np.ascontiguousarray(dr.astype(np.float32))
    for k in ('Fh', 'Gh', 'dec_f', 'dec_b'):
        c.pop(k, None)
    _CONST_CACHE.update(c)
    return c


CONST_SPECS = [('zfeat', [33, SEQ], F32), ('ident', [128, 128], BF16),
               ('f1cat', [128, 256], BF16), ('g1re', [128, 128], BF16), ('g1imn', [128, 128], BF16),
               ('tf_re', [128, 256], F32), ('tf_im', [128, 256], F32), ('tc_re', [128, 512], F32), ('tc_im', [128, 512], F32),
               ('bd_ere', [128, 128], BF16), ('bd_eren', [128, 128], BF16), ('bd_eim', [128, 128], BF16),
               ('bd_eimn', [128, 128], BF16), ('dec2', [8, 128, 32, 2, 64], F32), ('decr', [32, 128, 2, 512], F32),
               ('ones_blk', [128, 128], F32), ('ones_all', [128, 128], F32), ('jrev', [128, 128], F32), ('oh', [3, 32, 512], F32),
               ('bm', [4, 512], F32)]

WEIGHT_SHAPES = {
    'rel_bias': [32, 12], 'ffn1_norm': [2, 1024], 'ffn1_w_gate': [2, 1024, 2688], 'ffn1_w_up': [2, 1024, 2688],
    'ffn1_w_down': [2, 2688, 1024], 'mix_norm': [2, 1024], 'w_in': [2, 1024, 3840], 'w_gate': [2, 1024, 2048],
    'b_gate': [2, 2048], 'hy_conv_w': [2, 3, 1536], 'hy_conv_b': [2, 1536], 'hy_filt_w1': [2, 33, 64],
    'hy_filt_b1': [2, 64], 'hy_filt_w2': [2, 64, 64], 'hy_filt_b2': [2, 64], 'hy_filt_w3': [2, 64, 2048],
    'hy_skip': [2, 2, 512], 'q_norm': [2, 64], 'k_norm': [2, 64], 'w_hy_proj': [2, 512, 1024],
    'w_at_proj': [2, 256, 1024], 'w_out': [2, 1024, 1024], 'ffn2_norm': [2, 1024], 'ffn2_w_gate': [2, 1024, 2688],
    'ffn2_w_up': [2, 1024, 2688], 'ffn2_w_down': [2, 2688, 1024]}


def sl(start, n, step):
    return slice(start, start + (n - 1) * step + 1, step)


def bcast_rows(ap1d_tensor, offset, n, parts=128):
    return bass.AP(ap1d_tensor, offset, [[0, parts], [1, n]])


class Builder:
    def __init__(self, stages=None, dbg=False, mix_upto=None):
        self.mix_upto = mix_upto
        self.nc = bass.Bass("TRN2", target_bir_lowering=False)
        self.P = Prog(self.nc)
        self.stages = stages
        self.dbg = dbg
        self.uid = 0

    def sb(self, es, name, shape, dt):
        self.uid += 1
        t = es.enter_context(self.nc.sbuf_tensor("%s_%d" % (name, self.uid), shape, dt))
        return t, self.P.reg(name)

    def ps(self, es, name, shape, dt):
        self.uid += 1
        t = es.enter_context(self.nc.psum_tensor("%s_%d" % (name, self.uid), shape, dt))
        return t, self.P.reg(name)

    def dram(self, name, shape, dt, kind="Internal"):
        t = self.nc.dram_tensor(name, shape, dt, kind=kind)
        return t, self.P.reg(name)

    def declare(self):
        nc = self.nc
        self.x_t, self.x_r = self.dram("x", [SEQ, D_MODEL], F32, "ExternalInput")
        SK = "ExternalOutput" if self.dbg else "Internal"
        self.y_t, self.y_r = self.dram("y", [SEQ, D_MODEL], F32, "ExternalOutput")
        self.w = {}
        self.wr = self.P.reg("weights")
        for k in WEIGHT_NAMES:
            self.w[k] = nc.dram_tensor(k, WEIGHT_SHAPES[k], F32, kind="ExternalInput")
        self.c = {}
        for k, shp, dt in CONST_SPECS:
            self.c[k] = nc.dram_tensor(k, shp, dt, kind="ExternalInput")
        self.xs_t = []
        for i in range(6):
            self.xs_t.append(self.dram("xres%d" % i, [SEQ, D_MODEL], F32))
        self.hy_t, self.hy_r = self.dram("hy_s", [SEQ, 1536], F32, SK)
        self.z1_t, self.z1_r = self.dram("z1_s", [SEQ, HYW], F32, SK)
        self.qk_t, self.qk_r = self.dram("qk_s", [12, 128, SEQ], BF16, SK)
        self.gt_t, self.gt_r = self.dram("gt_s", [16, 128, SEQ], F32, SK)
        self.vd_t = []
        for g in range(3):
            D = DILS[g]
            m = SEQ // D
            self.vd_t.append(self.dram("vd_s%d" % g, [D, m + 128, 4, 128], BF16, SK))
        self.kf_t, self.kf_r = self.dram("kf_s", [2, 32, 128, 4, 2, 128], F32, SK)
        self.yh_t, self.yh_r = self.dram("yh_s", [4, 128, SEQ], BF16, SK)
        self.att_t, self.att_r = self.dram("att_s", [3, SEQ, 4, 128], F32, SK)
        self.rv_t, self.rv_r = self.dram("rv_s", [12, 512], F32, SK)
        self.z2_t, self.z2_r = self.dram("z2_s", [SEQ, HYW], F32, SK)
        self.zdbg_t, self.zdbg_r = self.dram("zdbg_s", [2, 128, 2048], BF16, SK)

    def norm_to_hT(self, es, src_t, src_r, tok0, ntb, gbc, hT, hT_r, hoff, tiles):
        nc, P = self.nc, self.P
        (xs, xs_rs, junk, junk_r, ss, ss_r, rs, rs_r, hb, hb_r, pst, pst_r, ident, ident_r) = tiles
        src = src_t.ap()
        for tb in range(ntb):
            xb = xs[:, tb % len(xs_rs), :]
            xr = xs_rs[tb % len(xs_rs)]
            r0 = tok0 + tb * 128
            P.dma('sp', xb, src[r0:r0 + 128, :], reads=[src_r], writes=[xr])
            b2 = tb % 2
            P.op('act', lambda xb=xb, tb=tb: nc.scalar.activation(out=junk[:], in_=xb, func=AF.Square,
                                                                  accum_out=ss[:, tb:tb + 1]),
                 reads=[xr], writes=[junk_r, ss_r[tb]])
            P.op('dve', lambda tb=tb: nc.vector.tensor_scalar(rs[:, tb:tb + 1], ss[:, tb:tb + 1], 1.0 / D_MODEL, EPS,
                                                              ALU.mult, ALU.add),
                 reads=[ss_r[tb]], writes=[rs_r[tb]])
            P.op('act', lambda tb=tb: nc.scalar.activation(out=rs[:, tb:tb + 1], in_=rs[:, tb:tb + 1], func=AF.Sqrt),
                 reads=[rs_r[tb]], writes=[rs_r[tb]])
            P.op('dve', lambda tb=tb: nc.vector.reciprocal(rs[:, tb:tb + 1], rs[:, tb:tb + 1]),
                 reads=[rs_r[tb]], writes=[rs_r[tb]])
            P.op('dve', lambda xb=xb, tb=tb, b2=b2: nc.vector.scalar_tensor_tensor(
                hb[b2][:], xb, rs[:, tb:tb + 1], gbc[0][:], ALU.mult, ALU.mult),
                reads=[xr, rs_r[tb], gbc[1]], writes=[hb_r[b2]])
            for kc in range(8):
                P.op('pe', lambda kc=kc, b2=b2: nc.tensor.transpose(pst[b2][:, kc, :], hb[b2][:, kc * 128:(kc + 1) * 128],
                                                                   ident[:]),
                     reads=[hb_r[b2], ident_r], writes=[pst_r[b2]])
            c0 = hoff + tb * 128
            P.op('act', lambda b2=b2, c0=c0: nc.scalar.copy(out=hT[:, :, c0:c0 + 128], in_=pst[b2][:]),
                 reads=[pst_r[b2]], writes=[hT_r], conc=True)

    def load_const_tiles(self, es):
        nc, P = self.nc, self.P
        ident, ident_r = self.sb(es, "ident", [128, 128], BF16)
        P.dma('sp', ident[:], self.c['ident'].ap(), reads=[self.wr], writes=[ident_r])
        return ident, ident_r

    def ffn_phase(self, l, which, src, dst):
        nc, P = self.nc, self.P
        src_t, src_r = src
        dst_t, dst_r = dst
        wg_d = self.w['ffn%d_w_gate' % which].ap()
        wu_d = self.w['ffn%d_w_up' % which].ap()
        wd_d = self.w['ffn%d_w_down' % which].ap()
        gn_t = self.w['ffn%d_norm' % which]
        T = 1024
        with ExitStack() as es:
            ident, ident_r = self.load_const_tiles(es)
            gbc = self.sb(es, "gbc", [128, D_MODEL], F32)
            P.dma('sp', gbc[0][:], bcast_rows(gn_t, l * D_MODEL, D_MODEL), reads=[self.wr], writes=[gbc[1]])
            xs, _ = self.sb(es, "xs", [128, 8, D_MODEL], F32)
            xs_rs = [P.reg("xs%d" % i) for i in range(8)]
            junk, junk_r = self.sb(es, "junk", [128, D_MODEL], F32)
            ss, _ = self.sb(es, "ss", [128, 8], F32)
            ss_r = [P.reg("ss%d" % i) for i in range(8)]
            rs, _ = self.sb(es, "rs", [128, 8], F32)
            rs_r = [P.reg("rs%d" % i) for i in range(8)]
            hb, hb_r = [], []
            pst, pst_r = [], []
            for i in range(2):
                a, b = self.sb(es, "hb", [128, D_MODEL], BF16)
                hb.append(a)
                hb_r.append(b)
                a, b = self.ps(es, "pst", [128, 8, 128], BF16)
                pst.append(a)
                pst_r.append(b)
            hT, hT_r = self.sb(es, "hT", [128, 8, T], BF16)
            act, act_r = self.sb(es, "act", [128, NFC, T], BF16)
            wd, wd_r = self.sb(es, "wd", [128, NFC, D_MODEL], BF16)
            wst, wst_r, wbf, wbf_r = [], [], [], []
            for i in range(4):
                a, b = self.sb(es, "wst", [128, 8, 128], F32)
                wst.append(a)
                wst_r.append(b)
                a, b = self.sb(es, "wbf", [128, 8, 128], BF16)
                wbf.append(a)
                wbf_r.append(b)
            wds, wds_r = [], []
            for i in range(2):
                a, b = self.sb(es, "wds", [128, D_MODEL], F32)
                wds.append(a)
                wds_r.append(b)
            sg, sg_r = [], []
            pg, pg_r, pu, pu_r, po, po_r = [], [], [], [], [], []
            for i in range(2):
                a, b = self.sb(es, "sg", [128, 512], F32)
                sg.append(a)
                sg_r.append(b)
                a, b = self.ps(es, "pg", [128, 512], F32)
                pg.append(a)
                pg_r.append(b)
                a, b = self.ps(es, "pu", [128, 512], F32)
                pu.append(a)
                pu_r.append(b)
                a, b = self.ps(es, "po", [128, 512], F32)
                po.append(a)
                po_r.append(b)
            for j in range(NFC):
                b2 = j % 2
                P.dma('sp', wds[b2][:], wd_d[l, j * 128:(j + 1) * 128, :], reads=[self.wr], writes=[wds_r[b2]])
                P.op('pool', lambda j=j, b2=b2: nc.gpsimd.tensor_copy(out=wd[:, j, :], in_=wds[b2][:]),
                     reads=[wds_r[b2]], writes=[wd_r], conc=True)
            tiles = (xs, xs_rs, junk, junk_r, ss, ss_r, rs, rs_r, hb, hb_r, pst, pst_r, ident, ident_r)
            src_ap = src_t.ap()
            dst_ap = dst_t.ap()
            for st in range(SEQ // T):
                self.norm_to_hT(es, src_t, src_r, st * T, 8, gbc, hT, hT_r, 0, tiles)
                for j in range(NFC):
                    bg = (2 * j) % 4
                    bu = (2 * j + 1) % 4
                    P.dma('sp', wst[bg][:], wg_d[l, :, j * 128:(j + 1) * 128].rearrange("(kc p) m -> p kc m", p=128),
                          reads=[self.wr], writes=[wst_r[bg]])
                    P.dma('sp', wst[bu][:], wu_d[l, :, j * 128:(j + 1) * 128].rearrange("(kc p) m -> p kc m", p=128),
                          reads=[self.wr], writes=[wst_r[bu]])
                    P.op('pool', lambda bg=bg: nc.gpsimd.tensor_copy(out=wbf[bg][:], in_=wst[bg][:]),
                         reads=[wst_r[bg]], writes=[wbf_r[bg]])
                    P.op('pool', lambda bu=bu: nc.gpsimd.tensor_copy(out=wbf[bu][:], in_=wst[bu][:]),
                         reads=[wst_r[bu]], writes=[wbf_r[bu]])
                    for th in range(2):
                        pb = (2 * j + th) % 2
                        for kc in range(8):
                            P.op('pe', lambda kc=kc, th=th, pb=pb, bg=bg: nc.tensor.matmul(
                                pg[pb][:], wbf[bg][:, kc, :], hT[:, kc, th * 512:(th + 1) * 512],
                                start=(kc == 0), stop=(kc == 7)),
                                reads=[wbf_r[bg], hT_r], writes=[pg_r[pb]])
                        for kc in range(8):
                            P.op('pe', lambda kc=kc, th=th, pb=pb, bu=bu: nc.tensor.matmul(
                                pu[pb][:], wbf[bu][:, kc, :], hT[:, kc, th * 512:(th + 1) * 512],
                                start=(kc == 0), stop=(kc == 7)),
                                reads=[wbf_r[bu], hT_r], writes=[pu_r[pb]])
                        P.op('act', lambda pb=pb: nc.scalar.activation(out=sg[pb][:], in_=pg[pb][:], func=AF.Silu),
                             reads=[pg_r[pb]], writes=[sg_r[pb]])
                        P.op('dve', lambda pb=pb, j=j, th=th: nc.vector.tensor_tensor(
                            act[:, j, th * 512:(th + 1) * 512], sg[pb][:], pu[pb][:], ALU.mult),
                            reads=[sg_r[pb], pu_r[pb]], writes=[act_r], conc=True)
                for tb in range(8):
                    for dh in range(2):
                        pb = (2 * tb + dh) % 2
                        for j in range(NFC):
                            P.op('pe', lambda j=j, tb=tb, dh=dh, pb=pb: nc.tensor.matmul(
                                po[pb][:], act[:, j, tb * 128:(tb + 1) * 128], wd[:, j, dh * 512:(dh + 1) * 512],
                                start=(j == 0), stop=(j == NFC - 1)),
                                reads=[act_r, wd_r], writes=[po_r[pb]])
                        P.op('dve', lambda tb=tb, dh=dh, pb=pb: nc.vector.scalar_tensor_tensor(
                            xs[:, tb, dh * 512:(dh + 1) * 512], po[pb][:], 0.5, xs[:, tb, dh * 512:(dh + 1) * 512],
                            ALU.mult, ALU.add),
                            reads=[po_r[pb], xs_rs[tb]], writes=[xs_rs[tb]])
                    r0 = st * T + tb * 128
                    P.dma('sp', dst_ap[r0:r0 + 128, :], xs[:, tb, :], reads=[xs_rs[tb]], writes=[dst_r], conc=True)
            P.barrier()

    def attn_bias_setup(self, es_glob):
        nc, P = self.nc, self.P
        self.EB, self.EB_r = self.sb(es_glob, "EB", [128, 12, 256], F32)
        with ExitStack() as es:
            tb_, tb_r = self.sb(es, "relb", [32, 12], F32)
            oh, oh_r = self.sb(es, "oh", [32, 3, 512], F32)
            bm, bm_r = self.sb(es, "bm", [4, 512], F32)
            ee, ee_r = self.sb(es, "ee", [4, 512], F32)
            pp, pp_r = self.ps(es, "pbias", [4, 512], F32)
            P.dma('sp', tb_[:], self.w['rel_bias'].ap(), reads=[self.wr], writes=[tb_r])
            P.dma('sp', oh[:], self.c['oh'].ap().rearrange("g b i -> b g i"), reads=[self.wr], writes=[oh_r])
            P.dma('sp', bm[:], self.c['bm'].ap(), reads=[self.wr], writes=[bm_r])
            for g in range(3):
                P.op('pe', lambda g=g: nc.tensor.matmul(pp[:], tb_[:, g * 4:(g + 1) * 4], oh[:, g, :], start=True, stop=True),
                     reads=[tb_r, oh_r], writes=[pp_r])
                P.op('act', lambda: nc.scalar.activation(out=ee[:], in_=pp[:], func=AF.Exp), reads=[pp_r], writes=[ee_r])
                P.op('dve', lambda: nc.vector.tensor_tensor(ee[:], ee[:], bm[:], ALU.mult), reads=[ee_r, bm_r], writes=[ee_r])
                P.dma('sp', self.rv_t.ap()[g * 4:(g + 1) * 4, :], ee[:], reads=[ee_r], writes=[self.rv_r], conc=True)
            ebr, ebr_r = self.sb(es, "ebr", [128, 12, 256], F32)
            jrev, jrev_r = self.sb(es, "jrev", [128, 128], F32)
            pj, pj_r = self.ps(es, "pj", [128, 512], F32)
            P.dma('sp', jrev[:], self.c['jrev'].ap(), reads=[self.wr], writes=[jrev_r])
            for h in range(12):
                srcA = bass.AP(self.rv_t, h * 512 + 192, [[1, 128], [1, 128]])
                srcB = bass.AP(self.rv_t, h * 512 + 64, [[1, 128], [1, 128]])
                P.dma('sp', ebr[:, h, 0:128], srcA, reads=[self.rv_r], writes=[ebr_r], conc=True)
                P.dma('sp', ebr[:, h, 128:256], srcB, reads=[self.rv_r], writes=[ebr_r], conc=True)
            for hp in range(6):
                P.op('pe', lambda hp=hp: nc.tensor.matmul(pj[:], jrev[:], ebr[:, 2 * hp:2 * hp + 2, :].rearrange("p h e -> p (h e)"),
                                                          start=True, stop=True),
                     reads=[jrev_r, ebr_r], writes=[pj_r])
                P.op('act', lambda hp=hp: nc.scalar.copy(out=self.EB[:, 2 * hp:2 * hp + 2, :].rearrange("p h e -> p (h e)"),
                                                         in_=pj[:]),
                     reads=[pj_r], writes=[self.EB_r], conc=True)
            P.barrier()

    def mix_proj(self, l, src):
        nc, P = self.nc, self.P
        src_t, src_r = src
        w_in = self.w['w_in'].ap()
        HP = SEQ + 2
        with ExitStack() as es:
            ident, ident_r = self.load_const_tiles(es)
            gbc = self.sb(es, "gbc", [128, D_MODEL], F32)
            P.dma('sp', gbc[0][:], bcast_rows(self.w['mix_norm'], l * D_MODEL, D_MODEL), reads=[self.wr], writes=[gbc[1]])
            hT, hT_r = self.hT, self.hT_r
            P.op('dve', lambda: nc.vector.memset(hT[:, :, 0:1], 0.0), writes=[hT_r], conc=True)
            P.op('dve', lambda: nc.vector.memset(hT[:, :, HP - 1:HP], 0.0), writes=[hT_r], conc=True)
            with ExitStack() as es2:
                xs, _ = self.sb(es2, "xs", [128, 2, D_MODEL], F32)
                xs_rs = [P.reg("xsm%d" % i) for i in range(2)]
                junk, junk_r = self.sb(es2, "junk", [128, D_MODEL], F32)
                ss, _ = self.sb(es2, "ss", [128, NTB], F32)
                ss_r = [P.reg("ssm%d" % i) for i in range(NTB)]
                rs, _ = self.sb(es2, "rs", [128, NTB], F32)
                rs_r = [P.reg("rsm%d" % i) for i in range(NTB)]
                hb, hb_r, pst, pst_r = [], [], [], []
                for i in range(2):
                    a, b = self.sb(es2, "hb", [128, D_MODEL], BF16)
                    hb.append(a)
                    hb_r.append(b)
                    a, b = self.ps(es2, "pst", [128, 8, 128], BF16)
                    pst.append(a)
                    pst_r.append(b)
                tiles = (xs, xs_rs, junk, junk_r, ss, ss_r, rs, rs_r, hb, hb_r, pst, pst_r, ident, ident_r)
                self.norm_to_hT(es2, src_t, src_r, 0, NTB, gbc, hT, hT_r, 1, tiles)
                P.barrier()
            pA, pA_r, pS, pS_r = [], [], [], []
            for i in range(2):
                a, b = self.ps(es, "pA", [128, 512], F32)
                pA.append(a)
                pA_r.append(b)
                a, b = self.ps(es, "pS", [128, 512], F32)
                pS.append(a)
                pS_r.append(b)
            with ExitStack() as es2:
                wst, wst_r, wbf, wbf_r = [], [], [], []
                for i in range(2):
                    a, b = self.sb(es2, "wstq", [128, 8, 128], F32)
                    wst.append(a)
                    wst_r.append(b)
                    a, b = self.sb(es2, "wbfq", [128, 8, 128], BF16)
                    wbf.append(a)
                    wbf_r.append(b)
                onesb, onesb_r = self.sb(es2, "onesb", [128, 128], F32)
                P.dma('sp', onesb[:], self.c['ones_blk'].ap(), reads=[self.wr], writes=[onesb_r])
                gq, gq_r = self.sb(es2, "gq", [128, 1], F32)
                gk, gk_r = self.sb(es2, "gk", [128, 1], F32)
                for hlf in range(2):
                    P.dma('sp', gq[hlf * 64:(hlf + 1) * 64, :], bass.AP(self.w['q_norm'], l * 64, [[1, 64], [1, 1]]),
                          reads=[self.wr], writes=[gq_r], conc=True)
                    P.dma('sp', gk[hlf * 64:(hlf + 1) * 64, :], bass.AP(self.w['k_norm'], l * 64, [[1, 64], [1, 1]]),
                          reads=[self.wr], writes=[gk_r], conc=True)
                P.op('dve', lambda: nc.vector.tensor_scalar(gq[:], gq[:], 0.125, None, ALU.mult), reads=[gq_r], writes=[gq_r])
                bgt, bgt_r = self.sb(es2, "bgt", [128, 16], F32)
                P.dma('sp', bgt[:], bass.AP(self.w['b_gate'], l * 2048, [[1, 128], [128, 16]]), reads=[self.wr], writes=[bgt_r],
                      allow_slow_non_contiguous=True)
                qf, qf_r, sq, sq_r, rr, rr_r, qn, qn_r, gs, gs_r = [], [], [], [], [], [], [], [], [], []
                for i in range(2):
                    for lst, lstr, nm, dt in ((qf, qf_r, "qf", F32), (sq, sq_r, "sq", F32), (rr, rr_r, "rr", F32),
                                              (qn, qn_r, "qn", BF16), (gs, gs_r, "gs", F32)):
                        a, b = self.sb(es2, nm, [128, 512], dt)
                        lst.append(a)
                        lstr.append(b)
                it = 0
                for c in range(12):
                    wb = c % 2
                    col0 = 1536 + c * 128
                    P.dma('sp', wst[wb][:], w_in[l, :, col0:col0 + 128].rearrange("(kc p) m -> p kc m", p=128),
                          reads=[self.wr], writes=[wst_r[wb]])
                    P.op('pool', lambda wb=wb: nc.gpsimd.tensor_copy(out=wbf[wb][:], in_=wst[wb][:]),
                         reads=[wst_r[wb]], writes=[wbf_r[wb]])
                    gain, gain_r = (gq, gq_r) if c < 6 else (gk, gk_r)
                    for tq in range(8):
                        pb = it % 2
                        it += 1
                        for kc in range(8):
                            P.op('pe', lambda kc=kc, tq=tq, pb=pb, wb=wb: nc.tensor.matmul(
                                pA[pb][:], wbf[wb][:, kc, :], hT[:, kc, 1 + tq * 512:1 + (tq + 1) * 512],
                                start=(kc == 0), stop=(kc == 7)),
                                reads=[wbf_r[wb], hT_r], writes=[pA_r[pb]])
                        P.op('act', lambda pb=pb: nc.scalar.copy(out=qf[pb][:], in_=pA[pb][:]), reads=[pA_r[pb]], writes=[qf_r[pb]])
                        P.op('act', lambda pb=pb: nc.scalar.activation(out=sq[pb][:], in_=pA[pb][:], func=AF.Square),
                             reads=[pA_r[pb]], writes=[sq_r[pb]])
                        P.op('pe', lambda pb=pb: nc.tensor.matmul(pS[pb][:], onesb[:], sq[pb][:], start=True, stop=True),
                             reads=[onesb_r, sq_r[pb]], writes=[pS_r[pb]])
                        P.op('act', lambda pb=pb: nc.scalar.activation(out=rr[pb][:], in_=pS[pb][:], func=AF.Ln,
                                                                       scale=1.0 / 64, bias=self.eps_t[:]),
                             reads=[pS_r[pb], self.eps_r], writes=[rr_r[pb]])
                        P.op('act', lambda pb=pb: nc.scalar.activation(out=rr[pb][:], in_=rr[pb][:], func=AF.Exp, scale=-0.5),
                             reads=[rr_r[pb]], writes=[rr_r[pb]])
                        P.op('dve', lambda pb=pb, gain=gain: nc.vector.scalar_tensor_tensor(
                            qn[pb][:], qf[pb][:], gain[:], rr[pb][:], ALU.mult, ALU.mult),
                            reads=[qf_r[pb], gain_r, rr_r[pb]], writes=[qn_r[pb]])
                        P.dma('sp', self.qk_t.ap()[c, :, tq * 512:(tq + 1) * 512], qn[pb][:], reads=[qn_r[pb]],
                              writes=[self.qk_r], conc=True)
                w_gate = self.w['w_gate'].ap()
                for c in range(16):
                    wb = c % 2
                    P.dma('sp', wst[wb][:], w_gate[l, :, c * 128:(c + 1) * 128].rearrange("(kc p) m -> p kc m", p=128),
                          reads=[self.wr], writes=[wst_r[wb]])
                    P.op('pool', lambda wb=wb: nc.gpsimd.tensor_copy(out=wbf[wb][:], in_=wst[wb][:]),
                         reads=[wst_r[wb]], writes=[wbf_r[wb]])
                    for tq in range(8):
                        pb = it % 2
                        it += 1
                        for kc in range(8):
                            P.op('pe', lambda kc=kc, tq=tq, pb=pb, wb=wb: nc.tensor.matmul(
                                pA[pb][:], wbf[wb][:, kc, :], hT[:, kc, 1 + tq * 512:1 + (tq + 1) * 512],
                                start=(kc == 0), stop=(kc == 7)),
                                reads=[wbf_r[wb], hT_r], writes=[pA_r[pb]])
                        P.op('act', lambda pb=pb, c=c: nc.scalar.activation(out=gs[pb][:], in_=pA[pb][:], func=AF.Sigmoid,
                                                                            bias=bgt[:, c:c + 1]),
                             reads=[pA_r[pb], bgt_r], writes=[gs_r[pb]])
                        P.dma('sp', self.gt_t.ap()[c, :, tq * 512:(tq + 1) * 512], gs[pb][:], reads=[gs_r[pb]],
                              writes=[self.gt_r], conc=True)
                P.barrier()
            with ExitStack() as es2:
                wsv, wsv_r = self.sb(es2, "wsv", [128, 8, 256], F32)
                wbv, wbv_r = self.sb(es2, "wbv", [128, 8, 256], BF16)
                zt, zt_r = self.sb(es2, "zt", [64, 512], BF16)
                P.op('dve', lambda: nc.vector.memset(zt[:], 0.0), writes=[zt_r])
                vst, vst_r = [], []
                for i in range(2):
                    a, b = self.sb(es2, "vst", [128, 4, 128], BF16)
                    vst.append(a)
                    vst_r.append(b)
                    P.op('dve', lambda a=a: nc.vector.memset(a[:, :, 64:128], 1.0), writes=[b])
                it = 0
                for g in range(3):
                    D = DILS[g]
                    m = SEQ // D
                    vd_t, vd_r = self.vd_t[g]
                    col0 = 3072 + g * 256
                    P.dma('sp', wsv[:], w_in[l, :, col0:col0 + 256].rearrange("(kc p) m -> p kc m", p=128),
                          reads=[self.wr], writes=[wsv_r])
                    P.op('pool', lambda: nc.gpsimd.tensor_copy(out=wbv[:], in_=wsv[:]), reads=[wsv_r], writes=[wbv_r])
                    for r in range(D):
                        P.dma('sp', vd_t.ap()[r, 0:64, :, :].rearrange("t h e -> t (h e)"), zt[:], reads=[zt_r], writes=[vd_r],
                              conc=True)
                        P.dma('sp', vd_t.ap()[r, m + 64:m + 128, :, :].rearrange("t h e -> t (h e)"), zt[:], reads=[zt_r],
                              writes=[vd_r], conc=True)
                        for b in range(m // 128):
                            pb = it % 2
                            it += 1
                            t0 = 1 + r + D * 128 * b
                            for kc in range(8):
                                P.op('pe', lambda kc=kc, t0=t0, D=D, pb=pb: nc.tensor.matmul(
                                    pA[pb][:, 0:256], hT[:, kc, sl(t0, 128, D)], wbv[:, kc, :], start=(kc == 0), stop=(kc == 7)),
                                    reads=[hT_r, wbv_r], writes=[pA_r[pb]])
                            P.op('act', lambda pb=pb: nc.scalar.copy(
                                out=vst[pb][:, :, 0:64], in_=pA[pb][:, 0:256].rearrange("p (h e) -> p h e", h=4)),
                                reads=[pA_r[pb]], writes=[vst_r[pb]])
                            P.dma('sp', vd_t.ap()[r, 64 + 128 * b:64 + 128 * (b + 1), :, :], vst[pb][:], reads=[vst_r[pb]],
                                  writes=[vd_r], conc=True)
                P.barrier()

    def mix_filters(self, l):
        nc, P = self.nc, self.P
        with ExitStack() as es:
            es_h = ExitStack()
            h2, h2_r = self.sb(es, "h2", [HID, SEQ], F32)
            w3, w3_r = self.sb(es, "w3", [HID, 2048], F32)
            ph, ph_r = [], []
            for i in range(2):
                a, b = self.ps(es, "ph", [128, 512], F32)
                ph.append(a)
                ph_r.append(b)
            zT, zT_r = self.sb(es_h, "zT", [HY_EMB, SEQ], F32)
            P.dma('sp', zT[:], self.c['zfeat'].ap(), reads=[self.wr], writes=[zT_r])
            w1, w1_r = self.sb(es_h, "w1", [HY_EMB, HID], F32)
            w2, w2_r = self.sb(es_h, "w2", [HID, HID], F32)
            P.dma('sp', w1[:], self.w['hy_filt_w1'].ap()[l], reads=[self.wr], writes=[w1_r])
            P.dma('sp', w2[:], self.w['hy_filt_w2'].ap()[l], reads=[self.wr], writes=[w2_r])
            P.dma('sp', w3[:], self.w['hy_filt_w3'].ap()[l], reads=[self.wr], writes=[w3_r])
            bq, bq_r = self.sb(es_h, "bq", [HID, 4], F32)
            P.dma('sp', bq[:, 0:1], bass.AP(self.w['hy_filt_b1'], l * 64, [[1, 64], [1, 1]]), reads=[self.wr], writes=[bq_r],
                  conc=True)
            P.dma('sp', bq[:, 2:3], bass.AP(self.w['hy_filt_b2'], l * 64, [[1, 64], [1, 1]]), reads=[self.wr], writes=[bq_r],
                  conc=True)
            for c in (0, 2):
                P.op('dve', lambda c=c: nc.vector.tensor_scalar(bq[:, c:c + 1], bq[:, c:c + 1], 0.25, None, ALU.mult),
                     reads=[bq_r], writes=[bq_r])
                P.op('dve', lambda c=c: nc.vector.tensor_scalar(bq[:, c + 1:c + 2], bq[:, c:c + 1], math.pi / 2, None, ALU.add),
                     reads=[bq_r], writes=[bq_r])
            h1, h1_r = self.sb(es_h, "h1", [HID, SEQ], F32)
            s4, s4_r = self.sb(es_h, "s4", [HID, 512], F32)
            c4, c4_r = self.sb(es_h, "c4", [HID, 512], F32)
            tt, tt_r = self.sb(es_h, "tt", [HID, 512], F32)
            for layer in range(2):
                wt, wt_r, K_, src, src_r, dstt, dst_r = ((w1, w1_r, HY_EMB, zT, zT_r, h1, h1_r) if layer == 0 else
                                                        (w2, w2_r, HID, h1, h1_r, h2, h2_r))
                bc = 2 * layer
                for tq in range(8):
                    pb = tq % 2
                    P.op('pe', lambda tq=tq, pb=pb, wt=wt, src=src, K_=K_: nc.tensor.matmul(
                        ph[pb][0:HID, :], wt[0:K_, :], src[0:K_, tq * 512:(tq + 1) * 512], start=True, stop=True),
                        reads=[wt_r, src_r], writes=[ph_r[pb]])
                    P.op('act', lambda pb=pb, bc=bc: nc.scalar.activation(out=s4[:], in_=ph[pb][0:HID, :], func=AF.Sin, scale=0.25,
                                                                          bias=bq[:, bc:bc + 1]),
                         reads=[ph_r[pb], bq_r], writes=[s4_r])
                    P.op('act', lambda pb=pb, bc=bc: nc.scalar.activation(out=c4[:], in_=ph[pb][0:HID, :], func=AF.Sin, scale=0.25,
                                                                          bias=bq[:, bc + 1:bc + 2]),
                         reads=[ph_r[pb], bq_r], writes=[c4_r])
                    P.op('dve', lambda: nc.vector.tensor_tensor(tt[:], s4[:], c4[:], ALU.mult), reads=[s4_r, c4_r], writes=[tt_r])
                    P.op('dve', lambda: nc.vector.tensor_tensor(c4[:], s4[:], s4[:], ALU.mult), reads=[s4_r], writes=[c4_r])
                    P.op('dve', lambda: nc.vector.tensor_scalar(c4[:], c4[:], -8.0, 4.0, ALU.mult, ALU.add), reads=[c4_r],
                         writes=[c4_r])
                    P.op('dve', lambda tq=tq, dstt=dstt: nc.vector.tensor_tensor(dstt[:, tq * 512:(tq + 1) * 512], tt[:], c4[:],
                                                                                ALU.mult),
                         reads=[tt_r, c4_r], writes=[dst_r], conc=True)
            P.barrier()
            es_h.close()
            onesa, onesa_r = self.sb(es, "onesa", [128, 128], F32)
            P.dma('sp', onesa[:], self.c['ones_all'].ap(), reads=[self.wr], writes=[onesa_r])
            a_t, a_r = self.sb(es, "a_t", [128, NTB, 512], BF16)
            b_t, b_r = self.sb(es, "b_t", [128, NTB, 512], BF16)
            pq, pq_r = self.ps(es, "pq", [128, 512], F32)
            pk, pk_r = [], []
            for i in range(2):
                a, b = self.ps(es, "pk", [128, 512], F32)
                pk.append(a)
                pk_r.append(b)
            dfb, dfb_r, dbb, dbb_r, kfb, kfb_r, kbb, kbb_r, sqf, sqf_r, sqb, sqb_r = ([] for _ in range(12))
            for i in range(2):
                for lst, lstr, nm in ((dfb, dfb_r, "dfb"), (dbb, dbb_r, "dbb"), (kfb, kfb_r, "kfb"), (kbb, kbb_r, "kbb"),
                                      (sqf, sqf_r, "sqf"), (sqb, sqb_r, "sqb")):
                    a, b = self.sb(es, nm, [128, 512], F32)
                    lst.append(a)
                    lstr.append(b)
            nrm, nrm_r = self.sb(es, "nrm", [128, 512], F32)
            ft, ft_r = [], []
            for i in range(2):
                a, b = self.sb(es, "ft", [128, NTB, 128], BF16)
                ft.append(a)
                ft_r.append(b)
            ko, ko_r = [], []
            for i in range(2):
                a, b = self.sb(es, "ko", [128, 512], F32)
                ko.append(a)
                ko_r.append(b)
            for o in range(2):
                cf = o * 1024
                cb = o * 1024 + 512
                for tb in range(NTB):
                    pb = tb % 2
                    P.dma('sp', dfb[pb][:], self.c['dec_f'].ap()[tb * 128:(tb + 1) * 128, :], reads=[self.wr], writes=[dfb_r[pb]])
                    P.dma('sp', dbb[pb][:], self.c['dec_b'].ap()[tb * 128:(tb + 1) * 128, :], reads=[self.wr], writes=[dbb_r[pb]])
                    P.op('pe', lambda tb=tb, pb=pb, cf=cf: nc.tensor.matmul(ph[pb][:], h2[:, tb * 128:(tb + 1) * 128],
                                                                           w3[:, cf:cf + 512], start=True, stop=True),
                         reads=[h2_r, w3_r], writes=[ph_r[pb]])
                    P.op('pe', lambda tb=tb, pb=pb, cb=cb: nc.tensor.matmul(pk[pb][:], h2[:, tb * 128:(tb + 1) * 128],
                                                                           w3[:, cb:cb + 512], start=True, stop=True),
                         reads=[h2_r, w3_r], writes=[pk_r[pb]])
                    P.op('dve', lambda pb=pb: nc.vector.tensor_tensor(kfb[pb][:], ph[pb][:], dfb[pb][:], ALU.mult),
                         reads=[ph_r[pb], dfb_r[pb]], writes=[kfb_r[pb]])
                    P.op('dve', lambda pb=pb: nc.vector.tensor_tensor(kbb[pb][:], pk[pb][:], dbb[pb][:], ALU.mult),
                         reads=[pk_r[pb], dbb_r[pb]], writes=[kbb_r[pb]])
                    P.op('act', lambda pb=pb: nc.scalar.activation(out=sqf[pb][:], in_=kfb[pb][:], func=AF.Square),
                         reads=[kfb_r[pb]], writes=[sqf_r[pb]])
                    P.op('act', lambda pb=pb: nc.scalar.activation(out=sqb[pb][:], in_=kbb[pb][:], func=AF.Square),
                         reads=[kbb_r[pb]], writes=[sqb_r[pb]])
                    P.op('pe', lambda pb=pb, tb=tb: nc.tensor.matmul(pq[:], onesa[:], sqf[pb][:], start=(tb == 0), stop=False),
                         reads=[onesa_r, sqf_r[pb]], writes=[pq_r])
                    P.op('pe', lambda pb=pb, tb=tb: nc.tensor.matmul(pq[:], onesa[:], sqb[pb][:], start=False,
                                                                     stop=(tb == NTB - 1)),
                         reads=[onesa_r, sqb_r[pb]], writes=[pq_r])
                    P.op('pool', lambda pb=pb, tb=tb: nc.gpsimd.tensor_tensor(a_t[:, tb, :], kfb[pb][:], kbb[pb][:], ALU.add),
                         reads=[kfb_r[pb], kbb_r[pb]], writes=[a_r], conc=True)
                    P.op('pool', lambda pb=pb, tb=tb: nc.gpsimd.tensor_tensor(b_t[:, tb, :], kfb[pb][:], kbb[pb][:], ALU.subtract),
                         reads=[kfb_r[pb], kbb_r[pb]], writes=[b_r], conc=True)
                P.op('act', lambda: nc.scalar.activation(out=nrm[:], in_=pq[:], func=AF.Ln, bias=self.eps_t[:]),
                     reads=[pq_r, self.eps_r], writes=[nrm_r])
                P.op('act', lambda: nc.scalar.activation(out=nrm[:], in_=nrm[:], func=AF.Exp, scale=-0.5), reads=[nrm_r],
                     writes=[nrm_r])
                for s in range(64):
                    pb = s % 2
                    P.dma('sp', ft[pb][:], self.c['Fh'].ap()[s], reads=[self.wr], writes=[ft_r[pb]])
                    srct, srcr = (a_t, a_r) if s < 32 else (b_t, b_r)
                    for nb in range(NTB):
                        P.op('pe', lambda nb=nb, pb=pb, srct=srct: nc.tensor.matmul(pk[pb][:], ft[pb][:, nb, :], srct[:, nb, :],
                                                                                  start=(nb == 0), stop=(nb == NTB - 1)),
                             reads=[ft_r[pb], srcr], writes=[pk_r[pb]])
                    P.op('dve', lambda pb=pb: nc.vector.tensor_tensor(ko[pb][:], pk[pb][:], nrm[:], ALU.mult),
                         reads=[pk_r[pb], nrm_r], writes=[ko_r[pb]])
                    P.dma('sp', self.kf_t.ap()[o, s], ko[pb][:], reads=[ko_r[pb]], writes=[self.kf_r], conc=True)
            P.barrier()

    def mix_conv(self, l):
        nc, P = self.nc, self.P
        with ExitStack() as es:
            Y, Y_r = self.sb(es, "Y", [128, 64, 512], BF16)
            ftr, ftr_r, fti, fti_r, kre, kre_r, kim, kim_r = ([] for _ in range(8))
            pre, pre_r, pim, pim_r = [], [], [], []
            for i in range(2):
                for lst, lstr, nm in ((ftr, ftr_r, "ftr"), (fti, fti_r, "fti")):
                    a, b = self.sb(es, nm, [128, NTB, 128], BF16)
                    lst.append(a)
                    lstr.append(b)
                for lst, lstr, nm in ((kre, kre_r, "kre"), (kim, kim_r, "kim")):
                    a, b = self.sb(es, nm, [128, 512], F32)
                    lst.append(a)
                    lstr.append(b)
                a, b = self.ps(es, "pre", [128, 512], F32)
                pre.append(a)
                pre_r.append(b)
                a, b = self.ps(es, "pim", [128, 512], F32)
                pim.append(a)
                pim_r.append(b)
            t1, t1_r = self.sb(es, "t1", [128, 512], F32)
            t2, t2_r = self.sb(es, "t2", [128, 512], F32)
            gt, gt_r = [], []
            for i in range(2):
                a, b = self.sb(es, "gtile", [128, 64, 128], BF16)
                gt.append(a)
                gt_r.append(b)
            pc, pc_r, gate, gate_r, zo, zo_r, zn, zn_r = ([] for _ in range(8))
            for i in range(2):
                a, b = self.ps(es, "pc", [128, 512], F32)
                pc.append(a)
                pc_r.append(b)
                for lst, lstr, nm in ((gate, gate_r, "gate"), (zo, zo_r, "zo"), (zn, zn_r, "zn")):
                    a, b = self.sb(es, nm, [128, 512], F32)
                    lst.append(a)
                    lstr.append(b)
            dbc, dbc_r = self.sb(es, "dbc", [128, 512], F32)
            for o in range(2):
                P.dma('sp', dbc[:], bcast_rows(self.w['hy_skip'], (l * 2 + o) * 512, 512), reads=[self.wr], writes=[dbc_r])
                for j in range(32):
                    pb = j % 2
                    P.dma('sp', ftr[pb][:], self.c['Fh'].ap()[j], reads=[self.wr], writes=[ftr_r[pb]])
                    P.dma('sp', fti[pb][:], self.c['Fh'].ap()[32 + j], reads=[self.wr], writes=[fti_r[pb]])
                    P.dma('sp', kre[pb][:], self.kf_t.ap()[o, j], reads=[self.kf_r], writes=[kre_r[pb]])
                    P.dma('sp', kim[pb][:], self.kf_t.ap()[o, 32 + j], reads=[self.kf_r], writes=[kim_r[pb]])
                    for nb in range(NTB):
                        P.op('pe', lambda nb=nb, pb=pb: nc.tensor.matmul(pre[pb][:], ftr[pb][:, nb, :], u[:, nb, :],
                                                                        start=(nb == 0), stop=(nb == NTB - 1)),
                             reads=[ftr_r[pb], u_r], writes=[pre_r[pb]])
                    for nb in range(NTB):
                        P.op('pe', lambda nb=nb, pb=pb: nc.tensor.matmul(pim[pb][:], fti[pb][:, nb, :], u[:, nb, :],
                                                                        start=(nb == 0), stop=(nb == NTB - 1)),
                             reads=[fti_r[pb], u_r], writes=[pim_r[pb]])
                    P.op('dve', lambda pb=pb: nc.vector.tensor_tensor(t1[:], pre[pb][:], kre[pb][:], ALU.mult),
                         reads=[pre_r[pb], kre_r[pb]], writes=[t1_r])
                    P.op('dve', lambda pb=pb: nc.vector.tensor_tensor(t2[:], pim[pb][:], kim[pb][:], ALU.mult),
                         reads=[pim_r[pb], kim_r[pb]], writes=[t2_r])
                    P.op('dve', lambda j=j: nc.vector.tensor_tensor(Y[:, j, :], t1[:], t2[:], ALU.subtract),
                         reads=[t1_r, t2_r], writes=[Y_r], conc=True)
                    P.op('dve', lambda pb=pb: nc.vector.tensor_tensor(t1[:], pre[pb][:], kim[pb][:], ALU.mult),
                         reads=[pre_r[pb], kim_r[pb]], writes=[t1_r])
                    P.op('dve', lambda pb=pb: nc.vector.tensor_tensor(t2[:], pim[pb][:], kre[pb][:], ALU.mult),
                         reads=[pim_r[pb], kre_r[pb]], writes=[t2_r])
                    P.op('dve', lambda j=j: nc.vector.tensor_tensor(Y[:, 32 + j, :], t1[:], t2[:], ALU.add),
                         reads=[t1_r, t2_r], writes=[Y_r], conc=True)
                for nb in range(NTB):
                    pb = nb % 2
                    P.dma('sp', gt[pb][:], self.c['Gh'].ap()[nb], reads=[self.wr], writes=[gt_r[pb]])
                    rows = slice(nb * 128, (nb + 1) * 128)
                    P.dma('sp', gate[pb][:], self.hy_t.ap()[rows, 512 * (1 + o):512 * (2 + o)], reads=[self.hy_r],
                          writes=[gate_r[pb]])
                    if o == 0:
                        P.dma('sp', zo[pb][:], self.hy_t.ap()[rows, 0:512], reads=[self.hy_r], writes=[zo_r[pb]])
                    else:
                        P.dma('sp', zo[pb][:], self.z1_t.ap()[rows, :], reads=[self.z1_r], writes=[zo_r[pb]])
                    for s in range(64):
                        P.op('pe', lambda s=s, pb=pb: nc.tensor.matmul(pc[pb][:], gt[pb][:, s, :], Y[:, s, :], start=(s == 0),
                                                                      stop=(s == 63)),
                             reads=[gt_r[pb], Y_r], writes=[pc_r[pb]])
                    P.op('dve', lambda pb=pb: nc.vector.tensor_tensor(zo[pb][:], zo[pb][:], dbc[:], ALU.mult),
                         reads=[zo_r[pb], dbc_r], writes=[zo_r[pb]])
                    P.op('dve', lambda pb=pb: nc.vector.tensor_tensor(zo[pb][:], pc[pb][:], zo[pb][:], ALU.add),
                         reads=[pc_r[pb], zo_r[pb]], writes=[zo_r[pb]])
                    P.op('dve', lambda pb=pb: nc.vector.tensor_tensor(zn[pb][:], zo[pb][:], gate[pb][:], ALU.mult),
                         reads=[zo_r[pb], gate_r[pb]], writes=[zn_r[pb]])
                    if o == 0 or self.dbg:
                        dstt, dstr = (self.z1_t, self.z1_r) if o == 0 else (self.z2_t, self.z2_r)
                        P.dma('sp', dstt.ap()[rows, :], zn[pb][:], reads=[zn_r[pb]], writes=[dstr], conc=True)
                    P.op('act', lambda pb=pb, nb=nb: nc.scalar.copy(out=u[:, nb, :], in_=zn[pb][:]), reads=[zn_r[pb]],
                         writes=[u_r], conc=True)
            P.barrier()

    def ct_consts(self, es):
        nc, P = self.nc, self.P
        C = {}
        for name, shp, dt in (('f1cat', [128, 256], BF16), ('g1re', [128, 128], BF16), ('g1imn', [128, 128], BF16),
                              ('tf_re', [128, 256], F32), ('tf_im', [128, 256], F32), ('tc_re', [128, 512], F32),
                              ('tc_im', [128, 512], F32), ('bd_ere', [128, 128], BF16), ('bd_eren', [128, 128], BF16),
                              ('bd_eim', [128, 128], BF16), ('bd_eimn', [128, 128], BF16)):
            t, r = self.sb(es, name, shp, dt)
            P.dma('sp', t[:], self.c[name].ap(), reads=[self.wr], writes=[r])
            C[name] = (t, r)
        C['tmp'] = [self.sb(es, "cttmp", [128, 512], F32) for _ in range(4)]
        C['ps1'] = [self.ps(es, "ctps1", [128, 512], F32) for _ in range(2)]
        C['psx'] = [self.ps(es, "ctpsx", [128, 512], F32) for _ in range(4)]
        return C

    def ct_s1_twiddle(self, C, src, src_r, A, A_r):
        nc, P = self.nc, self.P
        f1, f1_r = C['f1cat']
        tfr, tfr_r = C['tf_re']
        tfi, tfi_r = C['tf_im']
        for q in range(8):
            ps, ps_r = C['ps1'][q % 2]
            for h in range(2):
                cg = 2 * q + h
                P.op('pe', lambda cg=cg, h=h, ps=ps: nc.tensor.matmul(
                    ps[:, h * 256:(h + 1) * 256], src[:, 4 * cg:4 * cg + 4, :].rearrange("p c r -> p (c r)"), f1[:],
                    start=True, stop=True), reads=[src_r, f1_r], writes=[ps_r])
            pv = ps[:].rearrange("p (g h k) -> p g h k", g=2, h=2)
            (t1, t1_r), (t2, t2_r), (t3, t3_r), (t4, t4_r) = C['tmp']
            v3 = lambda t: t[:, 0:256].rearrange("p (g k) -> p g k", g=2)
            tf3 = lambda t: t[:].rearrange("p (g k) -> p g k", g=2)
            P.op('dve', lambda pv=pv: nc.vector.tensor_tensor(v3(t1), pv[:, :, 0, :], tf3(tfr), ALU.mult),
                 reads=[ps_r, tfr_r], writes=[t1_r])
            P.op('dve', lambda pv=pv: nc.vector.tensor_tensor(v3(t2), pv[:, :, 1, :], tf3(tfi), ALU.mult),
                 reads=[ps_r, tfi_r], writes=[t2_r])
            P.op('dve', lambda pv=pv: nc.vector.tensor_tensor(v3(t3), pv[:, :, 0, :], tf3(tfi), ALU.mult),
                 reads=[ps_r, tfi_r], writes=[t3_r])
            P.op('dve', lambda pv=pv: nc.vector.tensor_tensor(v3(t4), pv[:, :, 1, :], tf3(tfr), ALU.mult),
                 reads=[ps_r, tfr_r], writes=[t4_r])
            P.op('pool', lambda q=q: nc.gpsimd.tensor_tensor(A[:, 2 * q:2 * q + 2, 0, :], v3(t1), v3(t2), ALU.subtract),
                 reads=[t1_r, t2_r], writes=[A_r], conc=True)
            P.op('pool', lambda q=q: nc.gpsimd.tensor_tensor(A[:, 2 * q:2 * q + 2, 1, :], v3(t3), v3(t4), ALU.add),
                 reads=[t3_r, t4_r], writes=[A_r], conc=True)

    def mix_filters2(self, l):
        nc, P = self.nc, self.P
        with ExitStack() as es:
            es_h = ExitStack()
            h2, h2_r = self.sb(es, "h2", [HID, SEQ], F32)
            w3, w3_r = self.sb(es, "w3", [HID, 2048], F32)
            ph, ph_r = [], []
            for i in range(2):
                a, b = self.ps(es, "ph", [128, 512], F32)
                ph.append(a)
                ph_r.append(b)
            zT, zT_r = self.sb(es_h, "zT", [HY_EMB, SEQ], F32)
            P.dma('sp', zT[:], self.c['zfeat'].ap(), reads=[self.wr], writes=[zT_r])
            w1, w1_r = self.sb(es_h, "w1", [HY_EMB, HID], F32)
            w2, w2_r = self.sb(es_h, "w2", [HID, HID], F32)
            P.dma('sp', w1[:], self.w['hy_filt_w1'].ap()[l], reads=[self.wr], writes=[w1_r])
            P.dma('sp', w2[:], self.w['hy_filt_w2'].ap()[l], reads=[self.wr], writes=[w2_r])
            P.dma('sp', w3[:], self.w['hy_filt_w3'].ap()[l], reads=[self.wr], writes=[w3_r])
            bq, bq_r = self.sb(es_h, "bq", [HID, 4], F32)
            P.dma('sp', bq[:, 0:1], bass.AP(self.w['hy_filt_b1'], l * 64, [[1, 64], [1, 1]]), reads=[self.wr], writes=[bq_r],
                  conc=True)
            P.dma('sp', bq[:, 2:3], bass.AP(self.w['hy_filt_b2'], l * 64, [[1, 64], [1, 1]]), reads=[self.wr], writes=[bq_r],
                  conc=True)
            for c in (0, 2):
                P.op('dve', lambda c=c: nc.vector.tensor_scalar(bq[:, c:c + 1], bq[:, c:c + 1], 0.25, None, ALU.mult),
                     reads=[bq_r], writes=[bq_r])
                P.op('dve', lambda c=c: nc.vector.tensor_scalar(bq[:, c + 1:c + 2], bq[:, c:c + 1], math.pi / 2, None, ALU.add),
                     reads=[bq_r], writes=[bq_r])
            h1, h1_r = self.sb(es_h, "h1", [HID, SEQ], F32)
            s4, s4_r = self.sb(es_h, "s4", [HID, 512], F32)
            c4, c4_r = self.sb(es_h, "c4", [HID, 512], F32)
            tt, tt_r = self.sb(es_h, "tt", [HID, 512], F32)
            for layer in range(2):
                wt, wt_r, K_, src, src_r, dstt, dst_r = ((w1, w1_r, HY_EMB, zT, zT_r, h1, h1_r) if layer == 0 else
                                                        (w2, w2_r, HID, h1, h1_r, h2, h2_r))
                bc = 2 * layer
                for tq in range(8):
                    pb = tq % 2
                    P.op('pe', lambda tq=tq, pb=pb, wt=wt, src=src, K_=K_: nc.tensor.matmul(
                        ph[pb][0:HID, :], wt[0:K_, :], src[0:K_, tq * 512:(tq + 1) * 512], start=True, stop=True),
                        reads=[wt_r, src_r], writes=[ph_r[pb]])
                    P.op('act', lambda pb=pb, bc=bc: nc.scalar.activation(out=s4[:], in_=ph[pb][0:HID, :], func=AF.Sin, scale=0.25,
                                                                          bias=bq[:, bc:bc + 1]),
                         reads=[ph_r[pb], bq_r], writes=[s4_r])
                    P.op('act', lambda pb=pb, bc=bc: nc.scalar.activation(out=c4[:], in_=ph[pb][0:HID, :], func=AF.Sin, scale=0.25,
                                                                          bias=bq[:, bc + 1:bc + 2]),
                         reads=[ph_r[pb], bq_r], writes=[c4_r])
                    P.op('dve', lambda: nc.vector.tensor_tensor(tt[:], s4[:], c4[:], ALU.mult), reads=[s4_r, c4_r], writes=[tt_r])
                    P.op('dve', lambda: nc.vector.tensor_tensor(c4[:], s4[:], s4[:], ALU.mult), reads=[s4_r], writes=[c4_r])
                    P.op('dve', lambda: nc.vector.tensor_scalar(c4[:], c4[:], -8.0, 4.0, ALU.mult, ALU.add), reads=[c4_r],
                         writes=[c4_r])
                    P.op('dve', lambda tq=tq, dstt=dstt: nc.vector.tensor_tensor(dstt[:, tq * 512:(tq + 1) * 512], tt[:], c4[:],
                                                                                ALU.mult),
                         reads=[tt_r, c4_r], writes=[dst_r], conc=True)
            P.barrier()
            es_h.close()
            onesa, onesa_r = self.sb(es, "onesa", [128, 128], F32)
            P.dma('sp', onesa[:], self.c['ones_all'].ap(), reads=[self.wr], writes=[onesa_r])
            nrm, nrm_r = [], []
            for o in range(2):
                a, b = self.sb(es, "nrm", [128, 512], F32)
                nrm.append(a)
                nrm_r.append(b)
            with ExitStack() as es2:
                pq, pq_r = self.ps(es2, "pq", [128, 512], F32)
                pk, pk_r = [], []
                for i in range(2):
                    a, b = self.ps(es2, "pk", [128, 512], F32)
                    pk.append(a)
                    pk_r.append(b)
                dcr, dcr_r, kk, kk_r = [], [], [], []
                for i in range(2):
                    a, b = self.sb(es2, "dcr", [128, 2, 512], F32)
                    dcr.append(a)
                    dcr_r.append(b)
                    a, b = self.sb(es2, "kk", [128, 2, 512], F32)
                    kk.append(a)
                    kk_r.append(b)
                for o in range(2):
                    for r in range(32):
                        pb = r % 2
                        P.dma('sp', dcr[pb][:], self.c['decr'].ap()[r], reads=[self.wr], writes=[dcr_r[pb]])
                        P.op('pe', lambda r=r, pb=pb, o=o: nc.tensor.matmul(ph[pb][:], h2[:, sl(r, 128, 32)],
                                                                           w3[:, o * 1024:o * 1024 + 512], start=True, stop=True),
                             reads=[h2_r, w3_r], writes=[ph_r[pb]])
                        P.op('pe', lambda r=r, pb=pb, o=o: nc.tensor.matmul(pk[pb][:], h2[:, sl(r, 128, 32)],
                                                                           w3[:, o * 1024 + 512:o * 1024 + 1024], start=True, stop=True),
                             reads=[h2_r, w3_r], writes=[pk_r[pb]])
                        P.op('dve', lambda pb=pb: nc.vector.tensor_tensor(kk[pb][:, 0, :], ph[pb][:], dcr[pb][:, 0, :], ALU.mult),
                             reads=[ph_r[pb], dcr_r[pb]], writes=[kk_r[pb]], conc=True)
                        P.op('dve', lambda pb=pb: nc.vector.tensor_tensor(kk[pb][:, 1, :], pk[pb][:], dcr[pb][:, 1, :], ALU.mult),
                             reads=[pk_r[pb], dcr_r[pb]], writes=[kk_r[pb]], conc=True)
                        P.op('act', lambda pb=pb: nc.scalar.activation(out=kk[pb][:], in_=kk[pb][:], func=AF.Square),
                             reads=[kk_r[pb]], writes=[kk_r[pb]])
                        P.op('pe', lambda pb=pb, r=r: nc.tensor.matmul(pq[:], onesa[:], kk[pb][:, 0, :], start=(r == 0), stop=False),
                             reads=[onesa_r, kk_r[pb]], writes=[pq_r])
                        P.op('pe', lambda pb=pb, r=r: nc.tensor.matmul(pq[:], onesa[:], kk[pb][:, 1, :], start=False, stop=(r == 31)),
                             reads=[onesa_r, kk_r[pb]], writes=[pq_r])
                    P.op('act', lambda o=o: nc.scalar.activation(out=nrm[o][:], in_=pq[:], func=AF.Ln, bias=self.eps_t[:]),
                         reads=[pq_r, self.eps_r], writes=[nrm_r[o]])
                    P.op('act', lambda o=o: nc.scalar.activation(out=nrm[o][:], in_=nrm[o][:], func=AF.Exp, scale=-0.5),
                         reads=[nrm_r[o]], writes=[nrm_r[o]])
                P.barrier()
            C = self.ct_consts(es)
            dec2, dec2_r, ktf, ktf_r, ktb, ktb_r, Af, Af_r, Ab, Ab_r, Kt, Kt_r, db, db_r = ([] for _ in range(14))
            for i in range(2):
                for lst, lstr, nm, shp, dt in ((dec2, dec2_r, "dec2", [128, 32, 2, 64], F32), (ktf, ktf_r, "ktf", [128, 64, 32], BF16),
                                               (ktb, ktb_r, "ktb", [128, 64, 32], BF16), (Af, Af_r, "Af", [128, 16, 2, 128], BF16),
                                               (Ab, Ab_r, "Ab", [128, 16, 2, 128], BF16), (Kt, Kt_r, "Kt", [128, 4, 2, 128], F32),
                                               (db, db_r, "db", [1, 64], F32)):
                    a, b = self.sb(es, nm, shp, dt)
                    lst.append(a)
                    lstr.append(b)
            tk, tk_r = [], []
            for i in range(2):
                a, b = self.sb(es, "tk", [128, 2, 64], F32)
                tk.append(a)
                tk_r.append(b)
            bde, bde_r = C['bd_ere']
            bden, bden_r = C['bd_eren']
            bdi, bdi_r = C['bd_eim']
            bdin, bdin_r = C['bd_eimn']
            it = 0
            ig = 0
            for o in range(2):
                for bt in range(8):
                    b2 = it % 2
                    it += 1
                    c0 = 64 * bt
                    P.dma('sp', dec2[b2][:], self.c['dec2'].ap()[bt], reads=[self.wr], writes=[dec2_r[b2]])
                    P.dma('sp', db[b2][:], bass.AP(self.w['hy_skip'], (l * 2 + o) * 512 + c0, [[1, 1], [1, 64]]), reads=[self.wr],
                          writes=[db_r[b2]])
                    for r in range(32):
                        pb = r % 2
                        for dirn in range(2):
                            col = o * 1024 + dirn * 512 + c0
                            P.op('pe', lambda r=r, pb=pb, dirn=dirn, col=col: nc.tensor.matmul(
                                ph[pb][:, dirn * 64:(dirn + 1) * 64], h2[:, sl(r, 128, 32)], w3[:, col:col + 64], start=True, stop=True),
                                reads=[h2_r, w3_r], writes=[ph_r[pb]])
                        P.op('dve', lambda pb=pb, b2=b2, r=r: nc.vector.tensor_tensor(
                            tk[pb][:], ph[pb][:, 0:128].rearrange("p (d c) -> p d c", d=2), dec2[b2][:, r, :, :], ALU.mult),
                            reads=[ph_r[pb], dec2_r[b2]], writes=[tk_r[pb]])
                        P.op('pool', lambda pb=pb, b2=b2, r=r, o=o, c0=c0: nc.gpsimd.tensor_tensor(
                            ktf[b2][:, :, r], tk[pb][:, 0, :], nrm[o][:, c0:c0 + 64], ALU.mult),
                            reads=[tk_r[pb], nrm_r[o]], writes=[ktf_r[b2]], conc=True)
                        P.op('pool', lambda pb=pb, b2=b2, r=r, o=o, c0=c0: nc.gpsimd.tensor_tensor(
                            ktb[b2][:, :, r], tk[pb][:, 1, :], nrm[o][:, c0:c0 + 64], ALU.mult),
                            reads=[tk_r[pb], nrm_r[o]], writes=[ktb_r[b2]], conc=True)
                    P.op('pool', lambda b2=b2: nc.gpsimd.tensor_tensor(ktf[b2][0:1, :, 0], ktf[b2][0:1, :, 0], db[b2][:], ALU.add),
                         reads=[ktf_r[b2], db_r[b2]], writes=[ktf_r[b2]])
                    self.ct_s1_twiddle(C, ktf[b2], ktf_r[b2], Af[b2], Af_r[b2])
                    self.ct_s1_twiddle(C, ktb[b2], ktb_r[b2], Ab[b2], Ab_r[b2])
                    for g in range(4):
                        k2b = ig % 2
                        ig += 1
                        (pre, pre_r), (pim, pim_r) = C['psx'][2 * k2b], C['psx'][2 * k2b + 1]
                        fr = Af[b2][:, 4 * g:4 * g + 4, 0, :]
                        fi = Af[b2][:, 4 * g:4 * g + 4, 1, :]
                        br = Ab[b2][:, 4 * g:4 * g + 4, 0, :]
                        bi = Ab[b2][:, 4 * g:4 * g + 4, 1, :]
                        seq_re = ((bde, bde_r, fr), (bdin, bdin_r, fi), (bde, bde_r, br), (bdin, bdin_r, bi))
                        seq_im = ((bde, bde_r, fi), (bdi, bdi_r, fr), (bden, bden_r, bi), (bdin, bdin_r, br))
                        for (pp, pp_r, seq) in ((pre, pre_r, seq_re), (pim, pim_r, seq_im)):
                            for n_, (wt_, wt_r_, rhs_) in enumerate(seq):
                                P.op('pe', lambda pp=pp, wt_=wt_, rhs_=rhs_, n_=n_: nc.tensor.matmul(
                                    pp[:], wt_[:], rhs_, start=(n_ == 0), stop=(n_ == 3)),
                                    reads=[wt_r_, Af_r[b2], Ab_r[b2]], writes=[pp_r])
                        P.op('act', lambda pre=pre, k2b=k2b: nc.scalar.copy(
                            out=Kt[k2b][:, :, 0, :], in_=pre[:].rearrange("p (g k) -> p g k", g=4)),
                            reads=[pre_r], writes=[Kt_r[k2b]], conc=True)
                        P.op('act', lambda pim=pim, k2b=k2b: nc.scalar.copy(
                            out=Kt[k2b][:, :, 1, :], in_=pim[:].rearrange("p (g k) -> p g k", g=4)),
                            reads=[pim_r], writes=[Kt_r[k2b]], conc=True)
                        P.dma('sp', self.kf_t.ap()[o, bt * 4 + g], Kt[k2b][:], reads=[Kt_r[k2b]], writes=[self.kf_r], conc=True)
            P.barrier()

    def mix_hyconv(self, l):
        nc, P = self.nc, self.P
        hT, hT_r = self.hT, self.hT_r
        w_in = self.w['w_in'].ap()
        with ExitStack() as es:
            ident, ident_r = self.load_const_tiles(es)
            C = self.ct_consts(es)
            bde, bde_r = C['bd_ere']
            bdi, bdi_r = C['bd_eim']
            bdin, bdin_r = C['bd_eimn']
            g1r, g1r_r = C['g1re']
            g1i, g1i_r = C['g1imn']
            tcr, tcr_r = C['tc_re']
            tci, tci_r = C['tc_im']
            wsth, wsth_r, cwb, cwb_r, bbc, bbc_r, hb3, hb3_r = ([] for _ in range(8))
            for i in range(2):
                a, b = self.sb(es, "wsth", [128, 8, 192], F32)
                wsth.append(a)
                wsth_r.append(b)
                a, b = self.sb(es, "cwb", [128, 3, 192], F32)
                cwb.append(a)
                cwb_r.append(b)
                a, b = self.sb(es, "bbc", [128, 192], F32)
                bbc.append(a)
                bbc_r.append(b)
                a, b = self.sb(es, "hb3", [128, 3, 64, 32], BF16)
                hb3.append(a)
                hb3_r.append(b)
            wj, wj_r = [], []
            for j in range(3):
                a, b = self.sb(es, "wj", [128, 8, 192], BF16)
                wj.append(a)
                wj_r.append(b)
            A, A_r = self.sb(es, "A", [128, 16, 2, 128], BF16)
            Y, Y_r = self.sb(es, "Y", [128, 16, 2, 128], BF16)
            Z, Z_r = self.sb(es, "Z", [128, 2, 2048], BF16)
            Kt, Kt_r = [], []
            for i in range(2):
                a, b = self.sb(es, "Ktl", [128, 4, 2, 128], F32)
                Kt.append(a)
                Kt_r.append(b)
            pproj = [C['ps1'][0][0], C['ps1'][1][0]]
            pproj_r = [C['ps1'][0][1], C['ps1'][1][1]]
            pstt, pstt_r = self.ps(es, "pstt", [128, 4, 128], BF16)
            yst, yst_r = [], []
            for i in range(2):
                a, b = self.sb(es, "yst", [128, SEQ], BF16)
                yst.append(a)
                yst_r.append(b)
            (t1, t1_r), (t2, t2_r), (t3, t3_r), (t4, t4_r) = C['tmp']
            ik = 0
            for bt in range(8):
                b2 = bt % 2
                c0 = 64 * bt
                H, H_r = hb3[b2], hb3_r[b2]
                for part in range(3):
                    col = part * 512 + c0
                    P.dma('sp', wsth[b2][:, :, part * 64:(part + 1) * 64],
                          w_in[l, :, col:col + 64].rearrange("(kc p) m -> p kc m", p=128), reads=[self.wr], writes=[wsth_r[b2]],
                          conc=True)
                    for j in range(3):
                        P.dma('sp', cwb[b2][:, j, part * 64:(part + 1) * 64],
                              bcast_rows(self.w['hy_conv_w'], (l * 3 + j) * 1536 + col, 64), reads=[self.wr], writes=[cwb_r[b2]],
                              conc=True)
                    P.dma('sp', bbc[b2][:, part * 64:(part + 1) * 64], bcast_rows(self.w['hy_conv_b'], l * 1536 + col, 64),
                          reads=[self.wr], writes=[bbc_r[b2]], conc=True)
                for j in range(3):
                    for kc in range(8):
                        P.op('pool', lambda j=j, kc=kc, b2=b2: nc.gpsimd.tensor_tensor(wj[j][:, kc, :], wsth[b2][:, kc, :],
                                                                                      cwb[b2][:, j, :], ALU.mult),
                             reads=[wsth_r[b2], cwb_r[b2]], writes=[wj_r[j]], conc=True)
                for r in range(32):
                    pb = r % 2
                    n = 0
                    for j in range(3):
                        for kc in range(8):
                            t0 = 1 + r + (j - 1)
                            P.op('pe', lambda j=j, kc=kc, t0=t0, pb=pb, n=n: nc.tensor.matmul(
                                pproj[pb][:, 0:192], hT[:, kc, sl(t0, 128, 32)], wj[j][:, kc, :], start=(n == 0), stop=(n == 23)),
                                reads=[hT_r, wj_r[j]], writes=[pproj_r[pb]])
                            n += 1
                    P.op('dve', lambda pb=pb, r=r, H=H, b2=b2: nc.vector.tensor_tensor(
                        H[:, :, :, r], pproj[pb][:, 0:192].rearrange("p (a c) -> p a c", a=3),
                        bbc[b2][:].rearrange("p (a c) -> p a c", a=3), ALU.add),
                        reads=[pproj_r[pb], bbc_r[b2]], writes=[H_r], conc=True)
                for o in range(2):
                    self.ct_s1_twiddle(C, H[:, 0, :, :], H_r, A, A_r)
                    for g in range(4):
                        k2b = ik % 2
                        ik += 1
                        (pre, pre_r), (pim, pim_r) = C['psx'][2 * k2b], C['psx'][2 * k2b + 1]
                        P.dma('sp', Kt[k2b][:], self.kf_t.ap()[o, bt * 4 + g], reads=[self.kf_r], writes=[Kt_r[k2b]])
                        ar = A[:, 4 * g:4 * g + 4, 0, :]
                        ai = A[:, 4 * g:4 * g + 4, 1, :]
                        for (pp, pp_r, seq) in ((pre, pre_r, ((bde, bde_r, ar), (bdin, bdin_r, ai))),
                                                (pim, pim_r, ((bde, bde_r, ai), (bdi, bdi_r, ar)))):
                            for n_, (wt_, wt_r_, rhs_) in enumerate(seq):
                                P.op('pe', lambda pp=pp, wt_=wt_, rhs_=rhs_, n_=n_: nc.tensor.matmul(
                                    pp[:], wt_[:], rhs_, start=(n_ == 0), stop=(n_ == 1)),
                                    reads=[wt_r_, A_r], writes=[pp_r])
                        v4 = lambda t: t[:].rearrange("p (g k) -> p g k", g=4)
                        kre = Kt[k2b][:, :, 0, :]
                        kim = Kt[k2b][:, :, 1, :]
                        P.op('dve', lambda pre=pre, kre=kre: nc.vector.tensor_tensor(v4(t1), v4(pre), kre, ALU.mult),
                             reads=[pre_r, Kt_r[k2b]], writes=[t1_r])
                        P.op('dve', lambda pim=pim, kim=kim: nc.vector.tensor_tensor(v4(t2), v4(pim), kim, ALU.mult),
                             reads=[pim_r, Kt_r[k2b]], writes=[t2_r])
                        P.op('dve', lambda pre=pre, kim=kim: nc.vector.tensor_tensor(v4(t3), v4(pre), kim, ALU.mult),
                             reads=[pre_r, Kt_r[k2b]], writes=[t3_r])
                        P.op('dve', lambda pim=pim, kre=kre: nc.vector.tensor_tensor(v4(t4), v4(pim), kre, ALU.mult),
                             reads=[pim_r, Kt_r[k2b]], writes=[t4_r])
                        P.op('pool', lambda g=g: nc.gpsimd.tensor_tensor(Y[:, 4 * g:4 * g + 4, 0, :], v4(t1), v4(t2), ALU.subtract),
                             reads=[t1_r, t2_r], writes=[Y_r], conc=True)
                        P.op('pool', lambda g=g: nc.gpsimd.tensor_tensor(Y[:, 4 * g:4 * g + 4, 1, :], v4(t3), v4(t4), ALU.add),
                             reads=[t3_r, t4_r], writes=[Y_r], conc=True)
                    for g in range(4):
                        k2b = ik % 2
                        ik += 1
                        (zre, zre_r), (zim, zim_r) = C['psx'][2 * k2b], C['psx'][2 * k2b + 1]
                        for h in range(4):
                            cg = 4 * g + h
                            yr = Y[:, cg, 0, :]
                            yi = Y[:, cg, 1, :]
                            for (pp, pp_r, seq) in ((zre, zre_r, ((yr, bde, bde_r), (yi, bdi, bdi_r))),
                                                    (zim, zim_r, ((yi, bde, bde_r), (yr, bdin, bdin_r)))):
                                for n_, (lh_, wt_, wt_r_) in enumerate(seq):
                                    P.op('pe', lambda pp=pp, lh_=lh_, wt_=wt_, n_=n_, h=h: nc.tensor.matmul(
                                        pp[:, h * 128:(h + 1) * 128], lh_, wt_[:], start=(n_ == 0), stop=(n_ == 1)),
                                        reads=[wt_r_, Y_r], writes=[pp_r])
                        P.op('dve', lambda zre=zre: nc.vector.tensor_tensor(t1[:], zre[:], tcr[:], ALU.mult),
                             reads=[zre_r, tcr_r], writes=[t1_r])
                        P.op('dve', lambda zim=zim: nc.vector.tensor_tensor(t2[:], zim[:], tci[:], ALU.mult),
                             reads=[zim_r, tci_r], writes=[t2_r])
                        P.op('dve', lambda zre=zre: nc.vector.tensor_tensor(t3[:], zre[:], tci[:], ALU.mult),
                             reads=[zre_r, tci_r], writes=[t3_r])
                        P.op('dve', lambda zim=zim: nc.vector.tensor_tensor(t4[:], zim[:], tcr[:], ALU.mult),
                             reads=[zim_r, tcr_r], writes=[t4_r])
                        P.op('pool', lambda g=g: nc.gpsimd.tensor_tensor(Z[:, 0, g * 512:(g + 1) * 512], t1[:], t2[:], ALU.subtract),
                             reads=[t1_r, t2_r], writes=[Z_r], conc=True)
                        P.op('pool', lambda g=g: nc.gpsimd.tensor_tensor(Z[:, 1, g * 512:(g + 1) * 512], t3[:], t4[:], ALU.add),
                             reads=[t3_r, t4_r], writes=[Z_r], conc=True)
                    for g in range(4):
                        pb = g % 2
                        ps, ps_r = C['ps1'][pb]
                        P.op('pe', lambda ps=ps, g=g: nc.tensor.matmul(ps[:], g1r[:], Z[:, 0, g * 512:(g + 1) * 512], start=True,
                                                                       stop=False), reads=[g1r_r, Z_r], writes=[ps_r])
                        P.op('pe', lambda ps=ps, g=g: nc.tensor.matmul(ps[:], g1i[:], Z[:, 1, g * 512:(g + 1) * 512], start=False,
                                                                       stop=True), reads=[g1i_r, Z_r], writes=[ps_r])
                        P.op('dve', lambda ps=ps, g=g, H=H, o=o: nc.vector.tensor_tensor(
                            H[:, 0, 16 * g:16 * g + 16, :].rearrange("p c r -> p (c r)"), ps[:],
                            H[:, 1 + o, 16 * g:16 * g + 16, :].rearrange("p c r -> p (c r)"), ALU.mult),
                            reads=[ps_r, H_r], writes=[H_r])
                    if self.dbg and bt == 0:
                        P.dma('sp', self.zdbg_t.ap()[o], H[:, 0, :, :].rearrange("p c r -> p (c r)"), reads=[H_r],
                              writes=[self.zdbg_r], conc=True)
                hb_ = bt % 2
                ys, ys_r = yst[(bt // 2) % 2], yst_r[(bt // 2) % 2]
                for rq in range(8):
                    for h in range(4):
                        r = 4 * rq + h
                        P.op('pe', lambda r=r, h=h, H=H, hb_=hb_: nc.tensor.transpose(pstt[64 * hb_:64 * hb_ + 64, h, :], H[:, 0, :, r], ident[:]),
                             reads=[H_r, ident_r], writes=[pstt_r])
                    P.op('act', lambda rq=rq, hb_=hb_, ys=ys: nc.scalar.copy(
                        out=ys[64 * hb_:64 * hb_ + 64, :].rearrange("c (p r) -> c r p", r=32)[:, 4 * rq:4 * rq + 4, :],
                        in_=pstt[64 * hb_:64 * hb_ + 64, :, :]),
                        reads=[pstt_r], writes=[ys_r], conc=True)
                if hb_ == 1:
                    P.dma('sp', self.yh_t.ap()[bt // 2], ys[:], reads=[ys_r], writes=[self.yh_r], conc=True)
            P.barrier()

    def mix_attn(self, l):
        nc, P = self.nc, self.P
        EB, EB_r = self.EB, self.EB_r
        with ExitStack() as es:
            qh, qh_r = self.sb(es, "qh", [128, SEQ], BF16)
            kh, kh_r = self.sb(es, "kh", [128, SEQ + 2048], BF16)
            vt, vt_r = [], []
            for i in range(2):
                a, b = self.sb(es, "vt", [128, 33, 128], BF16)
                vt.append(a)
                vt_r.append(b)
            psc, psc_r, pov, pov_r, pe_, pe_r, pm, pm_r, ot, ot_r = ([] for _ in range(10))
            for i in range(2):
                a, b = self.ps(es, "psc", [128, 256], F32)
                psc.append(a)
                psc_r.append(b)
                a, b = self.ps(es, "pov", [128, 128], F32)
                pov.append(a)
                pov_r.append(b)
                a, b = self.sb(es, "pexp", [128, 256], F32)
                pe_.append(a)
                pe_r.append(b)
                a, b = self.sb(es, "pm", [128, 256], BF16)
                pm.append(a)
                pm_r.append(b)
                a, b = self.sb(es, "ot", [128, 128], F32)
                ot.append(a)
                ot_r.append(b)
            it = 0
            iv = 0
            for g in range(3):
                D = DILS[g]
                m = SEQ // D
                nblk = m // 128
                nch = nblk + 1
                vd_t, vd_r = self.vd_t[g]
                for hp in range(2):
                    cq = 2 * g + hp
                    ck = 6 + 2 * g + hp
                    P.dma('sp', qh[:], self.qk_t.ap()[cq], reads=[self.qk_r], writes=[qh_r])
                    P.op('pool', lambda: nc.gpsimd.memset(kh[:], 0.0), writes=[kh_r])
                    P.dma('sp', kh[:, 64 * D:64 * D + SEQ], self.qk_t.ap()[ck], reads=[self.qk_r], writes=[kh_r])
                    for hi in range(2):
                        hh = 2 * hp + hi
                        p0 = 64 * hi
                        for r in range(D):
                            vb = iv % 2
                            iv += 1
                            P.dma('sp', vt[vb][:, 0:nch, :], vd_t.ap()[r, :, hh, :].rearrange("(c p) e -> p c e", p=128),
                                  reads=[vd_r], writes=[vt_r[vb]])
                            for b in range(nblk):
                                pb = it % 2
                                it += 1
                                q0 = r + D * 128 * b
                                kA = r + D * 128 * b
                                kB = r + D * 128 * (b + 1)
                                P.op('pe', lambda pb=pb, p0=p0, q0=q0, kA=kA, D=D: nc.tensor.matmul(
                                    psc[pb][:, 0:128], kh[p0:p0 + 64, sl(kA, 128, D)], qh[p0:p0 + 64, sl(q0, 128, D)],
                                    start=True, stop=True), reads=[kh_r, qh_r], writes=[psc_r[pb]])
                                P.op('pe', lambda pb=pb, p0=p0, q0=q0, kB=kB, D=D: nc.tensor.matmul(
                                    psc[pb][:, 128:256], kh[p0:p0 + 64, sl(kB, 128, D)], qh[p0:p0 + 64, sl(q0, 128, D)],
                                    start=True, stop=True), reads=[kh_r, qh_r], writes=[psc_r[pb]])
                                P.op('act', lambda pb=pb: nc.scalar.activation(out=pe_[pb][:], in_=psc[pb][:], func=AF.Exp),
                                     reads=[psc_r[pb]], writes=[pe_r[pb]])
                                P.op('dve', lambda pb=pb, g=g, hh=hh: nc.vector.tensor_tensor(pm[pb][:], pe_[pb][:],
                                                                                             EB[:, g * 4 + hh, :], ALU.mult),
                                     reads=[pe_r[pb], EB_r], writes=[pm_r[pb]])
                                P.op('pe', lambda pb=pb, vb=vb, b=b: nc.tensor.matmul(pov[pb][:], pm[pb][:, 0:128], vt[vb][:, b, :],
                                                                                     start=True, stop=False),
                                     reads=[pm_r[pb], vt_r[vb]], writes=[pov_r[pb]])
                                P.op('pe', lambda pb=pb, vb=vb, b=b: nc.tensor.matmul(pov[pb][:], pm[pb][:, 128:256],
                                                                                     vt[vb][:, b + 1, :], start=False, stop=True),
                                     reads=[pm_r[pb], vt_r[vb]], writes=[pov_r[pb]])
                                P.op('act', lambda pb=pb: nc.scalar.copy(out=ot[pb][:], in_=pov[pb][:]), reads=[pov_r[pb]],
                                     writes=[ot_r[pb]])
                                P.dma('sp', self.att_t.ap()[g, sl(q0, 128, D), hh, :], ot[pb][:], reads=[ot_r[pb]],
                                      writes=[self.att_r], conc=True)
            P.barrier()

    def mix_out(self, l, src, dst):
        nc, P = self.nc, self.P
        src_t, src_r = src
        dst_t, dst_r = dst
        T = 1024
        with ExitStack() as es:
            ident, ident_r = self.load_const_tiles(es)
            whp, whp_r = self.sb(es, "whp", [128, 4, D_MODEL], BF16)
            wap, wap_r = self.sb(es, "wap", [128, 2, D_MODEL], BF16)
            wout, wout_r = self.sb(es, "wout", [128, 8, D_MODEL], BF16)
            wds, wds_r = [], []
            for i in range(2):
                a, b = self.sb(es, "wdso", [128, D_MODEL], F32)
                wds.append(a)
                wds_r.append(b)
            iw = 0
            for (wt, wt_r, nkc, name) in ((whp, whp_r, 4, 'w_hy_proj'), (wap, wap_r, 2, 'w_at_proj'), (wout, wout_r, 8, 'w_out')):
                for kc in range(nkc):
                    b2 = iw % 2
                    iw += 1
                    P.dma('sp', wds[b2][:], self.w[name].ap()[l, kc * 128:(kc + 1) * 128, :], reads=[self.wr], writes=[wds_r[b2]])
                    P.op('pool', lambda wt=wt, kc=kc, b2=b2: nc.gpsimd.tensor_copy(out=wt[:, kc, :], in_=wds[b2][:]),
                         reads=[wds_r[b2]], writes=[wt_r], conc=True)
            yhyT, yhyT_r = self.sb(es, "yhyT", [128, 4, T], BF16)
            yatT, yatT_r = self.sb(es, "yatT", [128, 2, T], BF16)
            yT, yT_r = self.sb(es, "yT", [128, 8, T], BF16)
            att, att_r, s2, s2_r, yab, yab_r, rden, rden_r = ([] for _ in range(8))
            pst, pst_r, pst2, pst2_r = [], [], [], []
            for i in range(2):
                a, b = self.sb(es, "attl", [128, 3, 4, 128], F32)
                att.append(a)
                att_r.append(b)
                a, b = self.sb(es, "s2", [128, 4, 128], F32)
                s2.append(a)
                s2_r.append(b)
                a, b = self.sb(es, "yab", [128, 4, 64], BF16)
                yab.append(a)
                yab_r.append(b)
                a, b = self.sb(es, "rden", [128, 4], F32)
                rden.append(a)
                rden_r.append(b)
            a, b = self.ps(es, "psty", [128, 4, 128], BF16)
            pst.append(a)
            pst_r.append(b)
            a, b = self.ps(es, "psta", [128, 2, 128], BF16)
            pst2.append(a)
            pst2_r.append(b)
            pa, pa_r, pbb, pbb_r, po, po_r = [], [], [], [], [], []
            gA, gA_r, gB, gB_r, ta, ta_r, tb_, tb_r, xin, xin_r, xo, xo_r = ([] for _ in range(12))
            for i in range(2):
                for lst, lstr, nm in ((pa, pa_r, "pa"), (pbb, pbb_r, "pbb"), (po, po_r, "poo")):
                    a, b = self.ps(es, nm, [128, 512], F32)
                    lst.append(a)
                    lstr.append(b)
                for lst, lstr, nm in ((gA, gA_r, "gA"), (gB, gB_r, "gB"), (ta, ta_r, "ta"), (tb_, tb_r, "tbt")):
                    a, b = self.sb(es, nm, [128, 512], F32)
                    lst.append(a)
                    lstr.append(b)
                a, b = self.sb(es, "xin", [128, D_MODEL], F32)
                xin.append(a)
                xin_r.append(b)
                a, b = self.sb(es, "xo", [128, D_MODEL], F32)
                xo.append(a)
                xo_r.append(b)
            it = 0
            for st in range(SEQ // T):
                for kc in range(4):
                    P.dma('sp', yhyT[:, kc, :], self.yh_t.ap()[kc, :, st * T:(st + 1) * T], reads=[self.yh_r], writes=[yhyT_r],
                          conc=True)
                for tb in range(8):
                    gb = st * 8 + tb
                    b2 = gb % 2
                    P.dma('sp', att[b2][:], self.att_t.ap()[:, gb * 128:(gb + 1) * 128, :, :].rearrange("g t h e -> t g h e"),
                          reads=[self.att_r], writes=[att_r[b2]])
                    P.op('dve', lambda b2=b2: nc.vector.tensor_tensor(s2[b2][:], att[b2][:, 0, :, :], att[b2][:, 1, :, :], ALU.add),
                         reads=[att_r[b2]], writes=[s2_r[b2]])
                    P.op('dve', lambda b2=b2: nc.vector.tensor_tensor(s2[b2][:], s2[b2][:], att[b2][:, 2, :, :], ALU.add),
                         reads=[att_r[b2], s2_r[b2]], writes=[s2_r[b2]])
                    P.op('dve', lambda b2=b2: nc.vector.reciprocal(rden[b2][:], s2[b2][:, :, 64]), reads=[s2_r[b2]],
                         writes=[rden_r[b2]])
                    for hh in range(4):
                        P.op('dve', lambda b2=b2, hh=hh: nc.vector.tensor_scalar(yab[b2][:, hh, :], s2[b2][:, hh, 0:64],
                                                                                rden[b2][:, hh:hh + 1], None, ALU.mult),
                             reads=[s2_r[b2], rden_r[b2]], writes=[yab_r[b2]], conc=True)
                    for c in range(2):
                        P.op('pe', lambda c=c, b2=b2: nc.tensor.transpose(
                            pst2[0][:, c, :], yab[b2][:, 2 * c:2 * c + 2, :].rearrange("p h e -> p (h e)"), ident[:]),
                            reads=[yab_r[b2], ident_r], writes=[pst2_r[0]])
                    P.op('act', lambda tb=tb: nc.scalar.copy(out=yatT[:, :, tb * 128:(tb + 1) * 128], in_=pst2[0][:]),
                         reads=[pst2_r[0]], writes=[yatT_r], conc=True)
                for dc in range(8):
                    for th in range(2):
                        pb = it % 2
                        it += 1
                        tsl = slice(st * T + th * 512, st * T + (th + 1) * 512)
                        P.dma('sp', gA[pb][:], self.gt_t.ap()[dc, :, tsl], reads=[self.gt_r], writes=[gA_r[pb]])
                        P.dma('sp', gB[pb][:], self.gt_t.ap()[8 + dc, :, tsl], reads=[self.gt_r], writes=[gB_r[pb]])
                        for kc in range(4):
                            P.op('pe', lambda kc=kc, dc=dc, th=th, pb=pb: nc.tensor.matmul(
                                pa[pb][:], whp[:, kc, dc * 128:(dc + 1) * 128], yhyT[:, kc, th * 512:(th + 1) * 512],
                                start=(kc == 0), stop=(kc == 3)), reads=[whp_r, yhyT_r], writes=[pa_r[pb]])
                        for kc in range(2):
                            P.op('pe', lambda kc=kc, dc=dc, th=th, pb=pb: nc.tensor.matmul(
                                pbb[pb][:], wap[:, kc, dc * 128:(dc + 1) * 128], yatT[:, kc, th * 512:(th + 1) * 512],
                                start=(kc == 0), stop=(kc == 1)), reads=[wap_r, yatT_r], writes=[pbb_r[pb]])
                        P.op('dve', lambda pb=pb: nc.vector.tensor_tensor(ta[pb][:], pa[pb][:], gA[pb][:], ALU.mult),
                             reads=[pa_r[pb], gA_r[pb]], writes=[ta_r[pb]])
                        P.op('dve', lambda pb=pb: nc.vector.tensor_tensor(tb_[pb][:], pbb[pb][:], gB[pb][:], ALU.mult),
                             reads=[pbb_r[pb], gB_r[pb]], writes=[tb_r[pb]])
                        P.op('pool', lambda pb=pb, dc=dc, th=th: nc.gpsimd.tensor_tensor(
                            yT[:, dc, th * 512:(th + 1) * 512], ta[pb][:], tb_[pb][:], ALU.add),
                            reads=[ta_r[pb], tb_r[pb]], writes=[yT_r], conc=True)
                for tb in range(8):
                    b2 = tb % 2
                    r0 = st * T + tb * 128
                    P.dma('sp', xin[b2][:], src_t.ap()[r0:r0 + 128, :], reads=[src_r], writes=[xin_r[b2]])
                    for dh in range(2):
                        pb = it % 2
                        it += 1
                        for kc in range(8):
                            P.op('pe', lambda kc=kc, tb=tb, dh=dh, pb=pb: nc.tensor.matmul(
                                po[pb][:], yT[:, kc, tb * 128:(tb + 1) * 128], wout[:, kc, dh * 512:(dh + 1) * 512],
                                start=(kc == 0), stop=(kc == 7)), reads=[yT_r, wout_r], writes=[po_r[pb]])
                        P.op('dve', lambda pb=pb, b2=b2, dh=dh: nc.vector.tensor_tensor(
                            xo[b2][:, dh * 512:(dh + 1) * 512], po[pb][:], xin[b2][:, dh * 512:(dh + 1) * 512], ALU.add),
                            reads=[po_r[pb], xin_r[b2]], writes=[xo_r[b2]], conc=True)
                    P.dma('sp', dst_t.ap()[r0:r0 + 128, :], xo[b2][:], reads=[xo_r[b2]], writes=[dst_r], conc=True)
            P.barrier()

    def mixer_phase(self, l, src, dst):
        upto = self.mix_upto
        order = ['filt', 'proj', 'conv', 'attn', 'out']
        n = len(order) if upto is None else order.index(upto) + 1
        names = order[:n]
        if 'filt' in names:
            self.mix_filters2(l)
        with ExitStack() as es:
            self.hT, self.hT_r = self.sb(es, "hTm", [128, 8, SEQ + 2], BF16)
            if 'proj' in names:
                self.mix_proj(l, src)
            if 'conv' in names:
                self.mix_hyconv(l)
            self.P.barrier()
        if 'attn' in names:
            self.mix_attn(l)
        if 'out' in names:
            self.mix_out(l, src, dst)
        self.P.barrier()

    def build(self):
        self.declare()
        P = self.P
        with ExitStack() as es_sem:
            self.eps_t, self.eps_r = self.sb(es_sem, "eps", [128, 1], F32)
            P.op('dve', lambda: self.nc.vector.memset(self.eps_t[:], EPS), writes=[self.eps_r])
            self.attn_bias_setup(es_sem)
            cur = (self.x_t, self.x_r)
            si = 0
            stages = self.stages
            for l in range(DEPTH):
                for ph in ('ffn1', 'mix', 'ffn2'):
                    name = "%s_%d" % (ph, l)
                    if stages is not None and name not in stages:
                        continue
                    last = (stages is not None and name == stages[-1]) or (stages is None and l == DEPTH - 1 and ph == 'ffn2')
                    dst = (self.y_t, self.y_r) if last else self.xs_t[si]
                    si += 1
                    if ph == 'ffn1':
                        self.ffn_phase(l, 1, cur, dst)
                    elif ph == 'ffn2':
                        self.ffn_phase(l, 2, cur, dst)
                    else:
                        self.mixer_phase(l, cur, dst)
                    cur = dst
            P.barrier()
            nw = P.emit(es_sem)
            print("ops", len(P.ops), "waits", nw, "slots", len(P.slot_cnt))
        return self.nc


_NC_CACHE = {}


def kernel(**inputs):
    x = np.ascontiguousarray(np.asarray(inputs['x'], dtype=np.float32))
    consts = host_constants()
    if 'nc' not in _NC_CACHE:
        _NC_CACHE['nc'] = Builder().build()
    nc = _NC_CACHE['nc']
    shared = {k: np.ascontiguousarray(np.asarray(inputs[k], dtype=np.float32)) for k in WEIGHT_NAMES}
    shared.update(consts)
    in_maps = []
    for b in range(BATCH):
        m = dict(shared)
        m['x'] = x[b]
        in_maps.append(m)
    res = run_bass_kernel_spmd(nc, in_maps, core_ids=list(range(BATCH)))
    return np.stack([np.asarray(r['y'], dtype=np.float32) for r in res.results], axis=0)
```

```python
import math
from contextlib import ExitStack

import numpy as np
import ml_dtypes

import concourse.bass as bass
import concourse.mybir as mybir
from concourse.alu_op_type import AluOpType as ALU
from concourse.bass_utils import run_bass_kernel_spmd

F32 = mybir.dt.float32
BF16 = mybir.dt.bfloat16
AF = mybir.ActivationFunctionType

D_MODEL = 1024
BATCH = 8
SEQ = 4096
DEPTH = 2
HEAD_DIM = 64
HYW = 512
HY_EMB = 33
HID = 64
WINDOWS = (128, 512, 2048)
DILS = (1, 4, 16)
D_FF = 2688
NFC = D_FF // 128
IN_WIDTH = 3840
EPS = 1e-6
NFFT = 8192
NTB = SEQ // 128

WEIGHT_NAMES = ['rel_bias', 'ffn1_norm', 'ffn1_w_gate', 'ffn1_w_up', 'ffn1_w_down', 'mix_norm', 'w_in',
                'w_gate', 'b_gate', 'hy_conv_w', 'hy_conv_b', 'hy_filt_w1', 'hy_filt_b1', 'hy_filt_w2',
                'hy_filt_b2', 'hy_filt_w3', 'hy_skip', 'q_norm', 'k_norm', 'w_hy_proj', 'w_at_proj',
                'w_out', 'ffn2_norm', 'ffn2_w_gate', 'ffn2_w_up', 'ffn2_w_down']


class Reg:
    __slots__ = ('name', 'prev', 'writers', 'readers', 'slot', 'dcnt', 'lastdma')

    def __init__(self, name):
        self.name = name
        self.prev = []
        self.writers = []
        self.readers = []
        self.slot = None
        self.dcnt = 0
        self.lastdma = None


class Op:
    __slots__ = ('eng', 'fn', 'deps', 'is_dma', 'reg', 'dval', 'sig', 'sigval', 'slot')


class Prog:
    ENG = {'pe': 'tensor', 'act': 'scalar', 'dve': 'vector', 'pool': 'gpsimd', 'sp': 'sync'}

    def __init__(self, nc):
        self.nc = nc
        self.ops = []
        self.regs = []
        self.last = {}
        self.dma_regs = set()
        self.slot_cnt = []
        self.free_slots = []

    def reg(self, name):
        r = Reg(name)
        self.regs.append(r)
        return r

    def _add(self, eng, fn, reads, writes, conc, is_dma):
        i = len(self.ops)
        deps = set()
        for r in reads:
            deps.update(r.writers)
            if not r.writers:
                deps.update(r.prev)
        for w in writes:
            if conc:
                if w.readers:
                    w.prev = w.readers + w.writers
                    w.writers = []
                    w.readers = []
                deps.update(w.prev)
            else:
                deps.update(w.prev)
                deps.update(w.writers)
                deps.update(w.readers)
        for w in writes:
            if conc:
                w.writers.append(i)
            else:
                w.prev = []
                w.writers = [i]
                w.readers = []
        for r in reads:
            r.readers.append(i)
        deps.discard(i)
        o = Op()
        o.eng = eng
        o.fn = fn
        o.deps = deps
        o.is_dma = is_dma
        o.reg = None
        o.dval = 0
        o.sig = False
        o.sigval = 0
        o.slot = None
        if is_dma:
            w = writes[0]
            if w.slot is None:
                if self.free_slots:
                    w.slot = self.free_slots.pop()
                else:
                    w.slot = len(self.slot_cnt)
                    self.slot_cnt.append(0)
            self.slot_cnt[w.slot] += 16
            w.dcnt = self.slot_cnt[w.slot]
            o.reg = w
            o.slot = w.slot
            o.dval = w.dcnt
            w.lastdma = i
            self.dma_regs.add(w)
        else:
            self.last[eng] = i
        self.ops.append(o)
        return i

    def op(self, eng, fn, reads=(), writes=(), conc=False):
        return self._add(eng, fn, list(reads), list(writes), conc, False)

    def dma(self, q, out, in_, reads=(), writes=(), conc=False, **kw):
        nc = self.nc
        e = getattr(nc, self.ENG[q])

        def fn():
            return e.dma_start(out=out, in_=in_, **kw)
        assert len(writes) == 1
        return self._add(q, fn, list(reads), list(writes), conc, True)

    def barrier(self):
        deps = set(self.last.values())
        for r in self.dma_regs:
            if r.lastdma is not None:
                deps.add(r.lastdma)
        for e in self.ENG:
            o = Op()
            o.eng = e
            o.fn = None
            o.deps = set(deps)
            o.is_dma = False
            o.reg = None
            o.dval = 0
            o.sig = False
            o.sigval = 0
            o.slot = None
            self.ops.append(o)
        for r in self.regs:
            r.prev = []
            r.writers = []
            r.readers = []
        for r in self.dma_regs:
            self.free_slots.append(r.slot)
            r.slot = None
            r.lastdma = None
        self.dma_regs = set()

    def emit(self, es):
        nc = self.nc
        ops = self.ops
        for o in ops:
            if o.eng == 'pe' and not o.is_dma:
                o.deps = {d for d in o.deps if ops[d].is_dma or ops[d].eng != 'pe'}
            for d in o.deps:
                ops[d].sig = True
        esem = {e: es.enter_context(nc.semaphore("sem_" + e)) for e in self.ENG}
        cnt = {e: 0 for e in self.ENG}
        ssem = [es.enter_context(nc.semaphore("semslot%d" % i)) for i in range(len(self.slot_cnt))]
        for o in ops:
            if o.is_dma:
                pass
            elif o.sig and o.fn is not None:
                cnt[o.eng] += 1
                o.sigval = cnt[o.eng]
        known = {e: {} for e in self.ENG}
        nwait = 0
        for o in ops:
            e = o.eng
            eng = getattr(nc, self.ENG[e])
            need = {}
            for d in o.deps:
                od = ops[d]
                if od.is_dma:
                    key = ('r', od.slot)
                    sem = ssem[od.slot]
                    val = od.dval
                else:
                    if od.fn is None:
                        continue
                    key = ('e', od.eng)
                    sem = esem[od.eng]
                    val = od.sigval
                if need.get(key, (None, 0))[1] < val:
                    need[key] = (sem, val)
            kn = known[e]
            for key, (sem, val) in need.items():
                if kn.get(key, 0) < val:
                    eng.wait_ge(sem, val)
                    kn[key] = val
                    nwait += 1
            if o.fn is None:
                continue
            ins = o.fn()
            if o.is_dma:
                ins.then_inc(ssem[o.slot], 16)
            elif o.sig:
                ins.then_inc(esem[e], 1)
        return nwait


_CONST_CACHE = {}


def t5_bucket_np(rel):
    half = 16
    exact = 8
    ret = np.where(rel > 0, half, 0)
    n = np.abs(rel)
    nf = np.maximum(n, 1).astype(np.float32)
    large = exact + (np.log(nf / exact) / np.float32(math.log(1024 / exact)) * (half - exact)).astype(np.int32)
    large = np.minimum(large, half - 1)
    return ret + np.where(n < exact, n, large)


def host_constants():
    if _CONST_CACHE:
        return _CONST_CACHE
    c = {}
    L = SEQ
    t = np.linspace(0.0, 1.0, L, dtype=np.float32)[:, None]
    w = (np.float32(2.0 * math.pi / L)) * np.arange(L, dtype=np.float32)[:, None]
    f = np.linspace(1e-4, 15, 16, dtype=np.float32)[None]
    z = np.concatenate([t, np.cos(f * w), -np.sin(f * w)], axis=-1).astype(np.float32)
    c['zfeat'] = np.ascontiguousarray(z.T)
    max_decay = math.log(1e-2) / 0.3
    min_decay = math.log(1e-2) / 1.5
    deltas = np.abs(np.linspace(min_decay, max_decay, HYW, dtype=np.float32))
    dec = np.exp(-t * deltas[None]).astype(np.float32)
    c['dec_f'] = dec
    decb = dec.copy()
    decb[0] = 0.0
    c['dec_b'] = decb
    c['ident'] = np.eye(128, dtype=np.float32).astype(ml_dtypes.bfloat16)
    ob = np.zeros((128, 128), np.float32)
    ob[:64, :64] = 1.0
    ob[64:, 64:] = 1.0
    c['ones_blk'] = ob
    c['ones_all'] = np.ones((128, 128), np.float32)
    c['jrev'] = np.ascontiguousarray(np.eye(128, dtype=np.float32)[::-1])
    oh = np.zeros((3, 32, 512), np.float32)
    bm = np.zeros((4, 512), np.float32)
    for g in range(3):
        for i in range(191, 320):
            rel = 255 - i
            b = int(t5_bucket_np(np.array([rel * DILS[g]]))[0])
            oh[g, b, i] = 1.0
    bm[:, 191:320] = 1.0
    c['oh'] = oh
    c['bm'] = bm
    th = 2.0 * math.pi / NFFT
    p_ = np.arange(128, dtype=np.float64)
    r_ = np.arange(32, dtype=np.float64)
    k1_ = np.arange(128, dtype=np.float64) + 0.5
    k2_ = np.arange(32, dtype=np.float64)
    a1 = 2.0 * math.pi * np.outer(p_, k1_) / 256.0
    c['f1cat'] = np.concatenate([np.cos(a1), -np.sin(a1)], axis=1).astype(np.float32).astype(ml_dtypes.bfloat16)
    c['g1re'] = (np.cos(a1).T * (2.0 / NFFT)).astype(np.float32).astype(ml_dtypes.bfloat16)
    c['g1imn'] = (-np.sin(a1).T * (2.0 / NFFT)).astype(np.float32).astype(ml_dtypes.bfloat16)
    at = th * np.outer(r_, k1_)
    tre = np.tile(np.cos(at), (4, 1))
    tim = np.tile(-np.sin(at), (4, 1))
    c['tf_re'] = np.ascontiguousarray(np.concatenate([tre, tre], axis=1).astype(np.float32))
    c['tf_im'] = np.ascontiguousarray(np.concatenate([tim, tim], axis=1).astype(np.float32))
    tcre = np.cos(at).T
    tcim = np.sin(at).T
    c['tc_re'] = np.ascontiguousarray(np.tile(tcre, (1, 16)).astype(np.float32))
    c['tc_im'] = np.ascontiguousarray(np.tile(tcim, (1, 16)).astype(np.float32))
    ae = 2.0 * math.pi * np.outer(r_, k2_) / 32.0
    ere = np.cos(ae)
    eim = -np.sin(ae)

    def bd(m):
        o_ = np.zeros((128, 128), np.float64)
        for i in range(4):
            o_[32 * i:32 * i + 32, 32 * i:32 * i + 32] = m
        return o_.astype(np.float32).astype(ml_dtypes.bfloat16)
    c['bd_ere'] = bd(ere)
    c['bd_eren'] = bd(-ere)
    c['bd_eim'] = bd(eim)
    c['bd_eimn'] = bd(-eim)
    d2 = np.stack([c['dec_f'], c['dec_b']], axis=0)
    d2 = d2.reshape(2, 128, 32, 8, 64).transpose(3, 1, 2, 0, 4)
    c['dec2'] = np.ascontiguousarray(d2.astype(np.float32))
    dr = np.stack([c['dec_f'], c['dec_b']], axis=0).reshape(2, 128, 32, 512).transpose(2, 1, 0, 3)
    c['decr'] = np.ascontiguousarray(dr.astype(np.float32))
    for k in ('Fh', 'Gh', 'dec_f', 'dec_b'):
        c.pop(k, None)
    _CONST_CACHE.update(c)
    return c


CONST_SPECS = [('zfeat', [33, SEQ], F32), ('ident', [128, 128], BF16),
               ('f1cat', [128, 256], BF16), ('g1re', [128, 128], BF16), ('g1imn', [128, 128], BF16),
               ('tf_re', [128, 256], F32), ('tf_im', [128, 256], F32), ('tc_re', [128, 512], F32), ('tc_im', [128, 512], F32),
               ('bd_ere', [128, 128], BF16), ('bd_eren', [128, 128], BF16), ('bd_eim', [128, 128], BF16),
               ('bd_eimn', [128, 128], BF16), ('dec2', [8, 128, 32, 2, 64], F32), ('decr', [32, 128, 2, 512], F32),
               ('ones_blk', [128, 128], F32), ('ones_all', [128, 128], F32), ('jrev', [128, 128], F32), ('oh', [3, 32, 512], F32),
               ('bm', [4, 512], F32)]

WEIGHT_SHAPES = {
    'rel_bias': [32, 12], 'ffn1_norm': [2, 1024], 'ffn1_w_gate': [2, 1024, 2688], 'ffn1_w_up': [2, 1024, 2688],
    'ffn1_w_down': [2, 2688, 1024], 'mix_norm': [2, 1024], 'w_in': [2, 1024, 3840], 'w_gate': [2, 1024, 2048],
    'b_gate': [2, 2048], 'hy_conv_w': [2, 3, 1536], 'hy_conv_b': [2, 1536], 'hy_filt_w1': [2, 33, 64],
    'hy_filt_b1': [2, 64], 'hy_filt_w2': [2, 64, 64], 'hy_filt_b2': [2, 64], 'hy_filt_w3': [2, 64, 2048],
    'hy_skip': [2, 2, 512], 'q_norm': [2, 64], 'k_norm': [2, 64], 'w_hy_proj': [2, 512, 1024],
    'w_at_proj': [2, 256, 1024], 'w_out': [2, 1024, 1024], 'ffn2_norm': [2, 1024], 'ffn2_w_gate': [2, 1024, 2688],
    'ffn2_w_up': [2, 1024, 2688], 'ffn2_w_down': [2, 2688, 1024]}


def sl(start, n, step):
    return slice(start, start + (n - 1) * step + 1, step)


def bcast_rows(ap1d_tensor, offset, n, parts=128):
    return bass.AP(ap1d_tensor, offset, [[0, parts], [1, n]])


class Builder:
    def __init__(self, stages=None, dbg=False, mix_upto=None):
        self.mix_upto = mix_upto
        self.nc = bass.Bass("TRN2", target_bir_lowering=False)
        self.P = Prog(self.nc)
        self.stages = stages
        self.dbg = dbg
        self.uid = 0

    def sb(self, es, name, shape, dt):
        self.uid += 1
        t = es.enter_context(self.nc.sbuf_tensor("%s_%d" % (name, self.uid), shape, dt))
        return t, self.P.reg(name)

    def ps(self, es, name, shape, dt):
        self.uid += 1
        t = es.enter_context(self.nc.psum_tensor("%s_%d" % (name, self.uid), shape, dt))
        return t, self.P.reg(name)

    def dram(self, name, shape, dt, kind="Internal"):
        t = self.nc.dram_tensor(name, shape, dt, kind=kind)
        return t, self.P.reg(name)

    def declare(self):
        nc = self.nc
        self.x_t, self.x_r = self.dram("x", [SEQ, D_MODEL], F32, "ExternalInput")
        SK = "ExternalOutput" if self.dbg else "Internal"
        self.y_t, self.y_r = self.dram("y", [SEQ, D_MODEL], F32, "ExternalOutput")
        self.w = {}
        self.wr = self.P.reg("weights")
        for k in WEIGHT_NAMES:
            self.w[k] = nc.dram_tensor(k, WEIGHT_SHAPES[k], F32, kind="ExternalInput")
        self.c = {}
        for k, shp, dt in CONST_SPECS:
            self.c[k] = nc.dram_tensor(k, shp, dt, kind="ExternalInput")
        self.xs_t = []
        for i in range(6):
            self.xs_t.append(self.dram("xres%d" % i, [SEQ, D_MODEL], F32))
        self.hy_t, self.hy_r = self.dram("hy_s", [SEQ, 1536], F32, SK)
        self.z1_t, self.z1_r = self.dram("z1_s", [SEQ, HYW], F32, SK)
        self.qk_t, self.qk_r = self.dram("qk_s", [12, 128, SEQ], BF16, SK)
        self.gt_t, self.gt_r = self.dram("gt_s", [16, 128, SEQ], F32, SK)
        self.vd_t = []
        for g in range(3):
            D = DILS[g]
            m = SEQ // D
            self.vd_t.append(self.dram("vd_s%d" % g, [D, m + 128, 4, 128], BF16, SK))
        self.kf_t, self.kf_r = self.dram("kf_s", [2, 32, 128, 4, 2, 128], F32, SK)
        self.yh_t, self.yh_r = self.dram("yh_s", [4, 128, SEQ], BF16, SK)
        self.att_t, self.att_r = self.dram("att_s", [3, SEQ, 4, 128], F32, SK)
        self.rv_t, self.rv_r = self.dram("rv_s", [12, 512], F32, SK)
        self.z2_t, self.z2_r = self.dram("z2_s", [SEQ, HYW], F32, SK)
        self.zdbg_t, self.zdbg_r = self.dram("zdbg_s", [2, 128, 2048], BF16, SK)

    def norm_to_hT(self, es, src_t, src_r, tok0, ntb, gbc, hT, hT_r, hoff, tiles):
        nc, P = self.nc, self.P
        (xs, xs_rs, junk, junk_r, ss, ss_r, rs, rs_r, hb, hb_r, pst, pst_r, ident, ident_r) = tiles
        src = src_t.ap()
        for tb in range(ntb):
            xb = xs[:, tb % len(xs_rs), :]
            xr = xs_rs[tb % len(xs_rs)]
            r0 = tok0 + tb * 128
            P.dma('sp', xb, src[r0:r0 + 128, :], reads=[src_r], writes=[xr])
            b2 = tb % 2
            P.op('act', lambda xb=xb, tb=tb: nc.scalar.activation(out=junk[:], in_=xb, func=AF.Square,
                                                                  accum_out=ss[:, tb:tb + 1]),
                 reads=[xr], writes=[junk_r, ss_r[tb]])
            P.op('dve', lambda tb=tb: nc.vector.tensor_scalar(rs[:, tb:tb + 1], ss[:, tb:tb + 1], 1.0 / D_MODEL, EPS,
                                                              ALU.mult, ALU.add),
                 reads=[ss_r[tb]], writes=[rs_r[tb]])
            P.op('act', lambda tb=tb: nc.scalar.activation(out=rs[:, tb:tb + 1], in_=rs[:, tb:tb + 1], func=AF.Sqrt),
                 reads=[rs_r[tb]], writes=[rs_r[tb]])
            P.op('dve', lambda tb=tb: nc.vector.reciprocal(rs[:, tb:tb + 1], rs[:, tb:tb + 1]),
                 reads=[rs_r[tb]], writes=[rs_r[tb]])
            P.op('dve', lambda xb=xb, tb=tb, b2=b2: nc.vector.scalar_tensor_tensor(
                hb[b2][:], xb, rs[:, tb:tb + 1], gbc[0][:], ALU.mult, ALU.mult),
                reads=[xr, rs_r[tb], gbc[1]], writes=[hb_r[b2]])
            for kc in range(8):
                P.op('pe', lambda kc=kc, b2=b2: nc.tensor.transpose(pst[b2][:, kc, :], hb[b2][:, kc * 128:(kc + 1) * 128],
                                                                   ident[:]),
                     reads=[hb_r[b2], ident_r], writes=[pst_r[b2]])
            c0 = hoff + tb * 128
            P.op('act', lambda b2=b2, c0=c0: nc.scalar.copy(out=hT[:, :, c0:c0 + 128], in_=pst[b2][:]),
                 reads=[pst_r[b2]], writes=[hT_r], conc=True)

    def load_const_tiles(self, es):
        nc, P = self.nc, self.P
        ident, ident_r = self.sb(es, "ident", [128, 128], BF16)
        P.dma('sp', ident[:], self.c['ident'].ap(), reads=[self.wr], writes=[ident_r])
        return ident, ident_r

    def ffn_phase(self, l, which, src, dst):
        nc, P = self.nc, self.P
        src_t, src_r = src
        dst_t, dst_r = dst
        wg_d = self.w['ffn%d_w_gate' % which].ap()
        wu_d = self.w['ffn%d_w_up' % which].ap()
        wd_d = self.w['ffn%d_w_down' % which].ap()
        gn_t = self.w['ffn%d_norm' % which]
        T = 1024
        with ExitStack() as es:
            ident, ident_r = self.load_const_tiles(es)
            gbc = self.sb(es, "gbc", [128, D_MODEL], F32)
            P.dma('sp', gbc[0][:], bcast_rows(gn_t, l * D_MODEL, D_MODEL), reads=[self.wr], writes=[gbc[1]])
            xs, _ = self.sb(es, "xs", [128, 8, D_MODEL], F32)
            xs_rs = [P.reg("xs%d" % i) for i in range(8)]
            junk, junk_r = self.sb(es, "junk", [128, D_MODEL], F32)
            ss, _ = self.sb(es, "ss", [128, 8], F32)
            ss_r = [P.reg("ss%d" % i) for i in range(8)]
            rs, _ = self.sb(es, "rs", [128, 8], F32)
            rs_r = [P.reg("rs%d" % i) for i in range(8)]
            hb, hb_r = [], []
            pst, pst_r = [], []
            for i in range(2):
                a, b = self.sb(es, "hb", [128, D_MODEL], BF16)
                hb.append(a)
                hb_r.append(b)
                a, b = self.ps(es, "pst", [128, 8, 128], BF16)
                pst.append(a)
                pst_r.append(b)
            hT, hT_r = self.sb(es, "hT", [128, 8, T], BF16)
            act, act_r = self.sb(es, "act", [128, NFC, T], BF16)
            wd, wd_r = self.sb(es, "wd", [128, NFC, D_MODEL], BF16)
            wst, wst_r, wbf, wbf_r = [], [], [], []
            for i in range(4):
                a, b = self.sb(es, "wst", [128, 8, 128], F32)
                wst.append(a)
                wst_r.append(b)
                a, b = self.sb(es, "wbf", [128, 8, 128], BF16)
                wbf.append(a)
                wbf_r.append(b)
            wds, wds_r = [], []
            for i in range(2):
                a, b = self.sb(es, "wds", [128, D_MODEL], F32)
                wds.append(a)
                wds_r.append(b)
            sg, sg_r = [], []
            pg, pg_r, pu, pu_r, po, po_r = [], [], [], [], [], []
            for i in range(2):
                a, b = self.sb(es, "sg", [128, 512], F32)
                sg.append(a)
                sg_r.append(b)
                a, b = self.ps(es, "pg", [128, 512], F32)
                pg.append(a)
                pg_r.append(b)
                a, b = self.ps(es, "pu", [128, 512], F32)
                pu.append(a)
                pu_r.append(b)
                a, b = self.ps(es, "po", [128, 512], F32)
                po.append(a)
                po_r.append(b)
            for j in range(NFC):
                b2 = j % 2
                P.dma('sp', wds[b2][:], wd_d[l, j * 128:(j + 1) * 128, :], reads=[self.wr], writes=[wds_r[b2]])
                P.op('pool', lambda j=j, b2=b2: nc.gpsimd.tensor_copy(out=wd[:, j, :], in_=wds[b2][:]),
                     reads=[wds_r[b2]], writes=[wd_r], conc=True)
            tiles = (xs, xs_rs, junk, junk_r, ss, ss_r, rs, rs_r, hb, hb_r, pst, pst_r, ident, ident_r)
            src_ap = src_t.ap()
            dst_ap = dst_t.ap()
            for st in range(SEQ // T):
                self.norm_to_hT(es, src_t, src_r, st * T, 8, gbc, hT, hT_r, 0, tiles)
                for j in range(NFC):
                    bg = (2 * j) % 4
                    bu = (2 * j + 1) % 4
                    P.dma('sp', wst[bg][:], wg_d[l, :, j * 128:(j + 1) * 128].rearrange("(kc p) m -> p kc m", p=128),
                          reads=[self.wr], writes=[wst_r[bg]])
                    P.dma('sp', wst[bu][:], wu_d[l, :, j * 128:(j + 1) * 128].rearrange("(kc p) m -> p kc m", p=128),
                          reads=[self.wr], writes=[wst_r[bu]])
                    P.op('pool', lambda bg=bg: nc.gpsimd.tensor_copy(out=wbf[bg][:], in_=wst[bg][:]),
                         reads=[wst_r[bg]], writes=[wbf_r[bg]])
                    P.op('pool', lambda bu=bu: nc.gpsimd.tensor_copy(out=wbf[bu][:], in_=wst[bu][:]),
                         reads=[wst_r[bu]], writes=[wbf_r[bu]])
                    for th in range(2):
                        pb = (2 * j + th) % 2
                        for kc in range(8):
                            P.op('pe', lambda kc=kc, th=th, pb=pb, bg=bg: nc.tensor.matmul(
                                pg[pb][:], wbf[bg][:, kc, :], hT[:, kc, th * 512:(th + 1) * 512],
                                start=(kc == 0), stop=(kc == 7)),
                                reads=[wbf_r[bg], hT_r], writes=[pg_r[pb]])
                        for kc in range(8):
                            P.op('pe', lambda kc=kc, th=th, pb=pb, bu=bu: nc.tensor.matmul(
                                pu[pb][:], wbf[bu][:, kc, :], hT[:, kc, th * 512:(th + 1) * 512],
                                start=(kc == 0), stop=(kc == 7)),
                                reads=[wbf_r[bu], hT_r], writes=[pu_r[pb]])
                        P.op('act', lambda pb=pb: nc.scalar.activation(out=sg[pb][:], in_=pg[pb][:], func=AF.Silu),
                             reads=[pg_r[pb]], writes=[sg_r[pb]])
                        P.op('dve', lambda pb=pb, j=j, th=th: nc.vector.tensor_tensor(
                            act[:, j, th * 512:(th + 1) * 512], sg[pb][:], pu[pb][:], ALU.mult),
                            reads=[sg_r[pb], pu_r[pb]], writes=[act_r], conc=True)
                for tb in range(8):
                    for dh in range(2):
                        pb = (2 * tb + dh) % 2
                        for j in range(NFC):
                            P.op('pe', lambda j=j, tb=tb, dh=dh, pb=pb: nc.tensor.matmul(
                                po[pb][:], act[:, j, tb * 128:(tb + 1) * 128], wd[:, j, dh * 512:(dh + 1) * 512],
                                start=(j == 0), stop=(j == NFC - 1)),
                                reads=[act_r, wd_r], writes=[po_r[pb]])
                        P.op('dve', lambda tb=tb, dh=dh, pb=pb: nc.vector.scalar_tensor_tensor(
                            xs[:, tb, dh * 512:(dh + 1) * 512], po[pb][:], 0.5, xs[:, tb, dh * 512:(dh + 1) * 512],
                            ALU.mult, ALU.add),
                            reads=[po_r[pb], xs_rs[tb]], writes=[xs_rs[tb]])
                    r0 = st * T + tb * 128
                    P.dma('sp', dst_ap[r0:r0 + 128, :], xs[:, tb, :], reads=[xs_rs[tb]], writes=[dst_r], conc=True)
            P.barrier()

    def attn_bias_setup(self, es_glob):
        nc, P = self.nc, self.P
        self.EB, self.EB_r = self.sb(es_glob, "EB", [128, 12, 256], F32)
        with ExitStack() as es:
            tb_, tb_r = self.sb(es, "relb", [32, 12], F32)
            oh, oh_r = self.sb(es, "oh", [32, 3, 512], F32)
            bm, bm_r = self.sb(es, "bm", [4, 512], F32)
            ee, ee_r = self.sb(es, "ee", [4, 512], F32)
            pp, pp_r = self.ps(es, "pbias", [4, 512], F32)
            P.dma('sp', tb_[:], self.w['rel_bias'].ap(), reads=[self.wr], writes=[tb_r])
            P.dma('sp', oh[:], self.c['oh'].ap().rearrange("g b i -> b g i"), reads=[self.wr], writes=[oh_r])
            P.dma('sp', bm[:], self.c['bm'].ap(), reads=[self.wr], writes=[bm_r])
            for g in range(3):
                P.op('pe', lambda g=g: nc.tensor.matmul(pp[:], tb_[:, g * 4:(g + 1) * 4], oh[:, g, :], start=True, stop=True),
                     reads=[tb_r, oh_r], writes=[pp_r])
                P.op('act', lambda: nc.scalar.activation(out=ee[:], in_=pp[:], func=AF.Exp), reads=[pp_r], writes=[ee_r])
                P.op('dve', lambda: nc.vector.tensor_tensor(ee[:], ee[:], bm[:], ALU.mult), reads=[ee_r, bm_r], writes=[ee_r])
                P.dma('sp', self.rv_t.ap()[g * 4:(g + 1) * 4, :], ee[:], reads=[ee_r], writes=[self.rv_r], conc=True)
            ebr, ebr_r = self.sb(es, "ebr", [128, 12, 256], F32)
            jrev, jrev_r = self.sb(es, "jrev", [128, 128], F32)
            pj, pj_r = self.ps(es, "pj", [128, 512], F32)
            P.dma('sp', jrev[:], self.c['jrev'].ap(), reads=[self.wr], writes=[jrev_r])
            for h in range(12):
                srcA = bass.AP(self.rv_t, h * 512 + 192, [[1, 128], [1, 128]])
                srcB = bass.AP(self.rv_t, h * 512 + 64, [[1, 128], [1, 128]])
                P.dma('sp', ebr[:, h, 0:128], srcA, reads=[self.rv_r], writes=[ebr_r], conc=True)
                P.dma('sp', ebr[:, h, 128:256], srcB, reads=[self.rv_r], writes=[ebr_r], conc=True)
            for hp in range(6):
                P.op('pe', lambda hp=hp: nc.tensor.matmul(pj[:], jrev[:], ebr[:, 2 * hp:2 * hp + 2, :].rearrange("p h e -> p (h e)"),
                                                          start=True, stop=True),
                     reads=[jrev_r, ebr_r], writes=[pj_r])
                P.op('act', lambda hp=hp: nc.scalar.copy(out=self.EB[:, 2 * hp:2 * hp + 2, :].rearrange("p h e -> p (h e)"),
                                                         in_=pj[:]),
                     reads=[pj_r], writes=[self.EB_r], conc=True)
            P.barrier()

    def mix_proj(self, l, src):
        nc, P = self.nc, self.P
        src_t, src_r = src
        w_in = self.w['w_in'].ap()
        HP = SEQ + 2
        with ExitStack() as es:
            ident, ident_r = self.load_const_tiles(es)
            gbc = self.sb(es, "gbc", [128, D_MODEL], F32)
            P.dma('sp', gbc[0][:], bcast_rows(self.w['mix_norm'], l * D_MODEL, D_MODEL), reads=[self.wr], writes=[gbc[1]])
            hT, hT_r = self.hT, self.hT_r
            P.op('dve', lambda: nc.vector.memset(hT[:, :, 0:1], 0.0), writes=[hT_r], conc=True)
            P.op('dve', lambda: nc.vector.memset(hT[:, :, HP - 1:HP], 0.0), writes=[hT_r], conc=True)
            with ExitStack() as es2:
                xs, _ = self.sb(es2, "xs", [128, 2, D_MODEL], F32)
                xs_rs = [P.reg("xsm%d" % i) for i in range(2)]
                junk, junk_r = self.sb(es2, "junk", [128, D_MODEL], F32)
                ss, _ = self.sb(es2, "ss", [128, NTB], F32)
                ss_r = [P.reg("ssm%d" % i) for i in range(NTB)]
                rs, _ = self.sb(es2, "rs", [128, NTB], F32)
                rs_r = [P.reg("rsm%d" % i) for i in range(NTB)]
                hb, hb_r, pst, pst_r = [], [], [], []
                for i in range(2):
                    a, b = self.sb(es2, "hb", [128, D_MODEL], BF16)
                    hb.append(a)
                    hb_r.append(b)
                    a, b = self.ps(es2, "pst", [128, 8, 128], BF16)
                    pst.append(a)
                    pst_r.append(b)
                tiles = (xs, xs_rs, junk, junk_r, ss, ss_r, rs, rs_r, hb, hb_r, pst, pst_r, ident, ident_r)
                self.norm_to_hT(es2, src_t, src_r, 0, NTB, gbc, hT, hT_r, 1, tiles)
                P.barrier()
            pA, pA_r, pS, pS_r = [], [], [], []
            for i in range(2):
                a, b = self.ps(es, "pA", [128, 512], F32)
                pA.append(a)
                pA_r.append(b)
                a, b = self.ps(es, "pS", [128, 512], F32)
                pS.append(a)
                pS_r.append(b)
            with ExitStack() as es2:
                wst, wst_r, wbf, wbf_r = [], [], [], []
                for i in range(2):
                    a, b = self.sb(es2, "wstq", [128, 8, 128], F32)
                    wst.append(a)
                    wst_r.append(b)
                    a, b = self.sb(es2, "wbfq", [128, 8, 128], BF16)
                    wbf.append(a)
                    wbf_r.append(b)
                onesb, onesb_r = self.sb(es2, "onesb", [128, 128], F32)
                P.dma('sp', onesb[:], self.c['ones_blk'].ap(), reads=[self.wr], writes=[onesb_r])
                gq, gq_r = self.sb(es2, "gq", [128, 1], F32)
                gk, gk_r = self.sb(es2, "gk", [128, 1], F32)
                for hlf in range(2):
                    P.dma('sp', gq[hlf * 64:(hlf + 1) * 64, :], bass.AP(self.w['q_norm'], l * 64, [[1, 64], [1, 1]]),
                          reads=[self.wr], writes=[gq_r], conc=True)
                    P.dma('sp', gk[hlf * 64:(hlf + 1) * 64, :], bass.AP(self.w['k_norm'], l * 64, [[1, 64], [1, 1]]),
                          reads=[self.wr], writes=[gk_r], conc=True)
                P.op('dve', lambda: nc.vector.tensor_scalar(gq[:], gq[:], 0.125, None, ALU.mult), reads=[gq_r], writes=[gq_r])
                bgt, bgt_r = self.sb(es2, "bgt", [128, 16], F32)
                P.dma('sp', bgt[:], bass.AP(self.w['b_gate'], l * 2048, [[1, 128], [128, 16]]), reads=[self.wr], writes=[bgt_r],
                      allow_slow_non_contiguous=True)
                qf, qf_r, sq, sq_r, rr, rr_r, qn, qn_r, gs, gs_r = [], [], [], [], [], [], [], [], [], []
                for i in range(2):
                    for lst, lstr, nm, dt in ((qf, qf_r, "qf", F32), (sq, sq_r, "sq", F32), (rr, rr_r, "rr", F32),
                                              (qn, qn_r, "qn", BF16), (gs, gs_r, "gs", F32)):
                        a, b = self.sb(es2, nm, [128, 512], dt)
                        lst.append(a)
                        lstr.append(b)
                it = 0
                for c in range(12):
                    wb = c % 2
                    col0 = 1536 + c * 128
                    P.dma('sp', wst[wb][:], w_in[l, :, col0:col0 + 128].rearrange("(kc p) m -> p kc m", p=128),
                          reads=[self.wr], writes=[wst_r[wb]])
                    P.op('pool', lambda wb=wb: nc.gpsimd.tensor_copy(out=wbf[wb][:], in_=wst[wb][:]),
                         reads=[wst_r[wb]], writes=[wbf_r[wb]])
                    gain, gain_r = (gq, gq_r) if c < 6 else (gk, gk_r)
                    for tq in range(8):
                        pb = it % 2
                        it += 1
                        for kc in range(8):
                            P.op('pe', lambda kc=kc, tq=tq, pb=pb, wb=wb: nc.tensor.matmul(
                                pA[pb][:], wbf[wb][:, kc, :], hT[:, kc, 1 + tq * 512:1 + (tq + 1) * 512],
                                start=(kc == 0), stop=(kc == 7)),
                                reads=[wbf_r[wb], hT_r], writes=[pA_r[pb]])
                        P.op('act', lambda pb=pb: nc.scalar.copy(out=qf[pb][:], in_=pA[pb][:]), reads=[pA_r[pb]], writes=[qf_r[pb]])
                        P.op('dve', lambda pb=pb: nc.vector.tensor_tensor(sq[pb][:], qf[pb][:], qf[pb][:], ALU.mult),
                             reads=[qf_r[pb]], writes=[sq_r[pb]])
                        P.op('pe', lambda pb=pb: nc.tensor.matmul(pS[pb][:], onesb[:], sq[pb][:], start=True, stop=True),
                             reads=[onesb_r, sq_r[pb]], writes=[pS_r[pb]])
                        P.op('act', lambda pb=pb: nc.scalar.activation(out=rr[pb][:], in_=pS[pb][:], func=AF.Ln,
                                                                       scale=1.0 / 64, bias=self.eps_t[:]),
                             reads=[pS_r[pb], self.eps_r], writes=[rr_r[pb]])
                        P.op('act', lambda pb=pb: nc.scalar.activation(out=rr[pb][:], in_=rr[pb][:], func=AF.Exp, scale=-0.5),
                             reads=[rr_r[pb]], writes=[rr_r[pb]])
                        P.op('dve', lambda pb=pb, gain=gain: nc.vector.scalar_tensor_tensor(
                            qn[pb][:], qf[pb][:], gain[:], rr[pb][:], ALU.mult, ALU.mult),
                            reads=[qf_r[pb], gain_r, rr_r[pb]], writes=[qn_r[pb]])
                        P.dma('sp', self.qk_t.ap()[c, :, tq * 512:(tq + 1) * 512], qn[pb][:], reads=[qn_r[pb]],
                              writes=[self.qk_r], conc=True)
                w_gate = self.w['w_gate'].ap()
                for c in range(16):
                    wb = c % 2
                    P.dma('sp', wst[wb][:], w_gate[l, :, c * 128:(c + 1) * 128].rearrange("(kc p) m -> p kc m", p=128),
                          reads=[self.wr], writes=[wst_r[wb]])
                    P.op('pool', lambda wb=wb: nc.gpsimd.tensor_copy(out=wbf[wb][:], in_=wst[wb][:]),
                         reads=[wst_r[wb]], writes=[wbf_r[wb]])
                    for tq in range(8):
                        pb = it % 2
                        it += 1
                        for kc in range(8):
                            P.op('pe', lambda kc=kc, tq=tq, pb=pb, wb=wb: nc.tensor.matmul(
                                pA[pb][:], wbf[wb][:, kc, :], hT[:, kc, 1 + tq * 512:1 + (tq + 1) * 512],
                                start=(kc == 0), stop=(kc == 7)),
                                reads=[wbf_r[wb], hT_r], writes=[pA_r[pb]])
                        P.op('act', lambda pb=pb, c=c: nc.scalar.activation(out=gs[pb][:], in_=pA[pb][:], func=AF.Sigmoid,
                                                                            bias=bgt[:, c:c + 1]),
                             reads=[pA_r[pb], bgt_r], writes=[gs_r[pb]])
                        P.dma('sp', self.gt_t.ap()[c, :, tq * 512:(tq + 1) * 512], gs[pb][:], reads=[gs_r[pb]],
                              writes=[self.gt_r], conc=True)
                P.barrier()
            with ExitStack() as es2:
                wsv, wsv_r = self.sb(es2, "wsv", [128, 8, 256], F32)
                wbv, wbv_r = self.sb(es2, "wbv", [128, 8, 256], BF16)
                zt, zt_r = self.sb(es2, "zt", [64, 512], BF16)
                P.op('dve', lambda: nc.vector.memset(zt[:], 0.0), writes=[zt_r])
                vst, vst_r = [], []
                for i in range(2):
                    a, b = self.sb(es2, "vst", [128, 4, 128], BF16)
                    vst.append(a)
                    vst_r.append(b)
                    P.op('dve', lambda a=a: nc.vector.memset(a[:, :, 64:128], 1.0), writes=[b])
                it = 0
                for g in range(3):
                    D = DILS[g]
                    m = SEQ // D
                    vd_t, vd_r = self.vd_t[g]
                    col0 = 3072 + g * 256
                    P.dma('sp', wsv[:], w_in[l, :, col0:col0 + 256].rearrange("(kc p) m -> p kc m", p=128),
                          reads=[self.wr], writes=[wsv_r])
                    P.op('pool', lambda: nc.gpsimd.tensor_copy(out=wbv[:], in_=wsv[:]), reads=[wsv_r], writes=[wbv_r])
                    for r in range(D):
                        P.dma('sp', vd_t.ap()[r, 0:64, :, :].rearrange("t h e -> t (h e)"), zt[:], reads=[zt_r], writes=[vd_r],
                              conc=True)
                        P.dma('sp', vd_t.ap()[r, m + 64:m + 128, :, :].rearrange("t h e -> t (h e)"), zt[:], reads=[zt_r],
                              writes=[vd_r], conc=True)
                        for b in range(m // 128):
                            pb = it % 2
                            it += 1
                            t0 = 1 + r + D * 128 * b
                            for kc in range(8):
                                P.op('pe', lambda kc=kc, t0=t0, D=D, pb=pb: nc.tensor.matmul(
                                    pA[pb][:, 0:256], hT[:, kc, sl(t0, 128, D)], wbv[:, kc, :], start=(kc == 0), stop=(kc == 7)),
                                    reads=[hT_r, wbv_r], writes=[pA_r[pb]])
                            P.op('act', lambda pb=pb: nc.scalar.copy(
                                out=vst[pb][:, :, 0:64], in_=pA[pb][:, 0:256].rearrange("p (h e) -> p h e", h=4)),
                                reads=[pA_r[pb]], writes=[vst_r[pb]])
                            P.dma('sp', vd_t.ap()[r, 64 + 128 * b:64 + 128 * (b + 1), :, :], vst[pb][:], reads=[vst_r[pb]],
                                  writes=[vd_r], conc=True)
                P.barrier()

    def mix_filters(self, l):
        nc, P = self.nc, self.P
        with ExitStack() as es:
            es_h = ExitStack()
            h2, h2_r = self.sb(es, "h2", [HID, SEQ], F32)
            w3, w3_r = self.sb(es, "w3", [HID, 2048], F32)
            ph, ph_r = [], []
            for i in range(2):
                a, b = self.ps(es, "ph", [128, 512], F32)
                ph.append(a)
                ph_r.append(b)
            zT, zT_r = self.sb(es_h, "zT", [HY_EMB, SEQ], F32)
            P.dma('sp', zT[:], self.c['zfeat'].ap(), reads=[self.wr], writes=[zT_r])
            w1, w1_r = self.sb(es_h, "w1", [HY_EMB, HID], F32)
            w2, w2_r = self.sb(es_h, "w2", [HID, HID], F32)
            P.dma('sp', w1[:], self.w['hy_filt_w1'].ap()[l], reads=[self.wr], writes=[w1_r])
            P.dma('sp', w2[:], self.w['hy_filt_w2'].ap()[l], reads=[self.wr], writes=[w2_r])
            P.dma('sp', w3[:], self.w['hy_filt_w3'].ap()[l], reads=[self.wr], writes=[w3_r])
            bq, bq_r = self.sb(es_h, "bq", [HID, 4], F32)
            P.dma('sp', bq[:, 0:1], bass.AP(self.w['hy_filt_b1'], l * 64, [[1, 64], [1, 1]]), reads=[self.wr], writes=[bq_r],
                  conc=True)
            P.dma('sp', bq[:, 2:3], bass.AP(self.w['hy_filt_b2'], l * 64, [[1, 64], [1, 1]]), reads=[self.wr], writes=[bq_r],
                  conc=True)
            for c in (0, 2):
                P.op('dve', lambda c=c: nc.vector.tensor_scalar(bq[:, c:c + 1], bq[:, c:c + 1], 0.25, None, ALU.mult),
                     reads=[bq_r], writes=[bq_r])
                P.op('dve', lambda c=c: nc.vector.tensor_scalar(bq[:, c + 1:c + 2], bq[:, c:c + 1], math.pi / 2, None, ALU.add),
                     reads=[bq_r], writes=[bq_r])
            h1, h1_r = self.sb(es_h, "h1", [HID, SEQ], F32)
            s4, s4_r = self.sb(es_h, "s4", [HID, 512], F32)
            c4, c4_r = self.sb(es_h, "c4", [HID, 512], F32)
            tt, tt_r = self.sb(es_h, "tt", [HID, 512], F32)
            for layer in range(2):
                wt, wt_r, K_, src, src_r, dstt, dst_r = ((w1, w1_r, HY_EMB, zT, zT_r, h1, h1_r) if layer == 0 else
                                                        (w2, w2_r, HID, h1, h1_r, h2, h2_r))
                bc = 2 * layer
                for tq in range(8):
                    pb = tq % 2
                    P.op('pe', lambda tq=tq, pb=pb, wt=wt, src=src, K_=K_: nc.tensor.matmul(
                        ph[pb][0:HID, :], wt[0:K_, :], src[0:K_, tq * 512:(tq + 1) * 512], start=True, stop=True),
                        reads=[wt_r, src_r], writes=[ph_r[pb]])
                    P.op('act', lambda pb=pb, bc=bc: nc.scalar.activation(out=s4[:], in_=ph[pb][0:HID, :], func=AF.Sin, scale=0.25,
                                                                          bias=bq[:, bc:bc + 1]),
                         reads=[ph_r[pb], bq_r], writes=[s4_r])
                    P.op('act', lambda pb=pb, bc=bc: nc.scalar.activation(out=c4[:], in_=ph[pb][0:HID, :], func=AF.Sin, scale=0.25,
                                                                          bias=bq[:, bc + 1:bc + 2]),
                         reads=[ph_r[pb], bq_r], writes=[c4_r])
                    P.op('dve', lambda: nc.vector.tensor_tensor(tt[:], s4[:], c4[:], ALU.mult), reads=[s4_r, c4_r], writes=[tt_r])
                    P.op('dve', lambda: nc.vector.tensor_tensor(c4[:], s4[:], s4[:], ALU.mult), reads=[s4_r], writes=[c4_r])
                    P.op('dve', lambda: nc.vector.tensor_scalar(c4[:], c4[:], -8.0, 4.0, ALU.mult, ALU.add), reads=[c4_r],
                         writes=[c4_r])
                    P.op('dve', lambda tq=tq, dstt=dstt: nc.vector.tensor_tensor(dstt[:, tq * 512:(tq + 1) * 512], tt[:], c4[:],
                                                                                ALU.mult),
                         reads=[tt_r, c4_r], writes=[dst_r], conc=True)
            P.barrier()
            es_h.close()
            onesa, onesa_r = self.sb(es, "onesa", [128, 128], F32)
            P.dma('sp', onesa[:], self.c['ones_all'].ap(), reads=[self.wr], writes=[onesa_r])
            a_t, a_r = self.sb(es, "a_t", [128, NTB, 512], BF16)
            b_t, b_r = self.sb(es, "b_t", [128, NTB, 512], BF16)
            pq, pq_r = self.ps(es, "pq", [128, 512], F32)
            pk, pk_r = [], []
            for i in range(2):
                a, b = self.ps(es, "pk", [128, 512], F32)
                pk.append(a)
                pk_r.append(b)
            dfb, dfb_r, dbb, dbb_r, kfb, kfb_r, kbb, kbb_r, sqf, sqf_r, sqb, sqb_r = ([] for _ in range(12))
            for i in range(2):
                for lst, lstr, nm in ((dfb, dfb_r, "dfb"), (dbb, dbb_r, "dbb"), (kfb, kfb_r, "kfb"), (kbb, kbb_r, "kbb"),
                                      (sqf, sqf_r, "sqf"), (sqb, sqb_r, "sqb")):
                    a, b = self.sb(es, nm, [128, 512], F32)
                    lst.append(a)
                    lstr.append(b)
            nrm, nrm_r = self.sb(es, "nrm", [128, 512], F32)
            ft, ft_r = [], []
            for i in range(2):
                a, b = self.sb(es, "ft", [128, NTB, 128], BF16)
                ft.append(a)
                ft_r.append(b)
            ko, ko_r = [], []
            for i in range(2):
                a, b = self.sb(es, "ko", [128, 512], F32)
                ko.append(a)
                ko_r.append(b)
            for o in range(2):
                cf = o * 1024
                cb = o * 1024 + 512
                for tb in range(NTB):
                    pb = tb % 2
                    P.dma('sp', dfb[pb][:], self.c['dec_f'].ap()[tb * 128:(tb + 1) * 128, :], reads=[self.wr], writes=[dfb_r[pb]])
                    P.dma('sp', dbb[pb][:], self.c['dec_b'].ap()[tb * 128:(tb + 1) * 128, :], reads=[self.wr], writes=[dbb_r[pb]])
                    P.op('pe', lambda tb=tb, pb=pb, cf=cf: nc.tensor.matmul(ph[pb][:], h2[:, tb * 128:(tb + 1) * 128],
                                                                           w3[:, cf:cf + 512], start=True, stop=True),
                         reads=[h2_r, w3_r], writes=[ph_r[pb]])
                    P.op('pe', lambda tb=tb, pb=pb, cb=cb: nc.tensor.matmul(pk[pb][:], h2[:, tb * 128:(tb + 1) * 128],
                                                                           w3[:, cb:cb + 512], start=True, stop=True),
                         reads=[h2_r, w3_r], writes=[pk_r[pb]])
                    P.op('dve', lambda pb=pb: nc.vector.tensor_tensor(kfb[pb][:], ph[pb][:], dfb[pb][:], ALU.mult),
                         reads=[ph_r[pb], dfb_r[pb]], writes=[kfb_r[pb]])
                    P.op('dve', lambda pb=pb: nc.vector.tensor_tensor(kbb[pb][:], pk[pb][:], dbb[pb][:], ALU.mult),
                         reads=[pk_r[pb], dbb_r[pb]], writes=[kbb_r[pb]])
                    P.op('act', lambda pb=pb: nc.scalar.activation(out=sqf[pb][:], in_=kfb[pb][:], func=AF.Square),
                         reads=[kfb_r[pb]], writes=[sqf_r[pb]])
                    P.op('act', lambda pb=pb: nc.scalar.activation(out=sqb[pb][:], in_=kbb[pb][:], func=AF.Square),
                         reads=[kbb_r[pb]], writes=[sqb_r[pb]])
                    P.op('pe', lambda pb=pb, tb=tb: nc.tensor.matmul(pq[:], onesa[:], sqf[pb][:], start=(tb == 0), stop=False),
                         reads=[onesa_r, sqf_r[pb]], writes=[pq_r])
                    P.op('pe', lambda pb=pb, tb=tb: nc.tensor.matmul(pq[:], onesa[:], sqb[pb][:], start=False,
                                                                     stop=(tb == NTB - 1)),
                         reads=[onesa_r, sqb_r[pb]], writes=[pq_r])
                    P.op('pool', lambda pb=pb, tb=tb: nc.gpsimd.tensor_tensor(a_t[:, tb, :], kfb[pb][:], kbb[pb][:], ALU.add),
                         reads=[kfb_r[pb], kbb_r[pb]], writes=[a_r], conc=True)
                    P.op('pool', lambda pb=pb, tb=tb: nc.gpsimd.tensor_tensor(b_t[:, tb, :], kfb[pb][:], kbb[pb][:], ALU.subtract),
                         reads=[kfb_r[pb], kbb_r[pb]], writes=[b_r], conc=True)
                P.op('act', lambda: nc.scalar.activation(out=nrm[:], in_=pq[:], func=AF.Ln, bias=self.eps_t[:]),
                     reads=[pq_r, self.eps_r], writes=[nrm_r])
                P.op('act', lambda: nc.scalar.activation(out=nrm[:], in_=nrm[:], func=AF.Exp, scale=-0.5), reads=[nrm_r],
                     writes=[nrm_r])
                for s in range(64):
                    pb = s % 2
                    P.dma('sp', ft[pb][:], self.c['Fh'].ap()[s], reads=[self.wr], writes=[ft_r[pb]])
                    srct, srcr = (a_t, a_r) if s < 32 else (b_t, b_r)
                    for nb in range(NTB):
                        P.op('pe', lambda nb=nb, pb=pb, srct=srct: nc.tensor.matmul(pk[pb][:], ft[pb][:, nb, :], srct[:, nb, :],
                                                                                  start=(nb == 0), stop=(nb == NTB - 1)),
                             reads=[ft_r[pb], srcr], writes=[pk_r[pb]])
                    P.op('dve', lambda pb=pb: nc.vector.tensor_tensor(ko[pb][:], pk[pb][:], nrm[:], ALU.mult),
                         reads=[pk_r[pb], nrm_r], writes=[ko_r[pb]])
                    P.dma('sp', self.kf_t.ap()[o, s], ko[pb][:], reads=[ko_r[pb]], writes=[self.kf_r], conc=True)
            P.barrier()

    def mix_conv(self, l):
        nc, P = self.nc, self.P
        with ExitStack() as es:
            Y, Y_r = self.sb(es, "Y", [128, 64, 512], BF16)
            ftr, ftr_r, fti, fti_r, kre, kre_r, kim, kim_r = ([] for _ in range(8))
            pre, pre_r, pim, pim_r = [], [], [], []
            for i in range(2):
                for lst, lstr, nm in ((ftr, ftr_r, "ftr"), (fti, fti_r, "fti")):
                    a, b = self.sb(es, nm, [128, NTB, 128], BF16)
                    lst.append(a)
                    lstr.append(b)
                for lst, lstr, nm in ((kre, kre_r, "kre"), (kim, kim_r, "kim")):
                    a, b = self.sb(es, nm, [128, 512], F32)
                    lst.append(a)
                    lstr.append(b)
                a, b = self.ps(es, "pre", [128, 512], F32)
                pre.append(a)
                pre_r.append(b)
                a, b = self.ps(es, "pim", [128, 512], F32)
                pim.append(a)
                pim_r.append(b)
            t1, t1_r = self.sb(es, "t1", [128, 512], F32)
            t2, t2_r = self.sb(es, "t2", [128, 512], F32)
            gt, gt_r = [], []
            for i in range(2):
                a, b = self.sb(es, "gtile", [128, 64, 128], BF16)
                gt.append(a)
                gt_r.append(b)
            pc, pc_r, gate, gate_r, zo, zo_r, zn, zn_r = ([] for _ in range(8))
            for i in range(2):
                a, b = self.ps(es, "pc", [128, 512], F32)
                pc.append(a)
                pc_r.append(b)
                for lst, lstr, nm in ((gate, gate_r, "gate"), (zo, zo_r, "zo"), (zn, zn_r, "zn")):
                    a, b = self.sb(es, nm, [128, 512], F32)
                    lst.append(a)
                    lstr.append(b)
            dbc, dbc_r = self.sb(es, "dbc", [128, 512], F32)
            for o in range(2):
                P.dma('sp', dbc[:], bcast_rows(self.w['hy_skip'], (l * 2 + o) * 512, 512), reads=[self.wr], writes=[dbc_r])
                for j in range(32):
                    pb = j % 2
                    P.dma('sp', ftr[pb][:], self.c['Fh'].ap()[j], reads=[self.wr], writes=[ftr_r[pb]])
                    P.dma('sp', fti[pb][:], self.c['Fh'].ap()[32 + j], reads=[self.wr], writes=[fti_r[pb]])
                    P.dma('sp', kre[pb][:], self.kf_t.ap()[o, j], reads=[self.kf_r], writes=[kre_r[pb]])
                    P.dma('sp', kim[pb][:], self.kf_t.ap()[o, 32 + j], reads=[self.kf_r], writes=[kim_r[pb]])
                    for nb in range(NTB):
                        P.op('pe', lambda nb=nb, pb=pb: nc.tensor.matmul(pre[pb][:], ftr[pb][:, nb, :], u[:, nb, :],
                                                                        start=(nb == 0), stop=(nb == NTB - 1)),
                             reads=[ftr_r[pb], u_r], writes=[pre_r[pb]])
                    for nb in range(NTB):
                        P.op('pe', lambda nb=nb, pb=pb: nc.tensor.matmul(pim[pb][:], fti[pb][:, nb, :], u[:, nb, :],
                                                                        start=(nb == 0), stop=(nb == NTB - 1)),
                             reads=[fti_r[pb], u_r], writes=[pim_r[pb]])
                    P.op('dve', lambda pb=pb: nc.vector.tensor_tensor(t1[:], pre[pb][:], kre[pb][:], ALU.mult),
                         reads=[pre_r[pb], kre_r[pb]], writes=[t1_r])
                    P.op('dve', lambda pb=pb: nc.vector.tensor_tensor(t2[:], pim[pb][:], kim[pb][:], ALU.mult),
                         reads=[pim_r[pb], kim_r[pb]], writes=[t2_r])
                    P.op('dve', lambda j=j: nc.vector.tensor_tensor(Y[:, j, :], t1[:], t2[:], ALU.subtract),
                         reads=[t1_r, t2_r], writes=[Y_r], conc=True)
                    P.op('dve', lambda pb=pb: nc.vector.tensor_tensor(t1[:], pre[pb][:], kim[pb][:], ALU.mult),
                         reads=[pre_r[pb], kim_r[pb]], writes=[t1_r])
                    P.op('dve', lambda pb=pb: nc.vector.tensor_tensor(t2[:], pim[pb][:], kre[pb][:], ALU.mult),
                         reads=[pim_r[pb], kre_r[pb]], writes=[t2_r])
                    P.op('dve', lambda j=j: nc.vector.tensor_tensor(Y[:, 32 + j, :], t1[:], t2[:], ALU.add),
                         reads=[t1_r, t2_r], writes=[Y_r], conc=True)
                for nb in range(NTB):
                    pb = nb % 2
                    P.dma('sp', gt[pb][:], self.c['Gh'].ap()[nb], reads=[self.wr], writes=[gt_r[pb]])
                    rows = slice(nb * 128, (nb + 1) * 128)
                    P.dma('sp', gate[pb][:], self.hy_t.ap()[rows, 512 * (1 + o):512 * (2 + o)], reads=[self.hy_r],
                          writes=[gate_r[pb]])
                    if o == 0:
                        P.dma('sp', zo[pb][:], self.hy_t.ap()[rows, 0:512], reads=[self.hy_r], writes=[zo_r[pb]])
                    else:
                        P.dma('sp', zo[pb][:], self.z1_t.ap()[rows, :], reads=[self.z1_r], writes=[zo_r[pb]])
                    for s in range(64):
                        P.op('pe', lambda s=s, pb=pb: nc.tensor.matmul(pc[pb][:], gt[pb][:, s, :], Y[:, s, :], start=(s == 0),
                                                                      stop=(s == 63)),
                             reads=[gt_r[pb], Y_r], writes=[pc_r[pb]])
                    P.op('dve', lambda pb=pb: nc.vector.tensor_tensor(zo[pb][:], zo[pb][:], dbc[:], ALU.mult),
                         reads=[zo_r[pb], dbc_r], writes=[zo_r[pb]])
                    P.op('dve', lambda pb=pb: nc.vector.tensor_tensor(zo[pb][:], pc[pb][:], zo[pb][:], ALU.add),
                         reads=[pc_r[pb], zo_r[pb]], writes=[zo_r[pb]])
                    P.op('dve', lambda pb=pb: nc.vector.tensor_tensor(zn[pb][:], zo[pb][:], gate[pb][:], ALU.mult),
                         reads=[zo_r[pb], gate_r[pb]], writes=[zn_r[pb]])
                    if o == 0 or self.dbg:
                        dstt, dstr = (self.z1_t, self.z1_r) if o == 0 else (self.z2_t, self.z2_r)
                        P.dma('sp', dstt.ap()[rows, :], zn[pb][:], reads=[zn_r[pb]], writes=[dstr], conc=True)
                    P.op('act', lambda pb=pb, nb=nb: nc.scalar.copy(out=u[:, nb, :], in_=zn[pb][:]), reads=[zn_r[pb]],
                         writes=[u_r], conc=True)
            P.barrier()

    def ct_consts(self, es):
        nc, P = self.nc, self.P
        C = {}
        for name, shp, dt in (('f1cat', [128, 256], BF16), ('g1re', [128, 128], BF16), ('g1imn', [128, 128], BF16),
                              ('tf_re', [128, 256], F32), ('tf_im', [128, 256], F32), ('tc_re', [128, 512], F32),
                              ('tc_im', [128, 512], F32), ('bd_ere', [128, 128], BF16), ('bd_eren', [128, 128], BF16),
                              ('bd_eim', [128, 128], BF16), ('bd_eimn', [128, 128], BF16)):
            t, r = self.sb(es, name, shp, dt)
            P.dma('sp', t[:], self.c[name].ap(), reads=[self.wr], writes=[r])
            C[name] = (t, r)
        C['tmp'] = [self.sb(es, "cttmp", [128, 512], F32) for _ in range(4)]
        C['ps1'] = [self.ps(es, "ctps1", [128, 512], F32) for _ in range(2)]
        C['psx'] = [self.ps(es, "ctpsx", [128, 512], F32) for _ in range(4)]
        return C

    def ct_s1_twiddle(self, C, src, src_r, A, A_r):
        nc, P = self.nc, self.P
        f1, f1_r = C['f1cat']
        tfr, tfr_r = C['tf_re']
        tfi, tfi_r = C['tf_im']
        for q in range(8):
            ps, ps_r = C['ps1'][q % 2]
            for h in range(2):
                cg = 2 * q + h
                P.op('pe', lambda cg=cg, h=h, ps=ps: nc.tensor.matmul(
                    ps[:, h * 256:(h + 1) * 256], src[:, 4 * cg:4 * cg + 4, :].rearrange("p c r -> p (c r)"), f1[:],
                    start=True, stop=True), reads=[src_r, f1_r], writes=[ps_r])
            pv = ps[:].rearrange("p (g h k) -> p g h k", g=2, h=2)
            (t1, t1_r), (t2, t2_r), (t3, t3_r), (t4, t4_r) = C['tmp']
            v3 = lambda t: t[:, 0:256].rearrange("p (g k) -> p g k", g=2)
            tf3 = lambda t: t[:].rearrange("p (g k) -> p g k", g=2)
            P.op('dve', lambda pv=pv: nc.vector.tensor_tensor(v3(t1), pv[:, :, 0, :], tf3(tfr), ALU.mult),
                 reads=[ps_r, tfr_r], writes=[t1_r])
            P.op('dve', lambda pv=pv: nc.vector.tensor_tensor(v3(t2), pv[:, :, 1, :], tf3(tfi), ALU.mult),
                 reads=[ps_r, tfi_r], writes=[t2_r])
            P.op('dve', lambda pv=pv: nc.vector.tensor_tensor(v3(t3), pv[:, :, 0, :], tf3(tfi), ALU.mult),
                 reads=[ps_r, tfi_r], writes=[t3_r])
            P.op('dve', lambda pv=pv: nc.vector.tensor_tensor(v3(t4), pv[:, :, 1, :], tf3(tfr), ALU.mult),
                 reads=[ps_r, tfr_r], writes=[t4_r])
            P.op('pool', lambda q=q: nc.gpsimd.tensor_tensor(A[:, 2 * q:2 * q + 2, 0, :], v3(t1), v3(t2), ALU.subtract),
                 reads=[t1_r, t2_r], writes=[A_r], conc=True)
            P.op('pool', lambda q=q: nc.gpsimd.tensor_tensor(A[:, 2 * q:2 * q + 2, 1, :], v3(t3), v3(t4), ALU.add),
                 reads=[t3_r, t4_r], writes=[A_r], conc=True)

    def mix_filters2(self, l):
        nc, P = self.nc, self.P
        with ExitStack() as es:
            es_h = ExitStack()
            h2, h2_r = self.sb(es, "h2", [HID, SEQ], F32)
            w3, w3_r = self.sb(es, "w3", [HID, 2048], F32)
            ph, ph_r = [], []
            for i in range(2):
                a, b = self.ps(es, "ph", [128, 512], F32)
                ph.append(a)
                ph_r.append(b)
            zT, zT_r = self.sb(es_h, "zT", [HY_EMB, SEQ], F32)
            P.dma('sp', zT[:], self.c['zfeat'].ap(), reads=[self.wr], writes=[zT_r])
            w1, w1_r = self.sb(es_h, "w1", [HY_EMB, HID], F32)
            w2, w2_r = self.sb(es_h, "w2", [HID, HID], F32)
            P.dma('sp', w1[:], self.w['hy_filt_w1'].ap()[l], reads=[self.wr], writes=[w1_r])
            P.dma('sp', w2[:], self.w['hy_filt_w2'].ap()[l], reads=[self.wr], writes=[w2_r])
            P.dma('sp', w3[:], self.w['hy_filt_w3'].ap()[l], reads=[self.wr], writes=[w3_r])
            bq, bq_r = self.sb(es_h, "bq", [HID, 4], F32)
            P.dma('sp', bq[:, 0:1], bass.AP(self.w['hy_filt_b1'], l * 64, [[1, 64], [1, 1]]), reads=[self.wr], writes=[bq_r],
                  conc=True)
            P.dma('sp', bq[:, 2:3], bass.AP(self.w['hy_filt_b2'], l * 64, [[1, 64], [1, 1]]), reads=[self.wr], writes=[bq_r],
                  conc=True)
            for c in (0, 2):
                P.op('dve', lambda c=c: nc.vector.tensor_scalar(bq[:, c:c + 1], bq[:, c:c + 1], 0.25, None, ALU.mult),
                     reads=[bq_r], writes=[bq_r])
                P.op('dve', lambda c=c: nc.vector.tensor_scalar(bq[:, c + 1:c + 2], bq[:, c:c + 1], math.pi / 2, None, ALU.add),
                     reads=[bq_r], writes=[bq_r])
            h1, h1_r = self.sb(es_h, "h1", [HID, SEQ], F32)
            s4, s4_r = self.sb(es_h, "s4", [HID, 512], F32)
            c4, c4_r = self.sb(es_h, "c4", [HID, 512], F32)
            tt, tt_r = self.sb(es_h, "tt", [HID, 512], F32)
            for layer in range(2):
                wt, wt_r, K_, src, src_r, dstt, dst_r = ((w1, w1_r, HY_EMB, zT, zT_r, h1, h1_r) if layer == 0 else
                                                        (w2, w2_r, HID, h1, h1_r, h2, h2_r))
                bc = 2 * layer
                for tq in range(8):
                    pb = tq % 2
                    P.op('pe', lambda tq=tq, pb=pb, wt=wt, src=src, K_=K_: nc.tensor.matmul(
                        ph[pb][0:HID, :], wt[0:K_, :], src[0:K_, tq * 512:(tq + 1) * 512], start=True, stop=True),
                        reads=[wt_r, src_r], writes=[ph_r[pb]])
                    P.op('act', lambda pb=pb, bc=bc: nc.scalar.activation(out=s4[:], in_=ph[pb][0:HID, :], func=AF.Sin, scale=0.25,
                                                                          bias=bq[:, bc:bc + 1]),
                         reads=[ph_r[pb], bq_r], writes=[s4_r])
                    P.op('act', lambda pb=pb, bc=bc: nc.scalar.activation(out=c4[:], in_=ph[pb][0:HID, :], func=AF.Sin, scale=0.25,
                                                                          bias=bq[:, bc + 1:bc + 2]),
                         reads=[ph_r[pb], bq_r], writes=[c4_r])
                    P.op('dve', lambda: nc.vector.tensor_tensor(tt[:], s4[:], c4[:], ALU.mult), reads=[s4_r, c4_r], writes=[tt_r])
                    P.op('dve', lambda: nc.vector.tensor_tensor(c4[:], s4[:], s4[:], ALU.mult), reads=[s4_r], writes=[c4_r])
                    P.op('dve', lambda: nc.vector.tensor_scalar(c4[:], c4[:], -8.0, 4.0, ALU.mult, ALU.add), reads=[c4_r],
                         writes=[c4_r])
                    P.op('dve', lambda tq=tq, dstt=dstt: nc.vector.tensor_tensor(dstt[:, tq * 512:(tq + 1) * 512], tt[:], c4[:],
                                                                                ALU.mult),
                         reads=[tt_r, c4_r], writes=[dst_r], conc=True)
            P.barrier()
            es_h.close()
            onesa, onesa_r = self.sb(es, "onesa", [128, 128], F32)
            P.dma('sp', onesa[:], self.c['ones_all'].ap(), reads=[self.wr], writes=[onesa_r])
            onesh, onesh_r = self.sb(es, "onesh", [128, 128], BF16)
            P.op('dve', lambda: nc.vector.tensor_copy(out=onesh[:], in_=onesa[:]), reads=[onesa_r], writes=[onesh_r])
            nrm, nrm_r = [], []
            for o in range(2):
                a, b = self.sb(es, "nrm", [128, 512], F32)
                nrm.append(a)
                nrm_r.append(b)
            with ExitStack() as es2:
                h2b, h2b_r = self.sb(es2, "h2b", [HID, SEQ], BF16)
                w3b, w3b_r = self.sb(es2, "w3b", [HID, 2048], BF16)
                P.op('dve', lambda: nc.vector.tensor_copy(out=h2b[:], in_=h2[:]), reads=[h2_r], writes=[h2b_r])
                P.op('pool', lambda: nc.gpsimd.tensor_copy(out=w3b[:], in_=w3[:]), reads=[w3_r], writes=[w3b_r])
                pq, pq_r = self.ps(es2, "pq", [128, 512], F32)
                pk, pk_r = [], []
                for i in range(2):
                    a, b = self.ps(es2, "pk", [128, 512], F32)
                    pk.append(a)
                    pk_r.append(b)
                dcr, dcr_r, kk, kk_r, ks, ks_r = [], [], [], [], [], []
                for i in range(2):
                    a, b = self.sb(es2, "dcr", [128, 2, 512], F32)
                    dcr.append(a)
                    dcr_r.append(b)
                    a, b = self.sb(es2, "kk", [128, 2, 512], F32)
                    kk.append(a)
                    kk_r.append(b)
                    a, b = self.sb(es2, "ks", [128, 2, 512], BF16)
                    ks.append(a)
                    ks_r.append(b)
                for o in range(2):
                    for r in range(32):
                        pb = r % 2
                        P.dma('sp', dcr[pb][:], self.c['decr'].ap()[r], reads=[self.wr], writes=[dcr_r[pb]])
                        P.op('pe', lambda r=r, pb=pb, o=o: nc.tensor.matmul(ph[pb][:], h2b[:, sl(r, 128, 32)],
                                                                           w3b[:, o * 1024:o * 1024 + 512], start=True, stop=True),
                             reads=[h2b_r, w3b_r], writes=[ph_r[pb]])
                        P.op('pe', lambda r=r, pb=pb, o=o: nc.tensor.matmul(pk[pb][:], h2b[:, sl(r, 128, 32)],
                                                                           w3b[:, o * 1024 + 512:o * 1024 + 1024], start=True,
                                                                           stop=True),
                             reads=[h2b_r, w3b_r], writes=[pk_r[pb]])
                        P.op('dve', lambda pb=pb: nc.vector.tensor_tensor(kk[pb][:, 0, :], ph[pb][:], dcr[pb][:, 0, :], ALU.mult),
                             reads=[ph_r[pb], dcr_r[pb]], writes=[kk_r[pb]], conc=True)
                        P.op('dve', lambda pb=pb: nc.vector.tensor_tensor(kk[pb][:, 1, :], pk[pb][:], dcr[pb][:, 1, :], ALU.mult),
                             reads=[pk_r[pb], dcr_r[pb]], writes=[kk_r[pb]], conc=True)
                        P.op('act', lambda pb=pb: nc.scalar.activation(out=ks[pb][:], in_=kk[pb][:], func=AF.Square),
                             reads=[kk_r[pb]], writes=[ks_r[pb]])
                        P.op('pe', lambda pb=pb, r=r: nc.tensor.matmul(pq[:], onesh[:], ks[pb][:, 0, :], start=(r == 0), stop=False),
                             reads=[onesh_r, ks_r[pb]], writes=[pq_r])
                        P.op('pe', lambda pb=pb, r=r: nc.tensor.matmul(pq[:], onesh[:], ks[pb][:, 1, :], start=False, stop=(r == 31)),
                             reads=[onesh_r, ks_r[pb]], writes=[pq_r])
                    P.op('act', lambda o=o: nc.scalar.activation(out=nrm[o][:], in_=pq[:], func=AF.Ln, bias=self.eps_t[:]),
                         reads=[pq_r, self.eps_r], writes=[nrm_r[o]])
                    P.op('act', lambda o=o: nc.scalar.activation(out=nrm[o][:], in_=nrm[o][:], func=AF.Exp, scale=-0.5),
                         reads=[nrm_r[o]], writes=[nrm_r[o]])
                P.barrier()
            for o in range(2):
                for dirn in range(2):
                    cs = slice(o * 1024 + dirn * 512, o * 1024 + dirn * 512 + 512)
                    P.op('dve', lambda cs=cs, o=o: nc.vector.tensor_tensor(w3[:, cs], w3[:, cs], nrm[o][0:HID, :], ALU.mult),
                         reads=[w3_r, nrm_r[o]], writes=[w3_r])
            C = self.ct_consts(es)
            dec2, dec2_r, ktf, ktf_r, ktb, ktb_r, Af, Af_r, Ab, Ab_r, Kt, Kt_r, db, db_r = ([] for _ in range(14))
            for i in range(2):
                for lst, lstr, nm, shp, dt in ((dec2, dec2_r, "dec2", [128, 32, 2, 64], F32), (ktf, ktf_r, "ktf", [128, 64, 32], BF16),
                                               (ktb, ktb_r, "ktb", [128, 64, 32], BF16), (Af, Af_r, "Af", [128, 16, 2, 128], BF16),
                                               (Ab, Ab_r, "Ab", [128, 16, 2, 128], BF16), (Kt, Kt_r, "Kt", [128, 4, 2, 128], F32),
                                               (db, db_r, "db", [1, 64], F32)):
                    a, b = self.sb(es, nm, shp, dt)
                    lst.append(a)
                    lstr.append(b)
            bde, bde_r = C['bd_ere']
            bden, bden_r = C['bd_eren']
            bdi, bdi_r = C['bd_eim']
            bdin, bdin_r = C['bd_eimn']
            it = 0
            ig = 0
            ipb = 0
            for o in range(2):
                for bt in range(8):
                    b2 = it % 2
                    it += 1
                    c0 = 64 * bt
                    P.dma('sp', dec2[b2][:], self.c['dec2'].ap()[bt], reads=[self.wr], writes=[dec2_r[b2]])
                    P.dma('sp', db[b2][:], bass.AP(self.w['hy_skip'], (l * 2 + o) * 512 + c0, [[1, 1], [1, 64]]), reads=[self.wr],
                          writes=[db_r[b2]])
                    for dirn in range(2):
                        col = o * 1024 + dirn * 512 + c0
                        kt, kt_r = (ktf[b2], ktf_r[b2]) if dirn == 0 else (ktb[b2], ktb_r[b2])
                        for q in range(4):
                            pb = ipb % 2
                            ipb += 1
                            for j in range(8):
                                r = 8 * q + j
                                P.op('pe', lambda r=r, pb=pb, j=j, col=col: nc.tensor.matmul(
                                    ph[pb][:, j * 64:(j + 1) * 64], h2[:, sl(r, 128, 32)], w3[:, col:col + 64], start=True, stop=True),
                                    reads=[h2_r, w3_r], writes=[ph_r[pb]])
                            P.op('dve', lambda pb=pb, b2=b2, q=q, dirn=dirn, kt=kt: nc.vector.tensor_tensor(
                                kt[:, :, 8 * q:8 * q + 8].rearrange("p c r -> p r c"),
                                ph[pb][:].rearrange("p (r c) -> p r c", r=8), dec2[b2][:, 8 * q:8 * q + 8, dirn, :], ALU.mult),
                                reads=[ph_r[pb], dec2_r[b2]], writes=[kt_r], conc=True)
                    P.op('dve', lambda b2=b2: nc.vector.tensor_tensor(ktf[b2][0:1, :, 0], ktf[b2][0:1, :, 0], db[b2][:], ALU.add),
                         reads=[ktf_r[b2], db_r[b2]], writes=[ktf_r[b2]])
                    self.ct_s1_twiddle(C, ktf[b2], ktf_r[b2], Af[b2], Af_r[b2])
                    self.ct_s1_twiddle(C, ktb[b2], ktb_r[b2], Ab[b2], Ab_r[b2])
                    for g in range(4):
                        k2b = ig % 2
                        ig += 1
                        (pre, pre_r), (pim, pim_r) = C['psx'][2 * k2b], C['psx'][2 * k2b + 1]
                        fr = Af[b2][:, 4 * g:4 * g + 4, 0, :]
                        fi = Af[b2][:, 4 * g:4 * g + 4, 1, :]
                        br = Ab[b2][:, 4 * g:4 * g + 4, 0, :]
                        bi = Ab[b2][:, 4 * g:4 * g + 4, 1, :]
                        seq_re = ((bde, bde_r, fr), (bdin, bdin_r, fi), (bde, bde_r, br), (bdin, bdin_r, bi))
                        seq_im = ((bde, bde_r, fi), (bdi, bdi_r, fr), (bden, bden_r, bi), (bdin, bdin_r, br))
                        for (pp, pp_r, seq) in ((pre, pre_r, seq_re), (pim, pim_r, seq_im)):
                            for n_, (wt_, wt_r_, rhs_) in enumerate(seq):
                                P.op('pe', lambda pp=pp, wt_=wt_, rhs_=rhs_, n_=n_: nc.tensor.matmul(
                                    pp[:], wt_[:], rhs_, start=(n_ == 0), stop=(n_ == 3)),
                                    reads=[wt_r_, Af_r[b2], Ab_r[b2]], writes=[pp_r])
                        P.op('act', lambda pre=pre, k2b=k2b: nc.scalar.copy(
                            out=Kt[k2b][:, :, 0, :], in_=pre[:].rearrange("p (g k) -> p g k", g=4)),
                            reads=[pre_r], writes=[Kt_r[k2b]], conc=True)
                        P.op('act', lambda pim=pim, k2b=k2b: nc.scalar.copy(
                            out=Kt[k2b][:, :, 1, :], in_=pim[:].rearrange("p (g k) -> p g k", g=4)),
                            reads=[pim_r], writes=[Kt_r[k2b]], conc=True)
                        P.dma('sp', self.kf_t.ap()[o, bt * 4 + g], Kt[k2b][:], reads=[Kt_r[k2b]], writes=[self.kf_r], conc=True)
            P.barrier()

    def mix_hyconv(self, l):
        nc, P = self.nc, self.P
        hT, hT_r = self.hT, self.hT_r
        w_in = self.w['w_in'].ap()
        with ExitStack() as es:
            ident, ident_r = self.load_const_tiles(es)
            C = self.ct_consts(es)
            bde, bde_r = C['bd_ere']
            bdi, bdi_r = C['bd_eim']
            bdin, bdin_r = C['bd_eimn']
            g1r, g1r_r = C['g1re']
            g1i, g1i_r = C['g1imn']
            tcr, tcr_r = C['tc_re']
            tci, tci_r = C['tc_im']
            wsth, wsth_r, cwb, cwb_r, bbc, bbc_r, hb3, hb3_r = ([] for _ in range(8))
            for i in range(2):
                a, b = self.sb(es, "wsth", [128, 8, 192], F32)
                wsth.append(a)
                wsth_r.append(b)
                a, b = self.sb(es, "cwb", [128, 3, 192], F32)
                cwb.append(a)
                cwb_r.append(b)
                a, b = self.sb(es, "bbc", [128, 192], F32)
                bbc.append(a)
                bbc_r.append(b)
                a, b = self.sb(es, "hb3", [128, 3, 64, 32], BF16)
                hb3.append(a)
                hb3_r.append(b)
            wj, wj_r = [], []
            for j in range(3):
                a, b = self.sb(es, "wj", [128, 8, 192], BF16)
                wj.append(a)
                wj_r.append(b)
            A, A_r = self.sb(es, "A", [128, 16, 2, 128], BF16)
            Y, Y_r = self.sb(es, "Y", [128, 16, 2, 128], BF16)
            Z, Z_r = self.sb(es, "Z", [128, 2, 2048], BF16)
            Kt, Kt_r = [], []
            for i in range(2):
                a, b = self.sb(es, "Ktl", [128, 4, 2, 128], F32)
                Kt.append(a)
                Kt_r.append(b)
            pp0, pp0_r = self.ps(es, "pproj", [128, 512], F32)
            pproj = [pp0, pp0]
            pproj_r = [pp0_r, pp0_r]
            pstt, pstt_r = self.ps(es, "pstt", [128, 4, 128], BF16)
            yst, yst_r = [], []
            for i in range(2):
                a, b = self.sb(es, "yst", [128, SEQ], BF16)
                yst.append(a)
                yst_r.append(b)
            (t1, t1_r), (t2, t2_r), (t3, t3_r), (t4, t4_r) = C['tmp']
            ctr = {'ik': 0, 'ip': 0}
            v4 = lambda t: t[:].rearrange("p (g k) -> p g k", g=4)

            def prep_weights(bt):
                b2 = bt % 2
                c0 = 64 * bt
                for part in range(3):
                    col = part * 512 + c0
                    P.dma('sp', wsth[b2][:, :, part * 64:(part + 1) * 64],
                          w_in[l, :, col:col + 64].rearrange("(kc p) m -> p kc m", p=128), reads=[self.wr], writes=[wsth_r[b2]],
                          conc=True)
                    for j in range(3):
                        P.dma('sp', cwb[b2][:, j, part * 64:(part + 1) * 64],
                              bcast_rows(self.w['hy_conv_w'], (l * 3 + j) * 1536 + col, 64), reads=[self.wr], writes=[cwb_r[b2]],
                              conc=True)
                    P.dma('sp', bbc[b2][:, part * 64:(part + 1) * 64], bcast_rows(self.w['hy_conv_b'], l * 1536 + col, 64),
                          reads=[self.wr], writes=[bbc_r[b2]], conc=True)
                for j in range(3):
                    for kc in range(8):
                        P.op('pool', lambda j=j, kc=kc, b2=b2: nc.gpsimd.tensor_tensor(wj[j][:, kc, :], wsth[b2][:, kc, :],
                                                                                      cwb[b2][:, j, :], ALU.mult),
                             reads=[wsth_r[b2], cwb_r[b2]], writes=[wj_r[j]], conc=True)

            def proj_chunk(bt, q):
                b2 = bt % 2
                H, H_r = hb3[b2], hb3_r[b2]
                for r in range(4 * q, 4 * q + 4):
                    pb = ctr['ip'] % 2
                    ctr['ip'] += 1
                    n = 0
                    for j in range(3):
                        for kc in range(8):
                            t0 = 1 + r + (j - 1)
                            P.op('pe', lambda j=j, kc=kc, t0=t0, pb=pb, n=n: nc.tensor.matmul(
                                pproj[pb][:, 0:192], hT[:, kc, sl(t0, 128, 32)], wj[j][:, kc, :], start=(n == 0), stop=(n == 23)),
                                reads=[hT_r, wj_r[j]], writes=[pproj_r[pb]])
                            n += 1
                    P.op('dve', lambda pb=pb, r=r, H=H, b2=b2: nc.vector.tensor_tensor(
                        H[:, :, :, r], pproj[pb][:, 0:192].rearrange("p (a c) -> p a c", a=3),
                        bbc[b2][:].rearrange("p (a c) -> p a c", a=3), ALU.add),
                        reads=[pproj_r[pb], bbc_r[b2]], writes=[H_r], conc=True)

            def conv_stage(bt, o, k):
                b2 = bt % 2
                H, H_r = hb3[b2], hb3_r[b2]
                if k == 0:
                    self.ct_s1_twiddle(C, H[:, 0, :, :], H_r, A, A_r)
                elif k == 1:
                    for g in range(4):
                        k2b = ctr['ik'] % 2
                        ctr['ik'] += 1
                        (pre, pre_r), (pim, pim_r) = C['psx'][2 * k2b], C['psx'][2 * k2b + 1]
                        P.dma('sp', Kt[k2b][:], self.kf_t.ap()[o, bt * 4 + g], reads=[self.kf_r], writes=[Kt_r[k2b]])
                        ar = A[:, 4 * g:4 * g + 4, 0, :]
                        ai = A[:, 4 * g:4 * g + 4, 1, :]
                        for (pp, pp_r, seq) in ((pre, pre_r, ((bde, bde_r, ar), (bdin, bdin_r, ai))),
                                                (pim, pim_r, ((bde, bde_r, ai), (bdi, bdi_r, ar)))):
                            for n_, (wt_, wt_r_, rhs_) in enumerate(seq):
                                P.op('pe', lambda pp=pp, wt_=wt_, rhs_=rhs_, n_=n_: nc.tensor.matmul(
                                    pp[:], wt_[:], rhs_, start=(n_ == 0), stop=(n_ == 1)),
                                    reads=[wt_r_, A_r], writes=[pp_r])
                        kre = Kt[k2b][:, :, 0, :]
                        kim = Kt[k2b][:, :, 1, :]
                        P.op('dve', lambda pre=pre, kre=kre: nc.vector.tensor_tensor(v4(t1), v4(pre), kre, ALU.mult),
                             reads=[pre_r, Kt_r[k2b]], writes=[t1_r])
                        P.op('dve', lambda pim=pim, kim=kim: nc.vector.tensor_tensor(v4(t2), v4(pim), kim, ALU.mult),
                             reads=[pim_r, Kt_r[k2b]], writes=[t2_r])
                        P.op('dve', lambda pre=pre, kim=kim: nc.vector.tensor_tensor(v4(t3), v4(pre), kim, ALU.mult),
                             reads=[pre_r, Kt_r[k2b]], writes=[t3_r])
                        P.op('dve', lambda pim=pim, kre=kre: nc.vector.tensor_tensor(v4(t4), v4(pim), kre, ALU.mult),
                             reads=[pim_r, Kt_r[k2b]], writes=[t4_r])
                        P.op('pool', lambda g=g: nc.gpsimd.tensor_tensor(Y[:, 4 * g:4 * g + 4, 0, :], v4(t1), v4(t2), ALU.subtract),
                             reads=[t1_r, t2_r], writes=[Y_r], conc=True)
                        P.op('pool', lambda g=g: nc.gpsimd.tensor_tensor(Y[:, 4 * g:4 * g + 4, 1, :], v4(t3), v4(t4), ALU.add),
                             reads=[t3_r, t4_r], writes=[Y_r], conc=True)
                elif k == 2:
                    for g in range(4):
                        k2b = ctr['ik'] % 2
                        ctr['ik'] += 1
                        (zre, zre_r), (zim, zim_r) = C['psx'][2 * k2b], C['psx'][2 * k2b + 1]
                        for h in range(4):
                            cg = 4 * g + h
                            yr = Y[:, cg, 0, :]
                            yi = Y[:, cg, 1, :]
                            for (pp, pp_r, seq) in ((zre, zre_r, ((yr, bde, bde_r), (yi, bdi, bdi_r))),
                                                    (zim, zim_r, ((yi, bde, bde_r), (yr, bdin, bdin_r)))):
                                for n_, (lh_, wt_, wt_r_) in enumerate(seq):
                                    P.op('pe', lambda pp=pp, lh_=lh_, wt_=wt_, n_=n_, h=h: nc.tensor.matmul(
                                        pp[:, h * 128:(h + 1) * 128], lh_, wt_[:], start=(n_ == 0), stop=(n_ == 1)),
                                        reads=[wt_r_, Y_r], writes=[pp_r])
                        P.op('dve', lambda zre=zre: nc.vector.tensor_tensor(t1[:], zre[:], tcr[:], ALU.mult),
                             reads=[zre_r, tcr_r], writes=[t1_r])
                        P.op('dve', lambda zim=zim: nc.vector.tensor_tensor(t2[:], zim[:], tci[:], ALU.mult),
                             reads=[zim_r, tci_r], writes=[t2_r])
                        P.op('dve', lambda zre=zre: nc.vector.tensor_tensor(t3[:], zre[:], tci[:], ALU.mult),
                             reads=[zre_r, tci_r], writes=[t3_r])
                        P.op('dve', lambda zim=zim: nc.vector.tensor_tensor(t4[:], zim[:], tcr[:], ALU.mult),
                             reads=[zim_r, tcr_r], writes=[t4_r])
                        P.op('pool', lambda g=g: nc.gpsimd.tensor_tensor(Z[:, 0, g * 512:(g + 1) * 512], t1[:], t2[:], ALU.subtract),
                             reads=[t1_r, t2_r], writes=[Z_r], conc=True)
                        P.op('pool', lambda g=g: nc.gpsimd.tensor_tensor(Z[:, 1, g * 512:(g + 1) * 512], t3[:], t4[:], ALU.add),
                             reads=[t3_r, t4_r], writes=[Z_r], conc=True)
                else:
                    for g in range(4):
                        k2b = ctr['ik'] % 2
                        ctr['ik'] += 1
                        ps, ps_r = C['psx'][2 * k2b]
                        P.op('pe', lambda ps=ps, g=g: nc.tensor.matmul(ps[:], g1r[:], Z[:, 0, g * 512:(g + 1) * 512], start=True,
                                                                       stop=False), reads=[g1r_r, Z_r], writes=[ps_r])
                        P.op('pe', lambda ps=ps, g=g: nc.tensor.matmul(ps[:], g1i[:], Z[:, 1, g * 512:(g + 1) * 512], start=False,
                                                                       stop=True), reads=[g1i_r, Z_r], writes=[ps_r])
                        P.op('dve', lambda ps=ps, g=g, H=H, o=o: nc.vector.tensor_tensor(
                            H[:, 0, 16 * g:16 * g + 16, :].rearrange("p c r -> p (c r)"), ps[:],
                            H[:, 1 + o, 16 * g:16 * g + 16, :].rearrange("p c r -> p (c r)"), ALU.mult),
                            reads=[ps_r, H_r], writes=[H_r])
                    if self.dbg and bt == 0:
                        P.dma('sp', self.zdbg_t.ap()[o], H[:, 0, :, :].rearrange("p c r -> p (c r)"), reads=[H_r],
                              writes=[self.zdbg_r], conc=True)

            def transposes(bt):
                b2 = bt % 2
                H, H_r = hb3[b2], hb3_r[b2]
                hb_ = bt % 2
                ys, ys_r = yst[(bt // 2) % 2], yst_r[(bt // 2) % 2]
                for rq in range(8):
                    for h in range(4):
                        r = 4 * rq + h
                        P.op('pe', lambda r=r, h=h, H=H, hb_=hb_: nc.tensor.transpose(pstt[64 * hb_:64 * hb_ + 64, h, :],
                                                                                     H[:, 0, :, r], ident[:]),
                             reads=[H_r, ident_r], writes=[pstt_r])
                    P.op('act', lambda rq=rq, hb_=hb_, ys=ys: nc.scalar.copy(
                        out=ys[64 * hb_:64 * hb_ + 64, :].rearrange("c (p r) -> c r p", r=32)[:, 4 * rq:4 * rq + 4, :],
                        in_=pstt[64 * hb_:64 * hb_ + 64, :, :]),
                        reads=[pstt_r], writes=[ys_r], conc=True)
                if hb_ == 1:
                    P.dma('sp', self.yh_t.ap()[bt // 2], ys[:], reads=[ys_r], writes=[self.yh_r], conc=True)

            prep_weights(0)
            for q in range(8):
                proj_chunk(0, q)
            for bt in range(8):
                if bt + 1 < 8:
                    prep_weights(bt + 1)
                i = 0
                for o in range(2):
                    for k in range(4):
                        conv_stage(bt, o, k)
                        if bt + 1 < 8:
                            proj_chunk(bt + 1, i)
                        i += 1
                transposes(bt)
            P.barrier()

    def mix_attn(self, l):
        nc, P = self.nc, self.P
        EB, EB_r = self.EB, self.EB_r
        with ExitStack() as es:
            EB2, EB2_r = self.sb(es, "EB2", [128, 12, 2, 256], F32)
            for j in range(2):
                P.op('pool', lambda j=j: nc.gpsimd.tensor_copy(out=EB2[:, :, j, :], in_=EB[:]), reads=[EB_r], writes=[EB2_r],
                     conc=True)
            qh, qh_r = self.sb(es, "qh", [128, SEQ], BF16)
            kh, kh_r = self.sb(es, "kh", [128, SEQ + 2048], BF16)
            vt, vt_r = [], []
            for i in range(2):
                a, b = self.sb(es, "vt", [128, 33, 128], BF16)
                vt.append(a)
                vt_r.append(b)
            NB = 3
            psc, psc_r, pov, pov_r, pe_, pe_r, pm, pm_r, ot, ot_r = ([] for _ in range(10))
            for i in range(NB):
                a, b = self.ps(es, "psc", [128, 512], F32)
                psc.append(a)
                psc_r.append(b)
                a, b = self.sb(es, "pexp", [128, 512], F32)
                pe_.append(a)
                pe_r.append(b)
                a, b = self.sb(es, "pm", [128, 512], BF16)
                pm.append(a)
                pm_r.append(b)
                a, b = self.sb(es, "ot", [128, 2, 128], F32)
                ot.append(a)
                ot_r.append(b)
            for i in range(2):
                a, b = self.ps(es, "pov", [128, 256], F32)
                pov.append(a)
                pov_r.append(b)
            it = 0
            iv = 0
            for g in range(3):
                D = DILS[g]
                m = SEQ // D
                nblk = m // 128
                nch = nblk + 1
                vd_t, vd_r = self.vd_t[g]
                for hp in range(2):
                    cq = 2 * g + hp
                    ck = 6 + 2 * g + hp
                    P.dma('sp', qh[:], self.qk_t.ap()[cq], reads=[self.qk_r], writes=[qh_r])
                    P.op('pool', lambda: nc.gpsimd.memset(kh[:], 0.0), writes=[kh_r])
                    P.dma('sp', kh[:, 64 * D:64 * D + SEQ], self.qk_t.ap()[ck], reads=[self.qk_r], writes=[kh_r])
                    for hi in range(2):
                        hh = 2 * hp + hi
                        p0 = 64 * hi
                        for r in range(D):
                            vb = iv % 2
                            iv += 1
                            P.dma('sp', vt[vb][:, 0:nch, :], vd_t.ap()[r, :, hh, :].rearrange("(c p) e -> p c e", p=128),
                                  reads=[vd_r], writes=[vt_r[vb]])
                            for b in range(0, nblk, 2):
                                pb = it % NB
                                po_ = it % 2
                                it += 1
                                for jb in range(2):
                                    q0 = r + D * 128 * (b + jb)
                                    kA = q0
                                    kB = r + D * 128 * (b + jb + 1)
                                    P.op('pe', lambda pb=pb, p0=p0, q0=q0, kA=kA, D=D, jb=jb: nc.tensor.matmul(
                                        psc[pb][:, jb * 256:jb * 256 + 128], kh[p0:p0 + 64, sl(kA, 128, D)],
                                        qh[p0:p0 + 64, sl(q0, 128, D)], start=True, stop=True),
                                        reads=[kh_r, qh_r], writes=[psc_r[pb]])
                                    P.op('pe', lambda pb=pb, p0=p0, q0=q0, kB=kB, D=D, jb=jb: nc.tensor.matmul(
                                        psc[pb][:, jb * 256 + 128:(jb + 1) * 256], kh[p0:p0 + 64, sl(kB, 128, D)],
                                        qh[p0:p0 + 64, sl(q0, 128, D)], start=True, stop=True),
                                        reads=[kh_r, qh_r], writes=[psc_r[pb]])
                                P.op('act', lambda pb=pb: nc.scalar.activation(out=pe_[pb][:], in_=psc[pb][:], func=AF.Exp),
                                     reads=[psc_r[pb]], writes=[pe_r[pb]])
                                P.op('dve', lambda pb=pb, g=g, hh=hh: nc.vector.tensor_tensor(
                                    pm[pb][:], pe_[pb][:], EB2[:, g * 4 + hh, :, :].rearrange("p j e -> p (j e)"), ALU.mult),
                                    reads=[pe_r[pb], EB2_r], writes=[pm_r[pb]])
                                for jb in range(2):
                                    P.op('pe', lambda pb=pb, po_=po_, vb=vb, b=b, jb=jb: nc.tensor.matmul(
                                        pov[po_][:, jb * 128:(jb + 1) * 128], pm[pb][:, jb * 256:jb * 256 + 128],
                                        vt[vb][:, b + jb, :], start=True, stop=False),
                                        reads=[pm_r[pb], vt_r[vb]], writes=[pov_r[po_]])
                                    P.op('pe', lambda pb=pb, po_=po_, vb=vb, b=b, jb=jb: nc.tensor.matmul(
                                        pov[po_][:, jb * 128:(jb + 1) * 128], pm[pb][:, jb * 256 + 128:(jb + 1) * 256],
                                        vt[vb][:, b + jb + 1, :], start=False, stop=True),
                                        reads=[pm_r[pb], vt_r[vb]], writes=[pov_r[po_]])
                                P.op('act', lambda pb=pb, po_=po_: nc.scalar.copy(
                                    out=ot[pb][:].rearrange("p j e -> p (j e)"), in_=pov[po_][:]),
                                    reads=[pov_r[po_]], writes=[ot_r[pb]])
                                q00 = r + D * 128 * b
                                P.dma('sp', self.att_t.ap()[g, sl(q00, 256, D), hh, :].rearrange("(j p) e -> p j e", p=128),
                                      ot[pb][:], reads=[ot_r[pb]], writes=[self.att_r], conc=True)
            P.barrier()

    def mix_out(self, l, src, dst):
        nc, P = self.nc, self.P
        src_t, src_r = src
        dst_t, dst_r = dst
        T = 1024
        with ExitStack() as es:
            ident, ident_r = self.load_const_tiles(es)
            whp, whp_r = self.sb(es, "whp", [128, 4, D_MODEL], BF16)
            wap, wap_r = self.sb(es, "wap", [128, 2, D_MODEL], BF16)
            wout, wout_r = self.sb(es, "wout", [128, 8, D_MODEL], BF16)
            wds, wds_r = [], []
            for i in range(2):
                a, b = self.sb(es, "wdso", [128, D_MODEL], F32)
                wds.append(a)
                wds_r.append(b)
            iw = 0
            for (wt, wt_r, nkc, name) in ((whp, whp_r, 4, 'w_hy_proj'), (wap, wap_r, 2, 'w_at_proj'), (wout, wout_r, 8, 'w_out')):
                for kc in range(nkc):
                    b2 = iw % 2
                    iw += 1
                    P.dma('sp', wds[b2][:], self.w[name].ap()[l, kc * 128:(kc + 1) * 128, :], reads=[self.wr], writes=[wds_r[b2]])
                    P.op('pool', lambda wt=wt, kc=kc, b2=b2: nc.gpsimd.tensor_copy(out=wt[:, kc, :], in_=wds[b2][:]),
                         reads=[wds_r[b2]], writes=[wt_r], conc=True)
            yhyT, yhyT_r = self.sb(es, "yhyT", [128, 4, T], BF16)
            yatT, yatT_r = self.sb(es, "yatT", [128, 2, T], BF16)
            yT, yT_r = self.sb(es, "yT", [128, 8, T], BF16)
            att, att_r, s2, s2_r, yab, yab_r, rden, rden_r = ([] for _ in range(8))
            pst, pst_r, pst2, pst2_r = [], [], [], []
            for i in range(2):
                a, b = self.sb(es, "attl", [128, 3, 4, 128], F32)
                att.append(a)
                att_r.append(b)
                a, b = self.sb(es, "s2", [128, 4, 128], F32)
                s2.append(a)
                s2_r.append(b)
                a, b = self.sb(es, "yab", [128, 4, 64], BF16)
                yab.append(a)
                yab_r.append(b)
                a, b = self.sb(es, "rden", [128, 4], F32)
                rden.append(a)
                rden_r.append(b)
            a, b = self.ps(es, "psty", [128, 4, 128], BF16)
            pst.append(a)
            pst_r.append(b)
            a, b = self.ps(es, "psta", [128, 2, 128], BF16)
            pst2.append(a)
            pst2_r.append(b)
            pa, pa_r, pbb, pbb_r, po, po_r = [], [], [], [], [], []
            gA, gA_r, gB, gB_r, ta, ta_r, tb_, tb_r, xin, xin_r, xo, xo_r = ([] for _ in range(12))
            for i in range(2):
                for lst, lstr, nm in ((pa, pa_r, "pa"), (pbb, pbb_r, "pbb"), (po, po_r, "poo")):
                    a, b = self.ps(es, nm, [128, 512], F32)
                    lst.append(a)
                    lstr.append(b)
                for lst, lstr, nm in ((gA, gA_r, "gA"), (gB, gB_r, "gB"), (ta, ta_r, "ta"), (tb_, tb_r, "tbt")):
                    a, b = self.sb(es, nm, [128, 512], F32)
                    lst.append(a)
                    lstr.append(b)
                a, b = self.sb(es, "xin", [128, D_MODEL], F32)
                xin.append(a)
                xin_r.append(b)
                a, b = self.sb(es, "xo", [128, D_MODEL], F32)
                xo.append(a)
                xo_r.append(b)
            it = 0
            for st in range(SEQ // T):
                for kc in range(4):
                    P.dma('sp', yhyT[:, kc, :], self.yh_t.ap()[kc, :, st * T:(st + 1) * T], reads=[self.yh_r], writes=[yhyT_r],
                          conc=True)
                for tb in range(8):
                    gb = st * 8 + tb
                    b2 = gb % 2
                    P.dma('sp', att[b2][:], self.att_t.ap()[:, gb * 128:(gb + 1) * 128, :, :].rearrange("g t h e -> t g h e"),
                          reads=[self.att_r], writes=[att_r[b2]])
                    P.op('dve', lambda b2=b2: nc.vector.tensor_tensor(s2[b2][:], att[b2][:, 0, :, :], att[b2][:, 1, :, :], ALU.add),
                         reads=[att_r[b2]], writes=[s2_r[b2]])
                    P.op('dve', lambda b2=b2: nc.vector.tensor_tensor(s2[b2][:], s2[b2][:], att[b2][:, 2, :, :], ALU.add),
                         reads=[att_r[b2], s2_r[b2]], writes=[s2_r[b2]])
                    P.op('dve', lambda b2=b2: nc.vector.reciprocal(rden[b2][:], s2[b2][:, :, 64]), reads=[s2_r[b2]],
                         writes=[rden_r[b2]])
                    for hh in range(4):
                        P.op('dve', lambda b2=b2, hh=hh: nc.vector.tensor_scalar(yab[b2][:, hh, :], s2[b2][:, hh, 0:64],
                                                                                rden[b2][:, hh:hh + 1], None, ALU.mult),
                             reads=[s2_r[b2], rden_r[b2]], writes=[yab_r[b2]], conc=True)
                    for c in range(2):
                        P.op('pe', lambda c=c, b2=b2: nc.tensor.transpose(
                            pst2[0][:, c, :], yab[b2][:, 2 * c:2 * c + 2, :].rearrange("p h e -> p (h e)"), ident[:]),
                            reads=[yab_r[b2], ident_r], writes=[pst2_r[0]])
                    P.op('act', lambda tb=tb: nc.scalar.copy(out=yatT[:, :, tb * 128:(tb + 1) * 128], in_=pst2[0][:]),
                         reads=[pst2_r[0]], writes=[yatT_r], conc=True)
                for dc in range(8):
                    for th in range(2):
                        pb = it % 2
                        it += 1
                        tsl = slice(st * T + th * 512, st * T + (th + 1) * 512)
                        P.dma('sp', gA[pb][:], self.gt_t.ap()[dc, :, tsl], reads=[self.gt_r], writes=[gA_r[pb]])
                        P.dma('sp', gB[pb][:], self.gt_t.ap()[8 + dc, :, tsl], reads=[self.gt_r], writes=[gB_r[pb]])
                        for kc in range(4):
                            P.op('pe', lambda kc=kc, dc=dc, th=th, pb=pb: nc.tensor.matmul(
                                pa[pb][:], whp[:, kc, dc * 128:(dc + 1) * 128], yhyT[:, kc, th * 512:(th + 1) * 512],
                                start=(kc == 0), stop=(kc == 3)), reads=[whp_r, yhyT_r], writes=[pa_r[pb]])
                        for kc in range(2):
                            P.op('pe', lambda kc=kc, dc=dc, th=th, pb=pb: nc.tensor.matmul(
                                pbb[pb][:], wap[:, kc, dc * 128:(dc + 1) * 128], yatT[:, kc, th * 512:(th + 1) * 512],
                                start=(kc == 0), stop=(kc == 1)), reads=[wap_r, yatT_r], writes=[pbb_r[pb]])
                        P.op('dve', lambda pb=pb: nc.vector.tensor_tensor(ta[pb][:], pa[pb][:], gA[pb][:], ALU.mult),
                             reads=[pa_r[pb], gA_r[pb]], writes=[ta_r[pb]])
                        P.op('dve', lambda pb=pb: nc.vector.tensor_tensor(tb_[pb][:], pbb[pb][:], gB[pb][:], ALU.mult),
                             reads=[pbb_r[pb], gB_r[pb]], writes=[tb_r[pb]])
                        P.op('pool', lambda pb=pb, dc=dc, th=th: nc.gpsimd.tensor_tensor(
                            yT[:, dc, th * 512:(th + 1) * 512], ta[pb][:], tb_[pb][:], ALU.add),
                            reads=[ta_r[pb], tb_r[pb]], writes=[yT_r], conc=True)
                for tb in range(8):
                    b2 = tb % 2
                    r0 = st * T + tb * 128
                    P.dma('sp', xin[b2][:], src_t.ap()[r0:r0 + 128, :], reads=[src_r], writes=[xin_r[b2]])
                    for dh in range(2):
                        pb = it % 2
                        it += 1
                        for kc in range(8):
                            P.op('pe', lambda kc=kc, tb=tb, dh=dh, pb=pb: nc.tensor.matmul(
                                po[pb][:], yT[:, kc, tb * 128:(tb + 1) * 128], wout[:, kc, dh * 512:(dh + 1) * 512],
                                start=(kc == 0), stop=(kc == 7)), reads=[yT_r, wout_r], writes=[po_r[pb]])
                        P.op('dve', lambda pb=pb, b2=b2, dh=dh: nc.vector.tensor_tensor(
                            xo[b2][:, dh * 512:(dh + 1) * 512], po[pb][:], xin[b2][:, dh * 512:(dh + 1) * 512], ALU.add),
                            reads=[po_r[pb], xin_r[b2]], writes=[xo_r[b2]], conc=True)
                    P.dma('sp', dst_t.ap()[r0:r0 + 128, :], xo[b2][:], reads=[xo_r[b2]], writes=[dst_r], conc=True)
            P.barrier()

    def mixer_phase(self, l, src, dst):
        upto = self.mix_upto
        order = ['filt', 'proj', 'conv', 'attn', 'out']
        n = len(order) if upto is None else order.index(upto) + 1
        names = order[:n]
        if 'filt' in names:
            self.mix_filters2(l)
        with ExitStack() as es:
            self.hT, self.hT_r = self.sb(es, "hTm", [128, 8, SEQ + 2], BF16)
            if 'proj' in names:
                self.mix_proj(l, src)
            if 'conv' in names:
                self.mix_hyconv(l)
            self.P.barrier()
        if 'attn' in names:
            self.mix_attn(l)
        if 'out' in names:
            self.mix_out(l, src, dst)
        self.P.barrier()

    def build(self):
        self.declare()
        P = self.P
        with ExitStack() as es_sem:
            self.eps_t, self.eps_r = self.sb(es_sem, "eps", [128, 1], F32)
            P.op('dve', lambda: self.nc.vector.memset(self.eps_t[:], EPS), writes=[self.eps_r])
            self.attn_bias_setup(es_sem)
            cur = (self.x_t, self.x_r)
            si = 0
            stages = self.stages
            for l in range(DEPTH):
                for ph in ('ffn1', 'mix', 'ffn2'):
                    name = "%s_%d" % (ph, l)
                    if stages is not None and name not in stages:
                        continue
                    last = (stages is not None and name == stages[-1]) or (stages is None and l == DEPTH - 1 and ph == 'ffn2')
                    dst = (self.y_t, self.y_r) if last else self.xs_t[si]
                    si += 1
                    if ph == 'ffn1':
                        self.ffn_phase(l, 1, cur, dst)
                    elif ph == 'ffn2':
                        self.ffn_phase(l, 2, cur, dst)
                    else:
                        self.mixer_phase(l, cur, dst)
                    cur = dst
            P.barrier()
            nw = P.emit(es_sem)
            print("ops", len(P.ops), "waits", nw, "slots", len(P.slot_cnt))
        return self.nc


_NC_CACHE = {}


def kernel(**inputs):
    x = np.ascontiguousarray(np.asarray(inputs['x'], dtype=np.float32))
    consts = host_constants()
    if 'nc' not in _NC_CACHE:
        _NC_CACHE['nc'] = Builder().build()
    nc = _NC_CACHE['nc']
    shared = {k: np.ascontiguousarray(np.asarray(inputs[k], dtype=np.float32)) for k in WEIGHT_NAMES}
    shared.update(consts)
    in_maps = []
    for b in range(BATCH):
        m = dict(shared)
        m['x'] = x[b]
        in_maps.append(m)
    res = run_bass_kernel_spmd(nc, in_maps, core_ids=list(range(BATCH)))
    return np.stack([np.asarray(r['y'], dtype=np.float32) for r in res.results], axis=0)
```

```python
import math
from contextlib import ExitStack

import numpy as np
import ml_dtypes

import concourse.bass as bass
import concourse.mybir as mybir
from concourse.alu_op_type import AluOpType as ALU
from concourse.bass_utils import run_bass_kernel_spmd

F32 = mybir.dt.float32
BF16 = mybir.dt.bfloat16
AF = mybir.ActivationFunctionType

D_MODEL = 1024
BATCH = 8
SEQ = 4096
DEPTH = 2
HEAD_DIM = 64
HYW = 512
HY_EMB = 33
HID = 64
WINDOWS = (128, 512, 2048)
DILS = (1, 4, 16)
D_FF = 2688
NFC = D_FF // 128
IN_WIDTH = 3840
EPS = 1e-6
NFFT = 8192
NTB = SEQ // 128

WEIGHT_NAMES = ['rel_bias', 'ffn1_norm', 'ffn1_w_gate', 'ffn1_w_up', 'ffn1_w_down', 'mix_norm', 'w_in',
                'w_gate', 'b_gate', 'hy_conv_w', 'hy_conv_b', 'hy_filt_w1', 'hy_filt_b1', 'hy_filt_w2',
                'hy_filt_b2', 'hy_filt_w3', 'hy_skip', 'q_norm', 'k_norm', 'w_hy_proj', 'w_at_proj',
                'w_out', 'ffn2_norm', 'ffn2_w_gate', 'ffn2_w_up', 'ffn2_w_down']


class Reg:
    __slots__ = ('name', 'prev', 'writers', 'readers', 'slot', 'dcnt', 'lastdma')

    def __init__(self, name):
        self.name = name
        self.prev = []
        self.writers = []
        self.readers = []
        self.slot = None
        self.dcnt = 0
        self.lastdma = None


class Op:
    __slots__ = ('eng', 'fn', 'deps', 'is_dma', 'reg', 'dval', 'sig', 'sigval', 'slot')


class Prog:
    ENG = {'pe': 'tensor', 'act': 'scalar', 'dve': 'vector', 'pool': 'gpsimd', 'sp': 'sync'}

    def __init__(self, nc):
        self.nc = nc
        self.ops = []
        self.regs = []
        self.last = {}
        self.dma_regs = set()
        self.slot_cnt = []
        self.free_slots = []

    def reg(self, name):
        r = Reg(name)
        self.regs.append(r)
        return r

    def _add(self, eng, fn, reads, writes, conc, is_dma):
        i = len(self.ops)
        deps = set()
        for r in reads:
            deps.update(r.writers)
            if not r.writers:
                deps.update(r.prev)
        for w in writes:
            if conc:
                if w.readers:
                    w.prev = w.readers + w.writers
                    w.writers = []
                    w.readers = []
                deps.update(w.prev)
            else:
                deps.update(w.prev)
                deps.update(w.writers)
                deps.update(w.readers)
        for w in writes:
            if conc:
                w.writers.append(i)
            else:
                w.prev = []
                w.writers = [i]
                w.readers = []
        for r in reads:
            r.readers.append(i)
        deps.discard(i)
        o = Op()
        o.eng = eng
        o.fn = fn
        o.deps = deps
        o.is_dma = is_dma
        o.reg = None
        o.dval = 0
        o.sig = False
        o.sigval = 0
        o.slot = None
        if is_dma:
            w = writes[0]
            if w.slot is None:
                if self.free_slots:
                    w.slot = self.free_slots.pop()
                else:
                    w.slot = len(self.slot_cnt)
                    self.slot_cnt.append(0)
            self.slot_cnt[w.slot] += 16
            w.dcnt = self.slot_cnt[w.slot]
            o.reg = w
            o.slot = w.slot
            o.dval = w.dcnt
            w.lastdma = i
            self.dma_regs.add(w)
        else:
            self.last[eng] = i
        self.ops.append(o)
        return i

    def op(self, eng, fn, reads=(), writes=(), conc=False):
        return self._add(eng, fn, list(reads), list(writes), conc, False)

    def dma(self, q, out, in_, reads=(), writes=(), conc=False, **kw):
        nc = self.nc
        e = getattr(nc, self.ENG[q])

        def fn():
            return e.dma_start(out=out, in_=in_, **kw)
        assert len(writes) == 1
        return self._add(q, fn, list(reads), list(writes), conc, True)

    def barrier(self):
        deps = set(self.last.values())
        for r in self.dma_regs:
            if r.lastdma is not None:
                deps.add(r.lastdma)
        for e in self.ENG:
            o = Op()
            o.eng = e
            o.fn = None
            o.deps = set(deps)
            o.is_dma = False
            o.reg = None
            o.dval = 0
            o.sig = False
            o.sigval = 0
            o.slot = None
            self.ops.append(o)
        for r in self.regs:
            r.prev = []
            r.writers = []
            r.readers = []
        for r in self.dma_regs:
            self.free_slots.append(r.slot)
            r.slot = None
            r.lastdma = None
        self.dma_regs = set()

    def emit(self, es):
        nc = self.nc
        ops = self.ops
        for o in ops:
            if o.eng == 'pe' and not o.is_dma:
                o.deps = {d for d in o.deps if ops[d].is_dma or ops[d].eng != 'pe'}
            for d in o.deps:
                ops[d].sig = True
        esem = {e: es.enter_context(nc.semaphore("sem_" + e)) for e in self.ENG}
        cnt = {e: 0 for e in self.ENG}
        ssem = [es.enter_context(nc.semaphore("semslot%d" % i)) for i in range(len(self.slot_cnt))]
        for o in ops:
            if o.is_dma:
                pass
            elif o.sig and o.fn is not None:
                cnt[o.eng] += 1
                o.sigval = cnt[o.eng]
        known = {e: {} for e in self.ENG}
        nwait = 0
        for o in ops:
            e = o.eng
            eng = getattr(nc, self.ENG[e])
            need = {}
            for d in o.deps:
                od = ops[d]
                if od.is_dma:
                    key = ('r', od.slot)
                    sem = ssem[od.slot]
                    val = od.dval
                else:
                    if od.fn is None:
                        continue
                    key = ('e', od.eng)
                    sem = esem[od.eng]
                    val = od.sigval
                if need.get(key, (None, 0))[1] < val:
                    need[key] = (sem, val)
            kn = known[e]
            for key, (sem, val) in need.items():
                if kn.get(key, 0) < val:
                    eng.wait_ge(sem, val)
                    kn[key] = val
                    nwait += 1
            if o.fn is None:
                continue
            ins = o.fn()
            if o.is_dma:
                ins.then_inc(ssem[o.slot], 16)
            elif o.sig:
                ins.then_inc(esem[e], 1)
        return nwait


_CONST_CACHE = {}


def t5_bucket_np(rel):
    half = 16
    exact = 8
    ret = np.where(rel > 0, half, 0)
    n = np.abs(rel)
    nf = np.maximum(n, 1).astype(np.float32)
    large = exact + (np.log(nf / exact) / np.float32(math.log(1024 / exact)) * (half - exact)).astype(np.int32)
    large = np.minimum(large, half - 1)
    return ret + np.where(n < exact, n, large)


def host_constants():
    if _CONST_CACHE:
        return _CONST_CACHE
    c = {}
    L = SEQ
    t = np.linspace(0.0, 1.0, L, dtype=np.float32)[:, None]
    w = (np.float32(2.0 * math.pi / L)) * np.arange(L, dtype=np.float32)[:, None]
    f = np.linspace(1e-4, 15, 16, dtype=np.float32)[None]
    z = np.concatenate([t, np.cos(f * w), -np.sin(f * w)], axis=-1).astype(np.float32)
    c['zfeat'] = np.ascontiguousarray(z.T)
    max_decay = math.log(1e-2) / 0.3
    min_decay = math.log(1e-2) / 1.5
    deltas = np.abs(np.linspace(min_decay, max_decay, HYW, dtype=np.float32))
    dec = np.exp(-t * deltas[None]).astype(np.float32)
    c['dec_f'] = dec
    decb = dec.copy()
    decb[0] = 0.0
    c['dec_b'] = decb
    c['ident'] = np.eye(128, dtype=np.float32).astype(ml_dtypes.bfloat16)
    ob = np.zeros((128, 128), np.float32)
    ob[:64, :64] = 1.0
    ob[64:, 64:] = 1.0
    c['ones_blk'] = ob
    c['ones_all'] = np.ones((128, 128), np.float32)
    c['jrev'] = np.ascontiguousarray(np.eye(128, dtype=np.float32)[::-1])
    oh = np.zeros((3, 32, 512), np.float32)
    bm = np.zeros((4, 512), np.float32)
    for g in range(3):
        for i in range(191, 320):
            rel = 255 - i
            b = int(t5_bucket_np(np.array([rel * DILS[g]]))[0])
            oh[g, b, i] = 1.0
    bm[:, 191:320] = 1.0
    c['oh'] = oh
    c['bm'] = bm
    th = 2.0 * math.pi / NFFT
    p_ = np.arange(128, dtype=np.float64)
    r_ = np.arange(32, dtype=np.float64)
    k1_ = np.arange(128, dtype=np.float64) + 0.5
    k2_ = np.arange(32, dtype=np.float64)
    a1 = 2.0 * math.pi * np.outer(p_, k1_) / 256.0
    c['f1cat'] = np.concatenate([np.cos(a1), -np.sin(a1)], axis=1).astype(np.float32).astype(ml_dtypes.bfloat16)
    c['g1re'] = (np.cos(a1).T * (2.0 / NFFT)).astype(np.float32).astype(ml_dtypes.bfloat16)
    c['g1imn'] = (-np.sin(a1).T * (2.0 / NFFT)).astype(np.float32).astype(ml_dtypes.bfloat16)
    at = th * np.outer(r_, k1_)
    tre = np.tile(np.cos(at), (4, 1))
    tim = np.tile(-np.sin(at), (4, 1))
    c['tf_re'] = np.ascontiguousarray(np.concatenate([tre, tre], axis=1).astype(np.float32))
    c['tf_im'] = np.ascontiguousarray(np.concatenate([tim, tim], axis=1).astype(np.float32))
    tcre = np.cos(at).T
    tcim = np.sin(at).T
    c['tc_re'] = np.ascontiguousarray(np.tile(tcre, (1, 16)).astype(np.float32))
    c['tc_im'] = np.ascontiguousarray(np.tile(tcim, (1, 16)).astype(np.float32))
    ae = 2.0 * math.pi * np.outer(r_, k2_) / 32.0
    ere = np.cos(ae)
    eim = -np.sin(ae)

    def bd(m):
        o_ = np.zeros((128, 128), np.float64)
        for i in range(4):
            o_[32 * i:32 * i + 32, 32 * i:32 * i + 32] = m
        return o_.astype(np.float32).astype(ml_dtypes.bfloat16)
    c['bd_ere'] = bd(ere)
    c['bd_eren'] = bd(-ere)
    c['bd_eim'] = bd(eim)
    c['bd_eimn'] = bd(-eim)
    d2 = np.stack([c['dec_f'], c['dec_b']], axis=0)
    d2 = d2.reshape(2, 128, 32, 8, 64).transpose(3, 1, 2, 0, 4)
    c['dec2'] = np.ascontiguousarray(d2.astype(np.float32))
    dr = np.stack([c['dec_f'], c['dec_b']], axis=0).reshape(2, 128, 32, 512).transpose(2, 1, 0, 3)
    c['decr'] = np.ascontiguousarray(dr.astype(np.float32))
    for k in ('Fh', 'Gh', 'dec_f', 'dec_b'):
        c.pop(k, None)
    _CONST_CACHE.update(c)
    return c


CONST_SPECS = [('zfeat', [33, SEQ], F32), ('ident', [128, 128], BF16),
               ('f1cat', [128, 256], BF16), ('g1re', [128, 128], BF16), ('g1imn', [128, 128], BF16),
               ('tf_re', [128, 256], F32), ('tf_im', [128, 256], F32), ('tc_re', [128, 512], F32), ('tc_im', [128, 512], F32),
               ('bd_ere', [128, 128], BF16), ('bd_eren', [128, 128], BF16), ('bd_eim', [128, 128], BF16),
               ('bd_eimn', [128, 128], BF16), ('dec2', [8, 128, 32, 2, 64], F32), ('decr', [32, 128, 2, 512], F32),
               ('ones_blk', [128, 128], F32), ('ones_all', [128, 128], F32), ('jrev', [128, 128], F32), ('oh', [3, 32, 512], F32),
               ('bm', [4, 512], F32)]

WEIGHT_SHAPES = {
    'rel_bias': [32, 12], 'ffn1_norm': [2, 1024], 'ffn1_w_gate': [2, 1024, 2688], 'ffn1_w_up': [2, 1024, 2688],
    'ffn1_w_down': [2, 2688, 1024], 'mix_norm': [2, 1024], 'w_in': [2, 1024, 3840], 'w_gate': [2, 1024, 2048],
    'b_gate': [2, 2048], 'hy_conv_w': [2, 3, 1536], 'hy_conv_b': [2, 1536], 'hy_filt_w1': [2, 33, 64],
    'hy_filt_b1': [2, 64], 'hy_filt_w2': [2, 64, 64], 'hy_filt_b2': [2, 64], 'hy_filt_w3': [2, 64, 2048],
    'hy_skip': [2, 2, 512], 'q_norm': [2, 64], 'k_norm': [2, 64], 'w_hy_proj': [2, 512, 1024],
    'w_at_proj': [2, 256, 1024], 'w_out': [2, 1024, 1024], 'ffn2_norm': [2, 1024], 'ffn2_w_gate': [2, 1024, 2688],
    'ffn2_w_up': [2, 1024, 2688], 'ffn2_w_down': [2, 2688, 1024]}


def sl(start, n, step):
    return slice(start, start + (n - 1) * step + 1, step)


def bcast_rows(ap1d_tensor, offset, n, parts=128):
    return bass.AP(ap1d_tensor, offset, [[0, parts], [1, n]])


class Builder:
    def __init__(self, stages=None, dbg=False, mix_upto=None):
        self.mix_upto = mix_upto
        self.nc = bass.Bass("TRN2", target_bir_lowering=False)
        self.P = Prog(self.nc)
        self.stages = stages
        self.dbg = dbg
        self.uid = 0

    def sb(self, es, name, shape, dt):
        self.uid += 1
        t = es.enter_context(self.nc.sbuf_tensor("%s_%d" % (name, self.uid), shape, dt))
        return t, self.P.reg(name)

    def ps(self, es, name, shape, dt):
        self.uid += 1
        t = es.enter_context(self.nc.psum_tensor("%s_%d" % (name, self.uid), shape, dt))
        return t, self.P.reg(name)

    def dram(self, name, shape, dt, kind="Internal"):
        t = self.nc.dram_tensor(name, shape, dt, kind=kind)
        return t, self.P.reg(name)

    def declare(self):
        nc = self.nc
        self.x_t, self.x_r = self.dram("x", [SEQ, D_MODEL], F32, "ExternalInput")
        SK = "ExternalOutput" if self.dbg else "Internal"
        self.y_t, self.y_r = self.dram("y", [SEQ, D_MODEL], F32, "ExternalOutput")
        self.w = {}
        self.wr = self.P.reg("weights")
        for k in WEIGHT_NAMES:
            self.w[k] = nc.dram_tensor(k, WEIGHT_SHAPES[k], F32, kind="ExternalInput")
        self.c = {}
        for k, shp, dt in CONST_SPECS:
            self.c[k] = nc.dram_tensor(k, shp, dt, kind="ExternalInput")
        self.xs_t = []
        for i in range(6):
            self.xs_t.append(self.dram("xres%d" % i, [SEQ, D_MODEL], F32))
        self.hy_t, self.hy_r = self.dram("hy_s", [SEQ, 1536], F32, SK)
        self.z1_t, self.z1_r = self.dram("z1_s", [SEQ, HYW], F32, SK)
        self.qk_t, self.qk_r = self.dram("qk_s", [12, 128, SEQ], BF16, SK)
        self.gt_t, self.gt_r = self.dram("gt_s", [16, 128, SEQ], F32, SK)
        self.vd_t = []
        for g in range(3):
            D = DILS[g]
            m = SEQ // D
            self.vd_t.append(self.dram("vd_s%d" % g, [D, m + 128, 4, 128], BF16, SK))
        self.kf_t, self.kf_r = self.dram("kf_s", [2, 32, 128, 4, 2, 128], F32, SK)
        self.yh_t, self.yh_r = self.dram("yh_s", [4, 128, SEQ], BF16, SK)
        self.att_t, self.att_r = self.dram("att_s", [3, SEQ, 4, 128], F32, SK)
        self.rv_t, self.rv_r = self.dram("rv_s", [12, 512], F32, SK)
        self.z2_t, self.z2_r = self.dram("z2_s", [SEQ, HYW], F32, SK)
        self.zdbg_t, self.zdbg_r = self.dram("zdbg_s", [2, 128, 2048], BF16, SK)

    def norm_to_hT(self, es, src_t, src_r, tok0, ntb, gbc, hT, hT_r, hoff, tiles):
        nc, P = self.nc, self.P
        (xs, xs_rs, junk, junk_r, ss, ss_r, rs, rs_r, hb, hb_r, pst, pst_r, ident, ident_r) = tiles
        src = src_t.ap()
        for tb in range(ntb):
            xb = xs[:, tb % len(xs_rs), :]
            xr = xs_rs[tb % len(xs_rs)]
            r0 = tok0 + tb * 128
            P.dma('sp', xb, src[r0:r0 + 128, :], reads=[src_r], writes=[xr])
            b2 = tb % 2
            P.op('act', lambda xb=xb, tb=tb: nc.scalar.activation(out=junk[:], in_=xb, func=AF.Square,
                                                                  accum_out=ss[:, tb:tb + 1]),
                 reads=[xr], writes=[junk_r, ss_r[tb]])
            P.op('dve', lambda tb=tb: nc.vector.tensor_scalar(rs[:, tb:tb + 1], ss[:, tb:tb + 1], 1.0 / D_MODEL, EPS,
                                                              ALU.mult, ALU.add),
                 reads=[ss_r[tb]], writes=[rs_r[tb]])
            P.op('act', lambda tb=tb: nc.scalar.activation(out=rs[:, tb:tb + 1], in_=rs[:, tb:tb + 1], func=AF.Sqrt),
                 reads=[rs_r[tb]], writes=[rs_r[tb]])
            P.op('dve', lambda tb=tb: nc.vector.reciprocal(rs[:, tb:tb + 1], rs[:, tb:tb + 1]),
                 reads=[rs_r[tb]], writes=[rs_r[tb]])
            P.op('dve', lambda xb=xb, tb=tb, b2=b2: nc.vector.scalar_tensor_tensor(
                hb[b2][:], xb, rs[:, tb:tb + 1], gbc[0][:], ALU.mult, ALU.mult),
                reads=[xr, rs_r[tb], gbc[1]], writes=[hb_r[b2]])
            for kc in range(8):
                P.op('pe', lambda kc=kc, b2=b2: nc.tensor.transpose(pst[b2][:, kc, :], hb[b2][:, kc * 128:(kc + 1) * 128],
                                                                   ident[:]),
                     reads=[hb_r[b2], ident_r], writes=[pst_r[b2]])
            c0 = hoff + tb * 128
            P.op('act', lambda b2=b2, c0=c0: nc.scalar.copy(out=hT[:, :, c0:c0 + 128], in_=pst[b2][:]),
                 reads=[pst_r[b2]], writes=[hT_r], conc=True)

    def load_const_tiles(self, es):
        nc, P = self.nc, self.P
        ident, ident_r = self.sb(es, "ident", [128, 128], BF16)
        P.dma('sp', ident[:], self.c['ident'].ap(), reads=[self.wr], writes=[ident_r])
        return ident, ident_r

    def ffn_phase(self, l, which, src, dst):
        nc, P = self.nc, self.P
        src_t, src_r = src
        dst_t, dst_r = dst
        wg_d = self.w['ffn%d_w_gate' % which].ap()
        wu_d = self.w['ffn%d_w_up' % which].ap()
        wd_d = self.w['ffn%d_w_down' % which].ap()
        gn_t = self.w['ffn%d_norm' % which]
        T = 1024
        with ExitStack() as es:
            ident, ident_r = self.load_const_tiles(es)
            gbc = self.sb(es, "gbc", [128, D_MODEL], F32)
            P.dma('sp', gbc[0][:], bcast_rows(gn_t, l * D_MODEL, D_MODEL), reads=[self.wr], writes=[gbc[1]])
            xs, _ = self.sb(es, "xs", [128, 8, D_MODEL], F32)
            xs_rs = [P.reg("xs%d" % i) for i in range(8)]
            junk, junk_r = self.sb(es, "junk", [128, D_MODEL], F32)
            ss, _ = self.sb(es, "ss", [128, 8], F32)
            ss_r = [P.reg("ss%d" % i) for i in range(8)]
            rs, _ = self.sb(es, "rs", [128, 8], F32)
            rs_r = [P.reg("rs%d" % i) for i in range(8)]
            hb, hb_r = [], []
            pst, pst_r = [], []
            for i in range(2):
                a, b = self.sb(es, "hb", [128, D_MODEL], BF16)
                hb.append(a)
                hb_r.append(b)
                a, b = self.ps(es, "pst", [128, 8, 128], BF16)
                pst.append(a)
                pst_r.append(b)
            hT, hT_r = self.sb(es, "hT", [128, 8, T], BF16)
            act, act_r = self.sb(es, "act", [128, NFC, T], BF16)
            wd, wd_r = self.sb(es, "wd", [128, NFC, D_MODEL], BF16)
            wst, wst_r, wbf, wbf_r = [], [], [], []
            for i in range(4):
                a, b = self.sb(es, "wst", [128, 8, 128], F32)
                wst.append(a)
                wst_r.append(b)
                a, b = self.sb(es, "wbf", [128, 8, 128], BF16)
                wbf.append(a)
                wbf_r.append(b)
            wds, wds_r = [], []
            for i in range(2):
                a, b = self.sb(es, "wds", [128, D_MODEL], F32)
                wds.append(a)
                wds_r.append(b)
            sg, sg_r = [], []
            pg, pg_r, pu, pu_r, po, po_r = [], [], [], [], [], []
            for i in range(2):
                a, b = self.sb(es, "sg", [128, 512], F32)
                sg.append(a)
                sg_r.append(b)
                a, b = self.ps(es, "pg", [128, 512], F32)
                pg.append(a)
                pg_r.append(b)
                a, b = self.ps(es, "pu", [128, 512], F32)
                pu.append(a)
                pu_r.append(b)
                a, b = self.ps(es, "po", [128, 512], F32)
                po.append(a)
                po_r.append(b)
            for j in range(NFC):
                b2 = j % 2
                P.dma('sp', wds[b2][:], wd_d[l, j * 128:(j + 1) * 128, :], reads=[self.wr], writes=[wds_r[b2]])
                P.op('pool', lambda j=j, b2=b2: nc.gpsimd.tensor_copy(out=wd[:, j, :], in_=wds[b2][:]),
                     reads=[wds_r[b2]], writes=[wd_r], conc=True)
            tiles = (xs, xs_rs, junk, junk_r, ss, ss_r, rs, rs_r, hb, hb_r, pst, pst_r, ident, ident_r)
            src_ap = src_t.ap()
            dst_ap = dst_t.ap()
            for st in range(SEQ // T):
                self.norm_to_hT(es, src_t, src_r, st * T, 8, gbc, hT, hT_r, 0, tiles)
                for j in range(NFC):
                    bg = (2 * j) % 4
                    bu = (2 * j + 1) % 4
                    P.dma('sp', wst[bg][:], wg_d[l, :, j * 128:(j + 1) * 128].rearrange("(kc p) m -> p kc m", p=128),
                          reads=[self.wr], writes=[wst_r[bg]])
                    P.dma('sp', wst[bu][:], wu_d[l, :, j * 128:(j + 1) * 128].rearrange("(kc p) m -> p kc m", p=128),
                          reads=[self.wr], writes=[wst_r[bu]])
                    P.op('pool', lambda bg=bg: nc.gpsimd.tensor_copy(out=wbf[bg][:], in_=wst[bg][:]),
                         reads=[wst_r[bg]], writes=[wbf_r[bg]])
                    P.op('pool', lambda bu=bu: nc.gpsimd.tensor_copy(out=wbf[bu][:], in_=wst[bu][:]),
                         reads=[wst_r[bu]], writes=[wbf_r[bu]])
                    for th in range(2):
                        pb = (2 * j + th) % 2
                        for kc in range(8):
                            P.op('pe', lambda kc=kc, th=th, pb=pb, bg=bg: nc.tensor.matmul(
                                pg[pb][:], wbf[bg][:, kc, :], hT[:, kc, th * 512:(th + 1) * 512],
                                start=(kc == 0), stop=(kc == 7)),
                                reads=[wbf_r[bg], hT_r], writes=[pg_r[pb]])
                        for kc in range(8):
                            P.op('pe', lambda kc=kc, th=th, pb=pb, bu=bu: nc.tensor.matmul(
                                pu[pb][:], wbf[bu][:, kc, :], hT[:, kc, th * 512:(th + 1) * 512],
                                start=(kc == 0), stop=(kc == 7)),
                                reads=[wbf_r[bu], hT_r], writes=[pu_r[pb]])
                        P.op('act', lambda pb=pb: nc.scalar.activation(out=sg[pb][:], in_=pg[pb][:], func=AF.Silu),
                             reads=[pg_r[pb]], writes=[sg_r[pb]])
                        P.op('dve', lambda pb=pb, j=j, th=th: nc.vector.tensor_tensor(
                            act[:, j, th * 512:(th + 1) * 512], sg[pb][:], pu[pb][:], ALU.mult),
                            reads=[sg_r[pb], pu_r[pb]], writes=[act_r], conc=True)
                for tb in range(8):
                    for dh in range(2):
                        pb = (2 * tb + dh) % 2
                        for j in range(NFC):
                            P.op('pe', lambda j=j, tb=tb, dh=dh, pb=pb: nc.tensor.matmul(
                                po[pb][:], act[:, j, tb * 128:(tb + 1) * 128], wd[:, j, dh * 512:(dh + 1) * 512],
                                start=(j == 0), stop=(j == NFC - 1)),
                                reads=[act_r, wd_r], writes=[po_r[pb]])
                        P.op('dve', lambda tb=tb, dh=dh, pb=pb: nc.vector.scalar_tensor_tensor(
                            xs[:, tb, dh * 512:(dh + 1) * 512], po[pb][:], 0.5, xs[:, tb, dh * 512:(dh + 1) * 512],
                            ALU.mult, ALU.add),
                            reads=[po_r[pb], xs_rs[tb]], writes=[xs_rs[tb]])
                    r0 = st * T + tb * 128
                    P.dma('sp', dst_ap[r0:r0 + 128, :], xs[:, tb, :], reads=[xs_rs[tb]], writes=[dst_r], conc=True)
            P.barrier()

    def attn_bias_setup(self, es_glob):
        nc, P = self.nc, self.P
        self.EB, self.EB_r = self.sb(es_glob, "EB", [128, 12, 256], F32)
        with ExitStack() as es:
            tb_, tb_r = self.sb(es, "relb", [32, 12], F32)
            oh, oh_r = self.sb(es, "oh", [32, 3, 512], F32)
            bm, bm_r = self.sb(es, "bm", [4, 512], F32)
            ee, ee_r = self.sb(es, "ee", [4, 512], F32)
            pp, pp_r = self.ps(es, "pbias", [4, 512], F32)
            P.dma('sp', tb_[:], self.w['rel_bias'].ap(), reads=[self.wr], writes=[tb_r])
            P.dma('sp', oh[:], self.c['oh'].ap().rearrange("g b i -> b g i"), reads=[self.wr], writes=[oh_r])
            P.dma('sp', bm[:], self.c['bm'].ap(), reads=[self.wr], writes=[bm_r])
            for g in range(3):
                P.op('pe', lambda g=g: nc.tensor.matmul(pp[:], tb_[:, g * 4:(g + 1) * 4], oh[:, g, :], start=True, stop=True),
                     reads=[tb_r, oh_r], writes=[pp_r])
                P.op('act', lambda: nc.scalar.activation(out=ee[:], in_=pp[:], func=AF.Exp), reads=[pp_r], writes=[ee_r])
                P.op('dve', lambda: nc.vector.tensor_tensor(ee[:], ee[:], bm[:], ALU.mult), reads=[ee_r, bm_r], writes=[ee_r])
                P.dma('sp', self.rv_t.ap()[g * 4:(g + 1) * 4, :], ee[:], reads=[ee_r], writes=[self.rv_r], conc=True)
            ebr, ebr_r = self.sb(es, "ebr", [128, 12, 256], F32)
            jrev, jrev_r = self.sb(es, "jrev", [128, 128], F32)
            pj, pj_r = self.ps(es, "pj", [128, 512], F32)
            P.dma('sp', jrev[:], self.c['jrev'].ap(), reads=[self.wr], writes=[jrev_r])
            for h in range(12):
                srcA = bass.AP(self.rv_t, h * 512 + 192, [[1, 128], [1, 128]])
                srcB = bass.AP(self.rv_t, h * 512 + 64, [[1, 128], [1, 128]])
                P.dma('sp', ebr[:, h, 0:128], srcA, reads=[self.rv_r], writes=[ebr_r], conc=True)
                P.dma('sp', ebr[:, h, 128:256], srcB, reads=[self.rv_r], writes=[ebr_r], conc=True)
            for hp in range(6):
                P.op('pe', lambda hp=hp: nc.tensor.matmul(pj[:], jrev[:], ebr[:, 2 * hp:2 * hp + 2, :].rearrange("p h e -> p (h e)"),
                                                          start=True, stop=True),
                     reads=[jrev_r, ebr_r], writes=[pj_r])
                P.op('act', lambda hp=hp: nc.scalar.copy(out=self.EB[:, 2 * hp:2 * hp + 2, :].rearrange("p h e -> p (h e)"),
                                                         in_=pj[:]),
                     reads=[pj_r], writes=[self.EB_r], conc=True)
            P.barrier()

    def mix_proj(self, l, src):
        nc, P = self.nc, self.P
        src_t, src_r = src
        w_in = self.w['w_in'].ap()
        HP = SEQ + 2
        with ExitStack() as es:
            ident, ident_r = self.load_const_tiles(es)
            gbc = self.sb(es, "gbc", [128, D_MODEL], F32)
            P.dma('sp', gbc[0][:], bcast_rows(self.w['mix_norm'], l * D_MODEL, D_MODEL), reads=[self.wr], writes=[gbc[1]])
            hT, hT_r = self.hT, self.hT_r
            P.op('dve', lambda: nc.vector.memset(hT[:, :, 0:1], 0.0), writes=[hT_r], conc=True)
            P.op('dve', lambda: nc.vector.memset(hT[:, :, HP - 1:HP], 0.0), writes=[hT_r], conc=True)
            with ExitStack() as es2:
                xs, _ = self.sb(es2, "xs", [128, 2, D_MODEL], F32)
                xs_rs = [P.reg("xsm%d" % i) for i in range(2)]
                junk, junk_r = self.sb(es2, "junk", [128, D_MODEL], F32)
                ss, _ = self.sb(es2, "ss", [128, NTB], F32)
                ss_r = [P.reg("ssm%d" % i) for i in range(NTB)]
                rs, _ = self.sb(es2, "rs", [128, NTB], F32)
                rs_r = [P.reg("rsm%d" % i) for i in range(NTB)]
                hb, hb_r, pst, pst_r = [], [], [], []
                for i in range(2):
                    a, b = self.sb(es2, "hb", [128, D_MODEL], BF16)
                    hb.append(a)
                    hb_r.append(b)
                    a, b = self.ps(es2, "pst", [128, 8, 128], BF16)
                    pst.append(a)
                    pst_r.append(b)
                tiles = (xs, xs_rs, junk, junk_r, ss, ss_r, rs, rs_r, hb, hb_r, pst, pst_r, ident, ident_r)
                self.norm_to_hT(es2, src_t, src_r, 0, NTB, gbc, hT, hT_r, 1, tiles)
                P.barrier()
            pA, pA_r, pS, pS_r = [], [], [], []
            for i in range(2):
                a, b = self.ps(es, "pA", [128, 512], F32)
                pA.append(a)
                pA_r.append(b)
                a, b = self.ps(es, "pS", [128, 512], F32)
                pS.append(a)
                pS_r.append(b)
            with ExitStack() as es2:
                wst, wst_r, wbf, wbf_r = [], [], [], []
                for i in range(2):
                    a, b = self.sb(es2, "wstq", [128, 8, 128], F32)
                    wst.append(a)
                    wst_r.append(b)
                    a, b = self.sb(es2, "wbfq", [128, 8, 128], BF16)
                    wbf.append(a)
                    wbf_r.append(b)
                onesb, onesb_r = self.sb(es2, "onesb", [128, 128], F32)
                P.dma('sp', onesb[:], self.c['ones_blk'].ap(), reads=[self.wr], writes=[onesb_r])
                gq, gq_r = self.sb(es2, "gq", [128, 1], F32)
                gk, gk_r = self.sb(es2, "gk", [128, 1], F32)
                for hlf in range(2):
                    P.dma('sp', gq[hlf * 64:(hlf + 1) * 64, :], bass.AP(self.w['q_norm'], l * 64, [[1, 64], [1, 1]]),
                          reads=[self.wr], writes=[gq_r], conc=True)
                    P.dma('sp', gk[hlf * 64:(hlf + 1) * 64, :], bass.AP(self.w['k_norm'], l * 64, [[1, 64], [1, 1]]),
                          reads=[self.wr], writes=[gk_r], conc=True)
                P.op('dve', lambda: nc.vector.tensor_scalar(gq[:], gq[:], 0.125, None, ALU.mult), reads=[gq_r], writes=[gq_r])
                bgt, bgt_r = self.sb(es2, "bgt", [128, 16], F32)
                P.dma('sp', bgt[:], bass.AP(self.w['b_gate'], l * 2048, [[1, 128], [128, 16]]), reads=[self.wr], writes=[bgt_r],
                      allow_slow_non_contiguous=True)
                qf, qf_r, sq, sq_r, rr, rr_r, qn, qn_r, gs, gs_r = [], [], [], [], [], [], [], [], [], []
                for i in range(2):
                    for lst, lstr, nm, dt in ((qf, qf_r, "qf", F32), (sq, sq_r, "sq", F32), (rr, rr_r, "rr", F32),
                                              (qn, qn_r, "qn", BF16), (gs, gs_r, "gs", F32)):
                        a, b = self.sb(es2, nm, [128, 512], dt)
                        lst.append(a)
                        lstr.append(b)
                it = 0
                qpend = [None]
                for c in range(12):
                    wb = c % 2
                    col0 = 1536 + c * 128
                    P.dma('sp', wst[wb][:], w_in[l, :, col0:col0 + 128].rearrange("(kc p) m -> p kc m", p=128),
                          reads=[self.wr], writes=[wst_r[wb]])
                    P.op('pool', lambda wb=wb: nc.gpsimd.tensor_copy(out=wbf[wb][:], in_=wst[wb][:]),
                         reads=[wst_r[wb]], writes=[wbf_r[wb]])
                    gain, gain_r = (gq, gq_r) if c < 6 else (gk, gk_r)
                    for tq in range(8):
                        pb = it % 2
                        it += 1
                        for kc in range(8):
                            P.op('pe', lambda kc=kc, tq=tq, pb=pb, wb=wb: nc.tensor.matmul(
                                pA[pb][:], wbf[wb][:, kc, :], hT[:, kc, 1 + tq * 512:1 + (tq + 1) * 512],
                                start=(kc == 0), stop=(kc == 7)),
                                reads=[wbf_r[wb], hT_r], writes=[pA_r[pb]])
                        P.op('act', lambda pb=pb: nc.scalar.copy(out=qf[pb][:], in_=pA[pb][:]), reads=[pA_r[pb]], writes=[qf_r[pb]])
                        P.op('dve', lambda pb=pb: nc.vector.tensor_tensor(sq[pb][:], qf[pb][:], qf[pb][:], ALU.mult),
                             reads=[qf_r[pb]], writes=[sq_r[pb]])
                        if qpend[0] is not None:
                            qpend[0]()

                        def qback(pb=pb, gain=gain, gain_r=gain_r, c=c, tq=tq):
                            P.op('pe', lambda: nc.tensor.matmul(pS[pb][:], onesb[:], sq[pb][:], start=True, stop=True),
                                 reads=[onesb_r, sq_r[pb]], writes=[pS_r[pb]])
                            P.op('act', lambda: nc.scalar.activation(out=rr[pb][:], in_=pS[pb][:], func=AF.Ln,
                                                                     scale=1.0 / 64, bias=self.eps_t[:]),
                                 reads=[pS_r[pb], self.eps_r], writes=[rr_r[pb]])
                            P.op('act', lambda: nc.scalar.activation(out=rr[pb][:], in_=rr[pb][:], func=AF.Exp, scale=-0.5),
                                 reads=[rr_r[pb]], writes=[rr_r[pb]])
                            P.op('dve', lambda: nc.vector.scalar_tensor_tensor(
                                qn[pb][:], qf[pb][:], gain[:], rr[pb][:], ALU.mult, ALU.mult),
                                reads=[qf_r[pb], gain_r, rr_r[pb]], writes=[qn_r[pb]])
                            P.dma('sp', self.qk_t.ap()[c, :, tq * 512:(tq + 1) * 512], qn[pb][:], reads=[qn_r[pb]],
                                  writes=[self.qk_r], conc=True)
                        qpend[0] = qback
                if qpend[0] is not None:
                    qpend[0]()
                w_gate = self.w['w_gate'].ap()
                for c in range(16):
                    wb = c % 2
                    P.dma('sp', wst[wb][:], w_gate[l, :, c * 128:(c + 1) * 128].rearrange("(kc p) m -> p kc m", p=128),
                          reads=[self.wr], writes=[wst_r[wb]])
                    P.op('pool', lambda wb=wb: nc.gpsimd.tensor_copy(out=wbf[wb][:], in_=wst[wb][:]),
                         reads=[wst_r[wb]], writes=[wbf_r[wb]])
                    for tq in range(8):
                        pb = it % 2
                        it += 1
                        for kc in range(8):
                            P.op('pe', lambda kc=kc, tq=tq, pb=pb, wb=wb: nc.tensor.matmul(
                                pA[pb][:], wbf[wb][:, kc, :], hT[:, kc, 1 + tq * 512:1 + (tq + 1) * 512],
                                start=(kc == 0), stop=(kc == 7)),
                                reads=[wbf_r[wb], hT_r], writes=[pA_r[pb]])
                        P.op('act', lambda pb=pb, c=c: nc.scalar.activation(out=gs[pb][:], in_=pA[pb][:], func=AF.Sigmoid,
                                                                            bias=bgt[:, c:c + 1]),
                             reads=[pA_r[pb], bgt_r], writes=[gs_r[pb]])
                        P.dma('sp', self.gt_t.ap()[c, :, tq * 512:(tq + 1) * 512], gs[pb][:], reads=[gs_r[pb]],
                              writes=[self.gt_r], conc=True)
                P.barrier()
            with ExitStack() as es2:
                wsv, wsv_r = self.sb(es2, "wsv", [128, 8, 256], F32)
                wbv, wbv_r = self.sb(es2, "wbv", [128, 8, 256], BF16)
                zt, zt_r = self.sb(es2, "zt", [64, 512], BF16)
                P.op('dve', lambda: nc.vector.memset(zt[:], 0.0), writes=[zt_r])
                vst, vst_r = [], []
                for i in range(2):
                    a, b = self.sb(es2, "vst", [128, 4, 128], BF16)
                    vst.append(a)
                    vst_r.append(b)
                    P.op('dve', lambda a=a: nc.vector.memset(a[:, :, 64:128], 1.0), writes=[b])
                it = 0
                for g in range(3):
                    D = DILS[g]
                    m = SEQ // D
                    vd_t, vd_r = self.vd_t[g]
                    col0 = 3072 + g * 256
                    P.dma('sp', wsv[:], w_in[l, :, col0:col0 + 256].rearrange("(kc p) m -> p kc m", p=128),
                          reads=[self.wr], writes=[wsv_r])
                    P.op('pool', lambda: nc.gpsimd.tensor_copy(out=wbv[:], in_=wsv[:]), reads=[wsv_r], writes=[wbv_r])
                    for r in range(D):
                        P.dma('sp', vd_t.ap()[r, 0:64, :, :].rearrange("t h e -> t (h e)"), zt[:], reads=[zt_r], writes=[vd_r],
                              conc=True)
                        P.dma('sp', vd_t.ap()[r, m + 64:m + 128, :, :].rearrange("t h e -> t (h e)"), zt[:], reads=[zt_r],
                              writes=[vd_r], conc=True)
                        for b in range(m // 128):
                            pb = it % 2
                            it += 1
                            t0 = 1 + r + D * 128 * b
                            for kc in range(8):
                                P.op('pe', lambda kc=kc, t0=t0, D=D, pb=pb: nc.tensor.matmul(
                                    pA[pb][:, 0:256], hT[:, kc, sl(t0, 128, D)], wbv[:, kc, :], start=(kc == 0), stop=(kc == 7)),
                                    reads=[hT_r, wbv_r], writes=[pA_r[pb]])
                            P.op('act', lambda pb=pb: nc.scalar.copy(
                                out=vst[pb][:, :, 0:64], in_=pA[pb][:, 0:256].rearrange("p (h e) -> p h e", h=4)),
                                reads=[pA_r[pb]], writes=[vst_r[pb]])
                            P.dma('sp', vd_t.ap()[r, 64 + 128 * b:64 + 128 * (b + 1), :, :], vst[pb][:], reads=[vst_r[pb]],
                                  writes=[vd_r], conc=True)
                P.barrier()

    def mix_filters(self, l):
        nc, P = self.nc, self.P
        with ExitStack() as es:
            es_h = ExitStack()
            h2, h2_r = self.sb(es, "h2", [HID, SEQ], F32)
            w3, w3_r = self.sb(es, "w3", [HID, 2048], F32)
            ph, ph_r = [], []
            for i in range(2):
                a, b = self.ps(es, "ph", [128, 512], F32)
                ph.append(a)
                ph_r.append(b)
            zT, zT_r = self.sb(es_h, "zT", [HY_EMB, SEQ], F32)
            P.dma('sp', zT[:], self.c['zfeat'].ap(), reads=[self.wr], writes=[zT_r])
            w1, w1_r = self.sb(es_h, "w1", [HY_EMB, HID], F32)
            w2, w2_r = self.sb(es_h, "w2", [HID, HID], F32)
            P.dma('sp', w1[:], self.w['hy_filt_w1'].ap()[l], reads=[self.wr], writes=[w1_r])
            P.dma('sp', w2[:], self.w['hy_filt_w2'].ap()[l], reads=[self.wr], writes=[w2_r])
            P.dma('sp', w3[:], self.w['hy_filt_w3'].ap()[l], reads=[self.wr], writes=[w3_r])
            bq, bq_r = self.sb(es_h, "bq", [HID, 4], F32)
            P.dma('sp', bq[:, 0:1], bass.AP(self.w['hy_filt_b1'], l * 64, [[1, 64], [1, 1]]), reads=[self.wr], writes=[bq_r],
                  conc=True)
            P.dma('sp', bq[:, 2:3], bass.AP(self.w['hy_filt_b2'], l * 64, [[1, 64], [1, 1]]), reads=[self.wr], writes=[bq_r],
                  conc=True)
            for c in (0, 2):
                P.op('dve', lambda c=c: nc.vector.tensor_scalar(bq[:, c:c + 1], bq[:, c:c + 1], 0.25, None, ALU.mult),
                     reads=[bq_r], writes=[bq_r])
                P.op('dve', lambda c=c: nc.vector.tensor_scalar(bq[:, c + 1:c + 2], bq[:, c:c + 1], math.pi / 2, None, ALU.add),
                     reads=[bq_r], writes=[bq_r])
            h1, h1_r = self.sb(es_h, "h1", [HID, SEQ], F32)
            s4, s4_r = self.sb(es_h, "s4", [HID, 512], F32)
            c4, c4_r = self.sb(es_h, "c4", [HID, 512], F32)
            tt, tt_r = self.sb(es_h, "tt", [HID, 512], F32)
            for layer in range(2):
                wt, wt_r, K_, src, src_r, dstt, dst_r = ((w1, w1_r, HY_EMB, zT, zT_r, h1, h1_r) if layer == 0 else
                                                        (w2, w2_r, HID, h1, h1_r, h2, h2_r))
                bc = 2 * layer
                for tq in range(8):
                    pb = tq % 2
                    P.op('pe', lambda tq=tq, pb=pb, wt=wt, src=src, K_=K_: nc.tensor.matmul(
                        ph[pb][0:HID, :], wt[0:K_, :], src[0:K_, tq * 512:(tq + 1) * 512], start=True, stop=True),
                        reads=[wt_r, src_r], writes=[ph_r[pb]])
                    P.op('act', lambda pb=pb, bc=bc: nc.scalar.activation(out=s4[:], in_=ph[pb][0:HID, :], func=AF.Sin, scale=0.25,
                                                                          bias=bq[:, bc:bc + 1]),
                         reads=[ph_r[pb], bq_r], writes=[s4_r])
                    P.op('act', lambda pb=pb, bc=bc: nc.scalar.activation(out=c4[:], in_=ph[pb][0:HID, :], func=AF.Sin, scale=0.25,
                                                                          bias=bq[:, bc + 1:bc + 2]),
                         reads=[ph_r[pb], bq_r], writes=[c4_r])
                    P.op('dve', lambda: nc.vector.tensor_tensor(tt[:], s4[:], c4[:], ALU.mult), reads=[s4_r, c4_r], writes=[tt_r])
                    P.op('dve', lambda: nc.vector.tensor_tensor(c4[:], s4[:], s4[:], ALU.mult), reads=[s4_r], writes=[c4_r])
                    P.op('dve', lambda: nc.vector.tensor_scalar(c4[:], c4[:], -8.0, 4.0, ALU.mult, ALU.add), reads=[c4_r],
                         writes=[c4_r])
                    P.op('dve', lambda tq=tq, dstt=dstt: nc.vector.tensor_tensor(dstt[:, tq * 512:(tq + 1) * 512], tt[:], c4[:],
                                                                                ALU.mult),
                         reads=[tt_r, c4_r], writes=[dst_r], conc=True)
            P.barrier()
            es_h.close()
            onesa, onesa_r = self.sb(es, "onesa", [128, 128], F32)
            P.dma('sp', onesa[:], self.c['ones_all'].ap(), reads=[self.wr], writes=[onesa_r])
            a_t, a_r = self.sb(es, "a_t", [128, NTB, 512], BF16)
            b_t, b_r = self.sb(es, "b_t", [128, NTB, 512], BF16)
            pq, pq_r = self.ps(es, "pq", [128, 512], F32)
            pk, pk_r = [], []
            for i in range(2):
                a, b = self.ps(es, "pk", [128, 512], F32)
                pk.append(a)
                pk_r.append(b)
            dfb, dfb_r, dbb, dbb_r, kfb, kfb_r, kbb, kbb_r, sqf, sqf_r, sqb, sqb_r = ([] for _ in range(12))
            for i in range(2):
                for lst, lstr, nm in ((dfb, dfb_r, "dfb"), (dbb, dbb_r, "dbb"), (kfb, kfb_r, "kfb"), (kbb, kbb_r, "kbb"),
                                      (sqf, sqf_r, "sqf"), (sqb, sqb_r, "sqb")):
                    a, b = self.sb(es, nm, [128, 512], F32)
                    lst.append(a)
                    lstr.append(b)
            nrm, nrm_r = self.sb(es, "nrm", [128, 512], F32)
            ft, ft_r = [], []
            for i in range(2):
                a, b = self.sb(es, "ft", [128, NTB, 128], BF16)
                ft.append(a)
                ft_r.append(b)
            ko, ko_r = [], []
            for i in range(2):
                a, b = self.sb(es, "ko", [128, 512], F32)
                ko.append(a)
                ko_r.append(b)
            for o in range(2):
                cf = o * 1024
                cb = o * 1024 + 512
                for tb in range(NTB):
                    pb = tb % 2
                    P.dma('sp', dfb[pb][:], self.c['dec_f'].ap()[tb * 128:(tb + 1) * 128, :], reads=[self.wr], writes=[dfb_r[pb]])
                    P.dma('sp', dbb[pb][:], self.c['dec_b'].ap()[tb * 128:(tb + 1) * 128, :], reads=[self.wr], writes=[dbb_r[pb]])
                    P.op('pe', lambda tb=tb, pb=pb, cf=cf: nc.tensor.matmul(ph[pb][:], h2[:, tb * 128:(tb + 1) * 128],
                                                                           w3[:, cf:cf + 512], start=True, stop=True),
                         reads=[h2_r, w3_r], writes=[ph_r[pb]])
                    P.op('pe', lambda tb=tb, pb=pb, cb=cb: nc.tensor.matmul(pk[pb][:], h2[:, tb * 128:(tb + 1) * 128],
                                                                           w3[:, cb:cb + 512], start=True, stop=True),
                         reads=[h2_r, w3_r], writes=[pk_r[pb]])
                    P.op('dve', lambda pb=pb: nc.vector.tensor_tensor(kfb[pb][:], ph[pb][:], dfb[pb][:], ALU.mult),
                         reads=[ph_r[pb], dfb_r[pb]], writes=[kfb_r[pb]])
                    P.op('dve', lambda pb=pb: nc.vector.tensor_tensor(kbb[pb][:], pk[pb][:], dbb[pb][:], ALU.mult),
                         reads=[pk_r[pb], dbb_r[pb]], writes=[kbb_r[pb]])
                    P.op('act', lambda pb=pb: nc.scalar.activation(out=sqf[pb][:], in_=kfb[pb][:], func=AF.Square),
                         reads=[kfb_r[pb]], writes=[sqf_r[pb]])
                    P.op('act', lambda pb=pb: nc.scalar.activation(out=sqb[pb][:], in_=kbb[pb][:], func=AF.Square),
                         reads=[kbb_r[pb]], writes=[sqb_r[pb]])
                    P.op('pe', lambda pb=pb, tb=tb: nc.tensor.matmul(pq[:], onesa[:], sqf[pb][:], start=(tb == 0), stop=False),
                         reads=[onesa_r, sqf_r[pb]], writes=[pq_r])
                    P.op('pe', lambda pb=pb, tb=tb: nc.tensor.matmul(pq[:], onesa[:], sqb[pb][:], start=False,
                                                                     stop=(tb == NTB - 1)),
                         reads=[onesa_r, sqb_r[pb]], writes=[pq_r])
                    P.op('pool', lambda pb=pb, tb=tb: nc.gpsimd.tensor_tensor(a_t[:, tb, :], kfb[pb][:], kbb[pb][:], ALU.add),
                         reads=[kfb_r[pb], kbb_r[pb]], writes=[a_r], conc=True)
                    P.op('pool', lambda pb=pb, tb=tb: nc.gpsimd.tensor_tensor(b_t[:, tb, :], kfb[pb][:], kbb[pb][:], ALU.subtract),
                         reads=[kfb_r[pb], kbb_r[pb]], writes=[b_r], conc=True)
                P.op('act', lambda: nc.scalar.activation(out=nrm[:], in_=pq[:], func=AF.Ln, bias=self.eps_t[:]),
                     reads=[pq_r, self.eps_r], writes=[nrm_r])
                P.op('act', lambda: nc.scalar.activation(out=nrm[:], in_=nrm[:], func=AF.Exp, scale=-0.5), reads=[nrm_r],
                     writes=[nrm_r])
                for s in range(64):
                    pb = s % 2
                    P.dma('sp', ft[pb][:], self.c['Fh'].ap()[s], reads=[self.wr], writes=[ft_r[pb]])
                    srct, srcr = (a_t, a_r) if s < 32 else (b_t, b_r)
                    for nb in range(NTB):
                        P.op('pe', lambda nb=nb, pb=pb, srct=srct: nc.tensor.matmul(pk[pb][:], ft[pb][:, nb, :], srct[:, nb, :],
                                                                                  start=(nb == 0), stop=(nb == NTB - 1)),
                             reads=[ft_r[pb], srcr], writes=[pk_r[pb]])
                    P.op('dve', lambda pb=pb: nc.vector.tensor_tensor(ko[pb][:], pk[pb][:], nrm[:], ALU.mult),
                         reads=[pk_r[pb], nrm_r], writes=[ko_r[pb]])
                    P.dma('sp', self.kf_t.ap()[o, s], ko[pb][:], reads=[ko_r[pb]], writes=[self.kf_r], conc=True)
            P.barrier()

    def mix_conv(self, l):
        nc, P = self.nc, self.P
        with ExitStack() as es:
            Y, Y_r = self.sb(es, "Y", [128, 64, 512], BF16)
            ftr, ftr_r, fti, fti_r, kre, kre_r, kim, kim_r = ([] for _ in range(8))
            pre, pre_r, pim, pim_r = [], [], [], []
            for i in range(2):
                for lst, lstr, nm in ((ftr, ftr_r, "ftr"), (fti, fti_r, "fti")):
                    a, b = self.sb(es, nm, [128, NTB, 128], BF16)
                    lst.append(a)
                    lstr.append(b)
                for lst, lstr, nm in ((kre, kre_r, "kre"), (kim, kim_r, "kim")):
                    a, b = self.sb(es, nm, [128, 512], F32)
                    lst.append(a)
                    lstr.append(b)
                a, b = self.ps(es, "pre", [128, 512], F32)
                pre.append(a)
                pre_r.append(b)
                a, b = self.ps(es, "pim", [128, 512], F32)
                pim.append(a)
                pim_r.append(b)
            t1, t1_r = self.sb(es, "t1", [128, 512], F32)
            t2, t2_r = self.sb(es, "t2", [128, 512], F32)
            gt, gt_r = [], []
            for i in range(2):
                a, b = self.sb(es, "gtile", [128, 64, 128], BF16)
                gt.append(a)
                gt_r.append(b)
            pc, pc_r, gate, gate_r, zo, zo_r, zn, zn_r = ([] for _ in range(8))
            for i in range(2):
                a, b = self.ps(es, "pc", [128, 512], F32)
                pc.append(a)
                pc_r.append(b)
                for lst, lstr, nm in ((gate, gate_r, "gate"), (zo, zo_r, "zo"), (zn, zn_r, "zn")):
                    a, b = self.sb(es, nm, [128, 512], F32)
                    lst.append(a)
                    lstr.append(b)
            dbc, dbc_r = self.sb(es, "dbc", [128, 512], F32)
            for o in range(2):
                P.dma('sp', dbc[:], bcast_rows(self.w['hy_skip'], (l * 2 + o) * 512, 512), reads=[self.wr], writes=[dbc_r])
                for j in range(32):
                    pb = j % 2
                    P.dma('sp', ftr[pb][:], self.c['Fh'].ap()[j], reads=[self.wr], writes=[ftr_r[pb]])
                    P.dma('sp', fti[pb][:], self.c['Fh'].ap()[32 + j], reads=[self.wr], writes=[fti_r[pb]])
                    P.dma('sp', kre[pb][:], self.kf_t.ap()[o, j], reads=[self.kf_r], writes=[kre_r[pb]])
                    P.dma('sp', kim[pb][:], self.kf_t.ap()[o, 32 + j], reads=[self.kf_r], writes=[kim_r[pb]])
                    for nb in range(NTB):
                        P.op('pe', lambda nb=nb, pb=pb: nc.tensor.matmul(pre[pb][:], ftr[pb][:, nb, :], u[:, nb, :],
                                                                        start=(nb == 0), stop=(nb == NTB - 1)),
                             reads=[ftr_r[pb], u_r], writes=[pre_r[pb]])
                    for nb in range(NTB):
                        P.op('pe', lambda nb=nb, pb=pb: nc.tensor.matmul(pim[pb][:], fti[pb][:, nb, :], u[:, nb, :],
                                                                        start=(nb == 0), stop=(nb == NTB - 1)),
                             reads=[fti_r[pb], u_r], writes=[pim_r[pb]])
                    P.op('dve', lambda pb=pb: nc.vector.tensor_tensor(t1[:], pre[pb][:], kre[pb][:], ALU.mult),
                         reads=[pre_r[pb], kre_r[pb]], writes=[t1_r])
                    P.op('dve', lambda pb=pb: nc.vector.tensor_tensor(t2[:], pim[pb][:], kim[pb][:], ALU.mult),
                         reads=[pim_r[pb], kim_r[pb]], writes=[t2_r])
                    P.op('dve', lambda j=j: nc.vector.tensor_tensor(Y[:, j, :], t1[:], t2[:], ALU.subtract),
                         reads=[t1_r, t2_r], writes=[Y_r], conc=True)
                    P.op('dve', lambda pb=pb: nc.vector.tensor_tensor(t1[:], pre[pb][:], kim[pb][:], ALU.mult),
                         reads=[pre_r[pb], kim_r[pb]], writes=[t1_r])
                    P.op('dve', lambda pb=pb: nc.vector.tensor_tensor(t2[:], pim[pb][:], kre[pb][:], ALU.mult),
                         reads=[pim_r[pb], kre_r[pb]], writes=[t2_r])
                    P.op('dve', lambda j=j: nc.vector.tensor_tensor(Y[:, 32 + j, :], t1[:], t2[:], ALU.add),
                         reads=[t1_r, t2_r], writes=[Y_r], conc=True)
                for nb in range(NTB):
                    pb = nb % 2
                    P.dma('sp', gt[pb][:], self.c['Gh'].ap()[nb], reads=[self.wr], writes=[gt_r[pb]])
                    rows = slice(nb * 128, (nb + 1) * 128)
                    P.dma('sp', gate[pb][:], self.hy_t.ap()[rows, 512 * (1 + o):512 * (2 + o)], reads=[self.hy_r],
                          writes=[gate_r[pb]])
                    if o == 0:
                        P.dma('sp', zo[pb][:], self.hy_t.ap()[rows, 0:512], reads=[self.hy_r], writes=[zo_r[pb]])
                    else:
                        P.dma('sp', zo[pb][:], self.z1_t.ap()[rows, :], reads=[self.z1_r], writes=[zo_r[pb]])
                    for s in range(64):
                        P.op('pe', lambda s=s, pb=pb: nc.tensor.matmul(pc[pb][:], gt[pb][:, s, :], Y[:, s, :], start=(s == 0),
                                                                      stop=(s == 63)),
                             reads=[gt_r[pb], Y_r], writes=[pc_r[pb]])
                    P.op('dve', lambda pb=pb: nc.vector.tensor_tensor(zo[pb][:], zo[pb][:], dbc[:], ALU.mult),
                         reads=[zo_r[pb], dbc_r], writes=[zo_r[pb]])
                    P.op('dve', lambda pb=pb: nc.vector.tensor_tensor(zo[pb][:], pc[pb][:], zo[pb][:], ALU.add),
                         reads=[pc_r[pb], zo_r[pb]], writes=[zo_r[pb]])
                    P.op('dve', lambda pb=pb: nc.vector.tensor_tensor(zn[pb][:], zo[pb][:], gate[pb][:], ALU.mult),
                         reads=[zo_r[pb], gate_r[pb]], writes=[zn_r[pb]])
                    if o == 0 or self.dbg:
                        dstt, dstr = (self.z1_t, self.z1_r) if o == 0 else (self.z2_t, self.z2_r)
                        P.dma('sp', dstt.ap()[rows, :], zn[pb][:], reads=[zn_r[pb]], writes=[dstr], conc=True)
                    P.op('act', lambda pb=pb, nb=nb: nc.scalar.copy(out=u[:, nb, :], in_=zn[pb][:]), reads=[zn_r[pb]],
                         writes=[u_r], conc=True)
            P.barrier()

    def ct_consts(self, es):
        nc, P = self.nc, self.P
        C = {}
        for name, shp, dt in (('f1cat', [128, 256], BF16), ('g1re', [128, 128], BF16), ('g1imn', [128, 128], BF16),
                              ('tf_re', [128, 256], F32), ('tf_im', [128, 256], F32), ('tc_re', [128, 512], F32),
                              ('tc_im', [128, 512], F32), ('bd_ere', [128, 128], BF16), ('bd_eren', [128, 128], BF16),
                              ('bd_eim', [128, 128], BF16), ('bd_eimn', [128, 128], BF16)):
            t, r = self.sb(es, name, shp, dt)
            P.dma('sp', t[:], self.c[name].ap(), reads=[self.wr], writes=[r])
            C[name] = (t, r)
        C['tmp'] = [self.sb(es, "cttmp", [128, 512], F32) for _ in range(8)]
        C['tmpi'] = 0
        C['ps1'] = [self.ps(es, "ctps1", [128, 512], F32) for _ in range(2)]
        C['psx'] = [self.ps(es, "ctpsx", [128, 512], F32) for _ in range(4)]
        return C

    def ct_s1_twiddle(self, C, src, src_r, A, A_r):
        nc, P = self.nc, self.P
        f1, f1_r = C['f1cat']
        tfr, tfr_r = C['tf_re']
        tfi, tfi_r = C['tf_im']
        for q in range(8):
            ps, ps_r = C['ps1'][q % 2]
            for h in range(2):
                cg = 2 * q + h
                P.op('pe', lambda cg=cg, h=h, ps=ps: nc.tensor.matmul(
                    ps[:, h * 256:(h + 1) * 256], src[:, 4 * cg:4 * cg + 4, :].rearrange("p c r -> p (c r)"), f1[:],
                    start=True, stop=True), reads=[src_r, f1_r], writes=[ps_r])
            pv = ps[:].rearrange("p (g h k) -> p g h k", g=2, h=2)
            ts_ = (C['tmpi'] % 2) * 4
            C['tmpi'] += 1
            (t1, t1_r), (t2, t2_r), (t3, t3_r), (t4, t4_r) = C['tmp'][ts_:ts_ + 4]
            v3 = lambda t: t[:, 0:256].rearrange("p (g k) -> p g k", g=2)
            tf3 = lambda t: t[:].rearrange("p (g k) -> p g k", g=2)
            P.op('dve', lambda pv=pv, t1=t1: nc.vector.tensor_tensor(v3(t1), pv[:, :, 0, :], tf3(tfr), ALU.mult),
                 reads=[ps_r, tfr_r], writes=[t1_r])
            P.op('dve', lambda pv=pv, t2=t2: nc.vector.tensor_tensor(v3(t2), pv[:, :, 1, :], tf3(tfi), ALU.mult),
                 reads=[ps_r, tfi_r], writes=[t2_r])
            P.op('dve', lambda pv=pv, t3=t3: nc.vector.tensor_tensor(v3(t3), pv[:, :, 0, :], tf3(tfi), ALU.mult),
                 reads=[ps_r, tfi_r], writes=[t3_r])
            P.op('dve', lambda pv=pv, t4=t4: nc.vector.tensor_tensor(v3(t4), pv[:, :, 1, :], tf3(tfr), ALU.mult),
                 reads=[ps_r, tfr_r], writes=[t4_r])
            P.op('pool', lambda q=q, t1=t1, t2=t2: nc.gpsimd.tensor_tensor(A[:, 2 * q:2 * q + 2, 0, :], v3(t1), v3(t2), ALU.subtract),
                 reads=[t1_r, t2_r], writes=[A_r], conc=True)
            P.op('pool', lambda q=q, t3=t3, t4=t4: nc.gpsimd.tensor_tensor(A[:, 2 * q:2 * q + 2, 1, :], v3(t3), v3(t4), ALU.add),
                 reads=[t3_r, t4_r], writes=[A_r], conc=True)

    def mix_filters2(self, l):
        nc, P = self.nc, self.P
        with ExitStack() as es:
            es_h = ExitStack()
            h2, h2_r = self.sb(es, "h2", [HID, SEQ], F32)
            w3, w3_r = self.sb(es, "w3", [HID, 2048], F32)
            ph, ph_r = [], []
            for i in range(2):
                a, b = self.ps(es, "ph", [128, 512], F32)
                ph.append(a)
                ph_r.append(b)
            zT, zT_r = self.sb(es_h, "zT", [HY_EMB, SEQ], F32)
            P.dma('sp', zT[:], self.c['zfeat'].ap(), reads=[self.wr], writes=[zT_r])
            w1, w1_r = self.sb(es_h, "w1", [HY_EMB, HID], F32)
            w2, w2_r = self.sb(es_h, "w2", [HID, HID], F32)
            P.dma('sp', w1[:], self.w['hy_filt_w1'].ap()[l], reads=[self.wr], writes=[w1_r])
            P.dma('sp', w2[:], self.w['hy_filt_w2'].ap()[l], reads=[self.wr], writes=[w2_r])
            P.dma('sp', w3[:], self.w['hy_filt_w3'].ap()[l], reads=[self.wr], writes=[w3_r])
            bq, bq_r = self.sb(es_h, "bq", [HID, 4], F32)
            P.dma('sp', bq[:, 0:1], bass.AP(self.w['hy_filt_b1'], l * 64, [[1, 64], [1, 1]]), reads=[self.wr], writes=[bq_r],
                  conc=True)
            P.dma('sp', bq[:, 2:3], bass.AP(self.w['hy_filt_b2'], l * 64, [[1, 64], [1, 1]]), reads=[self.wr], writes=[bq_r],
                  conc=True)
            for c in (0, 2):
                P.op('dve', lambda c=c: nc.vector.tensor_scalar(bq[:, c:c + 1], bq[:, c:c + 1], 0.25, None, ALU.mult),
                     reads=[bq_r], writes=[bq_r])
                P.op('dve', lambda c=c: nc.vector.tensor_scalar(bq[:, c + 1:c + 2], bq[:, c:c + 1], math.pi / 2, None, ALU.add),
                     reads=[bq_r], writes=[bq_r])
            h1, h1_r = self.sb(es_h, "h1", [HID, SEQ], F32)
            s4, s4_r = self.sb(es_h, "s4", [HID, 512], F32)
            c4, c4_r = self.sb(es_h, "c4", [HID, 512], F32)
            tt, tt_r = self.sb(es_h, "tt", [HID, 512], F32)
            for layer in range(2):
                wt, wt_r, K_, src, src_r, dstt, dst_r = ((w1, w1_r, HY_EMB, zT, zT_r, h1, h1_r) if layer == 0 else
                                                        (w2, w2_r, HID, h1, h1_r, h2, h2_r))
                bc = 2 * layer
                for tq in range(8):
                    pb = tq % 2
                    P.op('pe', lambda tq=tq, pb=pb, wt=wt, src=src, K_=K_: nc.tensor.matmul(
                        ph[pb][0:HID, :], wt[0:K_, :], src[0:K_, tq * 512:(tq + 1) * 512], start=True, stop=True),
                        reads=[wt_r, src_r], writes=[ph_r[pb]])
                    P.op('act', lambda pb=pb, bc=bc: nc.scalar.activation(out=s4[:], in_=ph[pb][0:HID, :], func=AF.Sin, scale=0.25,
                                                                          bias=bq[:, bc:bc + 1]),
                         reads=[ph_r[pb], bq_r], writes=[s4_r])
                    P.op('act', lambda pb=pb, bc=bc: nc.scalar.activation(out=c4[:], in_=ph[pb][0:HID, :], func=AF.Sin, scale=0.25,
                                                                          bias=bq[:, bc + 1:bc + 2]),
                         reads=[ph_r[pb], bq_r], writes=[c4_r])
                    P.op('dve', lambda: nc.vector.tensor_tensor(tt[:], s4[:], c4[:], ALU.mult), reads=[s4_r, c4_r], writes=[tt_r])
                    P.op('dve', lambda: nc.vector.tensor_tensor(c4[:], s4[:], s4[:], ALU.mult), reads=[s4_r], writes=[c4_r])
                    P.op('dve', lambda: nc.vector.tensor_scalar(c4[:], c4[:], -8.0, 4.0, ALU.mult, ALU.add), reads=[c4_r],
                         writes=[c4_r])
                    P.op('dve', lambda tq=tq, dstt=dstt: nc.vector.tensor_tensor(dstt[:, tq * 512:(tq + 1) * 512], tt[:], c4[:],
                                                                                ALU.mult),
                         reads=[tt_r, c4_r], writes=[dst_r], conc=True)
            P.barrier()
            es_h.close()
            onesa, onesa_r = self.sb(es, "onesa", [128, 128], F32)
            P.dma('sp', onesa[:], self.c['ones_all'].ap(), reads=[self.wr], writes=[onesa_r])
            onesh, onesh_r = self.sb(es, "onesh", [128, 128], BF16)
            P.op('dve', lambda: nc.vector.tensor_copy(out=onesh[:], in_=onesa[:]), reads=[onesa_r], writes=[onesh_r])
            nrm, nrm_r = [], []
            for o in range(2):
                a, b = self.sb(es, "nrm", [128, 512], F32)
                nrm.append(a)
                nrm_r.append(b)
            with ExitStack() as es2:
                h2b, h2b_r = self.sb(es2, "h2b", [HID, SEQ], BF16)
                w3b, w3b_r = self.sb(es2, "w3b", [HID, 2048], BF16)
                P.op('dve', lambda: nc.vector.tensor_copy(out=h2b[:], in_=h2[:]), reads=[h2_r], writes=[h2b_r])
                P.op('pool', lambda: nc.gpsimd.tensor_copy(out=w3b[:], in_=w3[:]), reads=[w3_r], writes=[w3b_r])
                pq, pq_r = self.ps(es2, "pq", [128, 512], F32)
                pk, pk_r = [], []
                for i in range(2):
                    a, b = self.ps(es2, "pk", [128, 512], F32)
                    pk.append(a)
                    pk_r.append(b)
                dcr, dcr_r, kk, kk_r, ks, ks_r = [], [], [], [], [], []
                for i in range(2):
                    a, b = self.sb(es2, "dcr", [128, 2, 512], F32)
                    dcr.append(a)
                    dcr_r.append(b)
                    a, b = self.sb(es2, "kk", [128, 2, 512], F32)
                    kk.append(a)
                    kk_r.append(b)
                    a, b = self.sb(es2, "ks", [128, 2, 512], BF16)
                    ks.append(a)
                    ks_r.append(b)
                for o in range(2):
                    for r in range(32):
                        pb = r % 2
                        P.dma('sp', dcr[pb][:], self.c['decr'].ap()[r], reads=[self.wr], writes=[dcr_r[pb]])
                        P.op('pe', lambda r=r, pb=pb, o=o: nc.tensor.matmul(ph[pb][:], h2b[:, sl(r, 128, 32)],
                                                                           w3b[:, o * 1024:o * 1024 + 512], start=True, stop=True),
                             reads=[h2b_r, w3b_r], writes=[ph_r[pb]])
                        P.op('pe', lambda r=r, pb=pb, o=o: nc.tensor.matmul(pk[pb][:], h2b[:, sl(r, 128, 32)],
                                                                           w3b[:, o * 1024 + 512:o * 1024 + 1024], start=True,
                                                                           stop=True),
                             reads=[h2b_r, w3b_r], writes=[pk_r[pb]])
                        P.op('dve', lambda pb=pb: nc.vector.tensor_tensor(kk[pb][:, 0, :], ph[pb][:], dcr[pb][:, 0, :], ALU.mult),
                             reads=[ph_r[pb], dcr_r[pb]], writes=[kk_r[pb]], conc=True)
                        P.op('dve', lambda pb=pb: nc.vector.tensor_tensor(kk[pb][:, 1, :], pk[pb][:], dcr[pb][:, 1, :], ALU.mult),
                             reads=[pk_r[pb], dcr_r[pb]], writes=[kk_r[pb]], conc=True)
                        P.op('act', lambda pb=pb: nc.scalar.activation(out=ks[pb][:], in_=kk[pb][:], func=AF.Square),
                             reads=[kk_r[pb]], writes=[ks_r[pb]])
                        P.op('pe', lambda pb=pb, r=r: nc.tensor.matmul(pq[:], onesh[:], ks[pb][:, 0, :], start=(r == 0), stop=False),
                             reads=[onesh_r, ks_r[pb]], writes=[pq_r])
                        P.op('pe', lambda pb=pb, r=r: nc.tensor.matmul(pq[:], onesh[:], ks[pb][:, 1, :], start=False, stop=(r == 31)),
                             reads=[onesh_r, ks_r[pb]], writes=[pq_r])
                    P.op('act', lambda o=o: nc.scalar.activation(out=nrm[o][:], in_=pq[:], func=AF.Ln, bias=self.eps_t[:]),
                         reads=[pq_r, self.eps_r], writes=[nrm_r[o]])
                    P.op('act', lambda o=o: nc.scalar.activation(out=nrm[o][:], in_=nrm[o][:], func=AF.Exp, scale=-0.5),
                         reads=[nrm_r[o]], writes=[nrm_r[o]])
                P.barrier()
            for o in range(2):
                for dirn in range(2):
                    cs = slice(o * 1024 + dirn * 512, o * 1024 + dirn * 512 + 512)
                    P.op('dve', lambda cs=cs, o=o: nc.vector.tensor_tensor(w3[:, cs], w3[:, cs], nrm[o][0:HID, :], ALU.mult),
                         reads=[w3_r, nrm_r[o]], writes=[w3_r])
            h2c, h2c_r = self.sb(es, "h2c", [HID, SEQ], BF16)
            w3c, w3c_r = self.sb(es, "w3c", [HID, 2048], BF16)
            P.op('dve', lambda: nc.vector.tensor_copy(out=h2c[:], in_=h2[:]), reads=[h2_r], writes=[h2c_r])
            P.op('pool', lambda: nc.gpsimd.tensor_copy(out=w3c[:], in_=w3[:]), reads=[w3_r], writes=[w3c_r])
            C = self.ct_consts(es)
            dec2, dec2_r, ktf, ktf_r, ktb, ktb_r, Af, Af_r, Ab, Ab_r, Kt, Kt_r, db, db_r = ([] for _ in range(14))
            for i in range(2):
                for lst, lstr, nm, shp, dt in ((dec2, dec2_r, "dec2", [128, 32, 2, 64], F32), (ktf, ktf_r, "ktf", [128, 64, 32], BF16),
                                               (ktb, ktb_r, "ktb", [128, 64, 32], BF16), (Af, Af_r, "Af", [128, 16, 2, 128], BF16),
                                               (Ab, Ab_r, "Ab", [128, 16, 2, 128], BF16), (Kt, Kt_r, "Kt", [128, 4, 2, 128], F32),
                                               (db, db_r, "db", [1, 64], F32)):
                    a, b = self.sb(es, nm, shp, dt)
                    lst.append(a)
                    lstr.append(b)
            bde, bde_r = C['bd_ere']
            bden, bden_r = C['bd_eren']
            bdi, bdi_r = C['bd_eim']
            bdin, bdin_r = C['bd_eimn']
            it = 0
            ig = 0
            ipb = 0
            for o in range(2):
                for bt in range(8):
                    b2 = it % 2
                    it += 1
                    c0 = 64 * bt
                    P.dma('sp', dec2[b2][:], self.c['dec2'].ap()[bt], reads=[self.wr], writes=[dec2_r[b2]])
                    P.dma('sp', db[b2][:], bass.AP(self.w['hy_skip'], (l * 2 + o) * 512 + c0, [[1, 1], [1, 64]]), reads=[self.wr],
                          writes=[db_r[b2]])
                    for dirn in range(2):
                        col = o * 1024 + dirn * 512 + c0
                        kt, kt_r = (ktf[b2], ktf_r[b2]) if dirn == 0 else (ktb[b2], ktb_r[b2])
                        for q in range(4):
                            pb = ipb % 2
                            ipb += 1
                            for j in range(8):
                                r = 8 * q + j
                                P.op('pe', lambda r=r, pb=pb, j=j, col=col: nc.tensor.matmul(
                                    ph[pb][:, j * 64:(j + 1) * 64], h2c[:, sl(r, 128, 32)], w3c[:, col:col + 64], start=True, stop=True),
                                    reads=[h2c_r, w3c_r], writes=[ph_r[pb]])
                            P.op('dve', lambda pb=pb, b2=b2, q=q, dirn=dirn, kt=kt: nc.vector.tensor_tensor(
                                kt[:, :, 8 * q:8 * q + 8].rearrange("p c r -> p r c"),
                                ph[pb][:].rearrange("p (r c) -> p r c", r=8), dec2[b2][:, 8 * q:8 * q + 8, dirn, :], ALU.mult),
                                reads=[ph_r[pb], dec2_r[b2]], writes=[kt_r], conc=True)
                    P.op('dve', lambda b2=b2: nc.vector.tensor_tensor(ktf[b2][0:1, :, 0], ktf[b2][0:1, :, 0], db[b2][:], ALU.add),
                         reads=[ktf_r[b2], db_r[b2]], writes=[ktf_r[b2]])
                    self.ct_s1_twiddle(C, ktf[b2], ktf_r[b2], Af[b2], Af_r[b2])
                    self.ct_s1_twiddle(C, ktb[b2], ktb_r[b2], Ab[b2], Ab_r[b2])
                    for g in range(4):
                        k2b = ig % 2
                        ig += 1
                        (pre, pre_r), (pim, pim_r) = C['psx'][2 * k2b], C['psx'][2 * k2b + 1]
                        fr = Af[b2][:, 4 * g:4 * g + 4, 0, :]
                        fi = Af[b2][:, 4 * g:4 * g + 4, 1, :]
                        br = Ab[b2][:, 4 * g:4 * g + 4, 0, :]
                        bi = Ab[b2][:, 4 * g:4 * g + 4, 1, :]
                        seq_re = ((bde, bde_r, fr), (bdin, bdin_r, fi), (bde, bde_r, br), (bdin, bdin_r, bi))
                        seq_im = ((bde, bde_r, fi), (bdi, bdi_r, fr), (bden, bden_r, bi), (bdin, bdin_r, br))
                        for (pp, pp_r, seq) in ((pre, pre_r, seq_re), (pim, pim_r, seq_im)):
                            for n_, (wt_, wt_r_, rhs_) in enumerate(seq):
                                P.op('pe', lambda pp=pp, wt_=wt_, rhs_=rhs_, n_=n_: nc.tensor.matmul(
                                    pp[:], wt_[:], rhs_, start=(n_ == 0), stop=(n_ == 3)),
                                    reads=[wt_r_, Af_r[b2], Ab_r[b2]], writes=[pp_r])
                        P.op('act', lambda pre=pre, k2b=k2b: nc.scalar.copy(
                            out=Kt[k2b][:, :, 0, :], in_=pre[:].rearrange("p (g k) -> p g k", g=4)),
                            reads=[pre_r], writes=[Kt_r[k2b]], conc=True)
                        P.op('act', lambda pim=pim, k2b=k2b: nc.scalar.copy(
                            out=Kt[k2b][:, :, 1, :], in_=pim[:].rearrange("p (g k) -> p g k", g=4)),
                            reads=[pim_r], writes=[Kt_r[k2b]], conc=True)
                        P.dma('sp', self.kf_t.ap()[o, bt * 4 + g], Kt[k2b][:], reads=[Kt_r[k2b]], writes=[self.kf_r], conc=True)
            P.barrier()

    def mix_hyconv(self, l):
        nc, P = self.nc, self.P
        hT, hT_r = self.hT, self.hT_r
        w_in = self.w['w_in'].ap()
        with ExitStack() as es:
            ident, ident_r = self.load_const_tiles(es)
            C = self.ct_consts(es)
            bde, bde_r = C['bd_ere']
            bdi, bdi_r = C['bd_eim']
            bdin, bdin_r = C['bd_eimn']
            g1r, g1r_r = C['g1re']
            g1i, g1i_r = C['g1imn']
            tcr, tcr_r = C['tc_re']
            tci, tci_r = C['tc_im']
            wsth, wsth_r, cwb, cwb_r, bbc, bbc_r, hb3, hb3_r = ([] for _ in range(8))
            for i in range(2):
                a, b = self.sb(es, "wsth", [128, 8, 192], F32)
                wsth.append(a)
                wsth_r.append(b)
                a, b = self.sb(es, "cwb", [128, 3, 192], F32)
                cwb.append(a)
                cwb_r.append(b)
                a, b = self.sb(es, "bbc", [128, 192], F32)
                bbc.append(a)
                bbc_r.append(b)
                a, b = self.sb(es, "hb3", [128, 3, 64, 32], BF16)
                hb3.append(a)
                hb3_r.append(b)
            wj, wj_r = [], []
            for j in range(3):
                a, b = self.sb(es, "wj", [128, 8, 192], BF16)
                wj.append(a)
                wj_r.append(b)
            A, A_r = self.sb(es, "A", [128, 16, 2, 128], BF16)
            Y, Y_r = self.sb(es, "Y", [128, 16, 2, 128], BF16)
            Z, Z_r = self.sb(es, "Z", [128, 2, 2048], BF16)
            Kt, Kt_r = [], []
            for i in range(2):
                a, b = self.sb(es, "Ktl", [128, 4, 2, 128], F32)
                Kt.append(a)
                Kt_r.append(b)
            pp0, pp0_r = self.ps(es, "pproj", [128, 512], F32)
            pproj = [pp0, pp0]
            pproj_r = [pp0_r, pp0_r]
            pstt, pstt_r = self.ps(es, "pstt", [128, 4, 128], BF16)
            yst, yst_r = [], []
            for i in range(2):
                a, b = self.sb(es, "yst", [128, SEQ], BF16)
                yst.append(a)
                yst_r.append(b)
            ctr = {'ik': 0, 'ip': 0}

            def tmps():
                ts_ = (C['tmpi'] % 2) * 4
                C['tmpi'] += 1
                return C['tmp'][ts_:ts_ + 4]
            v4 = lambda t: t[:].rearrange("p (g k) -> p g k", g=4)

            def prep_weights(bt):
                b2 = bt % 2
                c0 = 64 * bt
                for part in range(3):
                    col = part * 512 + c0
                    P.dma('sp', wsth[b2][:, :, part * 64:(part + 1) * 64],
                          w_in[l, :, col:col + 64].rearrange("(kc p) m -> p kc m", p=128), reads=[self.wr], writes=[wsth_r[b2]],
                          conc=True)
                    for j in range(3):
                        P.dma('sp', cwb[b2][:, j, part * 64:(part + 1) * 64],
                              bcast_rows(self.w['hy_conv_w'], (l * 3 + j) * 1536 + col, 64), reads=[self.wr], writes=[cwb_r[b2]],
                              conc=True)
                    P.dma('sp', bbc[b2][:, part * 64:(part + 1) * 64], bcast_rows(self.w['hy_conv_b'], l * 1536 + col, 64),
                          reads=[self.wr], writes=[bbc_r[b2]], conc=True)
                for j in range(3):
                    for kc in range(8):
                        P.op('pool', lambda j=j, kc=kc, b2=b2: nc.gpsimd.tensor_tensor(wj[j][:, kc, :], wsth[b2][:, kc, :],
                                                                                      cwb[b2][:, j, :], ALU.mult),
                             reads=[wsth_r[b2], cwb_r[b2]], writes=[wj_r[j]], conc=True)

            def proj_chunk(bt, q):
                b2 = bt % 2
                H, H_r = hb3[b2], hb3_r[b2]
                for r in range(4 * q, 4 * q + 4):
                    pb = ctr['ip'] % 2
                    ctr['ip'] += 1
                    n = 0
                    for j in range(3):
                        for kc in range(8):
                            t0 = 1 + r + (j - 1)
                            P.op('pe', lambda j=j, kc=kc, t0=t0, pb=pb, n=n: nc.tensor.matmul(
                                pproj[pb][:, 0:192], hT[:, kc, sl(t0, 128, 32)], wj[j][:, kc, :], start=(n == 0), stop=(n == 23)),
                                reads=[hT_r, wj_r[j]], writes=[pproj_r[pb]])
                            n += 1
                    P.op('dve', lambda pb=pb, r=r, H=H, b2=b2: nc.vector.tensor_tensor(
                        H[:, :, :, r], pproj[pb][:, 0:192].rearrange("p (a c) -> p a c", a=3),
                        bbc[b2][:].rearrange("p (a c) -> p a c", a=3), ALU.add),
                        reads=[pproj_r[pb], bbc_r[b2]], writes=[H_r], conc=True)

            def conv_stage(bt, o, k):
                b2 = bt % 2
                H, H_r = hb3[b2], hb3_r[b2]
                if k == 0:
                    self.ct_s1_twiddle(C, H[:, 0, :, :], H_r, A, A_r)
                elif k == 1:
                    for g in range(4):
                        k2b = ctr['ik'] % 2
                        ctr['ik'] += 1
                        (pre, pre_r), (pim, pim_r) = C['psx'][2 * k2b], C['psx'][2 * k2b + 1]
                        P.dma('sp', Kt[k2b][:], self.kf_t.ap()[o, bt * 4 + g], reads=[self.kf_r], writes=[Kt_r[k2b]])
                        ar = A[:, 4 * g:4 * g + 4, 0, :]
                        ai = A[:, 4 * g:4 * g + 4, 1, :]
                        for (pp, pp_r, seq) in ((pre, pre_r, ((bde, bde_r, ar), (bdin, bdin_r, ai))),
                                                (pim, pim_r, ((bde, bde_r, ai), (bdi, bdi_r, ar)))):
                            for n_, (wt_, wt_r_, rhs_) in enumerate(seq):
                                P.op('pe', lambda pp=pp, wt_=wt_, rhs_=rhs_, n_=n_: nc.tensor.matmul(
                                    pp[:], wt_[:], rhs_, start=(n_ == 0), stop=(n_ == 1)),
                                    reads=[wt_r_, A_r], writes=[pp_r])
                        kre = Kt[k2b][:, :, 0, :]
                        kim = Kt[k2b][:, :, 1, :]
                        (t1, t1_r), (t2, t2_r), (t3, t3_r), (t4, t4_r) = tmps()
                        P.op('dve', lambda pre=pre, kre=kre, t1=t1: nc.vector.tensor_tensor(v4(t1), v4(pre), kre, ALU.mult),
                             reads=[pre_r, Kt_r[k2b]], writes=[t1_r])
                        P.op('dve', lambda pim=pim, kim=kim, t2=t2: nc.vector.tensor_tensor(v4(t2), v4(pim), kim, ALU.mult),
                             reads=[pim_r, Kt_r[k2b]], writes=[t2_r])
                        P.op('dve', lambda pre=pre, kim=kim, t3=t3: nc.vector.tensor_tensor(v4(t3), v4(pre), kim, ALU.mult),
                             reads=[pre_r, Kt_r[k2b]], writes=[t3_r])
                        P.op('dve', lambda pim=pim, kre=kre, t4=t4: nc.vector.tensor_tensor(v4(t4), v4(pim), kre, ALU.mult),
                             reads=[pim_r, Kt_r[k2b]], writes=[t4_r])
                        P.op('pool', lambda g=g, t1=t1, t2=t2: nc.gpsimd.tensor_tensor(Y[:, 4 * g:4 * g + 4, 0, :], v4(t1), v4(t2),
                                                                                     ALU.subtract),
                             reads=[t1_r, t2_r], writes=[Y_r], conc=True)
                        P.op('pool', lambda g=g, t3=t3, t4=t4: nc.gpsimd.tensor_tensor(Y[:, 4 * g:4 * g + 4, 1, :], v4(t3), v4(t4),
                                                                                     ALU.add),
                             reads=[t3_r, t4_r], writes=[Y_r], conc=True)
                elif k == 2:
                    for g in range(4):
                        k2b = ctr['ik'] % 2
                        ctr['ik'] += 1
                        (zre, zre_r), (zim, zim_r) = C['psx'][2 * k2b], C['psx'][2 * k2b + 1]
                        for h in range(4):
                            cg = 4 * g + h
                            yr = Y[:, cg, 0, :]
                            yi = Y[:, cg, 1, :]
                            for (pp, pp_r, seq) in ((zre, zre_r, ((yr, bde, bde_r), (yi, bdi, bdi_r))),
                                                    (zim, zim_r, ((yi, bde, bde_r), (yr, bdin, bdin_r)))):
                                for n_, (lh_, wt_, wt_r_) in enumerate(seq):
                                    P.op('pe', lambda pp=pp, lh_=lh_, wt_=wt_, n_=n_, h=h: nc.tensor.matmul(
                                        pp[:, h * 128:(h + 1) * 128], lh_, wt_[:], start=(n_ == 0), stop=(n_ == 1)),
                                        reads=[wt_r_, Y_r], writes=[pp_r])
                        (t1, t1_r), (t2, t2_r), (t3, t3_r), (t4, t4_r) = tmps()
                        P.op('dve', lambda zre=zre, t1=t1: nc.vector.tensor_tensor(t1[:], zre[:], tcr[:], ALU.mult),
                             reads=[zre_r, tcr_r], writes=[t1_r])
                        P.op('dve', lambda zim=zim, t2=t2: nc.vector.tensor_tensor(t2[:], zim[:], tci[:], ALU.mult),
                             reads=[zim_r, tci_r], writes=[t2_r])
                        P.op('dve', lambda zre=zre, t3=t3: nc.vector.tensor_tensor(t3[:], zre[:], tci[:], ALU.mult),
                             reads=[zre_r, tci_r], writes=[t3_r])
                        P.op('dve', lambda zim=zim, t4=t4: nc.vector.tensor_tensor(t4[:], zim[:], tcr[:], ALU.mult),
                             reads=[zim_r, tcr_r], writes=[t4_r])
                        P.op('pool', lambda g=g, t1=t1, t2=t2: nc.gpsimd.tensor_tensor(Z[:, 0, g * 512:(g + 1) * 512], t1[:], t2[:],
                                                                                     ALU.subtract),
                             reads=[t1_r, t2_r], writes=[Z_r], conc=True)
                        P.op('pool', lambda g=g, t3=t3, t4=t4: nc.gpsimd.tensor_tensor(Z[:, 1, g * 512:(g + 1) * 512], t3[:], t4[:],
                                                                                     ALU.add),
                             reads=[t3_r, t4_r], writes=[Z_r], conc=True)
                else:
                    for g in range(4):
                        k2b = ctr['ik'] % 2
                        ctr['ik'] += 1
                        ps, ps_r = C['psx'][2 * k2b]
                        P.op('pe', lambda ps=ps, g=g: nc.tensor.matmul(ps[:], g1r[:], Z[:, 0, g * 512:(g + 1) * 512], start=True,
                                                                       stop=False), reads=[g1r_r, Z_r], writes=[ps_r])
                        P.op('pe', lambda ps=ps, g=g: nc.tensor.matmul(ps[:], g1i[:], Z[:, 1, g * 512:(g + 1) * 512], start=False,
                                                                       stop=True), reads=[g1i_r, Z_r], writes=[ps_r])
                        P.op('dve', lambda ps=ps, g=g, H=H, o=o: nc.vector.tensor_tensor(
                            H[:, 0, 16 * g:16 * g + 16, :].rearrange("p c r -> p (c r)"), ps[:],
                            H[:, 1 + o, 16 * g:16 * g + 16, :].rearrange("p c r -> p (c r)"), ALU.mult),
                            reads=[ps_r, H_r], writes=[H_r])
                    if self.dbg and bt == 0:
                        P.dma('sp', self.zdbg_t.ap()[o], H[:, 0, :, :].rearrange("p c r -> p (c r)"), reads=[H_r],
                              writes=[self.zdbg_r], conc=True)

            def transposes(bt):
                b2 = bt % 2
                H, H_r = hb3[b2], hb3_r[b2]
                hb_ = bt % 2
                ys, ys_r = yst[(bt // 2) % 2], yst_r[(bt // 2) % 2]
                for rq in range(8):
                    for h in range(4):
                        r = 4 * rq + h
                        P.op('pe', lambda r=r, h=h, H=H, hb_=hb_: nc.tensor.transpose(pstt[64 * hb_:64 * hb_ + 64, h, :],
                                                                                     H[:, 0, :, r], ident[:]),
                             reads=[H_r, ident_r], writes=[pstt_r])
                    P.op('act', lambda rq=rq, hb_=hb_, ys=ys: nc.scalar.copy(
                        out=ys[64 * hb_:64 * hb_ + 64, :].rearrange("c (p r) -> c r p", r=32)[:, 4 * rq:4 * rq + 4, :],
                        in_=pstt[64 * hb_:64 * hb_ + 64, :, :]),
                        reads=[pstt_r], writes=[ys_r], conc=True)
                if hb_ == 1:
                    P.dma('sp', self.yh_t.ap()[bt // 2], ys[:], reads=[ys_r], writes=[self.yh_r], conc=True)

            prep_weights(0)
            for q in range(8):
                proj_chunk(0, q)
            for bt in range(8):
                if bt + 1 < 8:
                    prep_weights(bt + 1)
                i = 0
                for o in range(2):
                    for k in range(4):
                        conv_stage(bt, o, k)
                        if bt + 1 < 8:
                            proj_chunk(bt + 1, i)
                        i += 1
                transposes(bt)
            P.barrier()

    def mix_attn(self, l):
        nc, P = self.nc, self.P
        EB, EB_r = self.EB, self.EB_r
        with ExitStack() as es:
            EB2, EB2_r = self.sb(es, "EB2", [128, 12, 2, 256], F32)
            for j in range(2):
                P.op('pool', lambda j=j: nc.gpsimd.tensor_copy(out=EB2[:, :, j, :], in_=EB[:]), reads=[EB_r], writes=[EB2_r],
                     conc=True)
            qh, qh_r = self.sb(es, "qh", [128, SEQ], BF16)
            kh, kh_r = self.sb(es, "kh", [128, SEQ + 2048], BF16)
            vt, vt_r = [], []
            for i in range(3):
                a, b = self.sb(es, "vt", [128, 33, 128], BF16)
                vt.append(a)
                vt_r.append(b)
            NB = 3
            psc, psc_r, pov, pov_r, pe_, pe_r, pm, pm_r, ot, ot_r = ([] for _ in range(10))
            for i in range(NB):
                a, b = self.ps(es, "psc", [128, 512], F32)
                psc.append(a)
                psc_r.append(b)
                a, b = self.sb(es, "pexp", [128, 512], F32)
                pe_.append(a)
                pe_r.append(b)
                a, b = self.sb(es, "pm", [128, 512], BF16)
                pm.append(a)
                pm_r.append(b)
                a, b = self.sb(es, "ot", [128, 2, 128], F32)
                ot.append(a)
                ot_r.append(b)
            for i in range(2):
                a, b = self.ps(es, "pov", [128, 256], F32)
                pov.append(a)
                pov_r.append(b)
            it = 0
            iv = 0
            pending = [None]
            for g in range(3):
                D = DILS[g]
                m = SEQ // D
                nblk = m // 128
                nch = nblk + 1
                vd_t, vd_r = self.vd_t[g]
                for hp in range(2):
                    cq = 2 * g + hp
                    ck = 6 + 2 * g + hp
                    P.dma('sp', qh[:], self.qk_t.ap()[cq], reads=[self.qk_r], writes=[qh_r])
                    P.op('pool', lambda: nc.gpsimd.memset(kh[:], 0.0), writes=[kh_r])
                    P.dma('sp', kh[:, 64 * D:64 * D + SEQ], self.qk_t.ap()[ck], reads=[self.qk_r], writes=[kh_r])
                    for hi in range(2):
                        hh = 2 * hp + hi
                        p0 = 64 * hi
                        for r in range(D):
                            vb = iv % 3
                            iv += 1
                            P.dma('sp', vt[vb][:, 0:nch, :], vd_t.ap()[r, :, hh, :].rearrange("(c p) e -> p c e", p=128),
                                  reads=[vd_r], writes=[vt_r[vb]])
                            for b in range(0, nblk, 2):
                                pb = it % NB
                                po_ = it % 2
                                it += 1
                                for jb in range(2):
                                    q0 = r + D * 128 * (b + jb)
                                    kA = q0
                                    kB = r + D * 128 * (b + jb + 1)
                                    P.op('pe', lambda pb=pb, p0=p0, q0=q0, kA=kA, D=D, jb=jb: nc.tensor.matmul(
                                        psc[pb][:, jb * 256:jb * 256 + 128], kh[p0:p0 + 64, sl(kA, 128, D)],
                                        qh[p0:p0 + 64, sl(q0, 128, D)], start=True, stop=True),
                                        reads=[kh_r, qh_r], writes=[psc_r[pb]])
                                    P.op('pe', lambda pb=pb, p0=p0, q0=q0, kB=kB, D=D, jb=jb: nc.tensor.matmul(
                                        psc[pb][:, jb * 256 + 128:(jb + 1) * 256], kh[p0:p0 + 64, sl(kB, 128, D)],
                                        qh[p0:p0 + 64, sl(q0, 128, D)], start=True, stop=True),
                                        reads=[kh_r, qh_r], writes=[psc_r[pb]])
                                if pending[0] is not None:
                                    pending[0]()

                                def back(pb=pb, po_=po_, vb=vb, b=b, g=g, hh=hh, r=r, D=D):
                                    P.op('act', lambda: nc.scalar.activation(out=pe_[pb][:], in_=psc[pb][:], func=AF.Exp),
                                         reads=[psc_r[pb]], writes=[pe_r[pb]])
                                    P.op('dve', lambda: nc.vector.tensor_tensor(
                                        pm[pb][:], pe_[pb][:], EB2[:, g * 4 + hh, :, :].rearrange("p j e -> p (j e)"), ALU.mult),
                                        reads=[pe_r[pb], EB2_r], writes=[pm_r[pb]])
                                    for jb in range(2):
                                        P.op('pe', lambda jb=jb: nc.tensor.matmul(
                                            pov[po_][:, jb * 128:(jb + 1) * 128], pm[pb][:, jb * 256:jb * 256 + 128],
                                            vt[vb][:, b + jb, :], start=True, stop=False),
                                            reads=[pm_r[pb], vt_r[vb]], writes=[pov_r[po_]])
                                        P.op('pe', lambda jb=jb: nc.tensor.matmul(
                                            pov[po_][:, jb * 128:(jb + 1) * 128], pm[pb][:, jb * 256 + 128:(jb + 1) * 256],
                                            vt[vb][:, b + jb + 1, :], start=False, stop=True),
                                            reads=[pm_r[pb], vt_r[vb]], writes=[pov_r[po_]])
                                    P.op('act', lambda: nc.scalar.copy(
                                        out=ot[pb][:].rearrange("p j e -> p (j e)"), in_=pov[po_][:]),
                                        reads=[pov_r[po_]], writes=[ot_r[pb]])
                                    q00 = r + D * 128 * b
                                    P.dma('sp', self.att_t.ap()[g, sl(q00, 256, D), hh, :].rearrange("(j p) e -> p j e", p=128),
                                          ot[pb][:], reads=[ot_r[pb]], writes=[self.att_r], conc=True)
                                pending[0] = back
            if pending[0] is not None:
                pending[0]()
            P.barrier()

    def mix_out(self, l, src, dst):
        nc, P = self.nc, self.P
        src_t, src_r = src
        dst_t, dst_r = dst
        T = 1024
        with ExitStack() as es:
            ident, ident_r = self.load_const_tiles(es)
            whp, whp_r = self.sb(es, "whp", [128, 4, D_MODEL], BF16)
            wap, wap_r = self.sb(es, "wap", [128, 2, D_MODEL], BF16)
            wout, wout_r = self.sb(es, "wout", [128, 8, D_MODEL], BF16)
            wds, wds_r = [], []
            for i in range(2):
                a, b = self.sb(es, "wdso", [128, D_MODEL], F32)
                wds.append(a)
                wds_r.append(b)
            iw = 0
            for (wt, wt_r, nkc, name) in ((whp, whp_r, 4, 'w_hy_proj'), (wap, wap_r, 2, 'w_at_proj'), (wout, wout_r, 8, 'w_out')):
                for kc in range(nkc):
                    b2 = iw % 2
                    iw += 1
                    P.dma('sp', wds[b2][:], self.w[name].ap()[l, kc * 128:(kc + 1) * 128, :], reads=[self.wr], writes=[wds_r[b2]])
                    P.op('pool', lambda wt=wt, kc=kc, b2=b2: nc.gpsimd.tensor_copy(out=wt[:, kc, :], in_=wds[b2][:]),
                         reads=[wds_r[b2]], writes=[wt_r], conc=True)
            yhyT, yhyT_r = self.sb(es, "yhyT", [128, 4, T], BF16)
            yatT, yatT_r = self.sb(es, "yatT", [128, 2, T], BF16)
            yT, yT_r = self.sb(es, "yT", [128, 8, T], BF16)
            att, att_r, s2, s2_r, yab, yab_r, rden, rden_r = ([] for _ in range(8))
            pst, pst_r, pst2, pst2_r = [], [], [], []
            for i in range(2):
                a, b = self.sb(es, "attl", [128, 3, 4, 128], F32)
                att.append(a)
                att_r.append(b)
                a, b = self.sb(es, "s2", [128, 4, 128], F32)
                s2.append(a)
                s2_r.append(b)
                a, b = self.sb(es, "yab", [128, 4, 64], BF16)
                yab.append(a)
                yab_r.append(b)
                a, b = self.sb(es, "rden", [128, 4], F32)
                rden.append(a)
                rden_r.append(b)
            a, b = self.ps(es, "psty", [128, 4, 128], BF16)
            pst.append(a)
            pst_r.append(b)
            a, b = self.ps(es, "psta", [128, 2, 128], BF16)
            pst2.append(a)
            pst2_r.append(b)
            pa, pa_r, pbb, pbb_r, po, po_r = [], [], [], [], [], []
            gA, gA_r, gB, gB_r, ta, ta_r, tb_, tb_r, xin, xin_r, xo, xo_r = ([] for _ in range(12))
            for i in range(2):
                for lst, lstr, nm in ((pa, pa_r, "pa"), (pbb, pbb_r, "pbb"), (po, po_r, "poo")):
                    a, b = self.ps(es, nm, [128, 512], F32)
                    lst.append(a)
                    lstr.append(b)
                for lst, lstr, nm in ((gA, gA_r, "gA"), (gB, gB_r, "gB"), (ta, ta_r, "ta"), (tb_, tb_r, "tbt")):
                    a, b = self.sb(es, nm, [128, 512], F32)
                    lst.append(a)
                    lstr.append(b)
                a, b = self.sb(es, "xin", [128, D_MODEL], F32)
                xin.append(a)
                xin_r.append(b)
                a, b = self.sb(es, "xo", [128, D_MODEL], F32)
                xo.append(a)
                xo_r.append(b)
            it = 0
            for st in range(SEQ // T):
                for kc in range(4):
                    P.dma('sp', yhyT[:, kc, :], self.yh_t.ap()[kc, :, st * T:(st + 1) * T], reads=[self.yh_r], writes=[yhyT_r],
                          conc=True)
                for tb in range(8):
                    gb = st * 8 + tb
                    b2 = gb % 2
                    P.dma('sp', att[b2][:], self.att_t.ap()[:, gb * 128:(gb + 1) * 128, :, :].rearrange("g t h e -> t g h e"),
                          reads=[self.att_r], writes=[att_r[b2]])
                    P.op('dve', lambda b2=b2: nc.vector.tensor_tensor(s2[b2][:], att[b2][:, 0, :, :], att[b2][:, 1, :, :], ALU.add),
                         reads=[att_r[b2]], writes=[s2_r[b2]])
                    P.op('dve', lambda b2=b2: nc.vector.tensor_tensor(s2[b2][:], s2[b2][:], att[b2][:, 2, :, :], ALU.add),
                         reads=[att_r[b2], s2_r[b2]], writes=[s2_r[b2]])
                    P.op('dve', lambda b2=b2: nc.vector.reciprocal(rden[b2][:], s2[b2][:, :, 64]), reads=[s2_r[b2]],
                         writes=[rden_r[b2]])
                    for hh in range(4):
                        P.op('dve', lambda b2=b2, hh=hh: nc.vector.tensor_scalar(yab[b2][:, hh, :], s2[b2][:, hh, 0:64],
                                                                                rden[b2][:, hh:hh + 1], None, ALU.mult),
                             reads=[s2_r[b2], rden_r[b2]], writes=[yab_r[b2]], conc=True)
                    for c in range(2):
                        P.op('pe', lambda c=c, b2=b2: nc.tensor.transpose(
                            pst2[0][:, c, :], yab[b2][:, 2 * c:2 * c + 2, :].rearrange("p h e -> p (h e)"), ident[:]),
                            reads=[yab_r[b2], ident_r], writes=[pst2_r[0]])
                    P.op('act', lambda tb=tb: nc.scalar.copy(out=yatT[:, :, tb * 128:(tb + 1) * 128], in_=pst2[0][:]),
                         reads=[pst2_r[0]], writes=[yatT_r], conc=True)
                for dc in range(8):
                    for th in range(2):
                        pb = it % 2
                        it += 1
                        tsl = slice(st * T + th * 512, st * T + (th + 1) * 512)
                        P.dma('sp', gA[pb][:], self.gt_t.ap()[dc, :, tsl], reads=[self.gt_r], writes=[gA_r[pb]])
                        P.dma('sp', gB[pb][:], self.gt_t.ap()[8 + dc, :, tsl], reads=[self.gt_r], writes=[gB_r[pb]])
                        for kc in range(4):
                            P.op('pe', lambda kc=kc, dc=dc, th=th, pb=pb: nc.tensor.matmul(
                                pa[pb][:], whp[:, kc, dc * 128:(dc + 1) * 128], yhyT[:, kc, th * 512:(th + 1) * 512],
                                start=(kc == 0), stop=(kc == 3)), reads=[whp_r, yhyT_r], writes=[pa_r[pb]])
                        for kc in range(2):
                            P.op('pe', lambda kc=kc, dc=dc, th=th, pb=pb: nc.tensor.matmul(
                                pbb[pb][:], wap[:, kc, dc * 128:(dc + 1) * 128], yatT[:, kc, th * 512:(th + 1) * 512],
                                start=(kc == 0), stop=(kc == 1)), reads=[wap_r, yatT_r], writes=[pbb_r[pb]])
                        P.op('dve', lambda pb=pb: nc.vector.tensor_tensor(ta[pb][:], pa[pb][:], gA[pb][:], ALU.mult),
                             reads=[pa_r[pb], gA_r[pb]], writes=[ta_r[pb]])
                        P.op('dve', lambda pb=pb: nc.vector.tensor_tensor(tb_[pb][:], pbb[pb][:], gB[pb][:], ALU.mult),
                             reads=[pbb_r[pb], gB_r[pb]], writes=[tb_r[pb]])
                        P.op('pool', lambda pb=pb, dc=dc, th=th: nc.gpsimd.tensor_tensor(
                            yT[:, dc, th * 512:(th + 1) * 512], ta[pb][:], tb_[pb][:], ALU.add),
                            reads=[ta_r[pb], tb_r[pb]], writes=[yT_r], conc=True)
                for tb in range(8):
                    b2 = tb % 2
                    r0 = st * T + tb * 128
                    P.dma('sp', xin[b2][:], src_t.ap()[r0:r0 + 128, :], reads=[src_r], writes=[xin_r[b2]])
                    for dh in range(2):
                        pb = it % 2
                        it += 1
                        for kc in range(8):
                            P.op('pe', lambda kc=kc, tb=tb, dh=dh, pb=pb: nc.tensor.matmul(
                                po[pb][:], yT[:, kc, tb * 128:(tb + 1) * 128], wout[:, kc, dh * 512:(dh + 1) * 512],
                                start=(kc == 0), stop=(kc == 7)), reads=[yT_r, wout_r], writes=[po_r[pb]])
                        P.op('dve', lambda pb=pb, b2=b2, dh=dh: nc.vector.tensor_tensor(
                            xo[b2][:, dh * 512:(dh + 1) * 512], po[pb][:], xin[b2][:, dh * 512:(dh + 1) * 512], ALU.add),
                            reads=[po_r[pb], xin_r[b2]], writes=[xo_r[b2]], conc=True)
                    P.dma('sp', dst_t.ap()[r0:r0 + 128, :], xo[b2][:], reads=[xo_r[b2]], writes=[dst_r], conc=True)
            P.barrier()

    def mixer_phase(self, l, src, dst):
        upto = self.mix_upto
        order = ['filt', 'proj', 'conv', 'attn', 'out']
        n = len(order) if upto is None else order.index(upto) + 1
        names = order[:n]
        if 'filt' in names:
            self.mix_filters2(l)
        with ExitStack() as es:
            self.hT, self.hT_r = self.sb(es, "hTm", [128, 8, SEQ + 2], BF16)
            if 'proj' in names:
                self.mix_proj(l, src)
            if 'conv' in names:
                self.mix_hyconv(l)
            self.P.barrier()
        if 'attn' in names:
            self.mix_attn(l)
        if 'out' in names:
            self.mix_out(l, src, dst)
        self.P.barrier()

    def build(self):
        self.declare()
        P = self.P
        with ExitStack() as es_sem:
            self.eps_t, self.eps_r = self.sb(es_sem, "eps", [128, 1], F32)
            P.op('dve', lambda: self.nc.vector.memset(self.eps_t[:], EPS), writes=[self.eps_r])
            self.attn_bias_setup(es_sem)
            cur = (self.x_t, self.x_r)
            si = 0
            stages = self.stages
            for l in range(DEPTH):
                for ph in ('ffn1', 'mix', 'ffn2'):
                    name = "%s_%d" % (ph, l)
                    if stages is not None and name not in stages:
                        continue
                    last = (stages is not None and name == stages[-1]) or (stages is None and l == DEPTH - 1 and ph == 'ffn2')
                    dst = (self.y_t, self.y_r) if last else self.xs_t[si]
                    si += 1
                    if ph == 'ffn1':
                        self.ffn_phase(l, 1, cur, dst)
                    elif ph == 'ffn2':
                        self.ffn_phase(l, 2, cur, dst)
                    else:
                        self.mixer_phase(l, cur, dst)
                    cur = dst
            P.barrier()
            nw = P.emit(es_sem)
            print("ops", len(P.ops), "waits", nw, "slots", len(P.slot_cnt))
        return self.nc


_NC_CACHE = {}


def kernel(**inputs):
    x = np.ascontiguousarray(np.asarray(inputs['x'], dtype=np.float32))
    consts = host_constants()
    if 'nc' not in _NC_CACHE:
        _NC_CACHE['nc'] = Builder().build()
    nc = _NC_CACHE['nc']
    shared = {k: np.ascontiguousarray(np.asarray(inputs[k], dtype=np.float32)) for k in WEIGHT_NAMES}
    shared.update(consts)
    in_maps = []
    for b in range(BATCH):
        m = dict(shared)
        m['x'] = x[b]
        in_maps.append(m)
    res = run_bass_kernel_spmd(nc, in_maps, core_ids=list(range(BATCH)))
    return np.stack([np.asarray(r['y'], dtype=np.float32) for r in res.results], axis=0)
```

```python
import math
from contextlib import ExitStack

import numpy as np
import ml_dtypes

import concourse.bass as bass
import concourse.mybir as mybir
from concourse.alu_op_type import AluOpType as ALU
from concourse.bass_utils import run_bass_kernel_spmd

F32 = mybir.dt.float32
BF16 = mybir.dt.bfloat16
AF = mybir.ActivationFunctionType

D_MODEL = 1024
BATCH = 8
SEQ = 4096
DEPTH = 2
HEAD_DIM = 64
HYW = 512
HY_EMB = 33
HID = 64
WINDOWS = (128, 512, 2048)
DILS = (1, 4, 16)
D_FF = 2688
NFC = D_FF // 128
IN_WIDTH = 3840
EPS = 1e-6
NFFT = 8192
NTB = SEQ // 128

WEIGHT_NAMES = ['rel_bias', 'ffn1_norm', 'ffn1_w_gate', 'ffn1_w_up', 'ffn1_w_down', 'mix_norm', 'w_in',
                'w_gate', 'b_gate', 'hy_conv_w', 'hy_conv_b', 'hy_filt_w1', 'hy_filt_b1', 'hy_filt_w2',
                'hy_filt_b2', 'hy_filt_w3', 'hy_skip', 'q_norm', 'k_norm', 'w_hy_proj', 'w_at_proj',
                'w_out', 'ffn2_norm', 'ffn2_w_gate', 'ffn2_w_up', 'ffn2_w_down']


class Reg:
    __slots__ = ('name', 'prev', 'writers', 'readers', 'slot', 'dcnt', 'lastdma')

    def __init__(self, name):
        self.name = name
        self.prev = []
        self.writers = []
        self.readers = []
        self.slot = None
        self.dcnt = 0
        self.lastdma = None


class Op:
    __slots__ = ('eng', 'fn', 'deps', 'is_dma', 'reg', 'dval', 'sig', 'sigval', 'slot')


class Prog:
    ENG = {'pe': 'tensor', 'act': 'scalar', 'dve': 'vector', 'pool': 'gpsimd', 'sp': 'sync'}

    def __init__(self, nc):
        self.nc = nc
        self.ops = []
        self.regs = []
        self.last = {}
        self.dma_regs = set()
        self.slot_cnt = []
        self.free_slots = []

    def reg(self, name):
        r = Reg(name)
        self.regs.append(r)
        return r

    def _add(self, eng, fn, reads, writes, conc, is_dma):
        i = len(self.ops)
        deps = set()
        for r in reads:
            deps.update(r.writers)
            if not r.writers:
                deps.update(r.prev)
        for w in writes:
            if conc:
                if w.readers:
                    w.prev = w.readers + w.writers
                    w.writers = []
                    w.readers = []
                deps.update(w.prev)
            else:
                deps.update(w.prev)
                deps.update(w.writers)
                deps.update(w.readers)
        for w in writes:
            if conc:
                w.writers.append(i)
            else:
                w.prev = []
                w.writers = [i]
                w.readers = []
        for r in reads:
            r.readers.append(i)
        deps.discard(i)
        o = Op()
        o.eng = eng
        o.fn = fn
        o.deps = deps
        o.is_dma = is_dma
        o.reg = None
        o.dval = 0
        o.sig = False
        o.sigval = 0
        o.slot = None
        if is_dma:
            w = writes[0]
            if w.slot is None:
                if self.free_slots:
                    w.slot = self.free_slots.pop()
                else:
                    w.slot = len(self.slot_cnt)
                    self.slot_cnt.append(0)
            self.slot_cnt[w.slot] += 16
            w.dcnt = self.slot_cnt[w.slot]
            o.reg = w
            o.slot = w.slot
            o.dval = w.dcnt
            w.lastdma = i
            self.dma_regs.add(w)
        else:
            self.last[eng] = i
        self.ops.append(o)
        return i

    def op(self, eng, fn, reads=(), writes=(), conc=False):
        return self._add(eng, fn, list(reads), list(writes), conc, False)

    def dma(self, q, out, in_, reads=(), writes=(), conc=False, **kw):
        nc = self.nc
        e = getattr(nc, self.ENG[q])

        def fn():
            return e.dma_start(out=out, in_=in_, **kw)
        assert len(writes) == 1
        return self._add(q, fn, list(reads), list(writes), conc, True)

    def barrier(self):
        deps = set(self.last.values())
        for r in self.dma_regs:
            if r.lastdma is not None:
                deps.add(r.lastdma)
        for e in self.ENG:
            o = Op()
            o.eng = e
            o.fn = None
            o.deps = set(deps)
            o.is_dma = False
            o.reg = None
            o.dval = 0
            o.sig = False
            o.sigval = 0
            o.slot = None
            self.ops.append(o)
        for r in self.regs:
            r.prev = []
            r.writers = []
            r.readers = []
        for r in self.dma_regs:
            self.free_slots.append(r.slot)
            r.slot = None
            r.lastdma = None
        self.dma_regs = set()

    def emit(self, es):
        nc = self.nc
        ops = self.ops
        for o in ops:
            if o.eng == 'pe' and not o.is_dma:
                o.deps = {d for d in o.deps if ops[d].is_dma or ops[d].eng != 'pe'}
            for d in o.deps:
                ops[d].sig = True
        esem = {e: es.enter_context(nc.semaphore("sem_" + e)) for e in self.ENG}
        cnt = {e: 0 for e in self.ENG}
        ssem = [es.enter_context(nc.semaphore("semslot%d" % i)) for i in range(len(self.slot_cnt))]
        for o in ops:
            if o.is_dma:
                pass
            elif o.sig and o.fn is not None:
                cnt[o.eng] += 1
                o.sigval = cnt[o.eng]
        known = {e: {} for e in self.ENG}
        nwait = 0
        for o in ops:
            e = o.eng
            eng = getattr(nc, self.ENG[e])
            need = {}
            for d in o.deps:
                od = ops[d]
                if od.is_dma:
                    key = ('r', od.slot)
                    sem = ssem[od.slot]
                    val = od.dval
                else:
                    if od.fn is None:
                        continue
                    key = ('e', od.eng)
                    sem = esem[od.eng]
                    val = od.sigval
                if need.get(key, (None, 0))[1] < val:
                    need[key] = (sem, val)
            kn = known[e]
            for key, (sem, val) in need.items():
                if kn.get(key, 0) < val:
                    eng.wait_ge(sem, val)
                    kn[key] = val
                    nwait += 1
            if o.fn is None:
                continue
            ins = o.fn()
            if o.is_dma:
                ins.then_inc(ssem[o.slot], 16)
            elif o.sig:
                ins.then_inc(esem[e], 1)
        return nwait


_CONST_CACHE = {}


def t5_bucket_np(rel):
    half = 16
    exact = 8
    ret = np.where(rel > 0, half, 0)
    n = np.abs(rel)
    nf = np.maximum(n, 1).astype(np.float32)
    large = exact + (np.log(nf / exact) / np.float32(math.log(1024 / exact)) * (half - exact)).astype(np.int32)
    large = np.minimum(large, half - 1)
    return ret + np.where(n < exact, n, large)


def host_constants():
    if _CONST_CACHE:
        return _CONST_CACHE
    c = {}
    L = SEQ
    t = np.linspace(0.0, 1.0, L, dtype=np.float32)[:, None]
    w = (np.float32(2.0 * math.pi / L)) * np.arange(L, dtype=np.float32)[:, None]
    f = np.linspace(1e-4, 15, 16, dtype=np.float32)[None]
    z = np.concatenate([t, np.cos(f * w), -np.sin(f * w)], axis=-1).astype(np.float32)
    c['zfeat'] = np.ascontiguousarray(z.T)
    max_decay = math.log(1e-2) / 0.3
    min_decay = math.log(1e-2) / 1.5
    deltas = np.abs(np.linspace(min_decay, max_decay, HYW, dtype=np.float32))
    dec = np.exp(-t * deltas[None]).astype(np.float32)
    c['dec_f'] = dec
    decb = dec.copy()
    decb[0] = 0.0
    c['dec_b'] = decb
    c['ident'] = np.eye(128, dtype=np.float32).astype(ml_dtypes.bfloat16)
    ob = np.zeros((128, 128), np.float32)
    ob[:64, :64] = 1.0
    ob[64:, 64:] = 1.0
    c['ones_blk'] = ob
    c['ones_all'] = np.ones((128, 128), np.float32)
    c['jrev'] = np.ascontiguousarray(np.eye(128, dtype=np.float32)[::-1])
    oh = np.zeros((3, 32, 512), np.float32)
    bm = np.zeros((4, 512), np.float32)
    for g in range(3):
        for i in range(191, 320):
            rel = 255 - i
            b = int(t5_bucket_np(np.array([rel * DILS[g]]))[0])
            oh[g, b, i] = 1.0
    bm[:, 191:320] = 1.0
    c['oh'] = oh
    c['bm'] = bm
    th = 2.0 * math.pi / NFFT
    p_ = np.arange(128, dtype=np.float64)
    r_ = np.arange(32, dtype=np.float64)
    k1_ = np.arange(128, dtype=np.float64) + 0.5
    k2_ = np.arange(32, dtype=np.float64)
    a1 = 2.0 * math.pi * np.outer(p_, k1_) / 256.0
    c['f1cat'] = np.concatenate([np.cos(a1), -np.sin(a1)], axis=1).astype(np.float32).astype(ml_dtypes.bfloat16)
    c['g1re'] = (np.cos(a1).T * (2.0 / NFFT)).astype(np.float32).astype(ml_dtypes.bfloat16)
    c['g1imn'] = (-np.sin(a1).T * (2.0 / NFFT)).astype(np.float32).astype(ml_dtypes.bfloat16)
    at = th * np.outer(r_, k1_)
    tre = np.tile(np.cos(at), (4, 1))
    tim = np.tile(-np.sin(at), (4, 1))
    c['tf_re'] = np.ascontiguousarray(np.concatenate([tre, tre], axis=1).astype(np.float32))
    c['tf_im'] = np.ascontiguousarray(np.concatenate([tim, tim], axis=1).astype(np.float32))
    tcre = np.cos(at).T
    tcim = np.sin(at).T
    c['tc_re'] = np.ascontiguousarray(np.tile(tcre, (1, 16)).astype(np.float32))
    c['tc_im'] = np.ascontiguousarray(np.tile(tcim, (1, 16)).astype(np.float32))
    ae = 2.0 * math.pi * np.outer(r_, k2_) / 32.0
    ere = np.cos(ae)
    eim = -np.sin(ae)

    def bd(m):
        o_ = np.zeros((128, 128), np.float64)
        for i in range(4):
            o_[32 * i:32 * i + 32, 32 * i:32 * i + 32] = m
        return o_.astype(np.float32).astype(ml_dtypes.bfloat16)
    c['bd_ere'] = bd(ere)
    c['bd_eren'] = bd(-ere)
    c['bd_eim'] = bd(eim)
    c['bd_eimn'] = bd(-eim)
    d2 = np.stack([c['dec_f'], c['dec_b']], axis=0)
    d2 = d2.reshape(2, 128, 32, 8, 64).transpose(3, 1, 2, 0, 4)
    c['dec2'] = np.ascontiguousarray(d2.astype(np.float32))
    dr = np.stack([c['dec_f'], c['dec_b']], axis=0).reshape(2, 128, 32, 512).transpose(2, 1, 0, 3)
    c['decr'] = np.ascontiguousarray(dr.astype(np.float32))
    for k in ('Fh', 'Gh', 'dec_f', 'dec_b'):
        c.pop(k, None)
    _CONST_CACHE.update(c)
    return c


CONST_SPECS = [('zfeat', [33, SEQ], F32), ('ident', [128, 128], BF16),
               ('f1cat', [128, 256], BF16), ('g1re', [128, 128], BF16), ('g1imn', [128, 128], BF16),
               ('tf_re', [128, 256], F32), ('tf_im', [128, 256], F32), ('tc_re', [128, 512], F32), ('tc_im', [128, 512], F32),
               ('bd_ere', [128, 128], BF16), ('bd_eren', [128, 128], BF16), ('bd_eim', [128, 128], BF16),
               ('bd_eimn', [128, 128], BF16), ('dec2', [8, 128, 32, 2, 64], F32), ('decr', [32, 128, 2, 512], F32),
               ('ones_blk', [128, 128], F32), ('ones_all', [128, 128], F32), ('jrev', [128, 128], F32), ('oh', [3, 32, 512], F32),
               ('bm', [4, 512], F32)]

WEIGHT_SHAPES = {
    'rel_bias': [32, 12], 'ffn1_norm': [2, 1024], 'ffn1_w_gate': [2, 1024, 2688], 'ffn1_w_up': [2, 1024, 2688],
    'ffn1_w_down': [2, 2688, 1024], 'mix_norm': [2, 1024], 'w_in': [2, 1024, 3840], 'w_gate': [2, 1024, 2048],
    'b_gate': [2, 2048], 'hy_conv_w': [2, 3, 1536], 'hy_conv_b': [2, 1536], 'hy_filt_w1': [2, 33, 64],
    'hy_filt_b1': [2, 64], 'hy_filt_w2': [2, 64, 64], 'hy_filt_b2': [2, 64], 'hy_filt_w3': [2, 64, 2048],
    'hy_skip': [2, 2, 512], 'q_norm': [2, 64], 'k_norm': [2, 64], 'w_hy_proj': [2, 512, 1024],
    'w_at_proj': [2, 256, 1024], 'w_out': [2, 1024, 1024], 'ffn2_norm': [2, 1024], 'ffn2_w_gate': [2, 1024, 2688],
    'ffn2_w_up': [2, 1024, 2688], 'ffn2_w_down': [2, 2688, 1024]}


def sl(start, n, step):
    return slice(start, start + (n - 1) * step + 1, step)


def bcast_rows(ap1d_tensor, offset, n, parts=128):
    return bass.AP(ap1d_tensor, offset, [[0, parts], [1, n]])


class Builder:
    def __init__(self, stages=None, dbg=False, mix_upto=None):
        self.mix_upto = mix_upto
        self.nc = bass.Bass("TRN2", target_bir_lowering=False)
        self.P = Prog(self.nc)
        self.stages = stages
        self.dbg = dbg
        self.uid = 0

    def sb(self, es, name, shape, dt):
        self.uid += 1
        t = es.enter_context(self.nc.sbuf_tensor("%s_%d" % (name, self.uid), shape, dt))
        return t, self.P.reg(name)

    def ps(self, es, name, shape, dt):
        self.uid += 1
        t = es.enter_context(self.nc.psum_tensor("%s_%d" % (name, self.uid), shape, dt))
        return t, self.P.reg(name)

    def dram(self, name, shape, dt, kind="Internal"):
        t = self.nc.dram_tensor(name, shape, dt, kind=kind)
        return t, self.P.reg(name)

    def declare(self):
        nc = self.nc
        self.x_t, self.x_r = self.dram("x", [SEQ, D_MODEL], F32, "ExternalInput")
        SK = "ExternalOutput" if self.dbg else "Internal"
        self.y_t, self.y_r = self.dram("y", [SEQ, D_MODEL], F32, "ExternalOutput")
        self.w = {}
        self.wr = self.P.reg("weights")
        for k in WEIGHT_NAMES:
            self.w[k] = nc.dram_tensor(k, WEIGHT_SHAPES[k], F32, kind="ExternalInput")
        self.c = {}
        for k, shp, dt in CONST_SPECS:
            self.c[k] = nc.dram_tensor(k, shp, dt, kind="ExternalInput")
        self.xs_t = []
        for i in range(6):
            self.xs_t.append(self.dram("xres%d" % i, [SEQ, D_MODEL], F32))
        self.hy_t, self.hy_r = self.dram("hy_s", [SEQ, 1536], F32, SK)
        self.z1_t, self.z1_r = self.dram("z1_s", [SEQ, HYW], F32, SK)
        self.qk_t, self.qk_r = self.dram("qk_s", [12, 128, SEQ], BF16, SK)
        self.gt_t, self.gt_r = self.dram("gt_s", [16, 128, SEQ], F32, SK)
        self.vd_t = []
        for g in range(3):
            D = DILS[g]
            m = SEQ // D
            self.vd_t.append(self.dram("vd_s%d" % g, [D, m + 128, 4, 128], BF16, SK))
        self.kf_t, self.kf_r = self.dram("kf_s", [2, 32, 128, 4, 2, 128], F32, SK)
        self.yh_t, self.yh_r = self.dram("yh_s", [4, 128, SEQ], BF16, SK)
        self.att_t, self.att_r = self.dram("att_s", [3, SEQ, 4, 128], F32, SK)
        self.rv_t, self.rv_r = self.dram("rv_s", [12, 512], F32, SK)
        self.z2_t, self.z2_r = self.dram("z2_s", [SEQ, HYW], F32, SK)
        self.zdbg_t, self.zdbg_r = self.dram("zdbg_s", [2, 128, 2048], BF16, SK)

    def norm_to_hT(self, es, src_t, src_r, tok0, ntb, gbc, hT, hT_r, hoff, tiles):
        nc, P = self.nc, self.P
        (xs, xs_rs, junk, junk_r, ss, ss_r, rs, rs_r, hb, hb_r, pst, pst_r, ident, ident_r) = tiles
        src = src_t.ap()
        for tb in range(ntb):
            xb = xs[:, tb % len(xs_rs), :]
            xr = xs_rs[tb % len(xs_rs)]
            r0 = tok0 + tb * 128
            P.dma('sp', xb, src[r0:r0 + 128, :], reads=[src_r], writes=[xr])
            b2 = tb % 2
            P.op('act', lambda xb=xb, tb=tb: nc.scalar.activation(out=junk[:], in_=xb, func=AF.Square,
                                                                  accum_out=ss[:, tb:tb + 1]),
                 reads=[xr], writes=[junk_r, ss_r[tb]])
            P.op('dve', lambda tb=tb: nc.vector.tensor_scalar(rs[:, tb:tb + 1], ss[:, tb:tb + 1], 1.0 / D_MODEL, EPS,
                                                              ALU.mult, ALU.add),
                 reads=[ss_r[tb]], writes=[rs_r[tb]])
            P.op('act', lambda tb=tb: nc.scalar.activation(out=rs[:, tb:tb + 1], in_=rs[:, tb:tb + 1], func=AF.Sqrt),
                 reads=[rs_r[tb]], writes=[rs_r[tb]])
            P.op('dve', lambda tb=tb: nc.vector.reciprocal(rs[:, tb:tb + 1], rs[:, tb:tb + 1]),
                 reads=[rs_r[tb]], writes=[rs_r[tb]])
            P.op('dve', lambda xb=xb, tb=tb, b2=b2: nc.vector.scalar_tensor_tensor(
                hb[b2][:], xb, rs[:, tb:tb + 1], gbc[0][:], ALU.mult, ALU.mult),
                reads=[xr, rs_r[tb], gbc[1]], writes=[hb_r[b2]])
            for kc in range(8):
                P.op('pe', lambda kc=kc, b2=b2: nc.tensor.transpose(pst[b2][:, kc, :], hb[b2][:, kc * 128:(kc + 1) * 128],
                                                                   ident[:]),
                     reads=[hb_r[b2], ident_r], writes=[pst_r[b2]])
            c0 = hoff + tb * 128
            P.op('act', lambda b2=b2, c0=c0: nc.scalar.copy(out=hT[:, :, c0:c0 + 128], in_=pst[b2][:]),
                 reads=[pst_r[b2]], writes=[hT_r], conc=True)

    def load_const_tiles(self, es):
        nc, P = self.nc, self.P
        ident, ident_r = self.sb(es, "ident", [128, 128], BF16)
        P.dma('sp', ident[:], self.c['ident'].ap(), reads=[self.wr], writes=[ident_r])
        return ident, ident_r

    def ffn_phase(self, l, which, src, dst):
        nc, P = self.nc, self.P
        src_t, src_r = src
        dst_t, dst_r = dst
        wg_d = self.w['ffn%d_w_gate' % which].ap()
        wu_d = self.w['ffn%d_w_up' % which].ap()
        wd_d = self.w['ffn%d_w_down' % which].ap()
        gn_t = self.w['ffn%d_norm' % which]
        T = 1024
        with ExitStack() as es:
            ident, ident_r = self.load_const_tiles(es)
            gbc = self.sb(es, "gbc", [128, D_MODEL], F32)
            P.dma('sp', gbc[0][:], bcast_rows(gn_t, l * D_MODEL, D_MODEL), reads=[self.wr], writes=[gbc[1]])
            xs, _ = self.sb(es, "xs", [128, 8, D_MODEL], F32)
            xs_rs = [P.reg("xs%d" % i) for i in range(8)]
            junk, junk_r = self.sb(es, "junk", [128, D_MODEL], F32)
            ss, _ = self.sb(es, "ss", [128, 8], F32)
            ss_r = [P.reg("ss%d" % i) for i in range(8)]
            rs, _ = self.sb(es, "rs", [128, 8], F32)
            rs_r = [P.reg("rs%d" % i) for i in range(8)]
            hb, hb_r = [], []
            pst, pst_r = [], []
            for i in range(2):
                a, b = self.sb(es, "hb", [128, D_MODEL], BF16)
                hb.append(a)
                hb_r.append(b)
                a, b = self.ps(es, "pst", [128, 8, 128], BF16)
                pst.append(a)
                pst_r.append(b)
            hT, hT_r = self.sb(es, "hT", [128, 8, T], BF16)
            act, act_r = self.sb(es, "act", [128, NFC, T], BF16)
            wd, wd_r = self.sb(es, "wd", [128, NFC, D_MODEL], BF16)
            wst, wst_r, wbf, wbf_r = [], [], [], []
            for i in range(4):
                a, b = self.sb(es, "wst", [128, 8, 128], F32)
                wst.append(a)
                wst_r.append(b)
                a, b = self.sb(es, "wbf", [128, 8, 128], BF16)
                wbf.append(a)
                wbf_r.append(b)
            wds, wds_r = [], []
            for i in range(2):
                a, b = self.sb(es, "wds", [128, D_MODEL], F32)
                wds.append(a)
                wds_r.append(b)
            sg, sg_r = [], []
            pg, pg_r, pu, pu_r, po, po_r = [], [], [], [], [], []
            for i in range(2):
                a, b = self.sb(es, "sg", [128, 512], F32)
                sg.append(a)
                sg_r.append(b)
                a, b = self.ps(es, "pg", [128, 512], F32)
                pg.append(a)
                pg_r.append(b)
                a, b = self.ps(es, "pu", [128, 512], F32)
                pu.append(a)
                pu_r.append(b)
                a, b = self.ps(es, "po", [128, 512], F32)
                po.append(a)
                po_r.append(b)
            for j in range(NFC):
                b2 = j % 2
                P.dma('sp', wds[b2][:], wd_d[l, j * 128:(j + 1) * 128, :], reads=[self.wr], writes=[wds_r[b2]])
                P.op('pool', lambda j=j, b2=b2: nc.gpsimd.tensor_copy(out=wd[:, j, :], in_=wds[b2][:]),
                     reads=[wds_r[b2]], writes=[wd_r], conc=True)
            tiles = (xs, xs_rs, junk, junk_r, ss, ss_r, rs, rs_r, hb, hb_r, pst, pst_r, ident, ident_r)
            src_ap = src_t.ap()
            dst_ap = dst_t.ap()
            for st in range(SEQ // T):
                self.norm_to_hT(es, src_t, src_r, st * T, 8, gbc, hT, hT_r, 0, tiles)
                for j in range(NFC):
                    bg = (2 * j) % 4
                    bu = (2 * j + 1) % 4
                    P.dma('sp', wst[bg][:], wg_d[l, :, j * 128:(j + 1) * 128].rearrange("(kc p) m -> p kc m", p=128),
                          reads=[self.wr], writes=[wst_r[bg]])
                    P.dma('sp', wst[bu][:], wu_d[l, :, j * 128:(j + 1) * 128].rearrange("(kc p) m -> p kc m", p=128),
                          reads=[self.wr], writes=[wst_r[bu]])
                    P.op('pool', lambda bg=bg: nc.gpsimd.tensor_copy(out=wbf[bg][:], in_=wst[bg][:]),
                         reads=[wst_r[bg]], writes=[wbf_r[bg]])
                    P.op('pool', lambda bu=bu: nc.gpsimd.tensor_copy(out=wbf[bu][:], in_=wst[bu][:]),
                         reads=[wst_r[bu]], writes=[wbf_r[bu]])
                    for th in range(2):
                        pb = (2 * j + th) % 2
                        for kc in range(8):
                            P.op('pe', lambda kc=kc, th=th, pb=pb, bg=bg: nc.tensor.matmul(
                                pg[pb][:], wbf[bg][:, kc, :], hT[:, kc, th * 512:(th + 1) * 512],
                                start=(kc == 0), stop=(kc == 7)),
                                reads=[wbf_r[bg], hT_r], writes=[pg_r[pb]])
                        for kc in range(8):
                            P.op('pe', lambda kc=kc, th=th, pb=pb, bu=bu: nc.tensor.matmul(
                                pu[pb][:], wbf[bu][:, kc, :], hT[:, kc, th * 512:(th + 1) * 512],
                                start=(kc == 0), stop=(kc == 7)),
                                reads=[wbf_r[bu], hT_r], writes=[pu_r[pb]])
                        P.op('act', lambda pb=pb: nc.scalar.activation(out=sg[pb][:], in_=pg[pb][:], func=AF.Silu),
                             reads=[pg_r[pb]], writes=[sg_r[pb]])
                        P.op('dve', lambda pb=pb, j=j, th=th: nc.vector.tensor_tensor(
                            act[:, j, th * 512:(th + 1) * 512], sg[pb][:], pu[pb][:], ALU.mult),
                            reads=[sg_r[pb], pu_r[pb]], writes=[act_r], conc=True)
                for tb in range(8):
                    for dh in range(2):
                        pb = (2 * tb + dh) % 2
                        for j in range(NFC):
                            P.op('pe', lambda j=j, tb=tb, dh=dh, pb=pb: nc.tensor.matmul(
                                po[pb][:], act[:, j, tb * 128:(tb + 1) * 128], wd[:, j, dh * 512:(dh + 1) * 512],
                                start=(j == 0), stop=(j == NFC - 1)),
                                reads=[act_r, wd_r], writes=[po_r[pb]])
                        P.op('dve', lambda tb=tb, dh=dh, pb=pb: nc.vector.scalar_tensor_tensor(
                            xs[:, tb, dh * 512:(dh + 1) * 512], po[pb][:], 0.5, xs[:, tb, dh * 512:(dh + 1) * 512],
                            ALU.mult, ALU.add),
                            reads=[po_r[pb], xs_rs[tb]], writes=[xs_rs[tb]])
                    r0 = st * T + tb * 128
                    P.dma('sp', dst_ap[r0:r0 + 128, :], xs[:, tb, :], reads=[xs_rs[tb]], writes=[dst_r], conc=True)
            P.barrier()

    def attn_bias_setup(self, es_glob):
        nc, P = self.nc, self.P
        self.EB, self.EB_r = self.sb(es_glob, "EB", [128, 12, 256], F32)
        with ExitStack() as es:
            tb_, tb_r = self.sb(es, "relb", [32, 12], F32)
            oh, oh_r = self.sb(es, "oh", [32, 3, 512], F32)
            bm, bm_r = self.sb(es, "bm", [4, 512], F32)
            ee, ee_r = self.sb(es, "ee", [4, 512], F32)
            pp, pp_r = self.ps(es, "pbias", [4, 512], F32)
            P.dma('sp', tb_[:], self.w['rel_bias'].ap(), reads=[self.wr], writes=[tb_r])
            P.dma('sp', oh[:], self.c['oh'].ap().rearrange("g b i -> b g i"), reads=[self.wr], writes=[oh_r])
            P.dma('sp', bm[:], self.c['bm'].ap(), reads=[self.wr], writes=[bm_r])
            for g in range(3):
                P.op('pe', lambda g=g: nc.tensor.matmul(pp[:], tb_[:, g * 4:(g + 1) * 4], oh[:, g, :], start=True, stop=True),
                     reads=[tb_r, oh_r], writes=[pp_r])
                P.op('act', lambda: nc.scalar.activation(out=ee[:], in_=pp[:], func=AF.Exp), reads=[pp_r], writes=[ee_r])
                P.op('dve', lambda: nc.vector.tensor_tensor(ee[:], ee[:], bm[:], ALU.mult), reads=[ee_r, bm_r], writes=[ee_r])
                P.dma('sp', self.rv_t.ap()[g * 4:(g + 1) * 4, :], ee[:], reads=[ee_r], writes=[self.rv_r], conc=True)
            ebr, ebr_r = self.sb(es, "ebr", [128, 12, 256], F32)
            jrev, jrev_r = self.sb(es, "jrev", [128, 128], F32)
            pj, pj_r = self.ps(es, "pj", [128, 512], F32)
            P.dma('sp', jrev[:], self.c['jrev'].ap(), reads=[self.wr], writes=[jrev_r])
            for h in range(12):
                srcA = bass.AP(self.rv_t, h * 512 + 192, [[1, 128], [1, 128]])
                srcB = bass.AP(self.rv_t, h * 512 + 64, [[1, 128], [1, 128]])
                P.dma('sp', ebr[:, h, 0:128], srcA, reads=[self.rv_r], writes=[ebr_r], conc=True)
                P.dma('sp', ebr[:, h, 128:256], srcB, reads=[self.rv_r], writes=[ebr_r], conc=True)
            for hp in range(6):
                P.op('pe', lambda hp=hp: nc.tensor.matmul(pj[:], jrev[:], ebr[:, 2 * hp:2 * hp + 2, :].rearrange("p h e -> p (h e)"),
                                                          start=True, stop=True),
                     reads=[jrev_r, ebr_r], writes=[pj_r])
                P.op('act', lambda hp=hp: nc.scalar.copy(out=self.EB[:, 2 * hp:2 * hp + 2, :].rearrange("p h e -> p (h e)"),
                                                         in_=pj[:]),
                     reads=[pj_r], writes=[self.EB_r], conc=True)
            P.barrier()

    def mix_proj(self, l, src):
        nc, P = self.nc, self.P
        src_t, src_r = src
        w_in = self.w['w_in'].ap()
        HP = SEQ + 2
        with ExitStack() as es:
            ident, ident_r = self.load_const_tiles(es)
            gbc = self.sb(es, "gbc", [128, D_MODEL], F32)
            P.dma('sp', gbc[0][:], bcast_rows(self.w['mix_norm'], l * D_MODEL, D_MODEL), reads=[self.wr], writes=[gbc[1]])
            hT, hT_r = self.hT, self.hT_r
            P.op('dve', lambda: nc.vector.memset(hT[:, :, 0:1], 0.0), writes=[hT_r], conc=True)
            P.op('dve', lambda: nc.vector.memset(hT[:, :, HP - 1:HP], 0.0), writes=[hT_r], conc=True)
            with ExitStack() as es2:
                xs, _ = self.sb(es2, "xs", [128, 2, D_MODEL], F32)
                xs_rs = [P.reg("xsm%d" % i) for i in range(2)]
                junk, junk_r = self.sb(es2, "junk", [128, D_MODEL], F32)
                ss, _ = self.sb(es2, "ss", [128, NTB], F32)
                ss_r = [P.reg("ssm%d" % i) for i in range(NTB)]
                rs, _ = self.sb(es2, "rs", [128, NTB], F32)
                rs_r = [P.reg("rsm%d" % i) for i in range(NTB)]
                hb, hb_r, pst, pst_r = [], [], [], []
                for i in range(2):
                    a, b = self.sb(es2, "hb", [128, D_MODEL], BF16)
                    hb.append(a)
                    hb_r.append(b)
                    a, b = self.ps(es2, "pst", [128, 8, 128], BF16)
                    pst.append(a)
                    pst_r.append(b)
                tiles = (xs, xs_rs, junk, junk_r, ss, ss_r, rs, rs_r, hb, hb_r, pst, pst_r, ident, ident_r)
                self.norm_to_hT(es2, src_t, src_r, 0, NTB, gbc, hT, hT_r, 1, tiles)
                P.barrier()
            pA, pA_r, pS, pS_r = [], [], [], []
            for i in range(2):
                a, b = self.ps(es, "pA", [128, 512], F32)
                pA.append(a)
                pA_r.append(b)
                a, b = self.ps(es, "pS", [128, 512], F32)
                pS.append(a)
                pS_r.append(b)
            with ExitStack() as es2:
                wst, wst_r, wbf, wbf_r = [], [], [], []
                for i in range(2):
                    a, b = self.sb(es2, "wstq", [128, 8, 128], F32)
                    wst.append(a)
                    wst_r.append(b)
                    a, b = self.sb(es2, "wbfq", [128, 8, 128], BF16)
                    wbf.append(a)
                    wbf_r.append(b)
                onesb, onesb_r = self.sb(es2, "onesb", [128, 128], F32)
                P.dma('sp', onesb[:], self.c['ones_blk'].ap(), reads=[self.wr], writes=[onesb_r])
                gq, gq_r = self.sb(es2, "gq", [128, 1], F32)
                gk, gk_r = self.sb(es2, "gk", [128, 1], F32)
                for hlf in range(2):
                    P.dma('sp', gq[hlf * 64:(hlf + 1) * 64, :], bass.AP(self.w['q_norm'], l * 64, [[1, 64], [1, 1]]),
                          reads=[self.wr], writes=[gq_r], conc=True)
                    P.dma('sp', gk[hlf * 64:(hlf + 1) * 64, :], bass.AP(self.w['k_norm'], l * 64, [[1, 64], [1, 1]]),
                          reads=[self.wr], writes=[gk_r], conc=True)
                P.op('dve', lambda: nc.vector.tensor_scalar(gq[:], gq[:], 0.125, None, ALU.mult), reads=[gq_r], writes=[gq_r])
                bgt, bgt_r = self.sb(es2, "bgt", [128, 16], F32)
                P.dma('sp', bgt[:], bass.AP(self.w['b_gate'], l * 2048, [[1, 128], [128, 16]]), reads=[self.wr], writes=[bgt_r],
                      allow_slow_non_contiguous=True)
                qf, qf_r, sq, sq_r, rr, rr_r, qn, qn_r, gs, gs_r = [], [], [], [], [], [], [], [], [], []
                for i in range(2):
                    for lst, lstr, nm, dt in ((qf, qf_r, "qf", F32), (sq, sq_r, "sq", F32), (rr, rr_r, "rr", F32),
                                              (qn, qn_r, "qn", BF16), (gs, gs_r, "gs", F32)):
                        a, b = self.sb(es2, nm, [128, 512], dt)
                        lst.append(a)
                        lstr.append(b)
                it = 0
                qpend = [None]
                for c in range(12):
                    wb = c % 2
                    col0 = 1536 + c * 128
                    P.dma('sp', wst[wb][:], w_in[l, :, col0:col0 + 128].rearrange("(kc p) m -> p kc m", p=128),
                          reads=[self.wr], writes=[wst_r[wb]])
                    P.op('pool', lambda wb=wb: nc.gpsimd.tensor_copy(out=wbf[wb][:], in_=wst[wb][:]),
                         reads=[wst_r[wb]], writes=[wbf_r[wb]])
                    gain, gain_r = (gq, gq_r) if c < 6 else (gk, gk_r)
                    for tq in range(8):
                        pb = it % 2
                        it += 1
                        for kc in range(8):
                            P.op('pe', lambda kc=kc, tq=tq, pb=pb, wb=wb: nc.tensor.matmul(
                                pA[pb][:], wbf[wb][:, kc, :], hT[:, kc, 1 + tq * 512:1 + (tq + 1) * 512],
                                start=(kc == 0), stop=(kc == 7)),
                                reads=[wbf_r[wb], hT_r], writes=[pA_r[pb]])
                        P.op('act', lambda pb=pb: nc.scalar.copy(out=qf[pb][:], in_=pA[pb][:]), reads=[pA_r[pb]], writes=[qf_r[pb]])
                        P.op('dve', lambda pb=pb: nc.vector.tensor_tensor(sq[pb][:], qf[pb][:], qf[pb][:], ALU.mult),
                             reads=[qf_r[pb]], writes=[sq_r[pb]])
                        if qpend[0] is not None:
                            qpend[0]()

                        def qback(pb=pb, gain=gain, gain_r=gain_r, c=c, tq=tq):
                            P.op('pe', lambda: nc.tensor.matmul(pS[pb][:], onesb[:], sq[pb][:], start=True, stop=True),
                                 reads=[onesb_r, sq_r[pb]], writes=[pS_r[pb]])
                            P.op('act', lambda: nc.scalar.activation(out=rr[pb][:], in_=pS[pb][:], func=AF.Ln,
                                                                     scale=1.0 / 64, bias=self.eps_t[:]),
                                 reads=[pS_r[pb], self.eps_r], writes=[rr_r[pb]])
                            P.op('act', lambda: nc.scalar.activation(out=rr[pb][:], in_=rr[pb][:], func=AF.Exp, scale=-0.5),
                                 reads=[rr_r[pb]], writes=[rr_r[pb]])
                            P.op('dve', lambda: nc.vector.scalar_tensor_tensor(
                                qn[pb][:], qf[pb][:], gain[:], rr[pb][:], ALU.mult, ALU.mult),
                                reads=[qf_r[pb], gain_r, rr_r[pb]], writes=[qn_r[pb]])
                            P.dma('sp', self.qk_t.ap()[c, :, tq * 512:(tq + 1) * 512], qn[pb][:], reads=[qn_r[pb]],
                                  writes=[self.qk_r], conc=True)
                        qpend[0] = qback
                if qpend[0] is not None:
                    qpend[0]()
                w_gate = self.w['w_gate'].ap()
                for c in range(16):
                    wb = c % 2
                    P.dma('sp', wst[wb][:], w_gate[l, :, c * 128:(c + 1) * 128].rearrange("(kc p) m -> p kc m", p=128),
                          reads=[self.wr], writes=[wst_r[wb]])
                    P.op('pool', lambda wb=wb: nc.gpsimd.tensor_copy(out=wbf[wb][:], in_=wst[wb][:]),
                         reads=[wst_r[wb]], writes=[wbf_r[wb]])
                    for tq in range(8):
                        pb = it % 2
                        it += 1
                        for kc in range(8):
                            P.op('pe', lambda kc=kc, tq=tq, pb=pb, wb=wb: nc.tensor.matmul(
                                pA[pb][:], wbf[wb][:, kc, :], hT[:, kc, 1 + tq * 512:1 + (tq + 1) * 512],
                                start=(kc == 0), stop=(kc == 7)),
                                reads=[wbf_r[wb], hT_r], writes=[pA_r[pb]])
                        P.op('act', lambda pb=pb, c=c: nc.scalar.activation(out=gs[pb][:], in_=pA[pb][:], func=AF.Sigmoid,
                                                                            bias=bgt[:, c:c + 1]),
                             reads=[pA_r[pb], bgt_r], writes=[gs_r[pb]])
                        P.dma('sp', self.gt_t.ap()[c, :, tq * 512:(tq + 1) * 512], gs[pb][:], reads=[gs_r[pb]],
                              writes=[self.gt_r], conc=True)
                P.barrier()
            with ExitStack() as es2:
                wsv, wsv_r = self.sb(es2, "wsv", [128, 8, 256], F32)
                wbv, wbv_r = self.sb(es2, "wbv", [128, 8, 256], BF16)
                zt, zt_r = self.sb(es2, "zt", [64, 512], BF16)
                P.op('dve', lambda: nc.vector.memset(zt[:], 0.0), writes=[zt_r])
                vst, vst_r = [], []
                for i in range(2):
                    a, b = self.sb(es2, "vst", [128, 4, 128], BF16)
                    vst.append(a)
                    vst_r.append(b)
                    P.op('dve', lambda a=a: nc.vector.memset(a[:, :, 64:128], 1.0), writes=[b])
                it = 0
                for g in range(3):
                    D = DILS[g]
                    m = SEQ // D
                    vd_t, vd_r = self.vd_t[g]
                    col0 = 3072 + g * 256
                    P.dma('sp', wsv[:], w_in[l, :, col0:col0 + 256].rearrange("(kc p) m -> p kc m", p=128),
                          reads=[self.wr], writes=[wsv_r])
                    P.op('pool', lambda: nc.gpsimd.tensor_copy(out=wbv[:], in_=wsv[:]), reads=[wsv_r], writes=[wbv_r])
                    for r in range(D):
                        P.dma('sp', vd_t.ap()[r, 0:64, :, :].rearrange("t h e -> t (h e)"), zt[:], reads=[zt_r], writes=[vd_r],
                              conc=True)
                        P.dma('sp', vd_t.ap()[r, m + 64:m + 128, :, :].rearrange("t h e -> t (h e)"), zt[:], reads=[zt_r],
                              writes=[vd_r], conc=True)
                        for b in range(m // 128):
                            pb = it % 2
                            it += 1
                            t0 = 1 + r + D * 128 * b
                            for kc in range(8):
                                P.op('pe', lambda kc=kc, t0=t0, D=D, pb=pb: nc.tensor.matmul(
                                    pA[pb][:, 0:256], hT[:, kc, sl(t0, 128, D)], wbv[:, kc, :], start=(kc == 0), stop=(kc == 7)),
                                    reads=[hT_r, wbv_r], writes=[pA_r[pb]])
                            P.op('act', lambda pb=pb: nc.scalar.copy(
                                out=vst[pb][:, :, 0:64], in_=pA[pb][:, 0:256].rearrange("p (h e) -> p h e", h=4)),
                                reads=[pA_r[pb]], writes=[vst_r[pb]])
                            P.dma('sp', vd_t.ap()[r, 64 + 128 * b:64 + 128 * (b + 1), :, :], vst[pb][:], reads=[vst_r[pb]],
                                  writes=[vd_r], conc=True)
                P.barrier()

    def mix_filters(self, l):
        nc, P = self.nc, self.P
        with ExitStack() as es:
            es_h = ExitStack()
            h2, h2_r = self.sb(es, "h2", [HID, SEQ], F32)
            w3, w3_r = self.sb(es, "w3", [HID, 2048], F32)
            ph, ph_r = [], []
            for i in range(2):
                a, b = self.ps(es, "ph", [128, 512], F32)
                ph.append(a)
                ph_r.append(b)
            zT, zT_r = self.sb(es_h, "zT", [HY_EMB, SEQ], F32)
            P.dma('sp', zT[:], self.c['zfeat'].ap(), reads=[self.wr], writes=[zT_r])
            w1, w1_r = self.sb(es_h, "w1", [HY_EMB, HID], F32)
            w2, w2_r = self.sb(es_h, "w2", [HID, HID], F32)
            P.dma('sp', w1[:], self.w['hy_filt_w1'].ap()[l], reads=[self.wr], writes=[w1_r])
            P.dma('sp', w2[:], self.w['hy_filt_w2'].ap()[l], reads=[self.wr], writes=[w2_r])
            P.dma('sp', w3[:], self.w['hy_filt_w3'].ap()[l], reads=[self.wr], writes=[w3_r])
            bq, bq_r = self.sb(es_h, "bq", [HID, 4], F32)
            P.dma('sp', bq[:, 0:1], bass.AP(self.w['hy_filt_b1'], l * 64, [[1, 64], [1, 1]]), reads=[self.wr], writes=[bq_r],
                  conc=True)
            P.dma('sp', bq[:, 2:3], bass.AP(self.w['hy_filt_b2'], l * 64, [[1, 64], [1, 1]]), reads=[self.wr], writes=[bq_r],
                  conc=True)
            for c in (0, 2):
                P.op('dve', lambda c=c: nc.vector.tensor_scalar(bq[:, c:c + 1], bq[:, c:c + 1], 0.25, None, ALU.mult),
                     reads=[bq_r], writes=[bq_r])
                P.op('dve', lambda c=c: nc.vector.tensor_scalar(bq[:, c + 1:c + 2], bq[:, c:c + 1], math.pi / 2, None, ALU.add),
                     reads=[bq_r], writes=[bq_r])
            h1, h1_r = self.sb(es_h, "h1", [HID, SEQ], F32)
            s4, s4_r = self.sb(es_h, "s4", [HID, 512], F32)
            c4, c4_r = self.sb(es_h, "c4", [HID, 512], F32)
            tt, tt_r = self.sb(es_h, "tt", [HID, 512], F32)
            for layer in range(2):
                wt, wt_r, K_, src, src_r, dstt, dst_r = ((w1, w1_r, HY_EMB, zT, zT_r, h1, h1_r) if layer == 0 else
                                                        (w2, w2_r, HID, h1, h1_r, h2, h2_r))
                bc = 2 * layer
                for tq in range(8):
                    pb = tq % 2
                    P.op('pe', lambda tq=tq, pb=pb, wt=wt, src=src, K_=K_: nc.tensor.matmul(
                        ph[pb][0:HID, :], wt[0:K_, :], src[0:K_, tq * 512:(tq + 1) * 512], start=True, stop=True),
                        reads=[wt_r, src_r], writes=[ph_r[pb]])
                    P.op('act', lambda pb=pb, bc=bc: nc.scalar.activation(out=s4[:], in_=ph[pb][0:HID, :], func=AF.Sin, scale=0.25,
                                                                          bias=bq[:, bc:bc + 1]),
                         reads=[ph_r[pb], bq_r], writes=[s4_r])
                    P.op('act', lambda pb=pb, bc=bc: nc.scalar.activation(out=c4[:], in_=ph[pb][0:HID, :], func=AF.Sin, scale=0.25,
                                                                          bias=bq[:, bc + 1:bc + 2]),
                         reads=[ph_r[pb], bq_r], writes=[c4_r])
                    P.op('dve', lambda: nc.vector.tensor_tensor(tt[:], s4[:], c4[:], ALU.mult), reads=[s4_r, c4_r], writes=[tt_r])
                    P.op('dve', lambda: nc.vector.tensor_tensor(c4[:], s4[:], s4[:], ALU.mult), reads=[s4_r], writes=[c4_r])
                    P.op('dve', lambda: nc.vector.tensor_scalar(c4[:], c4[:], -8.0, 4.0, ALU.mult, ALU.add), reads=[c4_r],
                         writes=[c4_r])
                    P.op('dve', lambda tq=tq, dstt=dstt: nc.vector.tensor_tensor(dstt[:, tq * 512:(tq + 1) * 512], tt[:], c4[:],
                                                                                ALU.mult),
                         reads=[tt_r, c4_r], writes=[dst_r], conc=True)
            P.barrier()
            es_h.close()
            onesa, onesa_r = self.sb(es, "onesa", [128, 128], F32)
            P.dma('sp', onesa[:], self.c['ones_all'].ap(), reads=[self.wr], writes=[onesa_r])
            a_t, a_r = self.sb(es, "a_t", [128, NTB, 512], BF16)
            b_t, b_r = self.sb(es, "b_t", [128, NTB, 512], BF16)
            pq, pq_r = self.ps(es, "pq", [128, 512], F32)
            pk, pk_r = [], []
            for i in range(2):
                a, b = self.ps(es, "pk", [128, 512], F32)
                pk.append(a)
                pk_r.append(b)
            dfb, dfb_r, dbb, dbb_r, kfb, kfb_r, kbb, kbb_r, sqf, sqf_r, sqb, sqb_r = ([] for _ in range(12))
            for i in range(2):
                for lst, lstr, nm in ((dfb, dfb_r, "dfb"), (dbb, dbb_r, "dbb"), (kfb, kfb_r, "kfb"), (kbb, kbb_r, "kbb"),
                                      (sqf, sqf_r, "sqf"), (sqb, sqb_r, "sqb")):
                    a, b = self.sb(es, nm, [128, 512], F32)
                    lst.append(a)
                    lstr.append(b)
            nrm, nrm_r = self.sb(es, "nrm", [128, 512], F32)
            ft, ft_r = [], []
            for i in range(2):
                a, b = self.sb(es, "ft", [128, NTB, 128], BF16)
                ft.append(a)
                ft_r.append(b)
            ko, ko_r = [], []
            for i in range(2):
                a, b = self.sb(es, "ko", [128, 512], F32)
                ko.append(a)
                ko_r.append(b)
            for o in range(2):
                cf = o * 1024
                cb = o * 1024 + 512
                for tb in range(NTB):
                    pb = tb % 2
                    P.dma('sp', dfb[pb][:], self.c['dec_f'].ap()[tb * 128:(tb + 1) * 128, :], reads=[self.wr], writes=[dfb_r[pb]])
                    P.dma('sp', dbb[pb][:], self.c['dec_b'].ap()[tb * 128:(tb + 1) * 128, :], reads=[self.wr], writes=[dbb_r[pb]])
                    P.op('pe', lambda tb=tb, pb=pb, cf=cf: nc.tensor.matmul(ph[pb][:], h2[:, tb * 128:(tb + 1) * 128],
                                                                           w3[:, cf:cf + 512], start=True, stop=True),
                         reads=[h2_r, w3_r], writes=[ph_r[pb]])
                    P.op('pe', lambda tb=tb, pb=pb, cb=cb: nc.tensor.matmul(pk[pb][:], h2[:, tb * 128:(tb + 1) * 128],
                                                                           w3[:, cb:cb + 512], start=True, stop=True),
                         reads=[h2_r, w3_r], writes=[pk_r[pb]])
                    P.op('dve', lambda pb=pb: nc.vector.tensor_tensor(kfb[pb][:], ph[pb][:], dfb[pb][:], ALU.mult),
                         reads=[ph_r[pb], dfb_r[pb]], writes=[kfb_r[pb]])
                    P.op('dve', lambda pb=pb: nc.vector.tensor_tensor(kbb[pb][:], pk[pb][:], dbb[pb][:], ALU.mult),
                         reads=[pk_r[pb], dbb_r[pb]], writes=[kbb_r[pb]])
                    P.op('act', lambda pb=pb: nc.scalar.activation(out=sqf[pb][:], in_=kfb[pb][:], func=AF.Square),
                         reads=[kfb_r[pb]], writes=[sqf_r[pb]])
                    P.op('act', lambda pb=pb: nc.scalar.activation(out=sqb[pb][:], in_=kbb[pb][:], func=AF.Square),
                         reads=[kbb_r[pb]], writes=[sqb_r[pb]])
                    P.op('pe', lambda pb=pb, tb=tb: nc.tensor.matmul(pq[:], onesa[:], sqf[pb][:], start=(tb == 0), stop=False),
                         reads=[onesa_r, sqf_r[pb]], writes=[pq_r])
                    P.op('pe', lambda pb=pb, tb=tb: nc.tensor.matmul(pq[:], onesa[:], sqb[pb][:], start=False,
                                                                     stop=(tb == NTB - 1)),
                         reads=[onesa_r, sqb_r[pb]], writes=[pq_r])
                    P.op('pool', lambda pb=pb, tb=tb: nc.gpsimd.tensor_tensor(a_t[:, tb, :], kfb[pb][:], kbb[pb][:], ALU.add),
                         reads=[kfb_r[pb], kbb_r[pb]], writes=[a_r], conc=True)
                    P.op('pool', lambda pb=pb, tb=tb: nc.gpsimd.tensor_tensor(b_t[:, tb, :], kfb[pb][:], kbb[pb][:], ALU.subtract),
                         reads=[kfb_r[pb], kbb_r[pb]], writes=[b_r], conc=True)
                P.op('act', lambda: nc.scalar.activation(out=nrm[:], in_=pq[:], func=AF.Ln, bias=self.eps_t[:]),
                     reads=[pq_r, self.eps_r], writes=[nrm_r])
                P.op('act', lambda: nc.scalar.activation(out=nrm[:], in_=nrm[:], func=AF.Exp, scale=-0.5), reads=[nrm_r],
                     writes=[nrm_r])
                for s in range(64):
                    pb = s % 2
                    P.dma('sp', ft[pb][:], self.c['Fh'].ap()[s], reads=[self.wr], writes=[ft_r[pb]])
                    srct, srcr = (a_t, a_r) if s < 32 else (b_t, b_r)
                    for nb in range(NTB):
                        P.op('pe', lambda nb=nb, pb=pb, srct=srct: nc.tensor.matmul(pk[pb][:], ft[pb][:, nb, :], srct[:, nb, :],
                                                                                  start=(nb == 0), stop=(nb == NTB - 1)),
                             reads=[ft_r[pb], srcr], writes=[pk_r[pb]])
                    P.op('dve', lambda pb=pb: nc.vector.tensor_tensor(ko[pb][:], pk[pb][:], nrm[:], ALU.mult),
                         reads=[pk_r[pb], nrm_r], writes=[ko_r[pb]])
                    P.dma('sp', self.kf_t.ap()[o, s], ko[pb][:], reads=[ko_r[pb]], writes=[self.kf_r], conc=True)
            P.barrier()

    def mix_conv(self, l):
        nc, P = self.nc, self.P
        with ExitStack() as es:
            Y, Y_r = self.sb(es, "Y", [128, 64, 512], BF16)
            ftr, ftr_r, fti, fti_r, kre, kre_r, kim, kim_r = ([] for _ in range(8))
            pre, pre_r, pim, pim_r = [], [], [], []
            for i in range(2):
                for lst, lstr, nm in ((ftr, ftr_r, "ftr"), (fti, fti_r, "fti")):
                    a, b = self.sb(es, nm, [128, NTB, 128], BF16)
                    lst.append(a)
                    lstr.append(b)
                for lst, lstr, nm in ((kre, kre_r, "kre"), (kim, kim_r, "kim")):
                    a, b = self.sb(es, nm, [128, 512], F32)
                    lst.append(a)
                    lstr.append(b)
                a, b = self.ps(es, "pre", [128, 512], F32)
                pre.append(a)
                pre_r.append(b)
                a, b = self.ps(es, "pim", [128, 512], F32)
                pim.append(a)
                pim_r.append(b)
            t1, t1_r = self.sb(es, "t1", [128, 512], F32)
            t2, t2_r = self.sb(es, "t2", [128, 512], F32)
            gt, gt_r = [], []
            for i in range(2):
                a, b = self.sb(es, "gtile", [128, 64, 128], BF16)
                gt.append(a)
                gt_r.append(b)
            pc, pc_r, gate, gate_r, zo, zo_r, zn, zn_r = ([] for _ in range(8))
            for i in range(2):
                a, b = self.ps(es, "pc", [128, 512], F32)
                pc.append(a)
                pc_r.append(b)
                for lst, lstr, nm in ((gate, gate_r, "gate"), (zo, zo_r, "zo"), (zn, zn_r, "zn")):
                    a, b = self.sb(es, nm, [128, 512], F32)
                    lst.append(a)
                    lstr.append(b)
            dbc, dbc_r = self.sb(es, "dbc", [128, 512], F32)
            for o in range(2):
                P.dma('sp', dbc[:], bcast_rows(self.w['hy_skip'], (l * 2 + o) * 512, 512), reads=[self.wr], writes=[dbc_r])
                for j in range(32):
                    pb = j % 2
                    P.dma('sp', ftr[pb][:], self.c['Fh'].ap()[j], reads=[self.wr], writes=[ftr_r[pb]])
                    P.dma('sp', fti[pb][:], self.c['Fh'].ap()[32 + j], reads=[self.wr], writes=[fti_r[pb]])
                    P.dma('sp', kre[pb][:], self.kf_t.ap()[o, j], reads=[self.kf_r], writes=[kre_r[pb]])
                    P.dma('sp', kim[pb][:], self.kf_t.ap()[o, 32 + j], reads=[self.kf_r], writes=[kim_r[pb]])
                    for nb in range(NTB):
                        P.op('pe', lambda nb=nb, pb=pb: nc.tensor.matmul(pre[pb][:], ftr[pb][:, nb, :], u[:, nb, :],
                                                                        start=(nb == 0), stop=(nb == NTB - 1)),
                             reads=[ftr_r[pb], u_r], writes=[pre_r[pb]])
                    for nb in range(NTB):
                        P.op('pe', lambda nb=nb, pb=pb: nc.tensor.matmul(pim[pb][:], fti[pb][:, nb, :], u[:, nb, :],
                                                                        start=(nb == 0), stop=(nb == NTB - 1)),
                             reads=[fti_r[pb], u_r], writes=[pim_r[pb]])
                    P.op('dve', lambda pb=pb: nc.vector.tensor_tensor(t1[:], pre[pb][:], kre[pb][:], ALU.mult),
                         reads=[pre_r[pb], kre_r[pb]], writes=[t1_r])
                    P.op('dve', lambda pb=pb: nc.vector.tensor_tensor(t2[:], pim[pb][:], kim[pb][:], ALU.mult),
                         reads=[pim_r[pb], kim_r[pb]], writes=[t2_r])
                    P.op('dve', lambda j=j: nc.vector.tensor_tensor(Y[:, j, :], t1[:], t2[:], ALU.subtract),
                         reads=[t1_r, t2_r], writes=[Y_r], conc=True)
                    P.op('dve', lambda pb=pb: nc.vector.tensor_tensor(t1[:], pre[pb][:], kim[pb][:], ALU.mult),
                         reads=[pre_r[pb], kim_r[pb]], writes=[t1_r])
                    P.op('dve', lambda pb=pb: nc.vector.tensor_tensor(t2[:], pim[pb][:], kre[pb][:], ALU.mult),
                         reads=[pim_r[pb], kre_r[pb]], writes=[t2_r])
                    P.op('dve', lambda j=j: nc.vector.tensor_tensor(Y[:, 32 + j, :], t1[:], t2[:], ALU.add),
                         reads=[t1_r, t2_r], writes=[Y_r], conc=True)
                for nb in range(NTB):
                    pb = nb % 2
                    P.dma('sp', gt[pb][:], self.c['Gh'].ap()[nb], reads=[self.wr], writes=[gt_r[pb]])
                    rows = slice(nb * 128, (nb + 1) * 128)
                    P.dma('sp', gate[pb][:], self.hy_t.ap()[rows, 512 * (1 + o):512 * (2 + o)], reads=[self.hy_r],
                          writes=[gate_r[pb]])
                    if o == 0:
                        P.dma('sp', zo[pb][:], self.hy_t.ap()[rows, 0:512], reads=[self.hy_r], writes=[zo_r[pb]])
                    else:
                        P.dma('sp', zo[pb][:], self.z1_t.ap()[rows, :], reads=[self.z1_r], writes=[zo_r[pb]])
                    for s in range(64):
                        P.op('pe', lambda s=s, pb=pb: nc.tensor.matmul(pc[pb][:], gt[pb][:, s, :], Y[:, s, :], start=(s == 0),
                                                                      stop=(s == 63)),
                             reads=[gt_r[pb], Y_r], writes=[pc_r[pb]])
                    P.op('dve', lambda pb=pb: nc.vector.tensor_tensor(zo[pb][:], zo[pb][:], dbc[:], ALU.mult),
                         reads=[zo_r[pb], dbc_r], writes=[zo_r[pb]])
                    P.op('dve', lambda pb=pb: nc.vector.tensor_tensor(zo[pb][:], pc[pb][:], zo[pb][:], ALU.add),
                         reads=[pc_r[pb], zo_r[pb]], writes=[zo_r[pb]])
                    P.op('dve', lambda pb=pb: nc.vector.tensor_tensor(zn[pb][:], zo[pb][:], gate[pb][:], ALU.mult),
                         reads=[zo_r[pb], gate_r[pb]], writes=[zn_r[pb]])
                    if o == 0 or self.dbg:
                        dstt, dstr = (self.z1_t, self.z1_r) if o == 0 else (self.z2_t, self.z2_r)
                        P.dma('sp', dstt.ap()[rows, :], zn[pb][:], reads=[zn_r[pb]], writes=[dstr], conc=True)
                    P.op('act', lambda pb=pb, nb=nb: nc.scalar.copy(out=u[:, nb, :], in_=zn[pb][:]), reads=[zn_r[pb]],
                         writes=[u_r], conc=True)
            P.barrier()

    def ct_consts(self, es):
        nc, P = self.nc, self.P
        C = {}
        for name, shp, dt in (('f1cat', [128, 256], BF16), ('g1re', [128, 128], BF16), ('g1imn', [128, 128], BF16),
                              ('tf_re', [128, 256], F32), ('tf_im', [128, 256], F32), ('tc_re', [128, 512], F32),
                              ('tc_im', [128, 512], F32), ('bd_ere', [128, 128], BF16), ('bd_eren', [128, 128], BF16),
                              ('bd_eim', [128, 128], BF16), ('bd_eimn', [128, 128], BF16)):
            t, r = self.sb(es, name, shp, dt)
            P.dma('sp', t[:], self.c[name].ap(), reads=[self.wr], writes=[r])
            C[name] = (t, r)
        C['tmp'] = [self.sb(es, "cttmp", [128, 512], F32) for _ in range(8)]
        C['tmpi'] = 0
        C['ps1'] = [self.ps(es, "ctps1", [128, 512], F32) for _ in range(2)]
        C['psx'] = [self.ps(es, "ctpsx", [128, 512], F32) for _ in range(4)]
        return C

    def ct_s1_twiddle(self, C, src, src_r, A, A_r):
        nc, P = self.nc, self.P
        f1, f1_r = C['f1cat']
        tfr, tfr_r = C['tf_re']
        tfi, tfi_r = C['tf_im']
        for q in range(8):
            ps, ps_r = C['ps1'][q % 2]
            for h in range(2):
                cg = 2 * q + h
                P.op('pe', lambda cg=cg, h=h, ps=ps: nc.tensor.matmul(
                    ps[:, h * 256:(h + 1) * 256], src[:, 4 * cg:4 * cg + 4, :].rearrange("p c r -> p (c r)"), f1[:],
                    start=True, stop=True), reads=[src_r, f1_r], writes=[ps_r])
            pv = ps[:].rearrange("p (g h k) -> p g h k", g=2, h=2)
            ts_ = (C['tmpi'] % 2) * 4
            C['tmpi'] += 1
            (t1, t1_r), (t2, t2_r), (t3, t3_r), (t4, t4_r) = C['tmp'][ts_:ts_ + 4]
            v3 = lambda t: t[:, 0:256].rearrange("p (g k) -> p g k", g=2)
            tf3 = lambda t: t[:].rearrange("p (g k) -> p g k", g=2)
            P.op('dve', lambda pv=pv, t1=t1: nc.vector.tensor_tensor(v3(t1), pv[:, :, 0, :], tf3(tfr), ALU.mult),
                 reads=[ps_r, tfr_r], writes=[t1_r])
            P.op('dve', lambda pv=pv, t2=t2: nc.vector.tensor_tensor(v3(t2), pv[:, :, 1, :], tf3(tfi), ALU.mult),
                 reads=[ps_r, tfi_r], writes=[t2_r])
            P.op('dve', lambda pv=pv, t3=t3: nc.vector.tensor_tensor(v3(t3), pv[:, :, 0, :], tf3(tfi), ALU.mult),
                 reads=[ps_r, tfi_r], writes=[t3_r])
            P.op('dve', lambda pv=pv, t4=t4: nc.vector.tensor_tensor(v3(t4), pv[:, :, 1, :], tf3(tfr), ALU.mult),
                 reads=[ps_r, tfr_r], writes=[t4_r])
            P.op('pool', lambda q=q, t1=t1, t2=t2: nc.gpsimd.tensor_tensor(A[:, 2 * q:2 * q + 2, 0, :], v3(t1), v3(t2), ALU.subtract),
                 reads=[t1_r, t2_r], writes=[A_r], conc=True)
            P.op('pool', lambda q=q, t3=t3, t4=t4: nc.gpsimd.tensor_tensor(A[:, 2 * q:2 * q + 2, 1, :], v3(t3), v3(t4), ALU.add),
                 reads=[t3_r, t4_r], writes=[A_r], conc=True)

    def mix_filters2(self, l):
        nc, P = self.nc, self.P
        with ExitStack() as es:
            es_h = ExitStack()
            h2, h2_r = self.sb(es, "h2", [HID, SEQ], F32)
            w3, w3_r = self.sb(es, "w3", [HID, 2048], F32)
            ph, ph_r = [], []
            for i in range(2):
                a, b = self.ps(es, "ph", [128, 512], F32)
                ph.append(a)
                ph_r.append(b)
            zT, zT_r = self.sb(es_h, "zT", [HY_EMB, SEQ], F32)
            P.dma('sp', zT[:], self.c['zfeat'].ap(), reads=[self.wr], writes=[zT_r])
            w1, w1_r = self.sb(es_h, "w1", [HY_EMB, HID], F32)
            w2, w2_r = self.sb(es_h, "w2", [HID, HID], F32)
            P.dma('sp', w1[:], self.w['hy_filt_w1'].ap()[l], reads=[self.wr], writes=[w1_r])
            P.dma('sp', w2[:], self.w['hy_filt_w2'].ap()[l], reads=[self.wr], writes=[w2_r])
            P.dma('sp', w3[:], self.w['hy_filt_w3'].ap()[l], reads=[self.wr], writes=[w3_r])
            bq, bq_r = self.sb(es_h, "bq", [HID, 4], F32)
            P.dma('sp', bq[:, 0:1], bass.AP(self.w['hy_filt_b1'], l * 64, [[1, 64], [1, 1]]), reads=[self.wr], writes=[bq_r],
                  conc=True)
            P.dma('sp', bq[:, 2:3], bass.AP(self.w['hy_filt_b2'], l * 64, [[1, 64], [1, 1]]), reads=[self.wr], writes=[bq_r],
                  conc=True)
            for c in (0, 2):
                P.op('dve', lambda c=c: nc.vector.tensor_scalar(bq[:, c:c + 1], bq[:, c:c + 1], 0.25, None, ALU.mult),
                     reads=[bq_r], writes=[bq_r])
                P.op('dve', lambda c=c: nc.vector.tensor_scalar(bq[:, c + 1:c + 2], bq[:, c:c + 1], math.pi / 2, None, ALU.add),
                     reads=[bq_r], writes=[bq_r])
            h1, h1_r = self.sb(es_h, "h1", [HID, SEQ], F32)
            s4, s4_r = self.sb(es_h, "s4", [HID, 512], F32)
            c4, c4_r = self.sb(es_h, "c4", [HID, 512], F32)
            tt, tt_r = self.sb(es_h, "tt", [HID, 512], F32)
            for layer in range(2):
                wt, wt_r, K_, src, src_r, dstt, dst_r = ((w1, w1_r, HY_EMB, zT, zT_r, h1, h1_r) if layer == 0 else
                                                        (w2, w2_r, HID, h1, h1_r, h2, h2_r))
                bc = 2 * layer
                for tq in range(8):
                    pb = tq % 2
                    P.op('pe', lambda tq=tq, pb=pb, wt=wt, src=src, K_=K_: nc.tensor.matmul(
                        ph[pb][0:HID, :], wt[0:K_, :], src[0:K_, tq * 512:(tq + 1) * 512], start=True, stop=True),
                        reads=[wt_r, src_r], writes=[ph_r[pb]])
                    P.op('act', lambda pb=pb, bc=bc: nc.scalar.activation(out=s4[:], in_=ph[pb][0:HID, :], func=AF.Sin, scale=0.25,
                                                                          bias=bq[:, bc:bc + 1]),
                         reads=[ph_r[pb], bq_r], writes=[s4_r])
                    P.op('act', lambda pb=pb, bc=bc: nc.scalar.activation(out=c4[:], in_=ph[pb][0:HID, :], func=AF.Sin, scale=0.25,
                                                                          bias=bq[:, bc + 1:bc + 2]),
                         reads=[ph_r[pb], bq_r], writes=[c4_r])
                    P.op('dve', lambda: nc.vector.tensor_tensor(tt[:], s4[:], c4[:], ALU.mult), reads=[s4_r, c4_r], writes=[tt_r])
                    P.op('dve', lambda: nc.vector.tensor_tensor(c4[:], s4[:], s4[:], ALU.mult), reads=[s4_r], writes=[c4_r])
                    P.op('dve', lambda: nc.vector.tensor_scalar(c4[:], c4[:], -8.0, 4.0, ALU.mult, ALU.add), reads=[c4_r],
                         writes=[c4_r])
                    P.op('dve', lambda tq=tq, dstt=dstt: nc.vector.tensor_tensor(dstt[:, tq * 512:(tq + 1) * 512], tt[:], c4[:],
                                                                                ALU.mult),
                         reads=[tt_r, c4_r], writes=[dst_r], conc=True)
            P.barrier()
            es_h.close()
            onesa, onesa_r = self.sb(es, "onesa", [128, 128], F32)
            P.dma('sp', onesa[:], self.c['ones_all'].ap(), reads=[self.wr], writes=[onesa_r])
            onesh, onesh_r = self.sb(es, "onesh", [128, 128], BF16)
            P.op('dve', lambda: nc.vector.tensor_copy(out=onesh[:], in_=onesa[:]), reads=[onesa_r], writes=[onesh_r])
            nrm, nrm_r = [], []
            for o in range(2):
                a, b = self.sb(es, "nrm", [128, 512], F32)
                nrm.append(a)
                nrm_r.append(b)
            with ExitStack() as es2:
                h2b, h2b_r = self.sb(es2, "h2b", [HID, SEQ], BF16)
                w3b, w3b_r = self.sb(es2, "w3b", [HID, 2048], BF16)
                P.op('dve', lambda: nc.vector.tensor_copy(out=h2b[:], in_=h2[:]), reads=[h2_r], writes=[h2b_r])
                P.op('pool', lambda: nc.gpsimd.tensor_copy(out=w3b[:], in_=w3[:]), reads=[w3_r], writes=[w3b_r])
                pq, pq_r = self.ps(es2, "pq", [128, 512], F32)
                pk, pk_r = [], []
                for i in range(2):
                    a, b = self.ps(es2, "pk", [128, 512], F32)
                    pk.append(a)
                    pk_r.append(b)
                dcr, dcr_r, kk, kk_r, ks, ks_r = [], [], [], [], [], []
                for i in range(2):
                    a, b = self.sb(es2, "dcr", [128, 2, 512], F32)
                    dcr.append(a)
                    dcr_r.append(b)
                    a, b = self.sb(es2, "kk", [128, 2, 512], F32)
                    kk.append(a)
                    kk_r.append(b)
                    a, b = self.sb(es2, "ks", [128, 2, 512], BF16)
                    ks.append(a)
                    ks_r.append(b)
                apend = [None]
                for o in range(2):
                    for r in range(32):
                        pb = r % 2
                        P.dma('sp', dcr[pb][:], self.c['decr'].ap()[r], reads=[self.wr], writes=[dcr_r[pb]])
                        P.op('pe', lambda r=r, pb=pb, o=o: nc.tensor.matmul(ph[pb][:], h2b[:, sl(r, 128, 32)],
                                                                           w3b[:, o * 1024:o * 1024 + 512], start=True, stop=True),
                             reads=[h2b_r, w3b_r], writes=[ph_r[pb]])
                        P.op('pe', lambda r=r, pb=pb, o=o: nc.tensor.matmul(pk[pb][:], h2b[:, sl(r, 128, 32)],
                                                                           w3b[:, o * 1024 + 512:o * 1024 + 1024], start=True,
                                                                           stop=True),
                             reads=[h2b_r, w3b_r], writes=[pk_r[pb]])
                        P.op('dve', lambda pb=pb: nc.vector.tensor_tensor(kk[pb][:, 0, :], ph[pb][:], dcr[pb][:, 0, :], ALU.mult),
                             reads=[ph_r[pb], dcr_r[pb]], writes=[kk_r[pb]], conc=True)
                        P.op('dve', lambda pb=pb: nc.vector.tensor_tensor(kk[pb][:, 1, :], pk[pb][:], dcr[pb][:, 1, :], ALU.mult),
                             reads=[pk_r[pb], dcr_r[pb]], writes=[kk_r[pb]], conc=True)
                        P.op('act', lambda pb=pb: nc.scalar.activation(out=ks[pb][:], in_=kk[pb][:], func=AF.Square),
                             reads=[kk_r[pb]], writes=[ks_r[pb]])
                        if apend[0] is not None:
                            apend[0]()

                        def aback(pb=pb, r=r):
                            P.op('pe', lambda: nc.tensor.matmul(pq[:], onesh[:], ks[pb][:, 0, :], start=(r == 0), stop=False),
                                 reads=[onesh_r, ks_r[pb]], writes=[pq_r])
                            P.op('pe', lambda: nc.tensor.matmul(pq[:], onesh[:], ks[pb][:, 1, :], start=False, stop=(r == 31)),
                                 reads=[onesh_r, ks_r[pb]], writes=[pq_r])
                        apend[0] = aback
                    apend[0]()
                    apend[0] = None
                    P.op('act', lambda o=o: nc.scalar.activation(out=nrm[o][:], in_=pq[:], func=AF.Ln, bias=self.eps_t[:]),
                         reads=[pq_r, self.eps_r], writes=[nrm_r[o]])
                    P.op('act', lambda o=o: nc.scalar.activation(out=nrm[o][:], in_=nrm[o][:], func=AF.Exp, scale=-0.5),
                         reads=[nrm_r[o]], writes=[nrm_r[o]])
                P.barrier()
            for o in range(2):
                for dirn in range(2):
                    cs = slice(o * 1024 + dirn * 512, o * 1024 + dirn * 512 + 512)
                    P.op('dve', lambda cs=cs, o=o: nc.vector.tensor_tensor(w3[:, cs], w3[:, cs], nrm[o][0:HID, :], ALU.mult),
                         reads=[w3_r, nrm_r[o]], writes=[w3_r])
            h2c, h2c_r = self.sb(es, "h2c", [HID, SEQ], BF16)
            w3c, w3c_r = self.sb(es, "w3c", [HID, 2048], BF16)
            P.op('dve', lambda: nc.vector.tensor_copy(out=h2c[:], in_=h2[:]), reads=[h2_r], writes=[h2c_r])
            P.op('pool', lambda: nc.gpsimd.tensor_copy(out=w3c[:], in_=w3[:]), reads=[w3_r], writes=[w3c_r])
            C = self.ct_consts(es)
            dec2, dec2_r, ktf, ktf_r, ktb, ktb_r, Af, Af_r, Ab, Ab_r, Kt, Kt_r, db, db_r = ([] for _ in range(14))
            for i in range(2):
                for lst, lstr, nm, shp, dt in ((dec2, dec2_r, "dec2", [128, 32, 2, 64], F32), (ktf, ktf_r, "ktf", [128, 64, 32], BF16),
                                               (ktb, ktb_r, "ktb", [128, 64, 32], BF16), (Af, Af_r, "Af", [128, 16, 2, 128], BF16),
                                               (Ab, Ab_r, "Ab", [128, 16, 2, 128], BF16), (Kt, Kt_r, "Kt", [128, 4, 2, 128], F32),
                                               (db, db_r, "db", [1, 64], F32)):
                    a, b = self.sb(es, nm, shp, dt)
                    lst.append(a)
                    lstr.append(b)
            bde, bde_r = C['bd_ere']
            bden, bden_r = C['bd_eren']
            bdi, bdi_r = C['bd_eim']
            bdin, bdin_r = C['bd_eimn']
            it = 0
            ipb = 0
            bpend = [None]
            bctr = [0]
            for o in range(2):
                for bt in range(8):
                    b2 = it % 2
                    it += 1
                    c0 = 64 * bt
                    P.dma('sp', dec2[b2][:], self.c['dec2'].ap()[bt], reads=[self.wr], writes=[dec2_r[b2]])
                    P.dma('sp', db[b2][:], bass.AP(self.w['hy_skip'], (l * 2 + o) * 512 + c0, [[1, 1], [1, 64]]), reads=[self.wr],
                          writes=[db_r[b2]])
                    for dirn in range(2):
                        col = o * 1024 + dirn * 512 + c0
                        kt, kt_r = (ktf[b2], ktf_r[b2]) if dirn == 0 else (ktb[b2], ktb_r[b2])
                        for q in range(4):
                            pb = ipb % 2
                            ipb += 1
                            for j in range(8):
                                r = 8 * q + j
                                P.op('pe', lambda r=r, pb=pb, j=j, col=col: nc.tensor.matmul(
                                    ph[pb][:, j * 64:(j + 1) * 64], h2c[:, sl(r, 128, 32)], w3c[:, col:col + 64], start=True, stop=True),
                                    reads=[h2c_r, w3c_r], writes=[ph_r[pb]])
                            P.op('dve', lambda pb=pb, b2=b2, q=q, dirn=dirn, kt=kt: nc.vector.tensor_tensor(
                                kt[:, :, 8 * q:8 * q + 8].rearrange("p c r -> p r c"),
                                ph[pb][:].rearrange("p (r c) -> p r c", r=8), dec2[b2][:, 8 * q:8 * q + 8, dirn, :], ALU.mult),
                                reads=[ph_r[pb], dec2_r[b2]], writes=[kt_r], conc=True)
                    P.op('dve', lambda b2=b2: nc.vector.tensor_tensor(ktf[b2][0:1, :, 0], ktf[b2][0:1, :, 0], db[b2][:], ALU.add),
                         reads=[ktf_r[b2], db_r[b2]], writes=[ktf_r[b2]])
                    self.ct_s1_twiddle(C, ktf[b2], ktf_r[b2], Af[b2], Af_r[b2])
                    self.ct_s1_twiddle(C, ktb[b2], ktb_r[b2], Ab[b2], Ab_r[b2])
                    if bpend[0] is not None:
                        bpend[0]()

                    def bback(b2=b2, o=o, bt=bt):
                        ig = bctr[0]
                        bctr[0] += 4
                        for g in range(4):
                            k2b = ig % 2
                            ig += 1
                            (pre, pre_r), (pim, pim_r) = C['psx'][2 * k2b], C['psx'][2 * k2b + 1]
                            fr = Af[b2][:, 4 * g:4 * g + 4, 0, :]
                            fi = Af[b2][:, 4 * g:4 * g + 4, 1, :]
                            br = Ab[b2][:, 4 * g:4 * g + 4, 0, :]
                            bi = Ab[b2][:, 4 * g:4 * g + 4, 1, :]
                            seq_re = ((bde, bde_r, fr), (bdin, bdin_r, fi), (bde, bde_r, br), (bdin, bdin_r, bi))
                            seq_im = ((bde, bde_r, fi), (bdi, bdi_r, fr), (bden, bden_r, bi), (bdin, bdin_r, br))
                            for (pp, pp_r, seq) in ((pre, pre_r, seq_re), (pim, pim_r, seq_im)):
                                for n_, (wt_, wt_r_, rhs_) in enumerate(seq):
                                    P.op('pe', lambda pp=pp, wt_=wt_, rhs_=rhs_, n_=n_: nc.tensor.matmul(
                                        pp[:], wt_[:], rhs_, start=(n_ == 0), stop=(n_ == 3)),
                                        reads=[wt_r_, Af_r[b2], Ab_r[b2]], writes=[pp_r])
                            P.op('act', lambda pre=pre, k2b=k2b: nc.scalar.copy(
                                out=Kt[k2b][:, :, 0, :], in_=pre[:].rearrange("p (g k) -> p g k", g=4)),
                                reads=[pre_r], writes=[Kt_r[k2b]], conc=True)
                            P.op('act', lambda pim=pim, k2b=k2b: nc.scalar.copy(
                                out=Kt[k2b][:, :, 1, :], in_=pim[:].rearrange("p (g k) -> p g k", g=4)),
                                reads=[pim_r], writes=[Kt_r[k2b]], conc=True)
                            P.dma('sp', self.kf_t.ap()[o, bt * 4 + g], Kt[k2b][:], reads=[Kt_r[k2b]], writes=[self.kf_r], conc=True)
                    bpend[0] = bback
            bpend[0]()
            P.barrier()

    def mix_hyconv(self, l):
        nc, P = self.nc, self.P
        hT, hT_r = self.hT, self.hT_r
        w_in = self.w['w_in'].ap()
        with ExitStack() as es:
            ident, ident_r = self.load_const_tiles(es)
            C = self.ct_consts(es)
            bde, bde_r = C['bd_ere']
            bdi, bdi_r = C['bd_eim']
            bdin, bdin_r = C['bd_eimn']
            g1r, g1r_r = C['g1re']
            g1i, g1i_r = C['g1imn']
            tcr, tcr_r = C['tc_re']
            tci, tci_r = C['tc_im']
            wsth, wsth_r, cwb, cwb_r, bbc, bbc_r, hb3, hb3_r = ([] for _ in range(8))
            for i in range(2):
                a, b = self.sb(es, "wsth", [128, 8, 192], F32)
                wsth.append(a)
                wsth_r.append(b)
                a, b = self.sb(es, "cwb", [128, 3, 192], F32)
                cwb.append(a)
                cwb_r.append(b)
                a, b = self.sb(es, "bbc", [128, 192], F32)
                bbc.append(a)
                bbc_r.append(b)
                a, b = self.sb(es, "hb3", [128, 3, 64, 32], BF16)
                hb3.append(a)
                hb3_r.append(b)
            wj, wj_r = [], []
            for j in range(3):
                a, b = self.sb(es, "wj", [128, 8, 192], BF16)
                wj.append(a)
                wj_r.append(b)
            A, A_r = self.sb(es, "A", [128, 16, 2, 128], BF16)
            Y, Y_r = self.sb(es, "Y", [128, 16, 2, 128], BF16)
            Z, Z_r = self.sb(es, "Z", [128, 2, 2048], BF16)
            Kt, Kt_r = [], []
            for i in range(2):
                a, b = self.sb(es, "Ktl", [128, 4, 2, 128], F32)
                Kt.append(a)
                Kt_r.append(b)
            pp0, pp0_r = self.ps(es, "pproj", [128, 512], F32)
            pproj = [pp0, pp0]
            pproj_r = [pp0_r, pp0_r]
            pstt, pstt_r = self.ps(es, "pstt", [128, 4, 128], BF16)
            yst, yst_r = [], []
            for i in range(2):
                a, b = self.sb(es, "yst", [128, SEQ], BF16)
                yst.append(a)
                yst_r.append(b)
            ctr = {'ik': 0, 'ip': 0}

            def tmps():
                ts_ = (C['tmpi'] % 2) * 4
                C['tmpi'] += 1
                return C['tmp'][ts_:ts_ + 4]
            v4 = lambda t: t[:].rearrange("p (g k) -> p g k", g=4)

            def prep_weights(bt):
                b2 = bt % 2
                c0 = 64 * bt
                for part in range(3):
                    col = part * 512 + c0
                    P.dma('sp', wsth[b2][:, :, part * 64:(part + 1) * 64],
                          w_in[l, :, col:col + 64].rearrange("(kc p) m -> p kc m", p=128), reads=[self.wr], writes=[wsth_r[b2]],
                          conc=True)
                    for j in range(3):
                        P.dma('sp', cwb[b2][:, j, part * 64:(part + 1) * 64],
                              bcast_rows(self.w['hy_conv_w'], (l * 3 + j) * 1536 + col, 64), reads=[self.wr], writes=[cwb_r[b2]],
                              conc=True)
                    P.dma('sp', bbc[b2][:, part * 64:(part + 1) * 64], bcast_rows(self.w['hy_conv_b'], l * 1536 + col, 64),
                          reads=[self.wr], writes=[bbc_r[b2]], conc=True)
                for j in range(3):
                    for kc in range(8):
                        P.op('pool', lambda j=j, kc=kc, b2=b2: nc.gpsimd.tensor_tensor(wj[j][:, kc, :], wsth[b2][:, kc, :],
                                                                                      cwb[b2][:, j, :], ALU.mult),
                             reads=[wsth_r[b2], cwb_r[b2]], writes=[wj_r[j]], conc=True)

            def proj_chunk(bt, q):
                b2 = bt % 2
                H, H_r = hb3[b2], hb3_r[b2]
                for r in range(4 * q, 4 * q + 4):
                    pb = ctr['ip'] % 2
                    ctr['ip'] += 1
                    n = 0
                    for j in range(3):
                        for kc in range(8):
                            t0 = 1 + r + (j - 1)
                            P.op('pe', lambda j=j, kc=kc, t0=t0, pb=pb, n=n: nc.tensor.matmul(
                                pproj[pb][:, 0:192], hT[:, kc, sl(t0, 128, 32)], wj[j][:, kc, :], start=(n == 0), stop=(n == 23)),
                                reads=[hT_r, wj_r[j]], writes=[pproj_r[pb]])
                            n += 1
                    P.op('dve', lambda pb=pb, r=r, H=H, b2=b2: nc.vector.tensor_tensor(
                        H[:, :, :, r], pproj[pb][:, 0:192].rearrange("p (a c) -> p a c", a=3),
                        bbc[b2][:].rearrange("p (a c) -> p a c", a=3), ALU.add),
                        reads=[pproj_r[pb], bbc_r[b2]], writes=[H_r], conc=True)

            def conv_stage(bt, o, k):
                b2 = bt % 2
                H, H_r = hb3[b2], hb3_r[b2]
                if k == 0:
                    self.ct_s1_twiddle(C, H[:, 0, :, :], H_r, A, A_r)
                elif k == 1:
                    for g in range(4):
                        k2b = ctr['ik'] % 2
                        ctr['ik'] += 1
                        (pre, pre_r), (pim, pim_r) = C['psx'][2 * k2b], C['psx'][2 * k2b + 1]
                        P.dma('sp', Kt[k2b][:], self.kf_t.ap()[o, bt * 4 + g], reads=[self.kf_r], writes=[Kt_r[k2b]])
                        ar = A[:, 4 * g:4 * g + 4, 0, :]
                        ai = A[:, 4 * g:4 * g + 4, 1, :]
                        for (pp, pp_r, seq) in ((pre, pre_r, ((bde, bde_r, ar), (bdin, bdin_r, ai))),
                                                (pim, pim_r, ((bde, bde_r, ai), (bdi, bdi_r, ar)))):
                            for n_, (wt_, wt_r_, rhs_) in enumerate(seq):
                                P.op('pe', lambda pp=pp, wt_=wt_, rhs_=rhs_, n_=n_: nc.tensor.matmul(
                                    pp[:], wt_[:], rhs_, start=(n_ == 0), stop=(n_ == 1)),
                                    reads=[wt_r_, A_r], writes=[pp_r])
                        kre = Kt[k2b][:, :, 0, :]
                        kim = Kt[k2b][:, :, 1, :]
                        (t1, t1_r), (t2, t2_r), (t3, t3_r), (t4, t4_r) = tmps()
                        P.op('dve', lambda pre=pre, kre=kre, t1=t1: nc.vector.tensor_tensor(v4(t1), v4(pre), kre, ALU.mult),
                             reads=[pre_r, Kt_r[k2b]], writes=[t1_r])
                        P.op('dve', lambda pim=pim, kim=kim, t2=t2: nc.vector.tensor_tensor(v4(t2), v4(pim), kim, ALU.mult),
                             reads=[pim_r, Kt_r[k2b]], writes=[t2_r])
                        P.op('dve', lambda pre=pre, kim=kim, t3=t3: nc.vector.tensor_tensor(v4(t3), v4(pre), kim, ALU.mult),
                             reads=[pre_r, Kt_r[k2b]], writes=[t3_r])
                        P.op('dve', lambda pim=pim, kre=kre, t4=t4: nc.vector.tensor_tensor(v4(t4), v4(pim), kre, ALU.mult),
                             reads=[pim_r, Kt_r[k2b]], writes=[t4_r])
                        P.op('pool', lambda g=g, t1=t1, t2=t2: nc.gpsimd.tensor_tensor(Y[:, 4 * g:4 * g + 4, 0, :], v4(t1), v4(t2),
                                                                                     ALU.subtract),
                             reads=[t1_r, t2_r], writes=[Y_r], conc=True)
                        P.op('pool', lambda g=g, t3=t3, t4=t4: nc.gpsimd.tensor_tensor(Y[:, 4 * g:4 * g + 4, 1, :], v4(t3), v4(t4),
                                                                                     ALU.add),
                             reads=[t3_r, t4_r], writes=[Y_r], conc=True)
                elif k == 2:
                    for g in range(4):
                        k2b = ctr['ik'] % 2
                        ctr['ik'] += 1
                        (zre, zre_r), (zim, zim_r) = C['psx'][2 * k2b], C['psx'][2 * k2b + 1]
                        for h in range(4):
                            cg = 4 * g + h
                            yr = Y[:, cg, 0, :]
                            yi = Y[:, cg, 1, :]
                            for (pp, pp_r, seq) in ((zre, zre_r, ((yr, bde, bde_r), (yi, bdi, bdi_r))),
                                                    (zim, zim_r, ((yi, bde, bde_r), (yr, bdin, bdin_r)))):
                                for n_, (lh_, wt_, wt_r_) in enumerate(seq):
                                    P.op('pe', lambda pp=pp, lh_=lh_, wt_=wt_, n_=n_, h=h: nc.tensor.matmul(
                                        pp[:, h * 128:(h + 1) * 128], lh_, wt_[:], start=(n_ == 0), stop=(n_ == 1)),
                                        reads=[wt_r_, Y_r], writes=[pp_r])
                        (t1, t1_r), (t2, t2_r), (t3, t3_r), (t4, t4_r) = tmps()
                        P.op('dve', lambda zre=zre, t1=t1: nc.vector.tensor_tensor(t1[:], zre[:], tcr[:], ALU.mult),
                             reads=[zre_r, tcr_r], writes=[t1_r])
                        P.op('dve', lambda zim=zim, t2=t2: nc.vector.tensor_tensor(t2[:], zim[:], tci[:], ALU.mult),
                             reads=[zim_r, tci_r], writes=[t2_r])
                        P.op('dve', lambda zre=zre, t3=t3: nc.vector.tensor_tensor(t3[:], zre[:], tci[:], ALU.mult),
                             reads=[zre_r, tci_r], writes=[t3_r])
                        P.op('dve', lambda zim=zim, t4=t4: nc.vector.tensor_tensor(t4[:], zim[:], tcr[:], ALU.mult),
                             reads=[zim_r, tcr_r], writes=[t4_r])
                        P.op('pool', lambda g=g, t1=t1, t2=t2: nc.gpsimd.tensor_tensor(Z[:, 0, g * 512:(g + 1) * 512], t1[:], t2[:],
                                                                                     ALU.subtract),
                             reads=[t1_r, t2_r], writes=[Z_r], conc=True)
                        P.op('pool', lambda g=g, t3=t3, t4=t4: nc.gpsimd.tensor_tensor(Z[:, 1, g * 512:(g + 1) * 512], t3[:], t4[:],
                                                                                     ALU.add),
                             reads=[t3_r, t4_r], writes=[Z_r], conc=True)
                else:
                    for g in range(4):
                        k2b = ctr['ik'] % 2
                        ctr['ik'] += 1
                        ps, ps_r = C['psx'][2 * k2b]
                        P.op('pe', lambda ps=ps, g=g: nc.tensor.matmul(ps[:], g1r[:], Z[:, 0, g * 512:(g + 1) * 512], start=True,
                                                                       stop=False), reads=[g1r_r, Z_r], writes=[ps_r])
                        P.op('pe', lambda ps=ps, g=g: nc.tensor.matmul(ps[:], g1i[:], Z[:, 1, g * 512:(g + 1) * 512], start=False,
                                                                       stop=True), reads=[g1i_r, Z_r], writes=[ps_r])
                        P.op('dve', lambda ps=ps, g=g, H=H, o=o: nc.vector.tensor_tensor(
                            H[:, 0, 16 * g:16 * g + 16, :].rearrange("p c r -> p (c r)"), ps[:],
                            H[:, 1 + o, 16 * g:16 * g + 16, :].rearrange("p c r -> p (c r)"), ALU.mult),
                            reads=[ps_r, H_r], writes=[H_r])
                    if self.dbg and bt == 0:
                        P.dma('sp', self.zdbg_t.ap()[o], H[:, 0, :, :].rearrange("p c r -> p (c r)"), reads=[H_r],
                              writes=[self.zdbg_r], conc=True)

            def transposes(bt):
                b2 = bt % 2
                H, H_r = hb3[b2], hb3_r[b2]
                hb_ = bt % 2
                ys, ys_r = yst[(bt // 2) % 2], yst_r[(bt // 2) % 2]
                for rq in range(8):
                    for h in range(4):
                        r = 4 * rq + h
                        P.op('pe', lambda r=r, h=h, H=H, hb_=hb_: nc.tensor.transpose(pstt[64 * hb_:64 * hb_ + 64, h, :],
                                                                                     H[:, 0, :, r], ident[:]),
                             reads=[H_r, ident_r], writes=[pstt_r])
                    P.op('act', lambda rq=rq, hb_=hb_, ys=ys: nc.scalar.copy(
                        out=ys[64 * hb_:64 * hb_ + 64, :].rearrange("c (p r) -> c r p", r=32)[:, 4 * rq:4 * rq + 4, :],
                        in_=pstt[64 * hb_:64 * hb_ + 64, :, :]),
                        reads=[pstt_r], writes=[ys_r], conc=True)
                if hb_ == 1:
                    P.dma('sp', self.yh_t.ap()[bt // 2], ys[:], reads=[ys_r], writes=[self.yh_r], conc=True)

            prep_weights(0)
            for q in range(8):
                proj_chunk(0, q)
            for bt in range(8):
                if bt + 1 < 8:
                    prep_weights(bt + 1)
                i = 0
                for o in range(2):
                    for k in range(4):
                        conv_stage(bt, o, k)
                        if bt + 1 < 8:
                            proj_chunk(bt + 1, i)
                        i += 1
                transposes(bt)
            P.barrier()

    def mix_attn(self, l):
        nc, P = self.nc, self.P
        EB, EB_r = self.EB, self.EB_r
        with ExitStack() as es:
            EB2, EB2_r = self.sb(es, "EB2", [128, 12, 2, 256], F32)
            for j in range(2):
                P.op('pool', lambda j=j: nc.gpsimd.tensor_copy(out=EB2[:, :, j, :], in_=EB[:]), reads=[EB_r], writes=[EB2_r],
                     conc=True)
            qh, qh_r = self.sb(es, "qh", [128, SEQ], BF16)
            kh, kh_r = self.sb(es, "kh", [128, SEQ + 2048], BF16)
            vt, vt_r = [], []
            for i in range(3):
                a, b = self.sb(es, "vt", [128, 33, 128], BF16)
                vt.append(a)
                vt_r.append(b)
            NB = 3
            psc, psc_r, pov, pov_r, pe_, pe_r, pm, pm_r, ot, ot_r = ([] for _ in range(10))
            for i in range(NB):
                a, b = self.ps(es, "psc", [128, 512], F32)
                psc.append(a)
                psc_r.append(b)
                a, b = self.sb(es, "pexp", [128, 512], F32)
                pe_.append(a)
                pe_r.append(b)
                a, b = self.sb(es, "pm", [128, 512], BF16)
                pm.append(a)
                pm_r.append(b)
                a, b = self.sb(es, "ot", [128, 2, 128], F32)
                ot.append(a)
                ot_r.append(b)
            for i in range(2):
                a, b = self.ps(es, "pov", [128, 256], F32)
                pov.append(a)
                pov_r.append(b)
            it = 0
            iv = 0
            pending = [None]
            for g in range(3):
                D = DILS[g]
                m = SEQ // D
                nblk = m // 128
                nch = nblk + 1
                vd_t, vd_r = self.vd_t[g]
                for hp in range(2):
                    cq = 2 * g + hp
                    ck = 6 + 2 * g + hp
                    P.dma('sp', qh[:], self.qk_t.ap()[cq], reads=[self.qk_r], writes=[qh_r])
                    P.op('pool', lambda: nc.gpsimd.memset(kh[:], 0.0), writes=[kh_r])
                    P.dma('sp', kh[:, 64 * D:64 * D + SEQ], self.qk_t.ap()[ck], reads=[self.qk_r], writes=[kh_r])
                    for hi in range(2):
                        hh = 2 * hp + hi
                        p0 = 64 * hi
                        for r in range(D):
                            vb = iv % 3
                            iv += 1
                            P.dma('sp', vt[vb][:, 0:nch, :], vd_t.ap()[r, :, hh, :].rearrange("(c p) e -> p c e", p=128),
                                  reads=[vd_r], writes=[vt_r[vb]])
                            for b in range(0, nblk, 2):
                                pb = it % NB
                                po_ = it % 2
                                it += 1
                                for jb in range(2):
                                    q0 = r + D * 128 * (b + jb)
                                    kA = q0
                                    kB = r + D * 128 * (b + jb + 1)
                                    P.op('pe', lambda pb=pb, p0=p0, q0=q0, kA=kA, D=D, jb=jb: nc.tensor.matmul(
                                        psc[pb][:, jb * 256:jb * 256 + 128], kh[p0:p0 + 64, sl(kA, 128, D)],
                                        qh[p0:p0 + 64, sl(q0, 128, D)], start=True, stop=True),
                                        reads=[kh_r, qh_r], writes=[psc_r[pb]])
                                    P.op('pe', lambda pb=pb, p0=p0, q0=q0, kB=kB, D=D, jb=jb: nc.tensor.matmul(
                                        psc[pb][:, jb * 256 + 128:(jb + 1) * 256], kh[p0:p0 + 64, sl(kB, 128, D)],
                                        qh[p0:p0 + 64, sl(q0, 128, D)], start=True, stop=True),
                                        reads=[kh_r, qh_r], writes=[psc_r[pb]])
                                if pending[0] is not None:
                                    pending[0]()

                                def back(pb=pb, po_=po_, vb=vb, b=b, g=g, hh=hh, r=r, D=D):
                                    P.op('act', lambda: nc.scalar.activation(out=pe_[pb][:], in_=psc[pb][:], func=AF.Exp),
                                         reads=[psc_r[pb]], writes=[pe_r[pb]])
                                    P.op('dve', lambda: nc.vector.tensor_tensor(
                                        pm[pb][:], pe_[pb][:], EB2[:, g * 4 + hh, :, :].rearrange("p j e -> p (j e)"), ALU.mult),
                                        reads=[pe_r[pb], EB2_r], writes=[pm_r[pb]])
                                    for jb in range(2):
                                        P.op('pe', lambda jb=jb: nc.tensor.matmul(
                                            pov[po_][:, jb * 128:(jb + 1) * 128], pm[pb][:, jb * 256:jb * 256 + 128],
                                            vt[vb][:, b + jb, :], start=True, stop=False),
                                            reads=[pm_r[pb], vt_r[vb]], writes=[pov_r[po_]])
                                        P.op('pe', lambda jb=jb: nc.tensor.matmul(
                                            pov[po_][:, jb * 128:(jb + 1) * 128], pm[pb][:, jb * 256 + 128:(jb + 1) * 256],
                                            vt[vb][:, b + jb + 1, :], start=False, stop=True),
                                            reads=[pm_r[pb], vt_r[vb]], writes=[pov_r[po_]])
                                    P.op('act', lambda: nc.scalar.copy(
                                        out=ot[pb][:].rearrange("p j e -> p (j e)"), in_=pov[po_][:]),
                                        reads=[pov_r[po_]], writes=[ot_r[pb]])
                                    q00 = r + D * 128 * b
                                    P.dma('sp', self.att_t.ap()[g, sl(q00, 256, D), hh, :].rearrange("(j p) e -> p j e", p=128),
                                          ot[pb][:], reads=[ot_r[pb]], writes=[self.att_r], conc=True)
                                pending[0] = back
            if pending[0] is not None:
                pending[0]()
            P.barrier()

    def mix_out(self, l, src, dst):
        nc, P = self.nc, self.P
        src_t, src_r = src
        dst_t, dst_r = dst
        T = 1024
        with ExitStack() as es:
            ident, ident_r = self.load_const_tiles(es)
            whp, whp_r = self.sb(es, "whp", [128, 4, D_MODEL], BF16)
            wap, wap_r = self.sb(es, "wap", [128, 2, D_MODEL], BF16)
            wout, wout_r = self.sb(es, "wout", [128, 8, D_MODEL], BF16)
            wds, wds_r = [], []
            for i in range(2):
                a, b = self.sb(es, "wdso", [128, D_MODEL], F32)
                wds.append(a)
                wds_r.append(b)
            iw = 0
            for (wt, wt_r, nkc, name) in ((whp, whp_r, 4, 'w_hy_proj'), (wap, wap_r, 2, 'w_at_proj'), (wout, wout_r, 8, 'w_out')):
                for kc in range(nkc):
                    b2 = iw % 2
                    iw += 1
                    P.dma('sp', wds[b2][:], self.w[name].ap()[l, kc * 128:(kc + 1) * 128, :], reads=[self.wr], writes=[wds_r[b2]])
                    P.op('pool', lambda wt=wt, kc=kc, b2=b2: nc.gpsimd.tensor_copy(out=wt[:, kc, :], in_=wds[b2][:]),
                         reads=[wds_r[b2]], writes=[wt_r], conc=True)
            yhyT, yhyT_r = self.sb(es, "yhyT", [128, 4, T], BF16)
            yatT, yatT_r = self.sb(es, "yatT", [128, 2, T], BF16)
            yT, yT_r = self.sb(es, "yT", [128, 8, T], BF16)
            att, att_r, s2, s2_r, yab, yab_r, rden, rden_r = ([] for _ in range(8))
            pst, pst_r, pst2, pst2_r = [], [], [], []
            for i in range(2):
                a, b = self.sb(es, "attl", [128, 3, 4, 128], F32)
                att.append(a)
                att_r.append(b)
                a, b = self.sb(es, "s2", [128, 4, 128], F32)
                s2.append(a)
                s2_r.append(b)
                a, b = self.sb(es, "yab", [128, 4, 64], BF16)
                yab.append(a)
                yab_r.append(b)
                a, b = self.sb(es, "rden", [128, 4], F32)
                rden.append(a)
                rden_r.append(b)
            a, b = self.ps(es, "psty", [128, 4, 128], BF16)
            pst.append(a)
            pst_r.append(b)
            a, b = self.ps(es, "psta", [128, 2, 128], BF16)
            pst2.append(a)
            pst2_r.append(b)
            pa, pa_r, pbb, pbb_r, po, po_r = [], [], [], [], [], []
            gA, gA_r, gB, gB_r, ta, ta_r, tb_, tb_r, xin, xin_r, xo, xo_r = ([] for _ in range(12))
            for i in range(2):
                for lst, lstr, nm in ((pa, pa_r, "pa"), (pbb, pbb_r, "pbb"), (po, po_r, "poo")):
                    a, b = self.ps(es, nm, [128, 512], F32)
                    lst.append(a)
                    lstr.append(b)
                for lst, lstr, nm in ((gA, gA_r, "gA"), (gB, gB_r, "gB"), (ta, ta_r, "ta"), (tb_, tb_r, "tbt")):
                    a, b = self.sb(es, nm, [128, 512], F32)
                    lst.append(a)
                    lstr.append(b)
                a, b = self.sb(es, "xin", [128, D_MODEL], F32)
                xin.append(a)
                xin_r.append(b)
                a, b = self.sb(es, "xo", [128, D_MODEL], F32)
                xo.append(a)
                xo_r.append(b)
            it = 0
            for st in range(SEQ // T):
                for kc in range(4):
                    P.dma('sp', yhyT[:, kc, :], self.yh_t.ap()[kc, :, st * T:(st + 1) * T], reads=[self.yh_r], writes=[yhyT_r],
                          conc=True)
                for tb in range(8):
                    gb = st * 8 + tb
                    b2 = gb % 2
                    P.dma('sp', att[b2][:], self.att_t.ap()[:, gb * 128:(gb + 1) * 128, :, :].rearrange("g t h e -> t g h e"),
                          reads=[self.att_r], writes=[att_r[b2]])
                    P.op('dve', lambda b2=b2: nc.vector.tensor_tensor(s2[b2][:], att[b2][:, 0, :, :], att[b2][:, 1, :, :], ALU.add),
                         reads=[att_r[b2]], writes=[s2_r[b2]])
                    P.op('dve', lambda b2=b2: nc.vector.tensor_tensor(s2[b2][:], s2[b2][:], att[b2][:, 2, :, :], ALU.add),
                         reads=[att_r[b2], s2_r[b2]], writes=[s2_r[b2]])
                    P.op('dve', lambda b2=b2: nc.vector.reciprocal(rden[b2][:], s2[b2][:, :, 64]), reads=[s2_r[b2]],
                         writes=[rden_r[b2]])
                    for hh in range(4):
                        P.op('dve', lambda b2=b2, hh=hh: nc.vector.tensor_scalar(yab[b2][:, hh, :], s2[b2][:, hh, 0:64],
                                                                                rden[b2][:, hh:hh + 1], None, ALU.mult),
                             reads=[s2_r[b2], rden_r[b2]], writes=[yab_r[b2]], conc=True)
                    for c in range(2):
                        P.op('pe', lambda c=c, b2=b2: nc.tensor.transpose(
                            pst2[0][:, c, :], yab[b2][:, 2 * c:2 * c + 2, :].rearrange("p h e -> p (h e)"), ident[:]),
                            reads=[yab_r[b2], ident_r], writes=[pst2_r[0]])
                    P.op('act', lambda tb=tb: nc.scalar.copy(out=yatT[:, :, tb * 128:(tb + 1) * 128], in_=pst2[0][:]),
                         reads=[pst2_r[0]], writes=[yatT_r], conc=True)
                for dc in range(8):
                    for th in range(2):
                        pb = it % 2
                        it += 1
                        tsl = slice(st * T + th * 512, st * T + (th + 1) * 512)
                        P.dma('sp', gA[pb][:], self.gt_t.ap()[dc, :, tsl], reads=[self.gt_r], writes=[gA_r[pb]])
                        P.dma('sp', gB[pb][:], self.gt_t.ap()[8 + dc, :, tsl], reads=[self.gt_r], writes=[gB_r[pb]])
                        for kc in range(4):
                            P.op('pe', lambda kc=kc, dc=dc, th=th, pb=pb: nc.tensor.matmul(
                                pa[pb][:], whp[:, kc, dc * 128:(dc + 1) * 128], yhyT[:, kc, th * 512:(th + 1) * 512],
                                start=(kc == 0), stop=(kc == 3)), reads=[whp_r, yhyT_r], writes=[pa_r[pb]])
                        for kc in range(2):
                            P.op('pe', lambda kc=kc, dc=dc, th=th, pb=pb: nc.tensor.matmul(
                                pbb[pb][:], wap[:, kc, dc * 128:(dc + 1) * 128], yatT[:, kc, th * 512:(th + 1) * 512],
                                start=(kc == 0), stop=(kc == 1)), reads=[wap_r, yatT_r], writes=[pbb_r[pb]])
                        P.op('dve', lambda pb=pb: nc.vector.tensor_tensor(ta[pb][:], pa[pb][:], gA[pb][:], ALU.mult),
                             reads=[pa_r[pb], gA_r[pb]], writes=[ta_r[pb]])
                        P.op('dve', lambda pb=pb: nc.vector.tensor_tensor(tb_[pb][:], pbb[pb][:], gB[pb][:], ALU.mult),
                             reads=[pbb_r[pb], gB_r[pb]], writes=[tb_r[pb]])
                        P.op('pool', lambda pb=pb, dc=dc, th=th: nc.gpsimd.tensor_tensor(
                            yT[:, dc, th * 512:(th + 1) * 512], ta[pb][:], tb_[pb][:], ALU.add),
                            reads=[ta_r[pb], tb_r[pb]], writes=[yT_r], conc=True)
                for tb in range(8):
                    b2 = tb % 2
                    r0 = st * T + tb * 128
                    P.dma('sp', xin[b2][:], src_t.ap()[r0:r0 + 128, :], reads=[src_r], writes=[xin_r[b2]])
                    for dh in range(2):
                        pb = it % 2
                        it += 1
                        for kc in range(8):
                            P.op('pe', lambda kc=kc, tb=tb, dh=dh, pb=pb: nc.tensor.matmul(
                                po[pb][:], yT[:, kc, tb * 128:(tb + 1) * 128], wout[:, kc, dh * 512:(dh + 1) * 512],
                                start=(kc == 0), stop=(kc == 7)), reads=[yT_r, wout_r], writes=[po_r[pb]])
                        P.op('dve', lambda pb=pb, b2=b2, dh=dh: nc.vector.tensor_tensor(
                            xo[b2][:, dh * 512:(dh + 1) * 512], po[pb][:], xin[b2][:, dh * 512:(dh + 1) * 512], ALU.add),
                            reads=[po_r[pb], xin_r[b2]], writes=[xo_r[b2]], conc=True)
                    P.dma('sp', dst_t.ap()[r0:r0 + 128, :], xo[b2][:], reads=[xo_r[b2]], writes=[dst_r], conc=True)
            P.barrier()

    def mixer_phase(self, l, src, dst):
        upto = self.mix_upto
        order = ['filt', 'proj', 'conv', 'attn', 'out']
        n = len(order) if upto is None else order.index(upto) + 1
        names = order[:n]
        if 'filt' in names:
            self.mix_filters2(l)
        with ExitStack() as es:
            self.hT, self.hT_r = self.sb(es, "hTm", [128, 8, SEQ + 2], BF16)
            if 'proj' in names:
                self.mix_proj(l, src)
            if 'conv' in names:
                self.mix_hyconv(l)
            self.P.barrier()
        if 'attn' in names:
            self.mix_attn(l)
        if 'out' in names:
            self.mix_out(l, src, dst)
        self.P.barrier()

    def build(self):
        self.declare()
        P = self.P
        with ExitStack() as es_sem:
            self.eps_t, self.eps_r = self.sb(es_sem, "eps", [128, 1], F32)
            P.op('dve', lambda: self.nc.vector.memset(self.eps_t[:], EPS), writes=[self.eps_r])
            self.attn_bias_setup(es_sem)
            cur = (self.x_t, self.x_r)
            si = 0
            stages = self.stages
            for l in range(DEPTH):
                for ph in ('ffn1', 'mix', 'ffn2'):
                    name = "%s_%d" % (ph, l)
                    if stages is not None and name not in stages:
                        continue
                    last = (stages is not None and name == stages[-1]) or (stages is None and l == DEPTH - 1 and ph == 'ffn2')
                    dst = (self.y_t, self.y_r) if last else self.xs_t[si]
                    si += 1
                    if ph == 'ffn1':
                        self.ffn_phase(l, 1, cur, dst)
                    elif ph == 'ffn2':
                        self.ffn_phase(l, 2, cur, dst)
                    else:
                        self.mixer_phase(l, cur, dst)
                    cur = dst
            P.barrier()
            nw = P.emit(es_sem)
            print("ops", len(P.ops), "waits", nw, "slots", len(P.slot_cnt))
        return self.nc


_NC_CACHE = {}


def kernel(**inputs):
    x = np.ascontiguousarray(np.asarray(inputs['x'], dtype=np.float32))
    consts = host_constants()
    if 'nc' not in _NC_CACHE:
        _NC_CACHE['nc'] = Builder().build()
    nc = _NC_CACHE['nc']
    shared = {k: np.ascontiguousarray(np.asarray(inputs[k], dtype=np.float32)) for k in WEIGHT_NAMES}
    shared.update(consts)
    in_maps = []
    for b in range(BATCH):
        m = dict(shared)
        m['x'] = x[b]
        in_maps.append(m)
    res = run_bass_kernel_spmd(nc, in_maps, core_ids=list(range(BATCH)))
    return np.stack([np.asarray(r['y'], dtype=np.float32) for r in res.results], axis=0)
```
